# Optimizing a Trainium2 kernel written in Bass

```python
import math
import jax, jax.numpy as jnp
from jax import lax
import numpy as np

D_MODEL = 2048
BATCH = 2
SEQ = 16384
DEPTH = 1

CHUNK = 64
D_MIX = D_MODEL
D_CONV = D_MIX // 2
D_MLSTM = D_MIX - D_CONV
CONV_K = 3
M_HEADS = 8
M_HEAD_DIM = D_MLSTM // M_HEADS
D_IN = 3 * D_CONV + 4 * D_MLSTM + 2 * M_HEADS
PEER_HEADS = 8
N_KEYS = 128
N_EXPERTS = N_KEYS * N_KEYS
PEER_TOPK = 16
PEER_QDIM = 256
PEER_HALF = PEER_QDIM // 2
PEER_BLOCK = 128
ALPHA = (2 * DEPTH) ** 0.25
BETA = (8 * DEPTH) ** -0.25
LN_EPS = 1e-5

kernel_name = "hybrid_conv_mlstm_peer_deepnorm"


def layernorm(x, g, b):
    xf = x.astype(jnp.float32)
    mu = jnp.mean(xf, axis=-1, keepdims=True)
    xc = xf - mu
    var = jnp.mean(xc * xc, axis=-1, keepdims=True)
    y = xc * lax.rsqrt(var + LN_EPS) * g.astype(jnp.float32) + b.astype(jnp.float32)
    return y.astype(x.dtype)


def causal_dwconv(z, w, b):
    S = z.shape[1]
    zp = jnp.pad(z, ((0, 0), (CONV_K - 1, 0), (0, 0)))
    y = b
    for j in range(CONV_K):
        y = y + w[j] * zp[:, j:j + S]
    return y


def mlstm_chunkwise(q, k, v, logi, logf):
    Bn, S, H, dh = q.shape
    L = CHUNK
    NC = S // L
    f32 = jnp.float32
    def to_chunks(t):
        return t.astype(f32).reshape(Bn, NC, L, H, dh).transpose(1, 0, 3, 2, 4)
    qc = to_chunks(q)
    kc = to_chunks(k) * (dh ** -0.5)
    vc = to_chunks(v)
    li = logi.astype(f32).reshape(Bn, NC, L, H).transpose(1, 0, 3, 2)
    lf = logf.astype(f32).reshape(Bn, NC, L, H).transpose(1, 0, 3, 2)
    mask = jnp.tril(jnp.ones((L, L), dtype=bool))

    def step(carry, inp):
        C, n, m = carry
        qb, kb, vb, lib, lfb = inp
        bcum = jnp.cumsum(lfb, axis=-1)
        Dm = bcum[..., :, None] - bcum[..., None, :] + lib[..., None, :]
        Dm = jnp.where(mask, Dm, -jnp.inf)
        inter = bcum + m[..., None]
        m_t = jnp.maximum(inter, jnp.max(Dm, axis=-1))
        w_intra = jnp.exp(Dm - m_t[..., None])
        w_inter = jnp.exp(inter - m_t)
        s = jnp.einsum('bhld,bhjd->bhlj', qb, kb) * w_intra
        num = (jnp.einsum('bhlj,bhjd->bhld', s, vb)
               + w_inter[..., None] * jnp.einsum('bhed,bhld->bhle', C, qb))
        den = jnp.sum(s, axis=-1) + w_inter * jnp.einsum('bhd,bhld->bhl', n, qb)
        h = num / jnp.maximum(jnp.abs(den), jnp.exp(-m_t))[..., None]
        bL = bcum[..., -1]
        g = bL[..., None] - bcum + lib
        m_new = jnp.maximum(bL + m, jnp.max(g, axis=-1))
        wk = jnp.exp(g - m_new[..., None])
        decay = jnp.exp(bL + m - m_new)
        C_new = decay[..., None, None] * C + jnp.einsum('bhl,bhle,bhld->bhed', wk, vb, kb)
        n_new = decay[..., None] * n + jnp.einsum('bhl,bhld->bhd', wk, kb)
        return (C_new, n_new, m_new), h

    init = (jnp.zeros((Bn, H, dh, dh), f32), jnp.zeros((Bn, H, dh), f32),
            jnp.zeros((Bn, H), f32))
    _, hs = lax.scan(step, init, (qc, kc, vc, li, lf))
    return hs.transpose(1, 0, 3, 2, 4).reshape(Bn, S, H, dh).astype(q.dtype)


def peer_ffn(h, wq, keys, u, v):
    Bn, S, D = h.shape
    xb = h.reshape(-1, PEER_BLOCK, D)

    def block(xt):
        q = (xt @ wq).reshape(-1, PEER_HEADS, 2, PEER_HALF)
        s = jnp.einsum('thpc,hpnc->thpn', q, keys)
        sv, si = lax.top_k(s, PEER_TOPK)
        cand = (sv[:, :, 0, :, None] + sv[:, :, 1, None, :]).reshape(-1, PEER_HEADS, PEER_TOPK * PEER_TOPK)
        cid = (si[:, :, 0, :, None] * N_KEYS + si[:, :, 1, None, :]).reshape(-1, PEER_HEADS, PEER_TOPK * PEER_TOPK)
        top_s, top_j = lax.top_k(cand, PEER_TOPK)
        eid = jnp.take_along_axis(cid, top_j, axis=-1)
        gate = jax.nn.softmax(top_s.astype(jnp.float32), axis=-1)
        ue = jnp.take(u, eid, axis=0)
        act = jax.nn.gelu(jnp.einsum('thkd,td->thk', ue, xt).astype(jnp.float32), approximate=False)
        ve = jnp.take(v, eid, axis=0)
        return jnp.einsum('thk,thkd->td', (gate * act).astype(xt.dtype), ve)

    return lax.map(block, xb).reshape(Bn, S, D)


def setup_inputs(seed: int = 0) -> dict:
    key = jax.random.key(seed)
    ks = jax.random.split(key, 20)
    f32 = jnp.float32
    nrm = lambda k, shape: jax.random.normal(k, shape, f32)
    x = nrm(ks[0], (BATCH, SEQ, D_MODEL))
    ln_in_g = 1.0 + 0.05 * nrm(ks[1], (D_MODEL,))
    ln_in_b = 0.01 * nrm(ks[2], (D_MODEL,))
    col_scale = jnp.concatenate([
        jnp.ones((2 * D_CONV,), f32), jnp.full((D_CONV,), BETA, f32),
        jnp.ones((2 * D_MLSTM,), f32), jnp.full((D_MLSTM,), BETA, f32),
        jnp.ones((D_MLSTM + 2 * M_HEADS,), f32)])
    w_in = nrm(ks[3], (DEPTH, D_MODEL, D_IN)) * (D_MODEL ** -0.5) * col_scale
    b_i = 0.1 * nrm(ks[4], (DEPTH, M_HEADS))
    b_f = jnp.linspace(3.0, 6.0, M_HEADS, dtype=f32) + 0.1 * nrm(ks[5], (DEPTH, M_HEADS))
    b_gate = jnp.concatenate([b_i, b_f], axis=-1)
    conv_w = nrm(ks[6], (DEPTH, CONV_K, D_CONV)) * (CONV_K ** -0.5)
    conv_b = 0.01 * nrm(ks[7], (DEPTH, D_CONV))
    mh_norm_g = 1.0 + 0.05 * nrm(ks[8], (DEPTH, D_MLSTM))
    w_out = nrm(ks[9], (DEPTH, D_MIX, D_MODEL)) * (D_MIX ** -0.5) * BETA
    ln1_g = 1.0 + 0.05 * nrm(ks[10], (DEPTH, D_MODEL))
    ln1_b = 0.01 * nrm(ks[11], (DEPTH, D_MODEL))
    peer_wq = nrm(ks[12], (DEPTH, D_MODEL, PEER_HEADS * PEER_QDIM)) * (D_MODEL ** -0.5)
    peer_keys = nrm(ks[13], (DEPTH, PEER_HEADS, 2, N_KEYS, PEER_HALF)) * (PEER_HALF ** -0.5)
    peer_u = nrm(ks[14], (DEPTH, N_EXPERTS, D_MODEL)) * (D_MODEL ** -0.5) * BETA
    peer_v = nrm(ks[15], (DEPTH, N_EXPERTS, D_MODEL)) * (PEER_HEADS ** -0.5) * BETA
    ln2_g = 1.0 + 0.05 * nrm(ks[16], (DEPTH, D_MODEL))
    ln2_b = 0.01 * nrm(ks[17], (DEPTH, D_MODEL))
    return {"x": x, "ln_in_g": ln_in_g, "ln_in_b": ln_in_b, "w_in": w_in, "b_gate": b_gate,
            "conv_w": conv_w, "conv_b": conv_b, "mh_norm_g": mh_norm_g, "w_out": w_out,
            "ln1_g": ln1_g, "ln1_b": ln1_b, "peer_wq": peer_wq, "peer_keys": peer_keys,
            "peer_u": peer_u, "peer_v": peer_v, "ln2_g": ln2_g, "ln2_b": ln2_b}


def reference(x, ln_in_g, ln_in_b, w_in, b_gate, conv_w, conv_b, mh_norm_g, w_out,
              ln1_g, ln1_b, peer_wq, peer_keys, peer_u, peer_v, ln2_g, ln2_b):
    Bn, S, D = x.shape
    split_at = np.cumsum([D_CONV, D_CONV, D_CONV, D_MLSTM, D_MLSTM, D_MLSTM, D_MLSTM, M_HEADS]).tolist()
    h = layernorm(x, ln_in_g, ln_in_b)
    for l in range(DEPTH):
        proj = h @ w_in[l]
        cB, cC, ch, q, k, v, o, gi, gf = jnp.split(proj, split_at, axis=-1)
        y_conv = cB * causal_dwconv(cC * ch, conv_w[l], conv_b[l])
        gates = (jnp.concatenate([gi, gf], axis=-1) + b_gate[l]).astype(jnp.float32)
        logi = gates[..., :M_HEADS]
        logf = jax.nn.log_sigmoid(gates[..., M_HEADS:])
        hd = lambda t: t.reshape(Bn, S, M_HEADS, M_HEAD_DIM)
        hm = mlstm_chunkwise(hd(q), hd(k), hd(v), logi, logf)
        hm = layernorm(hm, mh_norm_g[l].reshape(M_HEADS, M_HEAD_DIM),
                       jnp.zeros((M_HEADS, M_HEAD_DIM), hm.dtype))
        y_m = jax.nn.sigmoid(o) * hm.reshape(Bn, S, D_MLSTM)
        mix = jnp.concatenate([y_conv, y_m], axis=-1) @ w_out[l]
        h = layernorm(ALPHA * h + mix, ln1_g[l], ln1_b[l])
        ff = peer_ffn(h, peer_wq[l], peer_keys[l], peer_u[l], peer_v[l])
        h = layernorm(ALPHA * h + ff, ln2_g[l], ln2_b[l])
    return h
```

```python
import numpy as np
import concourse.bass as bass
import concourse.mybir as mybir
from concourse.bass_utils import run_bass_kernel_spmd

F32 = mybir.dt.float32
BF16 = mybir.dt.bfloat16
U32 = mybir.dt.uint32
ALU = mybir.AluOpType
AF = mybir.ActivationFunctionType
AX = mybir.AxisListType

SEM_WINDOW = 16384


class Buf:
    __slots__ = ("name", "writers", "readers", "dsem", "dcount")

    def __init__(self, name):
        self.name = name
        self.writers = {}
        self.readers = {}
        self.dsem = None
        self.dcount = 0


class Op:
    __slots__ = ("eng", "idx", "fn", "waits", "sig", "dma_ev")

    def __init__(self, eng, idx, fn):
        self.eng = eng
        self.idx = idx
        self.fn = fn
        self.waits = []
        self.sig = False
        self.dma_ev = None


class Tracker:
    ENGS = ("sync", "act", "dve", "pool", "pe")

    _uid = [0]

    def __init__(self, nc):
        self.nc = nc
        Tracker._uid[0] += 1
        self.uid = Tracker._uid[0]
        self.ops = {e: [] for e in self.ENGS}
        self.ndsem = 0
        self.dsem_total = {}

    def _dep(self, op, key, val):
        if key[0] == 'e':
            eng = key[1]
            if eng == op.eng and eng == "pe":
                return
            prod = self.ops[eng][val]
            prod.sig = True
            op.waits.append(('e', eng, val))
        else:
            op.waits.append(('d', key[1], self.dsem_total[key[1]]))

    def op(self, eng, fn, reads=(), writes=(), partial=False, dma_buf=None):
        lst = self.ops[eng]
        o = Op(eng, len(lst), fn)
        for b in reads:
            for k, v in b.writers.items():
                self._dep(o, k, v)
        for b in writes:
            for k, v in b.readers.items():
                self._dep(o, k, v)
            if not partial:
                for k, v in b.writers.items():
                    self._dep(o, k, v)
        if dma_buf is not None:
            if dma_buf.dsem is None:
                dma_buf.dsem = self.ndsem
                self.ndsem += 1
            dma_buf.dcount += 16
            o.dma_ev = (dma_buf.dsem, dma_buf.dcount)
            self.dsem_total[dma_buf.dsem] = dma_buf.dcount
            key, val = ('d', dma_buf.dsem), dma_buf.dcount
        else:
            key, val = ('e', eng), o.idx
        for b in reads:
            b.readers[key] = val
        for b in writes:
            if not partial:
                b.writers = {}
            b.readers = {}
            b.writers[key] = val
        lst.append(o)
        return o

    def replay(self, stack, bstack=None):
        nc = self.nc
        bstack = bstack or stack
        esems = {}
        for e in self.ENGS:
            nsig = sum(1 for o in self.ops[e] if o.sig and o.dma_ev is None)
            nwin = max(1, (nsig + SEM_WINDOW - 1) // SEM_WINDOW)
            esems[e] = [stack.enter_context(nc.semaphore(f"e{self.uid}_{e}_{i}")) for i in range(nwin)]
        dsems = {}
        for d in range(self.ndsem):
            dsems[d] = {}
        self._stack = stack
        sigcnt = {}
        for e in self.ENGS:
            c = 0
            arr = []
            for o in self.ops[e]:
                if o.sig and o.dma_ev is None:
                    c += 1
                arr.append(c)
            sigcnt[e] = arr

        def dsem_handle(d, w):
            if w not in dsems[d]:
                dsems[d][w] = stack.enter_context(nc.semaphore(f"d{self.uid}_{d}_{w}"))
            return dsems[d][w]

        DW = SEM_WINDOW * 2
        engh = {"sync": nc.sync, "act": nc.scalar, "dve": nc.vector, "pool": nc.gpsimd, "pe": nc.tensor}

        def run(e, eh):
            waited = {}
            for o in self.ops[e]:
                need = {}
                for w in o.waits:
                    if w[0] == 'e':
                        c = sigcnt[w[1]][w[2]]
                        k = ('e', w[1])
                    else:
                        c = w[2]
                        k = ('d', w[1])
                    if c > need.get(k, 0):
                        need[k] = c
                for k, c in need.items():
                    if waited.get(k, 0) >= c:
                        continue
                    waited[k] = c
                    if k[0] == 'e':
                        win = (c - 1) // SEM_WINDOW
                        eh.wait_ge(esems[k[1]][win], (c - 1) % SEM_WINDOW + 1)
                    else:
                        win = (c - 16) // DW
                        eh.wait_ge(dsem_handle(k[1], win), (c - 16) % DW + 16)
                if o.fn is None:
                    continue
                ins = o.fn(eh)
                if o.dma_ev is not None:
                    d, c = o.dma_ev
                    win = (c - 16) // DW
                    ins.then_inc(dsem_handle(d, win), 16)
                elif o.sig:
                    c = sigcnt[e][o.idx]
                    win = (c - 1) // SEM_WINDOW
                    ins.then_inc(esems[e][win], 1)

        block = bstack.enter_context(nc.Block())

        @block.sync
        def _(eh):
            run("sync", eh)

        @block.scalar
        def _(eh):
            run("act", eh)

        @block.vector
        def _(eh):
            run("dve", eh)

        @block.gpsimd
        def _(eh):
            run("pool", eh)

        @block.tensor
        def _(eh):
            run("pe", eh)

import math
from contextlib import ExitStack
import ml_dtypes

D = 2048
DIN = 7184
NG = 128
ALPHA = 2.0 ** 0.25
EPS = 1e-5
DH = 128
KSCALE = DH ** -0.5
NEG = -30000.0


class Ops:
    def __init__(self, T):
        self.T = T

    def mm(self, out, lhsT, rhs, start, stop, r, w):
        self.T.op("pe", lambda e: e.matmul(out, lhsT=lhsT, rhs=rhs, start=start, stop=stop), r, w, partial=True)

    def tr(self, out, in_, ident, r, w):
        self.T.op("pe", lambda e: e.transpose(out=out, in_=in_, identity=ident), r, w, partial=True)

    def act(self, out, in_, func, r, w, bias=0.0, scale=1.0, partial=False):
        self.T.op("act", lambda e: e.activation(out=out, in_=in_, func=func, bias=bias, scale=scale), r, w, partial=partial)

    def tt(self, eng, out, in0, in1, op, r, w, partial=False):
        self.T.op(eng, lambda e: e.tensor_tensor(out=out, in0=in0, in1=in1, op=op), r, w, partial=partial)

    def ts(self, eng, out, in0, s1, s2, op0, op1, r, w, partial=False):
        if s2 is None:
            self.T.op(eng, lambda e: e.tensor_scalar(out=out, in0=in0, scalar1=s1, scalar2=None, op0=op0), r, w, partial=partial)
        else:
            self.T.op(eng, lambda e: e.tensor_scalar(out=out, in0=in0, scalar1=s1, scalar2=s2, op0=op0, op1=op1), r, w, partial=partial)

    def stt(self, eng, out, in0, scalar, in1, op0, op1, r, w, partial=False):
        self.T.op(eng, lambda e: e.scalar_tensor_tensor(out=out, in0=in0, scalar=scalar, in1=in1, op0=op0, op1=op1), r, w, partial=partial)

    def cp(self, eng, out, in_, r, w, partial=False):
        if eng == "act":
            self.T.op("act", lambda e: e.copy(out=out, in_=in_), r, w, partial=partial)
        else:
            self.T.op(eng, lambda e: e.tensor_copy(out=out, in_=in_), r, w, partial=partial)

    def dma(self, out, in_, r, w, buf, partial=False):
        self.T.op("sync", lambda e: e.dma_start(out=out, in_=in_), r, w, partial=partial, dma_buf=buf)

    def memset(self, eng, ap, val, w):
        self.T.op(eng, lambda e: e.memset(ap, val), (), w)


class Stream:
    def __init__(self, O, slots, bufs, loads, depth=None):
        self.O = O
        self.slots = slots
        self.bufs = bufs
        self.loads = loads
        self.n = len(slots)
        self.depth = depth or (self.n - 1)
        self.issued = 0
        self.consumed = 0

    def _issue(self):
        if self.issued >= len(self.loads):
            return
        k = self.issued
        s = k % self.n
        for (o, i) in self.loads[k](self.slots[s]):
            self.O.dma(o, i, (), [self.bufs[s]], self.bufs[s], partial=True)
        self.issued += 1

    def next(self):
        while self.issued < len(self.loads) and self.issued < self.consumed + self.depth:
            self._issue()
        k = self.consumed
        self.consumed += 1
        s = k % self.n
        return self.slots[s], self.bufs[s]


def final_barrier(T, O, scratch, sbuf_bar, psum_bar):
    bars = {}
    for e in ("act", "dve", "pool"):
        b = Buf("bar_" + e)
        bars[e] = b
        col = {"act": 0, "dve": 1, "pool": 2}[e]
        O.memset(e, sbuf_bar[:, col:col + 1], 0.0, [b]) if e != "act" else T.op(
            "act", lambda en: en.copy(out=sbuf_bar[:, 0:1], in_=sbuf_bar[:, 4:5]), (), [b])
    bpe = Buf("bar_pe")
    bars["pe"] = bpe
    T.op("pe", lambda en: en.matmul(psum_bar, lhsT=sbuf_bar[:, 8:9].bitcast(F32), rhs=sbuf_bar[:, 8:9].bitcast(F32), start=True, stop=True), (), [bpe], partial=True)
    allb = list(bars.values())
    for e in ("sync", "act", "dve", "pool", "pe"):
        o = T.op(e, None, reads=allb)
        for d, tot in T.dsem_total.items():
            o.waits.append(('d', d, tot))


def build(NH, NO, debug=False):
    NT = NH + NO
    nc = bass.Bass("TRN2", target_bir_lowering=False)

    def din(name, shape, dt=F32):
        return nc.dram_tensor(name, list(shape), dt, kind="ExternalInput").ap()

    xin = din("xin", [NT * 128, D])
    hmask_d = din("hmask", [128, max(NH, 1)])
    hvalid_d = din("hvalid", [128, 1])
    ln_in_g = din("ln_in_g", [D]); ln_in_b = din("ln_in_b", [D])
    w_in = din("w_in", [D, DIN]); b_gate = din("b_gate", [16])
    conv_w = din("conv_w", [128, 24]); conv_b = din("conv_b", [128, 8])
    mh_g = din("mh_norm_g", [1024]); w_out = din("w_out", [D, D])
    ln1_g = din("ln1_g", [D]); ln1_b = din("ln1_b", [D])
    wq = din("peer_wq", [D, D]); keys = din("peer_keys", [8, 2, 128, 128])
    pu = din("peer_u", [16384, D]); pv = din("peer_v", [16384, D])
    ln2_g = din("ln2_g", [D]); ln2_b = din("ln2_b", [D])
    identb_d = din("identb", [128, 128], BF16); identf_d = din("identf", [128, 128])
    tri_d = din("tri", [128, 128]); onesf_d = din("onesf", [128, 128])
    iota3_d = din("iota3", [128, 16 * 128], BF16); iota16_d = din("iota16", [128, 8 * 16 * 16])
    out_d = nc.dram_tensor("out", [NO * 128, D], F32, kind="ExternalOutput").ap()
    dbg_d = nc.dram_tensor("dbg", [NO * 128, D], F32, kind="ExternalOutput").ap() if debug else None

    def dscr(name, shape, dt):
        return nc.dram_tensor(name, list(shape), dt).ap()

    Wi_b = dscr("Wi_b", [D, DIN], BF16); Wo_b = dscr("Wo_b", [D, D], BF16); Wq_b = dscr("Wq_b", [D, D], BF16)
    vb_d = dscr("vb_d", [16384, D], BF16); uT_d = dscr("uT_d", [NG, 128, 16 * 128], BF16)
    h0d = dscr("h0d", [NO * 128, D], F32); h1d = dscr("h1d", [NO * 128, D], F32)
    h1Td = dscr("h1Td", [NO, 128, 16 * 128], BF16)
    B_Wi = Buf("Wi_b"); B_Wo = Buf("Wo_b"); B_Wq = Buf("Wq_b"); B_vb = Buf("vb_d"); B_uT = Buf("uT_d")
    B_h0d = Buf("h0d"); B_h1d = Buf("h1d"); B_h1Td = Buf("h1Td")

    with ExitStack() as outer:
        with ExitStack() as st:
            T = Tracker(nc)
            O = Ops(T)
            bufs = {}

            def sb(name, shape, dt):
                t = st.enter_context(nc.sbuf_tensor("sb_" + name, list(shape), dt))
                bufs[name] = Buf(name)
                return t

            banks = [st.enter_context(nc.psum_tensor(f"bank{i}", [128, 512], F32)) for i in range(8)]
            BK = [Buf(f"bank{i}") for i in range(8)]

            identb = sb("identb", [128, 128], BF16); identf = sb("identf", [128, 128], F32)
            tri = sb("tri", [128, 128], F32); onesf = sb("onesf", [128, 128], F32)
            onesb = sb("onesb", [128, 8], BF16)
            bar = sb("bar", [128, 16], F32)
            hmask = sb("hmask", [128, max(NH, 1)], F32); hvalid = sb("hvalid", [128, 1], F32)
            g_in = sb("g_in", [128, D], F32); b_in = sb("b_in", [128, D], F32)
            g_1 = sb("g_1", [128, D], F32); b_1 = sb("b_1", [128, D], F32)
            mhg = sb("mhg", [128, 1024], F32); bgate = sb("bgate", [128, 16], F32)
            cw = sb("cw", [128, 3, 8], F32); cbias = sb("cbias", [128, 8], F32)
            wgate = sb("wgate", [128, 16, 16], BF16)
            wgate_f = sb("wgate_f", [128, 16, 16], F32)

            def load_const(t, src, name):
                O.dma(t, src, (), [bufs[name]], bufs[name])

            load_const(identb[:], identb_d, "identb"); load_const(identf[:], identf_d, "identf")
            load_const(tri[:], tri_d, "tri"); load_const(onesf[:], onesf_d, "onesf")
            load_const(hmask[:], hmask_d, "hmask"); load_const(hvalid[:], hvalid_d, "hvalid")
            load_const(g_in[:], ln_in_g.partition_broadcast(128), "g_in"); load_const(b_in[:], ln_in_b.partition_broadcast(128), "b_in")
            load_const(g_1[:], ln1_g.partition_broadcast(128), "g_1"); load_const(b_1[:], ln1_b.partition_broadcast(128), "b_1")
            load_const(mhg[:], mh_g.partition_broadcast(128), "mhg"); load_const(bgate[:], b_gate.partition_broadcast(128), "bgate")
            load_const(cw[:].rearrange("p j c -> p (j c)"), conv_w, "cw")
            load_const(cbias[:], conv_b, "cbias")
            load_const(wgate_f[:], w_in.rearrange("(k p) c -> p k c", p=128)[:, :, 7168:7184], "wgate_f")
            O.cp("dve", wgate[:], wgate_f[:], [bufs["wgate_f"]], [bufs["wgate"]])
            O.memset("pool", onesb[:], 1.0, [bufs["onesb"]])
            O.memset("pool", bar[:], 0.0, [bufs["bar"]])

            cin = [sb(f"cin{i}", [128, 2048], F32) for i in range(2)]
            cout = [sb(f"cout{i}", [128, 2048], BF16) for i in range(2)]
            cast_engs = ["dve", "pool", "act"]
            cnt = [0]

            def cast2d(src, dst, R, C, dstbuf):
                for r in range(R // 128):
                    for c0 in range(0, C, 2048):
                        cwid = min(2048, C - c0)
                        s = cnt[0] % 2
                        eng = cast_engs[cnt[0] % 3]
                        cnt[0] += 1
                        bi = bufs[f"cin{s}"]; bo = bufs[f"cout{s}"]
                        O.dma(cin[s][:, 0:cwid], src[r * 128:(r + 1) * 128, c0:c0 + cwid], (), [bi], bi)
                        O.cp(eng, cout[s][:, 0:cwid], cin[s][:, 0:cwid], [bi], [bo])
                        O.dma(dst[r * 128:(r + 1) * 128, c0:c0 + cwid], cout[s][:, 0:cwid], [bo], [dstbuf], bo, partial=True)

            cast2d(w_in, Wi_b, D, DIN, B_Wi)
            cast2d(w_out, Wo_b, D, D, B_Wo)
            cast2d(wq, Wq_b, D, D, B_Wq)
            cast2d(pv, vb_d, 16384, D, B_vb)
            for g in range(NG):
                s = cnt[0] % 2
                cnt[0] += 1
                bi = bufs[f"cin{s}"]; bo = bufs[f"cout{s}"]
                O.dma(cin[s][:], pu[g * 128:(g + 1) * 128, :], (), [bi], bi)
                for q4 in range(4):
                    bk = 4 + q4
                    for kk in range(4):
                        k = q4 * 4 + kk
                        O.tr(banks[bk][:, kk * 128:(kk + 1) * 128], cin[s][:, k * 128:(k + 1) * 128], identf[:],
                             [bi, bufs["identf"]], [BK[bk]])
                    eng = ["dve", "act"][q4 % 2]
                    O.cp(eng, cout[s][:, q4 * 512:(q4 + 1) * 512], banks[bk][:], [BK[bk]], [bo], partial=True)
                O.dma(uT_d[g], cout[s][:], [bo], [B_uT], bo, partial=True)

            xt = cin
            bufs["xt0"] = bufs["cin0"]; bufs["xt1"] = bufs["cin1"]
            h0 = sb("h0", [128, D], F32); h0b = cout[1]; bufs["h0b"] = bufs["cout1"]
            h0T = sb("h0T", [128, 16, 128], BF16)
            wslots = [sb(f"wg{i}", [128, 16, 512], BF16) for i in range(3)]
            Ktok = sb("Ktok", [128, 8, 128], BF16); Vt = sb("Vt", [128, 8, 128], BF16)
            sig = sb("sig", [128, 1024], F32)
            Cf = sb("Cf", [128, 8, 128], F32); zbuf = sb("zbuf", [128, 8, 130], F32); acc = sb("acc", [128, 128], F32)
            mixT = sb("mixT", [128, 16, 128], BF16)
            QT = sb("QT", [128, 8, 128], BF16); KT = sb("KT", [128, 8, 128], BF16)
            PT = sb("PT", [128, 8, 128], F32); sw = sb("sw", [128, 8, 128], BF16)
            EB = sb("EB", [128, 8, 128], F32); QsT = sb("QsT", [128, 8, 128], BF16)
            TriLF = sb("TriLF", [128, 8, 128], F32)
            wkV = sb("wkV", [128, 8, 128], BF16)
            yn = sb("yn", [128, 8, 128], F32); ym = sb("ym", [128, 1024], BF16)
            CT = sb("CT", [128, 8, 128], F32); CTb = sb("CTb", [128, 8, 128], BF16)
            nT = sb("nT", [128, 8], F32); nb = sb("nb", [128, 8], BF16)
            t1 = sb("t1", [128, D], F32); h1 = t1; bufs["h1"] = bufs["t1"]; h1b = cout[0]; bufs["h1b"] = bufs["cout0"]
            h1T = sb("h1T", [128, 16, 128], BF16)
            sm = sb("sm", [128, 256], F32)
            smb = sb("smb", [128, 16], BF16)
            stats = sb("stats", [128, 8, 6], F32); mv = sb("mv", [128, 8, 2], F32)
            Bf = bufs
            def smcol(name, c0, n):
                bufs[name] = Buf(name)
                return sm[:, c0:c0 + n]
            gx = smcol("gx", 0, 16); ef = smcol("ef", 16, 8); lf = smcol("lf", 24, 8)
            bc = smcol("bc", 32, 8); wkt = smcol("wkt", 40, 8); wk = smcol("wk", 48, 8)
            bias8 = smcol("bias8", 56, 8); dec = smcol("dec", 64, 8); rr = smcol("rr", 72, 8)
            t8 = smcol("t8", 80, 8); sc8 = smcol("sc8", 88, 8); lnmv = smcol("lnmv", 96, 2)
            lnr = smcol("lnr", 98, 1); lnst = smcol("lnst", 100, 24)
            bufs["wkb"] = Buf("wkb")
            wkb = smb[:, 0:8]

            O.memset("pool", CT[:], 0.0, [Bf["CT"]]); O.memset("pool", CTb[:], 0.0, [Bf["CTb"]])
            O.memset("pool", nT[:], 0.0, [Bf["nT"]]); O.memset("pool", nb[:], 0.0, [Bf["nb"]])
            O.memset("pool", zbuf[:], 0.0, [Bf["zbuf"]])

            Wi_v = Wi_b.rearrange("(k p) c -> p k c", p=128)
            Wo_v = Wo_b.rearrange("(k p) c -> p k c", p=128)
            loads = []
            plan = []

            def wload(view, c0, srcbuf):
                def f(slot):
                    return [(slot[:, :, :], view[:, :, c0:c0 + 512])]
                return f

            for i in range(NT):
                own = i >= NH
                tags = []
                if own or i == NH - 1:
                    for c0 in (1024, 1536):
                        tags.append(("fC", c0))
                    for c0 in (2048, 2560):
                        tags.append(("fh", c0))
                if own:
                    for c0 in (0, 512):
                        tags.append(("fB", c0))
                    for c0 in (3072, 3584):
                        tags.append(("fq", c0))
                for c0 in (4096, 4608):
                    tags.append(("k", c0))
                for c0 in (5120, 5632):
                    tags.append(("v", c0))
                if own:
                    for c0 in (6144, 6656):
                        tags.append(("o", c0))
                    for c0 in (0, 512, 1024, 1536):
                        tags.append(("wo", c0))
                plan.append(tags)
                for (tg, c0) in tags:
                    loads.append(wload(Wo_v if tg == "wo" else Wi_v, c0, None))
            wstream = Stream(O, wslots, [bufs[f"wg{i}"] for i in range(3)], loads)
            for i in range(3):
                pass
            orig_issue = wstream._issue

            def issue_with_deps():
                if wstream.issued >= len(wstream.loads):
                    return
                k = wstream.issued
                s = k % wstream.n
                for (o, i_) in wstream.loads[k](wstream.slots[s]):
                    O.dma(o, i_, [B_Wi, B_Wo], [wstream.bufs[s]], wstream.bufs[s], partial=True)
                wstream.issued += 1
            wstream._issue = issue_with_deps

            proj_banks = [6, 4, 5]
            pcount = [0]

            def next_pbank():
                b = proj_banks[pcount[0] % 3]
                pcount[0] += 1
                return b

            def layernorm_tile(src, src_bufs, dst, dst_buf, gt, bt, gbuf, bbuf):
                for q in range(4):
                    T.op("dve", (lambda q=q: (lambda e: e.bn_stats(out=lnst[:, q * 6:(q + 1) * 6], in_=src[:, q * 512:(q + 1) * 512])))(),
                         src_bufs, [Bf["lnst"]], partial=(q > 0))
                T.op("dve", lambda e: e.bn_aggr(out=lnmv, in_=lnst), [Bf["lnst"]], [Bf["lnmv"]])
                O.act(lnr, lnmv[:, 1:2], AF.Ln, [Bf["lnmv"]], [Bf["lnr"]], bias=EPS, scale=1.0)
                O.act(lnr, lnr, AF.Exp, [Bf["lnr"]], [Bf["lnr"]], scale=-0.5)
                O.ts("dve", dst, src, lnmv[:, 0:1], lnr[:, 0:1], ALU.subtract, ALU.mult, src_bufs + [Bf["lnmv"], Bf["lnr"]], [dst_buf])
                O.tt("pool", dst, dst, gt, ALU.mult, [dst_buf, gbuf], [dst_buf])
                O.tt("pool", dst, dst, bt, ALU.add, [dst_buf, bbuf], [dst_buf])

            def transpose_2048(srcb, srcb_buf, dstT, dstT_buf):
                for half in range(2):
                    bk = next_pbank()
                    pb = banks[bk][:].bitcast(BF16)
                    for kk in range(8):
                        k = half * 8 + kk
                        O.tr(pb[:, kk * 128:(kk + 1) * 128], srcb[:, k * 128:(k + 1) * 128], identb[:], [srcb_buf, Bf["identb"]], [BK[bk]])
                    eng = "act" if half == 0 else "dve"
                    O.cp(eng, dstT[:, half * 8:(half + 1) * 8, :].rearrange("p k t -> p (k t)"), pb, [BK[bk]], [dstT_buf], partial=True)

            def flat(ap3):
                return ap3.rearrange("p h l -> p (h l)")

            def mlstm_tile(own, i):
                tri_b = tri[:].unsqueeze(1).to_broadcast([128, 8, 128])
                if own:
                    O.tt("pool", TriLF[:], tri_b, lf.unsqueeze(2).to_broadcast([128, 8, 128]), ALU.mult, [Bf["tri"], Bf["lf"]], [Bf["TriLF"]])
                    TL2 = flat(TriLF[:])
                    for half in range(2):
                        O.mm(banks[half][:], onesf[:], TL2[:, half * 512:(half + 1) * 512], True, True, [Bf["onesf"], Bf["TriLF"]], [BK[half]])
                    O.tt("dve", bias8, gx[:, 0:8], bc, ALU.subtract, [Bf["gx"], Bf["bc"]], [Bf["bias8"]])
                    for h in range(8):
                        bk = h // 4; col = (h % 4) * 128
                        O.act(PT[:, h, :], banks[bk][:, col:col + 128], AF.Exp, [BK[bk], Bf["bias8"]], [Bf["PT"]],
                              bias=bias8[:, h:h + 1], scale=1.0, partial=True)
                    O.tt("pool", PT[:], PT[:], tri_b, ALU.mult, [Bf["PT"], Bf["tri"]], [Bf["PT"]])
                    for h in range(8):
                        bk = 2 + h // 4; col = (h % 4) * 128
                        O.mm(banks[bk][:, col:col + 128], KT[:, h, :], QT[:, h, :], True, True, [Bf["KT"], Bf["QT"]], [BK[bk]])
                    for half in range(2):
                        O.tt("dve", flat(sw[:, half * 4:(half + 1) * 4, :]), flat(PT[:, half * 4:(half + 1) * 4, :]), banks[2 + half][:], ALU.mult,
                             [Bf["PT"], BK[2 + half]], [Bf["sw"]], partial=True)
                    for half in range(2):
                        O.act(flat(EB[:, half * 4:(half + 1) * 4, :]), banks[half][:], AF.Exp, [BK[half]], [Bf["EB"]], partial=True)
                    O.tt("pool", QsT[:], QT[:], EB[:], ALU.mult, [Bf["QT"], Bf["EB"]], [Bf["QsT"]])
                    for h in range(8):
                        bk = 4 + h // 4; col = (h % 4) * 128
                        O.mm(banks[bk][:, col:col + 128], sw[:, h, :], Vt[:, h, :], True, False, [Bf["sw"], Bf["Vt"]], [BK[bk]])
                        O.mm(banks[bk][:, col:col + 128], QsT[:, h, :], CTb[:, h, :], False, True, [Bf["QsT"], Bf["CTb"]], [BK[bk]])
                    for h in range(8):
                        O.mm(banks[7][:, 32 + h:33 + h], sw[:, h, :], onesb[:, 0:1], True, False, [Bf["sw"], Bf["onesb"]], [BK[7]])
                        O.mm(banks[7][:, 32 + h:33 + h], QsT[:, h, :], nb[:, h:h + 1], False, True, [Bf["QsT"], Bf["nb"]], [BK[7]])
                    O.ts("dve", t8, banks[7][:, 32:40], -1.0, 1.0, ALU.mult, ALU.max, [BK[7]], [Bf["t8"]])
                    O.ts("dve", rr, banks[7][:, 32:40], 1.0, None, ALU.max, None, [BK[7]], [Bf["rr"]])
                    O.tt("dve", rr, rr, t8, ALU.max, [Bf["rr"], Bf["t8"]], [Bf["rr"]])
                    T.op("dve", lambda e: e.reciprocal(out=rr, in_=rr), [Bf["rr"]], [Bf["rr"]])
                    for h in range(8):
                        bk = 4 + h // 4; col = (h % 4) * 128
                        T.op("dve", (lambda h=h, bk=bk, col=col: (lambda e: e.bn_stats(out=stats[:, h, :], in_=banks[bk][:, col:col + 128])))(),
                             [BK[bk]], [Bf["stats"]], partial=True)
                    for h in range(8):
                        T.op("dve", (lambda h=h: (lambda e: e.bn_aggr(out=mv[:, h, :], in_=stats[:, h, :])))(), [Bf["stats"]], [Bf["mv"]], partial=True)
                    O.tt("dve", t8, rr, rr, ALU.mult, [Bf["rr"]], [Bf["t8"]])
                    O.tt("dve", t8, t8, mv[:, :, 1], ALU.mult, [Bf["t8"], Bf["mv"]], [Bf["t8"]])
                    O.act(t8, t8, AF.Ln, [Bf["t8"]], [Bf["t8"]], bias=EPS, scale=1.0)
                    O.act(t8, t8, AF.Exp, [Bf["t8"]], [Bf["t8"]], scale=-0.5)
                    O.tt("dve", sc8, t8, rr, ALU.mult, [Bf["t8"], Bf["rr"]], [Bf["sc8"]])
                    for h in range(8):
                        bk = 4 + h // 4; col = (h % 4) * 128
                        O.ts("dve", yn[:, h, :], banks[bk][:, col:col + 128], mv[:, h, 0:1], sc8[:, h:h + 1], ALU.subtract, ALU.mult,
                             [BK[bk], Bf["mv"], Bf["sc8"]], [Bf["yn"]], partial=True)
                    O.tt("pool", flat(yn[:]), flat(yn[:]), mhg[:], ALU.mult, [Bf["yn"], Bf["mhg"]], [Bf["yn"]])
                    O.tt("pool", ym[:], flat(yn[:]), sig[:], ALU.mult, [Bf["yn"], Bf["sig"]], [Bf["ym"]])
                    bk = next_pbank()
                    pb = banks[bk][:].bitcast(BF16)
                    for h in range(8):
                        O.tr(pb[:, h * 128:(h + 1) * 128], ym[:, h * 128:(h + 1) * 128], identb[:], [Bf["ym"], Bf["identb"]], [BK[bk]])
                    O.cp("act", flat(mixT[:, 8:16, :]), pb, [BK[bk]], [Bf["mixT"]], partial=True)
                O.tt("dve", wkt, banks[7][:, 24:32], bc, ALU.subtract, [BK[7], Bf["bc"]], [Bf["wkt"]])
                O.tt("dve", wkt, wkt, gx[:, 0:8], ALU.add, [Bf["wkt"], Bf["gx"]], [Bf["wkt"]])
                O.act(wk, wkt, AF.Exp, [Bf["wkt"]], [Bf["wk"]])
                O.cp("dve", wkb, wk, [Bf["wk"]], [Bf["wkb"]])
                O.act(dec, banks[7][:, 24:32], AF.Exp, [BK[7]], [Bf["dec"]])
                O.tt("pool", wkV[:], Vt[:], wk.unsqueeze(2).to_broadcast([128, 8, 128]), ALU.mult, [Bf["Vt"], Bf["wk"]], [Bf["wkV"]])
                for h in range(8):
                    bk = h // 4; col = (h % 4) * 128
                    O.mm(banks[bk][:, col:col + 128], Ktok[:, h, :], wkV[:, h, :], True, True, [Bf["Ktok"], Bf["wkV"]], [BK[bk]])
                for h in range(8):
                    O.mm(banks[7][:, 40 + h:41 + h], Ktok[:, h, :], wkb[:, h:h + 1], True, True, [Bf["Ktok"], Bf["wkb"]], [BK[7]])
                O.tt("pool", CT[:], CT[:], dec.unsqueeze(2).to_broadcast([128, 8, 128]), ALU.mult, [Bf["CT"], Bf["dec"]], [Bf["CT"]])
                for half in range(2):
                    O.tt("dve", flat(CT[:, half * 4:(half + 1) * 4, :]), flat(CT[:, half * 4:(half + 1) * 4, :]), banks[half][:], ALU.add,
                         [Bf["CT"], BK[half]], [Bf["CT"]], partial=True)
                O.cp("act", CTb[:], CT[:], [Bf["CT"]], [Bf["CTb"]])
                O.tt("dve", nT[:], nT[:], dec, ALU.mult, [Bf["nT"], Bf["dec"]], [Bf["nT"]])
                O.tt("dve", nT[:], nT[:], banks[7][:, 40:48], ALU.add, [Bf["nT"], BK[7]], [Bf["nT"]])
                O.cp("dve", nb[:], nT[:], [Bf["nT"]], [Bf["nb"]])

            own_idx = 0
            for i in range(NT):
                own = i >= NH
                last_hist = (i == NH - 1)
                xs = i % 2
                xtile = xt[xs]; xbuf = bufs[f"xt{xs}"]
                O.dma(xtile[:], xin[i * 128:(i + 1) * 128, :], (), [xbuf], xbuf)
                layernorm_tile(xtile[:], [xbuf], h0[:], Bf["h0"], g_in[:], b_in[:], Bf["g_in"], Bf["b_in"])
                if own:
                    O.dma(h0d[own_idx * 128:(own_idx + 1) * 128, :], h0[:], [Bf["h0"]], [B_h0d], Bf["h0"], partial=True)
                O.cp("act", h0b[:], h0[:], [Bf["h0"]], [Bf["h0b"]])
                transpose_2048(h0b, Bf["h0b"], h0T, Bf["h0T"])

                for k in range(16):
                    O.mm(banks[7][:, 0:16], h0T[:, k, :], wgate[:, k, :], k == 0, k == 15, [Bf["h0T"], Bf["wgate"]], [BK[7]])
                O.tt("dve", gx, banks[7][:, 0:16], bgate[:], ALU.add, [BK[7], Bf["bgate"]], [Bf["gx"]])
                if not own:
                    O.ts("dve", gx[:, 0:8], gx[:, 0:8], hmask[:, i:i + 1], None, ALU.add, None, [Bf["gx"], Bf["hmask"]], [Bf["gx"]])
                O.act(ef, gx[:, 8:16], AF.Exp, [Bf["gx"]], [Bf["ef"]], scale=-1.0)
                O.act(ef, ef, AF.Ln, [Bf["ef"]], [Bf["ef"]], bias=1.0, scale=1.0)
                O.ts("dve", lf, ef, -1.0, None, ALU.mult, None, [Bf["ef"]], [Bf["lf"]])
                O.mm(banks[7][:, 16:24], tri[:], lf, True, True, [Bf["tri"], Bf["lf"]], [BK[7]])
                O.mm(banks[7][:, 24:32], onesf[:], lf, True, True, [Bf["onesf"], Bf["lf"]], [BK[7]])
                O.cp("dve", bc, banks[7][:, 16:24], [BK[7]], [Bf["bc"]])

                for (tg, c0) in plan[i]:
                    slot, sbuf_ = wstream.next()
                    if tg in ("fC", "fh", "fB", "fq"):
                        bk = next_pbank()
                        for cc in range(4):
                            for k in range(16):
                                O.mm(banks[bk][:, cc * 128:(cc + 1) * 128], slot[:, k, cc * 128:(cc + 1) * 128], h0T[:, k, :],
                                     k == 0, k == 15, [sbuf_, Bf["h0T"]], [BK[bk]])
                        pv3 = banks[bk][:].rearrange("p (c t) -> p c t", c=4)
                        if tg == "fC":
                            ch0 = (c0 - 1024) // 128
                            O.cp("act", Cf[:, ch0:ch0 + 4, :], pv3, [BK[bk]], [Bf["Cf"]], partial=True)
                        elif tg == "fh":
                            ch0 = (c0 - 2048) // 128
                            O.tt("dve", zbuf[:, ch0:ch0 + 4, 2:130], pv3, Cf[:, ch0:ch0 + 4, :], ALU.mult, [BK[bk], Bf["Cf"]], [Bf["zbuf"]], partial=True)
                            if last_hist and c0 == 2560:
                                O.ts("pool", zbuf[:, :, 0:2], zbuf[:, :, 128:130], hvalid[:, 0:1], None, ALU.mult, None,
                                     [Bf["zbuf"], Bf["hvalid"]], [Bf["zbuf"]])
                        elif tg == "fB":
                            ch0 = c0 // 128
                            for cc in range(4):
                                c = ch0 + cc
                                O.ts("dve", acc[:], zbuf[:, c, 2:130], cw[:, 2, c:c + 1], cbias[:, c:c + 1], ALU.mult, ALU.add,
                                     [Bf["zbuf"], Bf["cw"], Bf["cbias"]], [Bf["acc"]])
                                O.stt("dve", acc[:], zbuf[:, c, 1:129], cw[:, 1, c:c + 1], acc[:], ALU.mult, ALU.add,
                                      [Bf["zbuf"], Bf["cw"], Bf["acc"]], [Bf["acc"]])
                                O.stt("dve", acc[:], zbuf[:, c, 0:128], cw[:, 0, c:c + 1], acc[:], ALU.mult, ALU.add,
                                      [Bf["zbuf"], Bf["cw"], Bf["acc"]], [Bf["acc"]])
                                O.tt("dve", mixT[:, c, :], banks[bk][:, cc * 128:(cc + 1) * 128], acc[:], ALU.mult, [BK[bk], Bf["acc"]], [Bf["mixT"]], partial=True)
                            if c0 == 512:
                                O.cp("pool", zbuf[:, :, 0:2], zbuf[:, :, 128:130], [Bf["zbuf"]], [Bf["zbuf"]])
                        elif tg == "fq":
                            ch0 = (c0 - 3072) // 128
                            O.cp("act", QT[:, ch0:ch0 + 4, :], pv3, [BK[bk]], [Bf["QT"]], partial=True)
                    elif tg == "k":
                        ch0 = (c0 - 4096) // 128
                        if own:
                            bk = next_pbank()
                            for cc in range(4):
                                for k in range(16):
                                    O.mm(banks[bk][:, cc * 128:(cc + 1) * 128], slot[:, k, cc * 128:(cc + 1) * 128], h0T[:, k, :],
                                         k == 0, k == 15, [sbuf_, Bf["h0T"]], [BK[bk]])
                            pv3 = banks[bk][:].rearrange("p (c t) -> p c t", c=4)
                            O.act(KT[:, ch0:ch0 + 4, :], pv3, AF.Copy, [BK[bk]], [Bf["KT"]], scale=KSCALE, partial=True)
                        bk = next_pbank()
                        for k in range(16):
                            O.mm(banks[bk][:], h0T[:, k, :], slot[:, k, :], k == 0, k == 15, [sbuf_, Bf["h0T"]], [BK[bk]])
                        O.act(Ktok[:, ch0:ch0 + 4, :].rearrange("p h d -> p (h d)"), banks[bk][:], AF.Copy, [BK[bk]], [Bf["Ktok"]], scale=KSCALE, partial=True)
                    elif tg == "v":
                        ch0 = (c0 - 5120) // 128
                        bk = next_pbank()
                        for k in range(16):
                            O.mm(banks[bk][:], h0T[:, k, :], slot[:, k, :], k == 0, k == 15, [sbuf_, Bf["h0T"]], [BK[bk]])
                        O.cp("dve", Vt[:, ch0:ch0 + 4, :].rearrange("p h d -> p (h d)"), banks[bk][:], [BK[bk]], [Bf["Vt"]], partial=True)
                    elif tg == "o":
                        cc0 = c0 - 6144
                        bk = next_pbank()
                        for k in range(16):
                            O.mm(banks[bk][:], h0T[:, k, :], slot[:, k, :], k == 0, k == 15, [sbuf_, Bf["h0T"]], [BK[bk]])
                        O.act(sig[:, cc0:cc0 + 512], banks[bk][:], AF.Exp, [BK[bk]], [Bf["sig"]], scale=-1.0, partial=True)
                        O.ts("pool", sig[:, cc0:cc0 + 512], sig[:, cc0:cc0 + 512], 1.0, None, ALU.add, None, [Bf["sig"]], [Bf["sig"]], partial=True)
                        if cc0 == 512:
                            T.op("dve", lambda e: e.reciprocal(out=sig[:], in_=sig[:]), [Bf["sig"]], [Bf["sig"]])
                            mlstm_tile(True, i)
                    elif tg == "wo":
                        bk = next_pbank()
                        for k in range(16):
                            O.mm(banks[bk][:], mixT[:, k, :], slot[:, k, :], k == 0, k == 15, [sbuf_, Bf["mixT"]], [BK[bk]])
                        O.stt("dve", t1[:, c0:c0 + 512], h0[:, c0:c0 + 512], ALPHA, banks[bk][:], ALU.mult, ALU.add,
                              [Bf["h0"], BK[bk]], [Bf["t1"]], partial=True)
                        if c0 == 1536:
                            layernorm_tile(t1[:], [Bf["t1"]], h1[:], Bf["h1"], g_1[:], b_1[:], Bf["g_1"], Bf["b_1"])
                            O.dma(h1d[own_idx * 128:(own_idx + 1) * 128, :], h1[:], [Bf["h1"]], [B_h1d], Bf["h1"], partial=True)
                            if debug:
                                O.dma(dbg_d[own_idx * 128:(own_idx + 1) * 128, :], h1[:], [Bf["h1"]], [], Bf["h1"])
                            O.cp("act", h1b[:], h1[:], [Bf["h1"]], [Bf["h1b"]])
                            transpose_2048(h1b, Bf["h1b"], h1T, Bf["h1T"])
                            O.dma(h1Td[own_idx], h1T[:].rearrange("p k t -> p (k t)"), [Bf["h1T"]], [B_h1Td], Bf["h1T"], partial=True)
                    if tg == "v" and c0 == 5632 and not own:
                        mlstm_tile(False, i)
                if own:
                    own_idx += 1

            final_barrier(T, O, None, bar, banks[7][0:1, 500:501])
            T.replay(outer, st)

        with ExitStack() as st:
            T = Tracker(nc)
            O = Ops(T)
            bufs = {}

            def sb(name, shape, dt):
                t = st.enter_context(nc.sbuf_tensor("sb_" + name, list(shape), dt))
                bufs[name] = Buf(name)
                return t

            banks = [st.enter_context(nc.psum_tensor(f"pbank{i}", [128, 512], F32)) for i in range(8)]
            BK = [Buf(f"pbank{i}") for i in range(8)]
            identb = sb("identb2", [128, 128], BF16); identf = sb("identf2", [128, 128], F32)
            bar = sb("bar2", [128, 16], F32)
            g_2 = sb("g_2", [128, D], F32); b_2 = sb("b_2", [128, D], F32)
            iota3 = sb("iota3", [128, 16, 128], BF16); iota16 = sb("iota16", [128, 8, 16, 16], F32)
            keysb = sb("keysb", [128, 16, 128], BF16)
            keysT = sb("keysT", [128, 16, 128], BF16)
            Bf = bufs

            def load_const(t, src, name):
                O.dma(t, src, (), [bufs[name]], bufs[name])
            load_const(identb[:], identb_d, "identb2"); load_const(identf[:], identf_d, "identf2")
            load_const(g_2[:], ln2_g.partition_broadcast(128), "g_2"); load_const(b_2[:], ln2_b.partition_broadcast(128), "b_2")
            load_const(iota3[:].rearrange("p t i -> p (t i)"), iota3_d, "iota3")
            load_const(iota16[:].rearrange("p h k a -> p (h k a)"), iota16_d, "iota16")
            h1T = sb("h1T2", [128, 16, 128], BF16); h1 = sb("h1_2", [128, D], F32)
            wslots = [sb(f"wq{i}", [128, 16, 256], BF16) for i in range(2)]
            qT = sb("qT", [128, 16, 128], BF16)
            s_ = sb("s_", [128, 16, 128], F32); s2 = sb("s2", [128, 128], F32)
            sv = sb("sv", [128, 16, 16], F32); si = sb("si", [128, 16, 16], U32); sif = sb("sif", [128, 16, 16], F32)
            cand = sb("cand", [128, 8, 256], F32); c2 = sb("c2", [128, 256], F32)
            tv = sb("tv", [128, 8, 16], F32); ci = sb("ci", [128, 8, 16], U32)
            ca = sb("ca", [128, 8, 16], U32); cb_ = sb("cb_", [128, 8, 16], U32)
            caf = sb("caf", [128, 8, 16], F32); cbf = sb("cbf", [128, 8, 16], F32)
            oh = sb("oh", [128, 8, 16, 16], F32)
            sel = sb("sel", [128, 3, 128], F32); selT = sb("selT", [128, 3, 128], F32)
            ee = sb("ee", [128, 8, 16], F32); zz = sb("zz", [128, 8], F32)
            Pb = [sb(f"Pb{i}", [128, 16, 128], BF16) for i in range(2)]
            Qb = [sb(f"Qb{i}", [128, 16, 128], BF16) for i in range(2)]
            Wsb = sb("Wsb", [128, 128, 128], BF16)
            uslots = [sb(f"us{i}", [128, 16, 128], BF16) for i in range(4)]
            vslots = [sb(f"vs{i}", [128, D], BF16) for i in range(4)]
            ga = [sb(f"ga{i}", [128, 4, 128], F32) for i in range(2)]
            Gt = [sb(f"Gt{i}", [128, 4, 128], BF16) for i in range(2)]
            t2 = sb("t2", [128, D], F32); o2 = t2; bufs["o2"] = bufs["t2"]
            sm = sb("sm2", [128, 64], F32)

            keysf = t2[:].rearrange("p (a n) -> p a n", a=16); bufs["keysf"] = bufs["t2"]
            load_const(keysf, keys.rearrange("h p n c -> n (h p) c"), "keysf")
            O.memset("pool", bar[:], 0.0, [Bf["bar2"]])
            O.cp("dve", keysb[:], keysf, [Bf["keysf"]], [Bf["keysb"]])
            for half in range(2):
                bk = 6 + half
                pb = banks[bk][:].bitcast(BF16)
                for kk in range(8):
                    hp = half * 8 + kk
                    O.tr(pb[:, kk * 128:(kk + 1) * 128], keysb[:, hp, :], identb[:], [Bf["keysb"], Bf["identb2"]], [BK[bk]])
                O.cp("dve", keysT[:, half * 8:(half + 1) * 8, :].rearrange("p k t -> p (k t)"), pb, [BK[bk]], [Bf["keysT"]], partial=True)

            def smcol(name, c0, n):
                bufs[name] = Buf(name)
                return sm[:, c0:c0 + n]
            lnmv = smcol("lnmv", 0, 2); lnr = smcol("lnr", 2, 1); lnst = smcol("lnst", 4, 24)

            def layernorm_tile(src, src_bufs, dst, dst_buf, gt, bt, gbuf, bbuf):
                for q in range(4):
                    T.op("dve", (lambda q=q: (lambda e: e.bn_stats(out=lnst[:, q * 6:(q + 1) * 6], in_=src[:, q * 512:(q + 1) * 512])))(),
                         src_bufs, [Bf["lnst"]], partial=(q > 0))
                T.op("dve", lambda e: e.bn_aggr(out=lnmv, in_=lnst), [Bf["lnst"]], [Bf["lnmv"]])
                O.act(lnr, lnmv[:, 1:2], AF.Ln, [Bf["lnmv"]], [Bf["lnr"]], bias=EPS, scale=1.0)
                O.act(lnr, lnr, AF.Exp, [Bf["lnr"]], [Bf["lnr"]], scale=-0.5)
                O.ts("dve", dst, src, lnmv[:, 0:1], lnr[:, 0:1], ALU.subtract, ALU.mult, src_bufs + [Bf["lnmv"], Bf["lnr"]], [dst_buf])
                O.tt("pool", dst, dst, gt, ALU.mult, [dst_buf, gbuf], [dst_buf])
                O.tt("pool", dst, dst, bt, ALU.add, [dst_buf, bbuf], [dst_buf])

            Wq_v = Wq_b.rearrange("(k p) c -> p k c", p=128)
            vb_v = vb_d.rearrange("(g j) d -> g j d", j=128)

            def mk_loads(fn_list):
                return fn_list
            qloads = []
            uloads = []
            vloads = []
            for ti in range(NO):
                for c0 in range(0, 2048, 256):
                    qloads.append((lambda c0=c0: (lambda slot: [(slot[:, :, :], Wq_v[:, :, c0:c0 + 256])]))())
                for g in range(NG):
                    uloads.append((lambda g=g: (lambda slot: [(slot[:].rearrange("p k j -> p (k j)"), uT_d[g])]))())
                    vloads.append((lambda g=g: (lambda slot: [(slot[:, :], vb_v[g])]))())
            qstream = Stream(O, wslots, [bufs[f"wq{i}"] for i in range(2)], qloads)
            ustream = Stream(O, uslots, [bufs[f"us{i}"] for i in range(4)], uloads)
            vstream = Stream(O, vslots, [bufs[f"vs{i}"] for i in range(4)], vloads)

            for ti in range(NO):
                O.dma(h1T[:].rearrange("p k t -> p (k t)"), h1Td[ti], (), [Bf["h1T2"]], Bf["h1T2"])
                O.dma(h1[:], h1d[ti * 128:(ti + 1) * 128, :], (), [Bf["h1_2"]], Bf["h1_2"])
                for grp in range(8):
                    slot, sbuf_ = qstream.next()
                    bk = 6 + (grp % 2)
                    for cc in range(2):
                        for k in range(16):
                            O.mm(banks[bk][:, cc * 128:(cc + 1) * 128], slot[:, k, cc * 128:(cc + 1) * 128], h1T[:, k, :],
                                 k == 0, k == 15, [sbuf_, Bf["h1T2"]], [BK[bk]])
                    O.cp("act" if grp % 2 == 0 else "dve", qT[:, grp * 2:(grp + 1) * 2, :].rearrange("p c t -> p (c t)"), banks[bk][:, 0:256], [BK[bk]], [Bf["qT"]], partial=True)
                for hp in range(16):
                    bk = hp // 4; col = (hp % 4) * 128
                    O.mm(banks[bk][:, col:col + 128], qT[:, hp, :], keysT[:, hp, :], True, True, [Bf["qT"], Bf["keysT"]], [BK[bk]])
                for q4 in range(4):
                    O.cp("act" if q4 % 2 == 0 else "dve", s_[:, q4 * 4:(q4 + 1) * 4, :].rearrange("p a n -> p (a n)"), banks[q4][:], [BK[q4]], [Bf["s_"]], partial=True)
                for hp in range(16):
                    T.op("dve", (lambda hp=hp: (lambda e: e.max(out=sv[:, hp, 0:8], in_=s_[:, hp, :])))(), [Bf["s_"]], [Bf["sv"]], partial=True)
                    T.op("dve", (lambda hp=hp: (lambda e: e.match_replace(out=s2[:], in_to_replace=sv[:, hp, 0:8], in_values=s_[:, hp, :], imm_value=-1e30)))(),
                         [Bf["s_"], Bf["sv"]], [Bf["s2"]])
                    T.op("dve", (lambda hp=hp: (lambda e: e.max(out=sv[:, hp, 8:16], in_=s2[:])))(), [Bf["s2"]], [Bf["sv"]], partial=True)
                    T.op("dve", (lambda hp=hp: (lambda e: e.max_index(out=si[:, hp, 0:8], in_max=sv[:, hp, 0:8], in_values=s_[:, hp, :])))(),
                         [Bf["s_"], Bf["sv"]], [Bf["si"]], partial=True)
                    T.op("dve", (lambda hp=hp: (lambda e: e.max_index(out=si[:, hp, 8:16], in_max=sv[:, hp, 8:16], in_values=s_[:, hp, :])))(),
                         [Bf["s_"], Bf["sv"]], [Bf["si"]], partial=True)
                O.cp("dve", sif[:], si[:], [Bf["si"]], [Bf["sif"]])
                sv4 = sv[:].rearrange("p (h two) a -> p h two a", two=2)
                sif4 = sif[:].rearrange("p (h two) a -> p h two a", two=2)
                O.tt("dve", cand[:].rearrange("p h (a b) -> p h a b", a=16),
                     sv4[:, :, 0, :].unsqueeze(3).to_broadcast([128, 8, 16, 16]),
                     sv4[:, :, 1, :].unsqueeze(2).to_broadcast([128, 8, 16, 16]), ALU.add, [Bf["sv"]], [Bf["cand"]])
                for h in range(8):
                    T.op("dve", (lambda h=h: (lambda e: e.max(out=tv[:, h, 0:8], in_=cand[:, h, :])))(), [Bf["cand"]], [Bf["tv"]], partial=True)
                    T.op("dve", (lambda h=h: (lambda e: e.match_replace(out=c2[:], in_to_replace=tv[:, h, 0:8], in_values=cand[:, h, :], imm_value=-1e30)))(),
                         [Bf["cand"], Bf["tv"]], [Bf["c2"]])
                    T.op("dve", (lambda h=h: (lambda e: e.max(out=tv[:, h, 8:16], in_=c2[:])))(), [Bf["c2"]], [Bf["tv"]], partial=True)
                    T.op("dve", (lambda h=h: (lambda e: e.max_index(out=ci[:, h, 0:8], in_max=tv[:, h, 0:8], in_values=cand[:, h, :])))(),
                         [Bf["cand"], Bf["tv"]], [Bf["ci"]], partial=True)
                    T.op("dve", (lambda h=h: (lambda e: e.max_index(out=ci[:, h, 8:16], in_max=tv[:, h, 8:16], in_values=cand[:, h, :])))(),
                         [Bf["cand"], Bf["tv"]], [Bf["ci"]], partial=True)
                T.op("dve", lambda e: e.tensor_single_scalar(out=ca[:], in_=ci[:], scalar=4, op=ALU.logical_shift_right), [Bf["ci"]], [Bf["ca"]])
                T.op("dve", lambda e: e.tensor_single_scalar(out=cb_[:], in_=ci[:], scalar=15, op=ALU.bitwise_and), [Bf["ci"]], [Bf["cb_"]])
                O.cp("dve", caf[:], ca[:], [Bf["ca"]], [Bf["caf"]])
                O.cp("dve", cbf[:], cb_[:], [Bf["cb_"]], [Bf["cbf"]])
                for which, idxf in ((0, caf), (1, cbf)):
                    O.tt("dve", oh[:], iota16[:], idxf[:].unsqueeze(3).to_broadcast([128, 8, 16, 16]), ALU.is_equal,
                         [Bf["iota16"], Bf["caf" if which == 0 else "cbf"]], [Bf["oh"]])
                    O.tt("pool", oh[:], oh[:], sif4[:, :, which, :].unsqueeze(2).to_broadcast([128, 8, 16, 16]), ALU.mult,
                         [Bf["oh"], Bf["sif"]], [Bf["oh"]])
                    T.op("dve", (lambda which=which: (lambda e: e.tensor_reduce(out=sel[:, which, :].rearrange("p (h k) -> p h k", h=8), in_=oh[:], axis=AX.X, op=ALU.add)))(),
                         [Bf["oh"]], [Bf["sel"]], partial=True)
                O.tt("dve", ee[:], tv[:], tv[:, :, 0:1].to_broadcast([128, 8, 16]), ALU.subtract, [Bf["tv"]], [Bf["ee"]])
                O.act(ee[:], ee[:], AF.Exp, [Bf["ee"]], [Bf["ee"]])
                T.op("dve", lambda e: e.tensor_reduce(out=zz[:], in_=ee[:], axis=AX.X, op=ALU.add), [Bf["ee"]], [Bf["zz"]])
                T.op("dve", lambda e: e.reciprocal(out=zz[:], in_=zz[:]), [Bf["zz"]], [Bf["zz"]])
                O.tt("dve", sel[:, 2, :].rearrange("p (h k) -> p h k", h=8), ee[:], zz[:].unsqueeze(2).to_broadcast([128, 8, 16]), ALU.mult,
                     [Bf["ee"], Bf["zz"]], [Bf["sel"]], partial=True)
                for w3 in range(3):
                    O.tr(banks[6][:, w3 * 128:(w3 + 1) * 128], sel[:, w3, :], identf[:], [Bf["sel"], Bf["identf2"]], [BK[6]])
                O.cp("dve", selT[:].rearrange("p w t -> p (w t)"), banks[6][:, 0:384], [BK[6]], [Bf["selT"]])
                wb_rot = 0
                for tb in range(8):
                    pq = tb % 2
                    i1b = selT[:, 0, tb * 16:(tb + 1) * 16].unsqueeze(2).to_broadcast([128, 16, 128])
                    i2b = selT[:, 1, tb * 16:(tb + 1) * 16].unsqueeze(2).to_broadcast([128, 16, 128])
                    gb = selT[:, 2, tb * 16:(tb + 1) * 16].unsqueeze(2).to_broadcast([128, 16, 128])
                    O.tt("dve", Pb[pq][:], iota3[:], i1b, ALU.is_equal, [Bf["iota3"], Bf["selT"]], [Bf[f"Pb{pq}"]])
                    O.tt("pool", Pb[pq][:], Pb[pq][:], gb, ALU.mult, [Bf[f"Pb{pq}"], Bf["selT"]], [Bf[f"Pb{pq}"]])
                    O.tt("dve", Qb[pq][:], iota3[:], i2b, ALU.is_equal, [Bf["iota3"], Bf["selT"]], [Bf[f"Qb{pq}"]])
                    for t4 in range(4):
                        bk = 4 + (wb_rot % 4)
                        wb_rot += 1
                        for tt_ in range(4):
                            tl = t4 * 4 + tt_
                            O.mm(banks[bk][:, tt_ * 128:(tt_ + 1) * 128], Qb[pq][:, tl, :], Pb[pq][:, tl, :], True, True,
                                 [Bf[f"Qb{pq}"], Bf[f"Pb{pq}"]], [BK[bk]])
                        t0 = tb * 16 + t4 * 4
                        O.cp("act", Wsb[:, :, t0:t0 + 4].rearrange("j g t -> j t g"), banks[bk][:].rearrange("p (t g) -> p t g", t=4),
                             [BK[bk]], [Bf["Wsb"]], partial=True)
                for gq in range(32):
                    bkA = 4 + (gq % 2)
                    sl = gq % 2
                    vsl = []
                    for gi in range(4):
                        us, ub = ustream.next()
                        for k in range(16):
                            O.mm(banks[bkA][:, gi * 128:(gi + 1) * 128], us[:, k, :], h1T[:, k, :], k == 0, k == 15, [ub, Bf["h1T2"]], [BK[bkA]])
                    O.act(ga[sl][:].rearrange("p g t -> p (g t)"), banks[bkA][:], AF.Gelu, [BK[bkA]], [Bf[f"ga{sl}"]])
                    O.tt("dve", Gt[sl][:], ga[sl][:], Wsb[:, gq * 4:(gq + 1) * 4, :], ALU.mult, [Bf[f"ga{sl}"], Bf["Wsb"]], [Bf[f"Gt{sl}"]])
                    for gi in range(4):
                        g = gq * 4 + gi
                        vs, vbuf = vstream.next()
                        for dc in range(4):
                            O.mm(banks[dc][:], Gt[sl][:, gi, :], vs[:, dc * 512:(dc + 1) * 512], g == 0, g == NG - 1, [Bf[f"Gt{sl}"], vbuf], [BK[dc]])
                for dc in range(4):
                    O.stt("dve", t2[:, dc * 512:(dc + 1) * 512], h1[:, dc * 512:(dc + 1) * 512], ALPHA, banks[dc][:], ALU.mult, ALU.add,
                          [Bf["h1_2"], BK[dc]], [Bf["t2"]], partial=True)
                layernorm_tile(t2[:], [Bf["t2"]], o2[:], Bf["o2"], g_2[:], b_2[:], Bf["g_2"], Bf["b_2"])
                O.dma(out_d[ti * 128:(ti + 1) * 128, :], o2[:], [Bf["o2"]], [], Bf["o2"])
            final_barrier(T, O, None, bar, banks[7][0:1, 500:501])
            T.replay(outer, st)

    return nc


NH_FULL = 96
NO_FULL = 32


def _consts():
    bf = ml_dtypes.bfloat16
    c = {}
    c["identb"] = np.eye(128, dtype=np.float32).astype(bf)
    c["identf"] = np.eye(128, dtype=np.float32)
    c["tri"] = np.triu(np.ones((128, 128), dtype=np.float32))
    c["onesf"] = np.ones((128, 128), dtype=np.float32)
    c["iota3"] = np.broadcast_to(np.arange(128, dtype=np.float32)[None, None, :], (128, 16, 128)).reshape(128, 16 * 128).astype(bf)
    c["iota16"] = np.ascontiguousarray(np.broadcast_to(np.arange(16, dtype=np.float32)[None, None, None, :], (128, 8, 16, 16)).reshape(128, 2048))
    return c


def _weights(inp):
    f = lambda a: np.ascontiguousarray(np.asarray(a, dtype=np.float32))
    return {
        "ln_in_g": f(inp["ln_in_g"]), "ln_in_b": f(inp["ln_in_b"]), "w_in": f(inp["w_in"][0]), "b_gate": f(inp["b_gate"][0]),
        "conv_w": f(np.asarray(inp["conv_w"][0]).reshape(3, 8, 128).transpose(2, 0, 1).reshape(128, 24)), "conv_b": f(np.asarray(inp["conv_b"][0]).reshape(8, 128).T), "mh_norm_g": f(inp["mh_norm_g"][0]), "w_out": f(inp["w_out"][0]),
        "ln1_g": f(inp["ln1_g"][0]), "ln1_b": f(inp["ln1_b"][0]), "peer_wq": f(inp["peer_wq"][0]), "peer_keys": f(inp["peer_keys"][0]),
        "peer_u": f(inp["peer_u"][0]), "peer_v": f(inp["peer_v"][0]), "ln2_g": f(inp["ln2_g"][0]), "ln2_b": f(inp["ln2_b"][0]),
    }


def core_inputs(x_seq, start, n_own_tok, NH, common):
    hist_tok = NH * 128
    xin = np.zeros((hist_tok + n_own_tok, D), dtype=np.float32)
    real = min(start, hist_tok)
    if real > 0:
        xin[hist_tok - real:hist_tok] = x_seq[start - real:start]
    xin[hist_tok:] = x_seq[start:start + n_own_tok]
    hm = np.zeros((128, max(NH, 1)), dtype=np.float32)
    ndummy = (hist_tok - real) // 128
    hm[:, :ndummy] = NEG
    m = dict(common)
    m["xin"] = xin
    m["hmask"] = hm
    m["hvalid"] = np.full((128, 1), 1.0 if real > 0 else 0.0, dtype=np.float32)
    return m


_NC_CACHE = {}


def kernel(**inputs):
    x = np.asarray(inputs["x"], dtype=np.float32)
    Bn, S, _ = x.shape
    common = _weights(inputs)
    common.update(_consts())
    ncores = 8
    per = (Bn * S) // ncores
    segs = S // per
    NO = per // 128
    NH = (segs - 1) * NO
    key = (NH, NO)
    if key not in _NC_CACHE:
        _NC_CACHE[key] = build(NH, NO)
    nc = _NC_CACHE[key]
    in_maps = []
    for c in range(ncores):
        b, sg = divmod(c, segs)
        in_maps.append(core_inputs(x[b], sg * per, per, NH, common))
    res = run_bass_kernel_spmd(nc, in_maps, core_ids=list(range(ncores)))
    out = np.empty((Bn, S, D), dtype=np.float32)
    for c in range(ncores):
        b, sg = divmod(c, segs)
        out[b, sg * per:(sg + 1) * per] = res.results[c]["out"]
    return out
```

```python
import numpy as np
import concourse.bass as bass
import concourse.mybir as mybir
from concourse.bass_utils import run_bass_kernel_spmd

F32 = mybir.dt.float32
BF16 = mybir.dt.bfloat16
U32 = mybir.dt.uint32
ALU = mybir.AluOpType
AF = mybir.ActivationFunctionType
AX = mybir.AxisListType

SEM_WINDOW = 16384


class Buf:
    __slots__ = ("name", "writers", "readers", "dsem", "dcount")

    def __init__(self, name):
        self.name = name
        self.writers = {}
        self.readers = {}
        self.dsem = None
        self.dcount = 0


class Op:
    __slots__ = ("eng", "idx", "fn", "waits", "sig", "dma_ev")

    def __init__(self, eng, idx, fn):
        self.eng = eng
        self.idx = idx
        self.fn = fn
        self.waits = []
        self.sig = False
        self.dma_ev = None


class Tracker:
    ENGS = ("sync", "act", "dve", "pool", "pe")

    _uid = [0]

    def __init__(self, nc):
        self.nc = nc
        Tracker._uid[0] += 1
        self.uid = Tracker._uid[0]
        self.ops = {e: [] for e in self.ENGS}
        self.ndsem = 0
        self.dsem_total = {}

    def _dep(self, op, key, val):
        if key[0] == 'e':
            eng = key[1]
            if eng == op.eng and eng == "pe":
                return
            prod = self.ops[eng][val]
            prod.sig = True
            op.waits.append(('e', eng, val))
        else:
            op.waits.append(('d', key[1], self.dsem_total[key[1]]))

    def op(self, eng, fn, reads=(), writes=(), partial=False, dma_buf=None):
        lst = self.ops[eng]
        o = Op(eng, len(lst), fn)
        for b in reads:
            for k, v in b.writers.items():
                self._dep(o, k, v)
        for b in writes:
            for k, v in b.readers.items():
                self._dep(o, k, v)
            if not partial:
                for k, v in b.writers.items():
                    self._dep(o, k, v)
        if dma_buf is not None:
            if dma_buf.dsem is None:
                dma_buf.dsem = self.ndsem
                self.ndsem += 1
            dma_buf.dcount += 16
            o.dma_ev = (dma_buf.dsem, dma_buf.dcount)
            self.dsem_total[dma_buf.dsem] = dma_buf.dcount
            key, val = ('d', dma_buf.dsem), dma_buf.dcount
        else:
            key, val = ('e', eng), o.idx
        for b in reads:
            b.readers[key] = val
        for b in writes:
            if not partial:
                b.writers = {}
            b.readers = {}
            b.writers[key] = val
        lst.append(o)
        return o

    def replay(self, stack, bstack=None):
        nc = self.nc
        bstack = bstack or stack
        esems = {}
        for e in self.ENGS:
            nsig = sum(1 for o in self.ops[e] if o.sig and o.dma_ev is None)
            nwin = max(1, (nsig + SEM_WINDOW - 1) // SEM_WINDOW)
            esems[e] = [stack.enter_context(nc.semaphore(f"e{self.uid}_{e}_{i}")) for i in range(nwin)]
        dsems = {}
        for d in range(self.ndsem):
            dsems[d] = {}
        self._stack = stack
        sigcnt = {}
        for e in self.ENGS:
            c = 0
            arr = []
            for o in self.ops[e]:
                if o.sig and o.dma_ev is None:
                    c += 1
                arr.append(c)
            sigcnt[e] = arr

        def dsem_handle(d, w):
            if w not in dsems[d]:
                dsems[d][w] = stack.enter_context(nc.semaphore(f"d{self.uid}_{d}_{w}"))
            return dsems[d][w]

        DW = SEM_WINDOW * 2
        engh = {"sync": nc.sync, "act": nc.scalar, "dve": nc.vector, "pool": nc.gpsimd, "pe": nc.tensor}

        def run(e, eh):
            waited = {}
            for o in self.ops[e]:
                need = {}
                for w in o.waits:
                    if w[0] == 'e':
                        c = sigcnt[w[1]][w[2]]
                        k = ('e', w[1])
                    else:
                        c = w[2]
                        k = ('d', w[1])
                    if c > need.get(k, 0):
                        need[k] = c
                for k, c in need.items():
                    if waited.get(k, 0) >= c:
                        continue
                    waited[k] = c
                    if k[0] == 'e':
                        win = (c - 1) // SEM_WINDOW
                        eh.wait_ge(esems[k[1]][win], (c - 1) % SEM_WINDOW + 1)
                    else:
                        win = (c - 16) // DW
                        eh.wait_ge(dsem_handle(k[1], win), (c - 16) % DW + 16)
                if o.fn is None:
                    continue
                ins = o.fn(eh)
                if o.dma_ev is not None:
                    d, c = o.dma_ev
                    win = (c - 16) // DW
                    ins.then_inc(dsem_handle(d, win), 16)
                elif o.sig:
                    c = sigcnt[e][o.idx]
                    win = (c - 1) // SEM_WINDOW
                    ins.then_inc(esems[e][win], 1)

        block = bstack.enter_context(nc.Block())

        @block.sync
        def _(eh):
            run("sync", eh)

        @block.scalar
        def _(eh):
            run("act", eh)

        @block.vector
        def _(eh):
            run("dve", eh)

        @block.gpsimd
        def _(eh):
            run("pool", eh)

        @block.tensor
        def _(eh):
            run("pe", eh)

import math
from contextlib import ExitStack
import ml_dtypes

D = 2048
DIN = 7184
NG = 128
ALPHA = 2.0 ** 0.25
EPS = 1e-5
DH = 128
KSCALE = DH ** -0.5
NEG = -30000.0


class Ops:
    def __init__(self, T):
        self.T = T

    def mm(self, out, lhsT, rhs, start, stop, r, w):
        self.T.op("pe", lambda e: e.matmul(out, lhsT=lhsT, rhs=rhs, start=start, stop=stop), r, w, partial=True)

    def tr(self, out, in_, ident, r, w):
        self.T.op("pe", lambda e: e.transpose(out=out, in_=in_, identity=ident), r, w, partial=True)

    def act(self, out, in_, func, r, w, bias=0.0, scale=1.0, partial=False):
        self.T.op("act", lambda e: e.activation(out=out, in_=in_, func=func, bias=bias, scale=scale), r, w, partial=partial)

    def tt(self, eng, out, in0, in1, op, r, w, partial=False):
        self.T.op(eng, lambda e: e.tensor_tensor(out=out, in0=in0, in1=in1, op=op), r, w, partial=partial)

    def ts(self, eng, out, in0, s1, s2, op0, op1, r, w, partial=False):
        if s2 is None:
            self.T.op(eng, lambda e: e.tensor_scalar(out=out, in0=in0, scalar1=s1, scalar2=None, op0=op0), r, w, partial=partial)
        else:
            self.T.op(eng, lambda e: e.tensor_scalar(out=out, in0=in0, scalar1=s1, scalar2=s2, op0=op0, op1=op1), r, w, partial=partial)

    def stt(self, eng, out, in0, scalar, in1, op0, op1, r, w, partial=False):
        self.T.op(eng, lambda e: e.scalar_tensor_tensor(out=out, in0=in0, scalar=scalar, in1=in1, op0=op0, op1=op1), r, w, partial=partial)

    def cp(self, eng, out, in_, r, w, partial=False):
        if eng == "act":
            self.T.op("act", lambda e: e.copy(out=out, in_=in_), r, w, partial=partial)
        else:
            self.T.op(eng, lambda e: e.tensor_copy(out=out, in_=in_), r, w, partial=partial)

    def dma(self, out, in_, r, w, buf, partial=False):
        self.T.op("sync", lambda e: e.dma_start(out=out, in_=in_), r, w, partial=partial, dma_buf=buf)

    def memset(self, eng, ap, val, w):
        self.T.op(eng, lambda e: e.memset(ap, val), (), w)


class Stream:
    def __init__(self, O, slots, bufs, loads, depth=None):
        self.O = O
        self.slots = slots
        self.bufs = bufs
        self.loads = loads
        self.n = len(slots)
        self.depth = depth or (self.n - 1)
        self.issued = 0
        self.consumed = 0

    def _issue(self):
        if self.issued >= len(self.loads):
            return
        k = self.issued
        s = k % self.n
        for (o, i) in self.loads[k](self.slots[s]):
            self.O.dma(o, i, (), [self.bufs[s]], self.bufs[s], partial=True)
        self.issued += 1

    def next(self):
        while self.issued < len(self.loads) and self.issued < self.consumed + self.depth:
            self._issue()
        k = self.consumed
        self.consumed += 1
        s = k % self.n
        return self.slots[s], self.bufs[s]


def final_barrier(T, O, scratch, sbuf_bar, psum_bar):
    bars = {}
    for e in ("act", "dve", "pool"):
        b = Buf("bar_" + e)
        bars[e] = b
        col = {"act": 0, "dve": 1, "pool": 2}[e]
        O.memset(e, sbuf_bar[:, col:col + 1], 0.0, [b]) if e != "act" else T.op(
            "act", lambda en: en.copy(out=sbuf_bar[:, 0:1], in_=sbuf_bar[:, 4:5]), (), [b])
    bpe = Buf("bar_pe")
    bars["pe"] = bpe
    T.op("pe", lambda en: en.matmul(psum_bar, lhsT=sbuf_bar[:, 8:9].bitcast(F32), rhs=sbuf_bar[:, 8:9].bitcast(F32), start=True, stop=True), (), [bpe], partial=True)
    allb = list(bars.values())
    for e in ("sync", "act", "dve", "pool", "pe"):
        o = T.op(e, None, reads=allb)
        for d, tot in T.dsem_total.items():
            o.waits.append(('d', d, tot))


def build(NH, NO, debug=False):
    NT = NH + NO
    nc = bass.Bass("TRN2", target_bir_lowering=False)

    def din(name, shape, dt=F32):
        return nc.dram_tensor(name, list(shape), dt, kind="ExternalInput").ap()

    xin = din("xin", [NT * 128, D])
    hmask_d = din("hmask", [128, max(NH, 1)])
    hvalid_d = din("hvalid", [128, 1])
    ln_in_g = din("ln_in_g", [D]); ln_in_b = din("ln_in_b", [D])
    w_in = din("w_in", [D, DIN]); b_gate = din("b_gate", [16])
    conv_w = din("conv_w", [128, 24]); conv_b = din("conv_b", [128, 8])
    mh_g = din("mh_norm_g", [1024]); w_out = din("w_out", [D, D])
    ln1_g = din("ln1_g", [D]); ln1_b = din("ln1_b", [D])
    wq = din("peer_wq", [D, D]); keys = din("peer_keys", [8, 2, 128, 128])
    pu = din("peer_u", [16384, D]); pv = din("peer_v", [16384, D])
    ln2_g = din("ln2_g", [D]); ln2_b = din("ln2_b", [D])
    identb_d = din("identb", [128, 128], BF16); identf_d = din("identf", [128, 128])
    tri_d = din("tri", [128, 128]); onesf_d = din("onesf", [128, 128])
    iota3_d = din("iota3", [128, 16 * 128], BF16); iota16_d = din("iota16", [128, 8 * 16 * 16])
    out_d = nc.dram_tensor("out", [NO * 128, D], F32, kind="ExternalOutput").ap()
    dbg_d = nc.dram_tensor("dbg", [NO * 128, D], F32, kind="ExternalOutput").ap() if debug else None

    def dscr(name, shape, dt):
        return nc.dram_tensor(name, list(shape), dt).ap()

    Wi_b = dscr("Wi_b", [D, DIN], BF16); Wo_b = dscr("Wo_b", [D, D], BF16); Wq_b = dscr("Wq_b", [D, D], BF16)
    vbh_d = dscr("vbh_d", [2, 16384, 1024], BF16); uT_d = dscr("uT_d", [NG, 128, 16 * 128], BF16)
    h0d = dscr("h0d", [NO * 128, D], F32); h1d = dscr("h1d", [NO * 128, D], F32)
    h1Td = dscr("h1Td", [NO, 128, 16 * 128], BF16)
    B_Wi = Buf("Wi_b"); B_Wo = Buf("Wo_b"); B_Wq = Buf("Wq_b"); B_vb = Buf("vb_d"); B_uT = Buf("uT_d")
    B_h0d = Buf("h0d"); B_h1d = Buf("h1d"); B_h1Td = Buf("h1Td")

    with ExitStack() as outer:
        with ExitStack() as st:
            T = Tracker(nc)
            O = Ops(T)
            bufs = {}

            def sb(name, shape, dt):
                t = st.enter_context(nc.sbuf_tensor("sb_" + name, list(shape), dt))
                bufs[name] = Buf(name)
                return t

            banks = [st.enter_context(nc.psum_tensor(f"bank{i}", [128, 512], F32)) for i in range(8)]
            BK = [Buf(f"bank{i}") for i in range(8)]

            identb = sb("identb", [128, 128], BF16); identf = sb("identf", [128, 128], F32)
            tri = sb("tri", [128, 128], F32); onesf = sb("onesf", [128, 128], F32)
            onesb = sb("onesb", [128, 8], BF16)
            bar = sb("bar", [128, 16], F32)
            hmask = sb("hmask", [128, max(NH, 1)], F32); hvalid = sb("hvalid", [128, 1], F32)
            g_in = sb("g_in", [128, D], F32); b_in = sb("b_in", [128, D], F32)
            g_1 = sb("g_1", [128, D], F32); b_1 = sb("b_1", [128, D], F32)
            mhg = sb("mhg", [128, 1024], F32); bgate = sb("bgate", [128, 16], F32)
            cw = sb("cw", [128, 3, 8], F32); cbias = sb("cbias", [128, 8], F32)
            wgate = sb("wgate", [128, 16, 16], BF16)
            wgate_f = sb("wgate_f", [128, 16, 16], F32)

            def load_const(t, src, name):
                O.dma(t, src, (), [bufs[name]], bufs[name])

            load_const(identb[:], identb_d, "identb"); load_const(identf[:], identf_d, "identf")
            load_const(tri[:], tri_d, "tri"); load_const(onesf[:], onesf_d, "onesf")
            load_const(hmask[:], hmask_d, "hmask"); load_const(hvalid[:], hvalid_d, "hvalid")
            load_const(g_in[:], ln_in_g.partition_broadcast(128), "g_in"); load_const(b_in[:], ln_in_b.partition_broadcast(128), "b_in")
            load_const(g_1[:], ln1_g.partition_broadcast(128), "g_1"); load_const(b_1[:], ln1_b.partition_broadcast(128), "b_1")
            load_const(mhg[:], mh_g.partition_broadcast(128), "mhg"); load_const(bgate[:], b_gate.partition_broadcast(128), "bgate")
            load_const(cw[:].rearrange("p j c -> p (j c)"), conv_w, "cw")
            load_const(cbias[:], conv_b, "cbias")
            load_const(wgate_f[:], w_in.rearrange("(k p) c -> p k c", p=128)[:, :, 7168:7184], "wgate_f")
            O.cp("dve", wgate[:], wgate_f[:], [bufs["wgate_f"]], [bufs["wgate"]])
            O.memset("pool", onesb[:], 1.0, [bufs["onesb"]])
            O.memset("pool", bar[:], 0.0, [bufs["bar"]])

            cin = [sb(f"cin{i}", [128, 2048], F32) for i in range(2)]
            cout = [sb(f"cout{i}", [128, 2048], BF16) for i in range(2)]
            cast_engs = ["dve", "pool", "act"]
            cnt = [0]

            def cast2d(src, dst, R, C, dstbuf, vsplit=False):
                for r in range(R // 128):
                    for c0 in range(0, C, 2048):
                        cwid = min(2048, C - c0)
                        s = cnt[0] % 2
                        eng = cast_engs[cnt[0] % 3]
                        cnt[0] += 1
                        bi = bufs[f"cin{s}"]; bo = bufs[f"cout{s}"]
                        O.dma(cin[s][:, 0:cwid], src[r * 128:(r + 1) * 128, c0:c0 + cwid], (), [bi], bi)
                        O.cp(eng, cout[s][:, 0:cwid], cin[s][:, 0:cwid], [bi], [bo])
                        if vsplit:
                            for hh in range(2):
                                O.dma(dst[hh][r * 128:(r + 1) * 128, :], cout[s][:, hh * 1024:(hh + 1) * 1024], [bo], [dstbuf], bo, partial=True)
                        else:
                            O.dma(dst[r * 128:(r + 1) * 128, c0:c0 + cwid], cout[s][:, 0:cwid], [bo], [dstbuf], bo, partial=True)

            cast2d(w_in, Wi_b, D, DIN, B_Wi)
            cast2d(w_out, Wo_b, D, D, B_Wo)
            cast2d(wq, Wq_b, D, D, B_Wq)
            cast2d(pv, vbh_d, 16384, D, B_vb, vsplit=True)
            for g in range(NG):
                s = cnt[0] % 2
                cnt[0] += 1
                bi = bufs[f"cin{s}"]; bo = bufs[f"cout{s}"]
                O.dma(cin[s][:], pu[g * 128:(g + 1) * 128, :], (), [bi], bi)
                for q4 in range(4):
                    bk = 4 + q4
                    for kk in range(4):
                        k = q4 * 4 + kk
                        O.tr(banks[bk][:, kk * 128:(kk + 1) * 128], cin[s][:, k * 128:(k + 1) * 128], identf[:],
                             [bi, bufs["identf"]], [BK[bk]])
                    eng = ["dve", "act"][q4 % 2]
                    O.cp(eng, cout[s][:, q4 * 512:(q4 + 1) * 512], banks[bk][:], [BK[bk]], [bo], partial=True)
                O.dma(uT_d[g], cout[s][:], [bo], [B_uT], bo, partial=True)

            xt = cin
            bufs["xt0"] = bufs["cin0"]; bufs["xt1"] = bufs["cin1"]
            h0 = sb("h0", [128, D], F32); h0b = cout[1]; bufs["h0b"] = bufs["cout1"]
            h0T = sb("h0T", [128, 16, 128], BF16)
            wslots = [sb(f"wg{i}", [128, 16, 512], BF16) for i in range(3)]
            Ktok = sb("Ktok", [128, 8, 128], BF16); Vt = sb("Vt", [128, 8, 128], BF16)
            sig = sb("sig", [128, 1024], F32)
            Cf = sb("Cf", [128, 8, 128], F32); zbuf = sb("zbuf", [128, 8, 130], F32); acc = sb("acc", [128, 128], F32)
            mixT = sb("mixT", [128, 16, 128], BF16)
            QT = sb("QT", [128, 8, 128], BF16); KT = sb("KT", [128, 8, 128], BF16)
            PT = sb("PT", [128, 8, 128], F32); sw = sb("sw", [128, 8, 128], BF16)
            EB = sb("EB", [128, 8, 128], F32); QsT = sb("QsT", [128, 8, 128], BF16)
            TriLF = sb("TriLF", [128, 8, 128], F32)
            wkV = sb("wkV", [128, 8, 128], BF16)
            yn = sb("yn", [128, 8, 128], F32); ym = sb("ym", [128, 1024], BF16)
            CT = sb("CT", [128, 8, 128], F32); CTb = sb("CTb", [128, 8, 128], BF16)
            nT = sb("nT", [128, 8], F32); nb = sb("nb", [128, 8], BF16)
            t1 = sb("t1", [128, D], F32); h1 = t1; bufs["h1"] = bufs["t1"]; h1b = cout[0]; bufs["h1b"] = bufs["cout0"]
            h1T = sb("h1T", [128, 16, 128], BF16)
            sm = sb("sm", [128, 256], F32)
            smb = sb("smb", [128, 16], BF16)
            stats = sb("stats", [128, 8, 6], F32); mv = sb("mv", [128, 8, 2], F32)
            Bf = bufs
            def smcol(name, c0, n):
                bufs[name] = Buf(name)
                return sm[:, c0:c0 + n]
            gx = smcol("gx", 0, 16); ef = smcol("ef", 16, 8); lf = smcol("lf", 24, 8)
            bc = smcol("bc", 32, 8); wkt = smcol("wkt", 40, 8); wk = smcol("wk", 48, 8)
            bias8 = smcol("bias8", 56, 8); dec = smcol("dec", 64, 8); rr = smcol("rr", 72, 8)
            t8 = smcol("t8", 80, 8); sc8 = smcol("sc8", 88, 8); lnmv = smcol("lnmv", 96, 2)
            lnr = smcol("lnr", 98, 1); lnst = smcol("lnst", 100, 24)
            bufs["wkb"] = Buf("wkb")
            wkb = smb[:, 0:8]

            O.memset("pool", CT[:], 0.0, [Bf["CT"]]); O.memset("pool", CTb[:], 0.0, [Bf["CTb"]])
            O.memset("pool", nT[:], 0.0, [Bf["nT"]]); O.memset("pool", nb[:], 0.0, [Bf["nb"]])
            O.memset("pool", zbuf[:], 0.0, [Bf["zbuf"]])

            Wi_v = Wi_b.rearrange("(k p) c -> p k c", p=128)
            Wo_v = Wo_b.rearrange("(k p) c -> p k c", p=128)
            loads = []
            plan = []

            def wload(view, c0, srcbuf):
                def f(slot):
                    return [(slot[:, :, :], view[:, :, c0:c0 + 512])]
                return f

            for i in range(NT):
                own = i >= NH
                tags = []
                if own or i == NH - 1:
                    for c0 in (1024, 1536):
                        tags.append(("fC", c0))
                    for c0 in (2048, 2560):
                        tags.append(("fh", c0))
                if own:
                    for c0 in (0, 512):
                        tags.append(("fB", c0))
                    for c0 in (3072, 3584):
                        tags.append(("fq", c0))
                for c0 in (4096, 4608):
                    tags.append(("k", c0))
                for c0 in (5120, 5632):
                    tags.append(("v", c0))
                if own:
                    for c0 in (6144, 6656):
                        tags.append(("o", c0))
                    for c0 in (0, 512, 1024, 1536):
                        tags.append(("wo", c0))
                plan.append(tags)
                for (tg, c0) in tags:
                    loads.append(wload(Wo_v if tg == "wo" else Wi_v, c0, None))
            wstream = Stream(O, wslots, [bufs[f"wg{i}"] for i in range(3)], loads)
            for i in range(3):
                pass
            orig_issue = wstream._issue

            def issue_with_deps():
                if wstream.issued >= len(wstream.loads):
                    return
                k = wstream.issued
                s = k % wstream.n
                for (o, i_) in wstream.loads[k](wstream.slots[s]):
                    O.dma(o, i_, [B_Wi, B_Wo], [wstream.bufs[s]], wstream.bufs[s], partial=True)
                wstream.issued += 1
            wstream._issue = issue_with_deps

            proj_banks = [6, 4, 5]
            pcount = [0]

            def next_pbank():
                b = proj_banks[pcount[0] % 3]
                pcount[0] += 1
                return b

            def layernorm_tile(src, src_bufs, dst, dst_buf, gt, bt, gbuf, bbuf):
                for q in range(4):
                    T.op("dve", (lambda q=q: (lambda e: e.bn_stats(out=lnst[:, q * 6:(q + 1) * 6], in_=src[:, q * 512:(q + 1) * 512])))(),
                         src_bufs, [Bf["lnst"]], partial=(q > 0))
                T.op("dve", lambda e: e.bn_aggr(out=lnmv, in_=lnst), [Bf["lnst"]], [Bf["lnmv"]])
                O.act(lnr, lnmv[:, 1:2], AF.Ln, [Bf["lnmv"]], [Bf["lnr"]], bias=EPS, scale=1.0)
                O.act(lnr, lnr, AF.Exp, [Bf["lnr"]], [Bf["lnr"]], scale=-0.5)
                O.ts("dve", dst, src, lnmv[:, 0:1], lnr[:, 0:1], ALU.subtract, ALU.mult, src_bufs + [Bf["lnmv"], Bf["lnr"]], [dst_buf])
                O.tt("pool", dst, dst, gt, ALU.mult, [dst_buf, gbuf], [dst_buf])
                O.tt("pool", dst, dst, bt, ALU.add, [dst_buf, bbuf], [dst_buf])

            def transpose_2048(srcb, srcb_buf, dstT, dstT_buf):
                for half in range(2):
                    bk = next_pbank()
                    pb = banks[bk][:].bitcast(BF16)
                    for kk in range(8):
                        k = half * 8 + kk
                        O.tr(pb[:, kk * 128:(kk + 1) * 128], srcb[:, k * 128:(k + 1) * 128], identb[:], [srcb_buf, Bf["identb"]], [BK[bk]])
                    eng = "act" if half == 0 else "dve"
                    O.cp(eng, dstT[:, half * 8:(half + 1) * 8, :].rearrange("p k t -> p (k t)"), pb, [BK[bk]], [dstT_buf], partial=True)

            def flat(ap3):
                return ap3.rearrange("p h l -> p (h l)")

            def mlstm_tile(own, i):
                tri_b = tri[:].unsqueeze(1).to_broadcast([128, 8, 128])
                if own:
                    O.tt("pool", TriLF[:], tri_b, lf.unsqueeze(2).to_broadcast([128, 8, 128]), ALU.mult, [Bf["tri"], Bf["lf"]], [Bf["TriLF"]])
                    TL2 = flat(TriLF[:])
                    for half in range(2):
                        O.mm(banks[half][:], onesf[:], TL2[:, half * 512:(half + 1) * 512], True, True, [Bf["onesf"], Bf["TriLF"]], [BK[half]])
                    O.tt("dve", bias8, gx[:, 0:8], bc, ALU.subtract, [Bf["gx"], Bf["bc"]], [Bf["bias8"]])
                    for h in range(8):
                        bk = h // 4; col = (h % 4) * 128
                        O.act(PT[:, h, :], banks[bk][:, col:col + 128], AF.Exp, [BK[bk], Bf["bias8"]], [Bf["PT"]],
                              bias=bias8[:, h:h + 1], scale=1.0, partial=True)
                    O.tt("pool", PT[:], PT[:], tri_b, ALU.mult, [Bf["PT"], Bf["tri"]], [Bf["PT"]])
                    for h in range(8):
                        bk = 2 + h // 4; col = (h % 4) * 128
                        O.mm(banks[bk][:, col:col + 128], KT[:, h, :], QT[:, h, :], True, True, [Bf["KT"], Bf["QT"]], [BK[bk]])
                    for half in range(2):
                        O.tt("dve", flat(sw[:, half * 4:(half + 1) * 4, :]), flat(PT[:, half * 4:(half + 1) * 4, :]), banks[2 + half][:], ALU.mult,
                             [Bf["PT"], BK[2 + half]], [Bf["sw"]], partial=True)
                    for half in range(2):
                        O.act(flat(EB[:, half * 4:(half + 1) * 4, :]), banks[half][:], AF.Exp, [BK[half]], [Bf["EB"]], partial=True)
                    O.tt("pool", QsT[:], QT[:], EB[:], ALU.mult, [Bf["QT"], Bf["EB"]], [Bf["QsT"]])
                    for h in range(8):
                        bk = 4 + h // 4; col = (h % 4) * 128
                        O.mm(banks[bk][:, col:col + 128], sw[:, h, :], Vt[:, h, :], True, False, [Bf["sw"], Bf["Vt"]], [BK[bk]])
                        O.mm(banks[bk][:, col:col + 128], QsT[:, h, :], CTb[:, h, :], False, True, [Bf["QsT"], Bf["CTb"]], [BK[bk]])
                    for h in range(8):
                        O.mm(banks[7][:, 32 + h:33 + h], sw[:, h, :], onesb[:, 0:1], True, False, [Bf["sw"], Bf["onesb"]], [BK[7]])
                        O.mm(banks[7][:, 32 + h:33 + h], QsT[:, h, :], nb[:, h:h + 1], False, True, [Bf["QsT"], Bf["nb"]], [BK[7]])
                    O.ts("dve", t8, banks[7][:, 32:40], -1.0, 1.0, ALU.mult, ALU.max, [BK[7]], [Bf["t8"]])
                    O.ts("dve", rr, banks[7][:, 32:40], 1.0, None, ALU.max, None, [BK[7]], [Bf["rr"]])
                    O.tt("dve", rr, rr, t8, ALU.max, [Bf["rr"], Bf["t8"]], [Bf["rr"]])
                    T.op("dve", lambda e: e.reciprocal(out=rr, in_=rr), [Bf["rr"]], [Bf["rr"]])
                    for h in range(8):
                        bk = 4 + h // 4; col = (h % 4) * 128
                        T.op("dve", (lambda h=h, bk=bk, col=col: (lambda e: e.bn_stats(out=stats[:, h, :], in_=banks[bk][:, col:col + 128])))(),
                             [BK[bk]], [Bf["stats"]], partial=True)
                    for h in range(8):
                        T.op("dve", (lambda h=h: (lambda e: e.bn_aggr(out=mv[:, h, :], in_=stats[:, h, :])))(), [Bf["stats"]], [Bf["mv"]], partial=True)
                    O.tt("dve", t8, rr, rr, ALU.mult, [Bf["rr"]], [Bf["t8"]])
                    O.tt("dve", t8, t8, mv[:, :, 1], ALU.mult, [Bf["t8"], Bf["mv"]], [Bf["t8"]])
                    O.act(t8, t8, AF.Ln, [Bf["t8"]], [Bf["t8"]], bias=EPS, scale=1.0)
                    O.act(t8, t8, AF.Exp, [Bf["t8"]], [Bf["t8"]], scale=-0.5)
                    O.tt("dve", sc8, t8, rr, ALU.mult, [Bf["t8"], Bf["rr"]], [Bf["sc8"]])
                    for h in range(8):
                        bk = 4 + h // 4; col = (h % 4) * 128
                        O.ts("dve", yn[:, h, :], banks[bk][:, col:col + 128], mv[:, h, 0:1], sc8[:, h:h + 1], ALU.subtract, ALU.mult,
                             [BK[bk], Bf["mv"], Bf["sc8"]], [Bf["yn"]], partial=True)
                    O.tt("pool", flat(yn[:]), flat(yn[:]), mhg[:], ALU.mult, [Bf["yn"], Bf["mhg"]], [Bf["yn"]])
                    O.tt("pool", ym[:], flat(yn[:]), sig[:], ALU.mult, [Bf["yn"], Bf["sig"]], [Bf["ym"]])
                    bk = next_pbank()
                    pb = banks[bk][:].bitcast(BF16)
                    for h in range(8):
                        O.tr(pb[:, h * 128:(h + 1) * 128], ym[:, h * 128:(h + 1) * 128], identb[:], [Bf["ym"], Bf["identb"]], [BK[bk]])
                    O.cp("act", flat(mixT[:, 8:16, :]), pb, [BK[bk]], [Bf["mixT"]], partial=True)
                O.tt("dve", wkt, banks[7][:, 24:32], bc, ALU.subtract, [BK[7], Bf["bc"]], [Bf["wkt"]])
                O.tt("dve", wkt, wkt, gx[:, 0:8], ALU.add, [Bf["wkt"], Bf["gx"]], [Bf["wkt"]])
                O.act(wk, wkt, AF.Exp, [Bf["wkt"]], [Bf["wk"]])
                O.cp("dve", wkb, wk, [Bf["wk"]], [Bf["wkb"]])
                O.act(dec, banks[7][:, 24:32], AF.Exp, [BK[7]], [Bf["dec"]])
                O.tt("pool", wkV[:], Vt[:], wk.unsqueeze(2).to_broadcast([128, 8, 128]), ALU.mult, [Bf["Vt"], Bf["wk"]], [Bf["wkV"]])
                for h in range(8):
                    bk = h // 4; col = (h % 4) * 128
                    O.mm(banks[bk][:, col:col + 128], Ktok[:, h, :], wkV[:, h, :], True, True, [Bf["Ktok"], Bf["wkV"]], [BK[bk]])
                for h in range(8):
                    O.mm(banks[7][:, 40 + h:41 + h], Ktok[:, h, :], wkb[:, h:h + 1], True, True, [Bf["Ktok"], Bf["wkb"]], [BK[7]])
                O.tt("pool", CT[:], CT[:], dec.unsqueeze(2).to_broadcast([128, 8, 128]), ALU.mult, [Bf["CT"], Bf["dec"]], [Bf["CT"]])
                for half in range(2):
                    O.tt("dve", flat(CT[:, half * 4:(half + 1) * 4, :]), flat(CT[:, half * 4:(half + 1) * 4, :]), banks[half][:], ALU.add,
                         [Bf["CT"], BK[half]], [Bf["CT"]], partial=True)
                O.cp("act", CTb[:], CT[:], [Bf["CT"]], [Bf["CTb"]])
                O.tt("dve", nT[:], nT[:], dec, ALU.mult, [Bf["nT"], Bf["dec"]], [Bf["nT"]])
                O.tt("dve", nT[:], nT[:], banks[7][:, 40:48], ALU.add, [Bf["nT"], BK[7]], [Bf["nT"]])
                O.cp("dve", nb[:], nT[:], [Bf["nT"]], [Bf["nb"]])

            own_idx = 0
            for i in range(NT):
                own = i >= NH
                last_hist = (i == NH - 1)
                xs = i % 2
                xtile = xt[xs]; xbuf = bufs[f"xt{xs}"]
                O.dma(xtile[:], xin[i * 128:(i + 1) * 128, :], (), [xbuf], xbuf)
                layernorm_tile(xtile[:], [xbuf], h0[:], Bf["h0"], g_in[:], b_in[:], Bf["g_in"], Bf["b_in"])
                if own:
                    O.dma(h0d[own_idx * 128:(own_idx + 1) * 128, :], h0[:], [Bf["h0"]], [B_h0d], Bf["h0"], partial=True)
                O.cp("act", h0b[:], h0[:], [Bf["h0"]], [Bf["h0b"]])
                transpose_2048(h0b, Bf["h0b"], h0T, Bf["h0T"])

                for k in range(16):
                    O.mm(banks[7][:, 0:16], h0T[:, k, :], wgate[:, k, :], k == 0, k == 15, [Bf["h0T"], Bf["wgate"]], [BK[7]])
                O.tt("dve", gx, banks[7][:, 0:16], bgate[:], ALU.add, [BK[7], Bf["bgate"]], [Bf["gx"]])
                if not own:
                    O.ts("dve", gx[:, 0:8], gx[:, 0:8], hmask[:, i:i + 1], None, ALU.add, None, [Bf["gx"], Bf["hmask"]], [Bf["gx"]])
                O.act(ef, gx[:, 8:16], AF.Exp, [Bf["gx"]], [Bf["ef"]], scale=-1.0)
                O.act(ef, ef, AF.Ln, [Bf["ef"]], [Bf["ef"]], bias=1.0, scale=1.0)
                O.ts("dve", lf, ef, -1.0, None, ALU.mult, None, [Bf["ef"]], [Bf["lf"]])
                O.mm(banks[7][:, 16:24], tri[:], lf, True, True, [Bf["tri"], Bf["lf"]], [BK[7]])
                O.mm(banks[7][:, 24:32], onesf[:], lf, True, True, [Bf["onesf"], Bf["lf"]], [BK[7]])
                O.cp("dve", bc, banks[7][:, 16:24], [BK[7]], [Bf["bc"]])

                for (tg, c0) in plan[i]:
                    slot, sbuf_ = wstream.next()
                    if tg in ("fC", "fh", "fB", "fq"):
                        bk = next_pbank()
                        for cc in range(4):
                            for k in range(16):
                                O.mm(banks[bk][:, cc * 128:(cc + 1) * 128], slot[:, k, cc * 128:(cc + 1) * 128], h0T[:, k, :],
                                     k == 0, k == 15, [sbuf_, Bf["h0T"]], [BK[bk]])
                        pv3 = banks[bk][:].rearrange("p (c t) -> p c t", c=4)
                        if tg == "fC":
                            ch0 = (c0 - 1024) // 128
                            O.cp("act", Cf[:, ch0:ch0 + 4, :], pv3, [BK[bk]], [Bf["Cf"]], partial=True)
                        elif tg == "fh":
                            ch0 = (c0 - 2048) // 128
                            O.tt("dve", zbuf[:, ch0:ch0 + 4, 2:130], pv3, Cf[:, ch0:ch0 + 4, :], ALU.mult, [BK[bk], Bf["Cf"]], [Bf["zbuf"]], partial=True)
                            if last_hist and c0 == 2560:
                                O.ts("pool", zbuf[:, :, 0:2], zbuf[:, :, 128:130], hvalid[:, 0:1], None, ALU.mult, None,
                                     [Bf["zbuf"], Bf["hvalid"]], [Bf["zbuf"]])
                        elif tg == "fB":
                            ch0 = c0 // 128
                            for cc in range(4):
                                c = ch0 + cc
                                O.ts("dve", acc[:], zbuf[:, c, 2:130], cw[:, 2, c:c + 1], cbias[:, c:c + 1], ALU.mult, ALU.add,
                                     [Bf["zbuf"], Bf["cw"], Bf["cbias"]], [Bf["acc"]])
                                O.stt("dve", acc[:], zbuf[:, c, 1:129], cw[:, 1, c:c + 1], acc[:], ALU.mult, ALU.add,
                                      [Bf["zbuf"], Bf["cw"], Bf["acc"]], [Bf["acc"]])
                                O.stt("dve", acc[:], zbuf[:, c, 0:128], cw[:, 0, c:c + 1], acc[:], ALU.mult, ALU.add,
                                      [Bf["zbuf"], Bf["cw"], Bf["acc"]], [Bf["acc"]])
                                O.tt("dve", mixT[:, c, :], banks[bk][:, cc * 128:(cc + 1) * 128], acc[:], ALU.mult, [BK[bk], Bf["acc"]], [Bf["mixT"]], partial=True)
                            if c0 == 512:
                                O.cp("pool", zbuf[:, :, 0:2], zbuf[:, :, 128:130], [Bf["zbuf"]], [Bf["zbuf"]])
                        elif tg == "fq":
                            ch0 = (c0 - 3072) // 128
                            O.cp("act", QT[:, ch0:ch0 + 4, :], pv3, [BK[bk]], [Bf["QT"]], partial=True)
                    elif tg == "k":
                        ch0 = (c0 - 4096) // 128
                        if own:
                            bk = next_pbank()
                            for cc in range(4):
                                for k in range(16):
                                    O.mm(banks[bk][:, cc * 128:(cc + 1) * 128], slot[:, k, cc * 128:(cc + 1) * 128], h0T[:, k, :],
                                         k == 0, k == 15, [sbuf_, Bf["h0T"]], [BK[bk]])
                            pv3 = banks[bk][:].rearrange("p (c t) -> p c t", c=4)
                            O.act(KT[:, ch0:ch0 + 4, :], pv3, AF.Copy, [BK[bk]], [Bf["KT"]], scale=KSCALE, partial=True)
                        bk = next_pbank()
                        for k in range(16):
                            O.mm(banks[bk][:], h0T[:, k, :], slot[:, k, :], k == 0, k == 15, [sbuf_, Bf["h0T"]], [BK[bk]])
                        O.act(Ktok[:, ch0:ch0 + 4, :].rearrange("p h d -> p (h d)"), banks[bk][:], AF.Copy, [BK[bk]], [Bf["Ktok"]], scale=KSCALE, partial=True)
                    elif tg == "v":
                        ch0 = (c0 - 5120) // 128
                        bk = next_pbank()
                        for k in range(16):
                            O.mm(banks[bk][:], h0T[:, k, :], slot[:, k, :], k == 0, k == 15, [sbuf_, Bf["h0T"]], [BK[bk]])
                        O.cp("dve", Vt[:, ch0:ch0 + 4, :].rearrange("p h d -> p (h d)"), banks[bk][:], [BK[bk]], [Bf["Vt"]], partial=True)
                    elif tg == "o":
                        cc0 = c0 - 6144
                        bk = next_pbank()
                        for k in range(16):
                            O.mm(banks[bk][:], h0T[:, k, :], slot[:, k, :], k == 0, k == 15, [sbuf_, Bf["h0T"]], [BK[bk]])
                        O.act(sig[:, cc0:cc0 + 512], banks[bk][:], AF.Exp, [BK[bk]], [Bf["sig"]], scale=-1.0, partial=True)
                        O.ts("pool", sig[:, cc0:cc0 + 512], sig[:, cc0:cc0 + 512], 1.0, None, ALU.add, None, [Bf["sig"]], [Bf["sig"]], partial=True)
                        if cc0 == 512:
                            T.op("dve", lambda e: e.reciprocal(out=sig[:], in_=sig[:]), [Bf["sig"]], [Bf["sig"]])
                            mlstm_tile(True, i)
                    elif tg == "wo":
                        bk = next_pbank()
                        for k in range(16):
                            O.mm(banks[bk][:], mixT[:, k, :], slot[:, k, :], k == 0, k == 15, [sbuf_, Bf["mixT"]], [BK[bk]])
                        O.stt("dve", t1[:, c0:c0 + 512], h0[:, c0:c0 + 512], ALPHA, banks[bk][:], ALU.mult, ALU.add,
                              [Bf["h0"], BK[bk]], [Bf["t1"]], partial=True)
                        if c0 == 1536:
                            layernorm_tile(t1[:], [Bf["t1"]], h1[:], Bf["h1"], g_1[:], b_1[:], Bf["g_1"], Bf["b_1"])
                            O.dma(h1d[own_idx * 128:(own_idx + 1) * 128, :], h1[:], [Bf["h1"]], [B_h1d], Bf["h1"], partial=True)
                            if debug:
                                O.dma(dbg_d[own_idx * 128:(own_idx + 1) * 128, :], h1[:], [Bf["h1"]], [], Bf["h1"])
                            O.cp("act", h1b[:], h1[:], [Bf["h1"]], [Bf["h1b"]])
                            transpose_2048(h1b, Bf["h1b"], h1T, Bf["h1T"])
                            O.dma(h1Td[own_idx], h1T[:].rearrange("p k t -> p (k t)"), [Bf["h1T"]], [B_h1Td], Bf["h1T"], partial=True)
                    if tg == "v" and c0 == 5632 and not own:
                        mlstm_tile(False, i)
                if own:
                    own_idx += 1

            final_barrier(T, O, None, bar, banks[7][0:1, 500:501])
            T.replay(outer, st)

        with ExitStack() as st:
            T = Tracker(nc)
            O = Ops(T)
            bufs = {}

            def sb(name, shape, dt):
                t = st.enter_context(nc.sbuf_tensor("sb_" + name, list(shape), dt))
                bufs[name] = Buf(name)
                return t

            NB = NO // 2
            banks = [st.enter_context(nc.psum_tensor(f"pbank{i}", [128, 512], F32)) for i in range(8)]
            BK = [Buf(f"pbank{i}") for i in range(8)]
            identb = sb("identb2", [128, 128], BF16); identf = sb("identf2", [128, 128], F32)
            bar = sb("bar2", [128, 16], F32)
            g_2 = sb("g_2", [128, D], F32); b_2 = sb("b_2", [128, D], F32)
            iota3 = sb("iota3", [128, 8, 128], BF16); iota16 = sb("iota16", [128, 16], F32)
            keysb = sb("keysb", [128, 16, 128], BF16)
            keysT = sb("keysT", [128, 16, 128], BF16)
            Bf = bufs

            def load_const(t, src, name):
                O.dma(t, src, (), [bufs[name]], bufs[name])
            load_const(identb[:], identb_d, "identb2"); load_const(identf[:], identf_d, "identf2")
            load_const(g_2[:], ln2_g.partition_broadcast(128), "g_2"); load_const(b_2[:], ln2_b.partition_broadcast(128), "b_2")
            load_const(iota3[:].rearrange("p t i -> p (t i)"), iota3_d[:, 0:1024], "iota3")
            load_const(iota16[:], iota16_d[:, 0:16], "iota16")

            h1T = [sb(f"h1T2_{i}", [128, 16, 256], BF16) for i in range(2)]
            wslots = [sb(f"wq{i}", [128, 16, 128], BF16) for i in range(2)]
            qT = sb("qT", [128, 16, 128], BF16)
            s2 = sb("s2", [128, 128], F32)
            sv = sb("sv", [128, 16, 16], F32); si = sb("si", [128, 16, 16], U32); sif = sb("sif", [128, 16, 16], F32)
            cand = sb("cand", [128, 8, 256], F32); c2 = sb("c2", [128, 256], F32)
            tv = sb("tv", [128, 8, 16], F32); ci = sb("ci", [128, 8, 16], U32)
            ca = sb("ca", [128, 8, 16], U32); cb_ = sb("cb_", [128, 8, 16], U32)
            caf = sb("caf", [128, 8, 16], F32); cbf = sb("cbf", [128, 8, 16], F32)
            oh = cand[:].rearrange("p h (a b) -> p h a b", a=16); bufs["oh"] = bufs["cand"]
            sel = sb("sel", [128, 3, 128], F32); selT = sb("selT", [128, 3, 256], F32)
            ee = sb("ee", [128, 8, 16], F32); zz = sb("zz", [128, 8], F32)
            Pb = [sb(f"Pb{i}", [128, 8, 128], BF16) for i in range(2)]
            Qb = [sb(f"Qb{i}", [128, 8, 128], BF16) for i in range(2)]
            Wsb = sb("Wsb", [128, 128, 256], BF16)
            WB = [Buf(f"WB{i}") for i in range(64)]
            uslots = [sb(f"us{i}", [128, 16, 128], BF16) for i in range(4)]
            vslots = [sb(f"vs{i}", [128, 2, 1024], BF16) for i in range(3)]
            ga = [sb(f"ga{i}", [128, 512], F32) for i in range(2)]
            t2 = [sb(f"t2_{i}", [128, D], F32) for i in range(2)]
            s_ = sb("s_", [128, 16, 128], F32)
            sm = sb("sm2", [128, 64], F32)

            keysf = t2[0][:].rearrange("p (a n) -> p a n", a=16); bufs["keysf"] = bufs["t2_0"]
            load_const(keysf, keys.rearrange("h p n c -> n (h p) c"), "keysf")
            O.memset("pool", bar[:], 0.0, [Bf["bar2"]])
            O.cp("dve", keysb[:], keysf, [Bf["keysf"]], [Bf["keysb"]])
            for half in range(2):
                bk = 6 + half
                pb = banks[bk][:].bitcast(BF16)
                for kk in range(8):
                    hp = half * 8 + kk
                    O.tr(pb[:, kk * 128:(kk + 1) * 128], keysb[:, hp, :], identb[:], [Bf["keysb"], Bf["identb2"]], [BK[bk]])
                O.cp("dve", keysT[:, half * 8:(half + 1) * 8, :].rearrange("p k t -> p (k t)"), pb, [BK[bk]], [Bf["keysT"]], partial=True)

            def smcol(name, c0, n):
                bufs[name] = Buf(name)
                return sm[:, c0:c0 + n]
            lnmv = smcol("lnmv", 0, 2); lnr = smcol("lnr", 2, 1); lnst = smcol("lnst", 4, 24)

            def layernorm_tile(src, src_bufs, dst, dst_buf, gt, bt, gbuf, bbuf):
                for q in range(4):
                    T.op("dve", (lambda q=q: (lambda e: e.bn_stats(out=lnst[:, q * 6:(q + 1) * 6], in_=src[:, q * 512:(q + 1) * 512])))(),
                         src_bufs, [Bf["lnst"]], partial=(q > 0))
                T.op("dve", lambda e: e.bn_aggr(out=lnmv, in_=lnst), [Bf["lnst"]], [Bf["lnmv"]])
                O.act(lnr, lnmv[:, 1:2], AF.Ln, [Bf["lnmv"]], [Bf["lnr"]], bias=EPS, scale=1.0)
                O.act(lnr, lnr, AF.Exp, [Bf["lnr"]], [Bf["lnr"]], scale=-0.5)
                O.ts("dve", dst, src, lnmv[:, 0:1], lnr[:, 0:1], ALU.subtract, ALU.mult, src_bufs + [Bf["lnmv"], Bf["lnr"]], [dst_buf])
                O.tt("pool", dst, dst, gt, ALU.mult, [dst_buf, gbuf], [dst_buf])
                O.tt("pool", dst, dst, bt, ALU.add, [dst_buf, bbuf], [dst_buf])

            Wq_v = Wq_b.rearrange("(k p) c -> p k c", p=128)
            qloads = []
            uloads = []
            vloads = []
            for blk in range(NB):
                for tile in range(2):
                    for c0 in range(0, 2048, 128):
                        qloads.append((lambda c0=c0: (lambda slot: [(slot[:, :, :], Wq_v[:, :, c0:c0 + 128])]))())
                for g in range(NG):
                    uloads.append((lambda g=g: (lambda slot: [(slot[:].rearrange("p k j -> p (k j)"), uT_d[g])]))())
                for half in range(2):
                    for gp in range(64):
                        vloads.append((lambda gp=gp, half=half: (lambda slot: [(slot[:, :, :], vbh_d[half][gp * 256:(gp + 1) * 256, :].rearrange("(g j) d -> j g d", j=128))]))())
            qstream = Stream(O, wslots, [bufs[f"wq{i}"] for i in range(2)], qloads)
            ustream = Stream(O, uslots, [bufs[f"us{i}"] for i in range(4)], uloads)
            vstream = Stream(O, vslots, [bufs[f"vs{i}"] for i in range(3)], vloads)

            iota16_b = iota16[:].unsqueeze(1).unsqueeze(1).to_broadcast([128, 8, 16, 16])

            def sel_gen(blk):
                hb = blk % 2
                hT = h1T[hb]; hTb = Bf[f"h1T2_{hb}"]
                for tile in range(2):
                    ti = blk * 2 + tile
                    O.dma(hT[:, :, tile * 128:(tile + 1) * 128], h1Td[ti].rearrange("p (k t) -> p k t", k=16), (), [hTb], hTb, partial=True)
                yield
                for tile in range(2):
                    tsl = slice(tile * 128, (tile + 1) * 128)
                    for grp in range(16):
                        slot, sbuf_ = qstream.next()
                        bk = 6 + (grp % 2)
                        for k in range(16):
                            O.mm(banks[bk][:, 0:128], slot[:, k, :], hT[:, k, tsl], k == 0, k == 15, [sbuf_, hTb], [BK[bk]])
                        O.cp("act", qT[:, grp, :], banks[bk][:, 0:128], [BK[bk]], [Bf["qT"]], partial=True)
                        if grp % 2 == 1:
                            yield
                    for q4 in range(4):
                        bk = 6 + (q4 % 2)
                        for a in range(4):
                            hp = q4 * 4 + a
                            O.mm(banks[bk][:, a * 128:(a + 1) * 128], qT[:, hp, :], keysT[:, hp, :], True, True, [Bf["qT"], Bf["keysT"]], [BK[bk]])
                        O.cp("act", s_[:, q4 * 4:(q4 + 1) * 4, :].rearrange("p a n -> p (a n)"), banks[bk][:], [BK[bk]], [Bf["s_"]], partial=True)
                    yield
                    for hp in range(16):
                        T.op("dve", (lambda hp=hp: (lambda e: e.max(out=sv[:, hp, 0:8], in_=s_[:, hp, :])))(), [Bf["s_"]], [Bf["sv"]], partial=True)
                        T.op("dve", (lambda hp=hp: (lambda e: e.match_replace(out=s2[:], in_to_replace=sv[:, hp, 0:8], in_values=s_[:, hp, :], imm_value=-1e30)))(),
                             [Bf["s_"], Bf["sv"]], [Bf["s2"]])
                        T.op("dve", (lambda hp=hp: (lambda e: e.max(out=sv[:, hp, 8:16], in_=s2[:])))(), [Bf["s2"]], [Bf["sv"]], partial=True)
                        T.op("dve", (lambda hp=hp: (lambda e: e.max_index(out=si[:, hp, 0:8], in_max=sv[:, hp, 0:8], in_values=s_[:, hp, :])))(),
                             [Bf["s_"], Bf["sv"]], [Bf["si"]], partial=True)
                        T.op("dve", (lambda hp=hp: (lambda e: e.max_index(out=si[:, hp, 8:16], in_max=sv[:, hp, 8:16], in_values=s_[:, hp, :])))(),
                             [Bf["s_"], Bf["sv"]], [Bf["si"]], partial=True)
                        yield
                    O.cp("dve", sif[:], si[:], [Bf["si"]], [Bf["sif"]])
                    sv4 = sv[:].rearrange("p (h two) a -> p h two a", two=2)
                    sif4 = sif[:].rearrange("p (h two) a -> p h two a", two=2)
                    O.tt("dve", cand[:].rearrange("p h (a b) -> p h a b", a=16),
                         sv4[:, :, 0, :].unsqueeze(3).to_broadcast([128, 8, 16, 16]),
                         sv4[:, :, 1, :].unsqueeze(2).to_broadcast([128, 8, 16, 16]), ALU.add, [Bf["sv"]], [Bf["cand"]])
                    yield
                    for h in range(8):
                        T.op("dve", (lambda h=h: (lambda e: e.max(out=tv[:, h, 0:8], in_=cand[:, h, :])))(), [Bf["cand"]], [Bf["tv"]], partial=True)
                        T.op("dve", (lambda h=h: (lambda e: e.match_replace(out=c2[:], in_to_replace=tv[:, h, 0:8], in_values=cand[:, h, :], imm_value=-1e30)))(),
                             [Bf["cand"], Bf["tv"]], [Bf["c2"]])
                        T.op("dve", (lambda h=h: (lambda e: e.max(out=tv[:, h, 8:16], in_=c2[:])))(), [Bf["c2"]], [Bf["tv"]], partial=True)
                        T.op("dve", (lambda h=h: (lambda e: e.max_index(out=ci[:, h, 0:8], in_max=tv[:, h, 0:8], in_values=cand[:, h, :])))(),
                             [Bf["cand"], Bf["tv"]], [Bf["ci"]], partial=True)
                        T.op("dve", (lambda h=h: (lambda e: e.max_index(out=ci[:, h, 8:16], in_max=tv[:, h, 8:16], in_values=cand[:, h, :])))(),
                             [Bf["cand"], Bf["tv"]], [Bf["ci"]], partial=True)
                        yield
                    T.op("dve", lambda e: e.tensor_single_scalar(out=ca[:], in_=ci[:], scalar=4, op=ALU.logical_shift_right), [Bf["ci"]], [Bf["ca"]])
                    T.op("dve", lambda e: e.tensor_single_scalar(out=cb_[:], in_=ci[:], scalar=15, op=ALU.bitwise_and), [Bf["ci"]], [Bf["cb_"]])
                    O.cp("dve", caf[:], ca[:], [Bf["ca"]], [Bf["caf"]])
                    O.cp("dve", cbf[:], cb_[:], [Bf["cb_"]], [Bf["cbf"]])
                    yield
                    for which, idxf in ((0, caf), (1, cbf)):
                        O.tt("dve", oh, iota16_b, idxf[:].unsqueeze(3).to_broadcast([128, 8, 16, 16]), ALU.is_equal,
                             [Bf["iota16"], Bf["caf" if which == 0 else "cbf"]], [Bf["oh"]])
                        yield
                        O.tt("pool", oh, oh, sif4[:, :, which, :].unsqueeze(2).to_broadcast([128, 8, 16, 16]), ALU.mult,
                             [Bf["oh"], Bf["sif"]], [Bf["oh"]])
                        T.op("dve", (lambda which=which: (lambda e: e.tensor_reduce(out=sel[:, which, :].rearrange("p (h k) -> p h k", h=8), in_=oh, axis=AX.X, op=ALU.add)))(),
                             [Bf["oh"]], [Bf["sel"]], partial=True)
                        yield
                    O.tt("dve", ee[:], tv[:], tv[:, :, 0:1].to_broadcast([128, 8, 16]), ALU.subtract, [Bf["tv"]], [Bf["ee"]])
                    O.act(ee[:], ee[:], AF.Exp, [Bf["ee"]], [Bf["ee"]])
                    T.op("dve", lambda e: e.tensor_reduce(out=zz[:], in_=ee[:], axis=AX.X, op=ALU.add), [Bf["ee"]], [Bf["zz"]])
                    T.op("dve", lambda e: e.reciprocal(out=zz[:], in_=zz[:]), [Bf["zz"]], [Bf["zz"]])
                    O.tt("dve", sel[:, 2, :].rearrange("p (h k) -> p h k", h=8), ee[:], zz[:].unsqueeze(2).to_broadcast([128, 8, 16]), ALU.mult,
                         [Bf["ee"], Bf["zz"]], [Bf["sel"]], partial=True)
                    yield
                    for w3 in range(3):
                        O.tr(banks[6][:, w3 * 128:(w3 + 1) * 128], sel[:, w3, :], identf[:], [Bf["sel"], Bf["identf2"]], [BK[6]])
                    O.cp("dve", selT[:, :, tsl], banks[6][:, 0:384].rearrange("p (w t) -> p w t", w=3), [BK[6]], [Bf["selT"]], partial=True)
                    yield

            def expand(blk):
                wb_rot = 0
                for tb in range(32):
                    pq = tb % 2
                    tk0 = tb * 8
                    i2b = selT[:, 1, tk0:tk0 + 8].unsqueeze(2).to_broadcast([128, 8, 128])
                    O.tt("dve", Qb[pq][:], iota3[:], i2b, ALU.is_equal, [Bf["iota3"], Bf["selT"]], [Bf[f"Qb{pq}"]])
                    for tl in range(8):
                        tk = tk0 + tl
                        O.ts("pool", Pb[pq][:, tl, :], iota3[:, 0, :], selT[:, 0, tk:tk + 1], selT[:, 2, tk:tk + 1], ALU.is_equal, ALU.mult,
                             [Bf["iota3"], Bf["selT"]], [Bf[f"Pb{pq}"]], partial=(tl > 0))
                    for t4 in range(2):
                        bk = 4 + (wb_rot % 4)
                        wb_rot += 1
                        for tt_ in range(4):
                            tl = t4 * 4 + tt_
                            O.mm(banks[bk][:, tt_ * 128:(tt_ + 1) * 128], Qb[pq][:, tl, :], Pb[pq][:, tl, :], True, True,
                                 [Bf[f"Qb{pq}"], Bf[f"Pb{pq}"]], [BK[bk]])
                        t0 = tk0 + t4 * 4
                        O.cp("act", Wsb[:, :, t0:t0 + 4].rearrange("j g t -> j t g"), banks[bk][:].rearrange("p (t g) -> p t g", t=4),
                             [BK[bk]], WB, partial=True)

            def step(gen):
                if gen is not None:
                    try:
                        next(gen)
                    except StopIteration:
                        return None
                return gen

            def drain(gen):
                while gen is not None:
                    gen = step(gen)

            drain(sel_gen(0))
            expand(0)
            for blk in range(NB):
                hb = blk % 2
                hT = h1T[hb]; hTb = Bf[f"h1T2_{hb}"]
                gen = sel_gen(blk + 1) if blk + 1 < NB else None
                for tile in range(2):
                    ti = blk * 2 + tile
                    O.dma(t2[tile][:], h1d[ti * 128:(ti + 1) * 128, :], (), [Bf[f"t2_{tile}"]], Bf[f"t2_{tile}"])

                def A_step(gp):
                    bkA = 4 + (gp % 2)
                    for gi in range(2):
                        us, ub = ustream.next()
                        for k in range(16):
                            O.mm(banks[bkA][:, gi * 256:(gi + 1) * 256], us[:, k, :], hT[:, k, :], k == 0, k == 15, [ub, hTb], [BK[bkA]])

                def V_step(gp, first, last):
                    vs, vbuf = vstream.next()
                    for gi in range(2):
                        g = gp * 2 + gi
                        for tile in range(2):
                            for dq in range(2):
                                O.mm(banks[tile * 2 + dq][:], Wsb[:, g, tile * 128:(tile + 1) * 128], vs[:, gi, dq * 512:(dq + 1) * 512],
                                     first and gi == 0, last and gi == 1, [WB[gp], vbuf], [BK[tile * 2 + dq]])

                A_step(0)
                for gp in range(64):
                    if gp + 1 < 64:
                        A_step(gp + 1)
                    bkA = 4 + (gp % 2)
                    sl = gp % 2
                    O.act(ga[sl][:], banks[bkA][:], AF.Gelu, [BK[bkA]], [Bf[f"ga{sl}"]])
                    wv = Wsb[:, gp * 2:(gp + 1) * 2, :].rearrange("p g t -> p (g t)")
                    O.tt("dve", wv, ga[sl][:], wv, ALU.mult, [Bf[f"ga{sl}"], WB[gp]], [WB[gp]])
                    V_step(gp, gp == 0, gp == 63)
                    gen = step(gen)
                for tile in range(2):
                    for dq in range(2):
                        cs = slice(dq * 512, (dq + 1) * 512)
                        O.stt("dve", t2[tile][:, cs], t2[tile][:, cs], ALPHA, banks[tile * 2 + dq][:], ALU.mult, ALU.add,
                              [Bf[f"t2_{tile}"], BK[tile * 2 + dq]], [Bf[f"t2_{tile}"]], partial=True)
                for gp in range(64):
                    V_step(gp, gp == 0, gp == 63)
                    gen = step(gen)
                drain(gen)
                for tile in range(2):
                    ti = blk * 2 + tile
                    for dq in range(2):
                        cs = slice(1024 + dq * 512, 1024 + (dq + 1) * 512)
                        O.stt("dve", t2[tile][:, cs], t2[tile][:, cs], ALPHA, banks[tile * 2 + dq][:], ALU.mult, ALU.add,
                              [Bf[f"t2_{tile}"], BK[tile * 2 + dq]], [Bf[f"t2_{tile}"]], partial=True)
                    layernorm_tile(t2[tile][:], [Bf[f"t2_{tile}"]], t2[tile][:], Bf[f"t2_{tile}"], g_2[:], b_2[:], Bf["g_2"], Bf["b_2"])
                    O.dma(out_d[ti * 128:(ti + 1) * 128, :], t2[tile][:], [Bf[f"t2_{tile}"]], [], Bf[f"t2_{tile}"])
                if blk + 1 < NB:
                    expand(blk + 1)
            final_barrier(T, O, None, bar, banks[7][0:1, 500:501])
            T.replay(outer, st)

    return nc


NH_FULL = 96
NO_FULL = 32


def _consts():
    bf = ml_dtypes.bfloat16
    c = {}
    c["identb"] = np.eye(128, dtype=np.float32).astype(bf)
    c["identf"] = np.eye(128, dtype=np.float32)
    c["tri"] = np.triu(np.ones((128, 128), dtype=np.float32))
    c["onesf"] = np.ones((128, 128), dtype=np.float32)
    c["iota3"] = np.broadcast_to(np.arange(128, dtype=np.float32)[None, None, :], (128, 16, 128)).reshape(128, 16 * 128).astype(bf)
    c["iota16"] = np.ascontiguousarray(np.broadcast_to(np.arange(16, dtype=np.float32)[None, None, None, :], (128, 8, 16, 16)).reshape(128, 2048))
    return c


def _weights(inp):
    f = lambda a: np.ascontiguousarray(np.asarray(a, dtype=np.float32))
    return {
        "ln_in_g": f(inp["ln_in_g"]), "ln_in_b": f(inp["ln_in_b"]), "w_in": f(inp["w_in"][0]), "b_gate": f(inp["b_gate"][0]),
        "conv_w": f(np.asarray(inp["conv_w"][0]).reshape(3, 8, 128).transpose(2, 0, 1).reshape(128, 24)), "conv_b": f(np.asarray(inp["conv_b"][0]).reshape(8, 128).T), "mh_norm_g": f(inp["mh_norm_g"][0]), "w_out": f(inp["w_out"][0]),
        "ln1_g": f(inp["ln1_g"][0]), "ln1_b": f(inp["ln1_b"][0]), "peer_wq": f(inp["peer_wq"][0]), "peer_keys": f(inp["peer_keys"][0]),
        "peer_u": f(inp["peer_u"][0]), "peer_v": f(inp["peer_v"][0]), "ln2_g": f(inp["ln2_g"][0]), "ln2_b": f(inp["ln2_b"][0]),
    }


def core_inputs(x_seq, start, n_own_tok, NH, common):
    hist_tok = NH * 128
    xin = np.zeros((hist_tok + n_own_tok, D), dtype=np.float32)
    real = min(start, hist_tok)
    if real > 0:
        xin[hist_tok - real:hist_tok] = x_seq[start - real:start]
    xin[hist_tok:] = x_seq[start:start + n_own_tok]
    hm = np.zeros((128, max(NH, 1)), dtype=np.float32)
    ndummy = (hist_tok - real) // 128
    hm[:, :ndummy] = NEG
    m = dict(common)
    m["xin"] = xin
    m["hmask"] = hm
    m["hvalid"] = np.full((128, 1), 1.0 if real > 0 else 0.0, dtype=np.float32)
    return m


_NC_CACHE = {}


def kernel(**inputs):
    x = np.asarray(inputs["x"], dtype=np.float32)
    Bn, S, _ = x.shape
    common = _weights(inputs)
    common.update(_consts())
    ncores = 8
    per = (Bn * S) // ncores
    segs = S // per
    NO = per // 128
    NH = (segs - 1) * NO
    key = (NH, NO)
    if key not in _NC_CACHE:
        _NC_CACHE[key] = build(NH, NO)
    nc = _NC_CACHE[key]
    in_maps = []
    for c in range(ncores):
        b, sg = divmod(c, segs)
        in_maps.append(core_inputs(x[b], sg * per, per, NH, common))
    res = run_bass_kernel_spmd(nc, in_maps, core_ids=list(range(ncores)))
    out = np.empty((Bn, S, D), dtype=np.float32)
    for c in range(ncores):
        b, sg = divmod(c, segs)
        out[b, sg * per:(sg + 1) * per] = res.results[c]["out"]
    return out
```

```python
import numpy as np
import concourse.bass as bass
import concourse.mybir as mybir
from concourse.bass_utils import run_bass_kernel_spmd

F32 = mybir.dt.float32
BF16 = mybir.dt.bfloat16
U32 = mybir.dt.uint32
ALU = mybir.AluOpType
AF = mybir.ActivationFunctionType
AX = mybir.AxisListType

SEM_WINDOW = 16384


class Buf:
    __slots__ = ("name", "writers", "readers", "dsem", "dcount")

    def __init__(self, name):
        self.name = name
        self.writers = {}
        self.readers = {}
        self.dsem = None
        self.dcount = 0


class Op:
    __slots__ = ("eng", "idx", "fn", "waits", "sig", "dma_ev")

    def __init__(self, eng, idx, fn):
        self.eng = eng
        self.idx = idx
        self.fn = fn
        self.waits = []
        self.sig = False
        self.dma_ev = None


class Tracker:
    ENGS = ("sync", "act", "dve", "pool", "pe")

    _uid = [0]

    def __init__(self, nc):
        self.nc = nc
        Tracker._uid[0] += 1
        self.uid = Tracker._uid[0]
        self.ops = {e: [] for e in self.ENGS}
        self.ndsem = 0
        self.dsem_total = {}

    def _dep(self, op, key, val):
        if key[0] == 'e':
            eng = key[1]
            if eng == op.eng and eng == "pe":
                return
            prod = self.ops[eng][val]
            prod.sig = True
            op.waits.append(('e', eng, val))
        else:
            op.waits.append(('d', key[1], self.dsem_total[key[1]]))

    def op(self, eng, fn, reads=(), writes=(), partial=False, dma_buf=None):
        lst = self.ops[eng]
        o = Op(eng, len(lst), fn)
        for b in reads:
            for k, v in b.writers.items():
                self._dep(o, k, v)
        for b in writes:
            for k, v in b.readers.items():
                self._dep(o, k, v)
            if not partial:
                for k, v in b.writers.items():
                    self._dep(o, k, v)
        if dma_buf is not None:
            if dma_buf.dsem is None:
                dma_buf.dsem = self.ndsem
                self.ndsem += 1
            dma_buf.dcount += 16
            o.dma_ev = (dma_buf.dsem, dma_buf.dcount)
            self.dsem_total[dma_buf.dsem] = dma_buf.dcount
            key, val = ('d', dma_buf.dsem), dma_buf.dcount
        else:
            key, val = ('e', eng), o.idx
        for b in reads:
            b.readers[key] = val
        for b in writes:
            if not partial:
                b.writers = {}
            b.readers = {}
            b.writers[key] = val
        lst.append(o)
        return o

    def replay(self, stack, bstack=None):
        nc = self.nc
        bstack = bstack or stack
        esems = {}
        for e in self.ENGS:
            nsig = sum(1 for o in self.ops[e] if o.sig and o.dma_ev is None)
            nwin = max(1, (nsig + SEM_WINDOW - 1) // SEM_WINDOW)
            esems[e] = [stack.enter_context(nc.semaphore(f"e{self.uid}_{e}_{i}")) for i in range(nwin)]
        dsems = {}
        for d in range(self.ndsem):
            dsems[d] = {}
        self._stack = stack
        sigcnt = {}
        for e in self.ENGS:
            c = 0
            arr = []
            for o in self.ops[e]:
                if o.sig and o.dma_ev is None:
                    c += 1
                arr.append(c)
            sigcnt[e] = arr

        def dsem_handle(d, w):
            if w not in dsems[d]:
                dsems[d][w] = stack.enter_context(nc.semaphore(f"d{self.uid}_{d}_{w}"))
            return dsems[d][w]

        DW = SEM_WINDOW * 2
        engh = {"sync": nc.sync, "act": nc.scalar, "dve": nc.vector, "pool": nc.gpsimd, "pe": nc.tensor}

        def run(e, eh):
            waited = {}
            for o in self.ops[e]:
                need = {}
                for w in o.waits:
                    if w[0] == 'e':
                        c = sigcnt[w[1]][w[2]]
                        k = ('e', w[1])
                    else:
                        c = w[2]
                        k = ('d', w[1])
                    if c > need.get(k, 0):
                        need[k] = c
                for k, c in need.items():
                    if waited.get(k, 0) >= c:
                        continue
                    waited[k] = c
                    if k[0] == 'e':
                        win = (c - 1) // SEM_WINDOW
                        eh.wait_ge(esems[k[1]][win], (c - 1) % SEM_WINDOW + 1)
                    else:
                        win = (c - 16) // DW
                        eh.wait_ge(dsem_handle(k[1], win), (c - 16) % DW + 16)
                if o.fn is None:
                    continue
                ins = o.fn(eh)
                if o.dma_ev is not None:
                    d, c = o.dma_ev
                    win = (c - 16) // DW
                    ins.then_inc(dsem_handle(d, win), 16)
                elif o.sig:
                    c = sigcnt[e][o.idx]
                    win = (c - 1) // SEM_WINDOW
                    ins.then_inc(esems[e][win], 1)

        block = bstack.enter_context(nc.Block())

        @block.sync
        def _(eh):
            run("sync", eh)

        @block.scalar
        def _(eh):
            run("act", eh)

        @block.vector
        def _(eh):
            run("dve", eh)

        @block.gpsimd
        def _(eh):
            run("pool", eh)

        @block.tensor
        def _(eh):
            run("pe", eh)

import math
from contextlib import ExitStack
import ml_dtypes

D = 2048
DIN = 7184
NG = 128
ALPHA = 2.0 ** 0.25
EPS = 1e-5
DH = 128
KSCALE = DH ** -0.5
NEG = -30000.0


class Ops:
    def __init__(self, T):
        self.T = T

    def mm(self, out, lhsT, rhs, start, stop, r, w):
        self.T.op("pe", lambda e: e.matmul(out, lhsT=lhsT, rhs=rhs, start=start, stop=stop), r, w, partial=True)

    def tr(self, out, in_, ident, r, w):
        self.T.op("pe", lambda e: e.transpose(out=out, in_=in_, identity=ident), r, w, partial=True)

    def act(self, out, in_, func, r, w, bias=0.0, scale=1.0, partial=False):
        self.T.op("act", lambda e: e.activation(out=out, in_=in_, func=func, bias=bias, scale=scale), r, w, partial=partial)

    def tt(self, eng, out, in0, in1, op, r, w, partial=False):
        self.T.op(eng, lambda e: e.tensor_tensor(out=out, in0=in0, in1=in1, op=op), r, w, partial=partial)

    def ts(self, eng, out, in0, s1, s2, op0, op1, r, w, partial=False):
        if s2 is None:
            self.T.op(eng, lambda e: e.tensor_scalar(out=out, in0=in0, scalar1=s1, scalar2=None, op0=op0), r, w, partial=partial)
        else:
            self.T.op(eng, lambda e: e.tensor_scalar(out=out, in0=in0, scalar1=s1, scalar2=s2, op0=op0, op1=op1), r, w, partial=partial)

    def stt(self, eng, out, in0, scalar, in1, op0, op1, r, w, partial=False):
        self.T.op(eng, lambda e: e.scalar_tensor_tensor(out=out, in0=in0, scalar=scalar, in1=in1, op0=op0, op1=op1), r, w, partial=partial)

    def cp(self, eng, out, in_, r, w, partial=False):
        if eng == "act":
            self.T.op("act", lambda e: e.copy(out=out, in_=in_), r, w, partial=partial)
        else:
            self.T.op(eng, lambda e: e.tensor_copy(out=out, in_=in_), r, w, partial=partial)

    def dma(self, out, in_, r, w, buf, partial=False, eng="sync"):
        self.T.op(eng, lambda e: e.dma_start(out=out, in_=in_), r, w, partial=partial, dma_buf=buf)

    def memset(self, eng, ap, val, w):
        self.T.op(eng, lambda e: e.memset(ap, val), (), w)


class Stream:
    def __init__(self, O, slots, bufs, loads, depth=None, eng="sync"):
        self.O = O
        self.eng = eng
        self.slots = slots
        self.bufs = bufs
        self.loads = loads
        self.n = len(slots)
        self.depth = depth or (self.n - 1)
        self.issued = 0
        self.consumed = 0

    def _issue(self):
        if self.issued >= len(self.loads):
            return
        k = self.issued
        s = k % self.n
        for (o, i) in self.loads[k](self.slots[s]):
            self.O.dma(o, i, (), [self.bufs[s]], self.bufs[s], partial=True, eng=self.eng)
        self.issued += 1

    def next(self):
        while self.issued < len(self.loads) and self.issued < self.consumed + self.depth:
            self._issue()
        k = self.consumed
        self.consumed += 1
        s = k % self.n
        return self.slots[s], self.bufs[s]


def final_barrier(T, O, scratch, sbuf_bar, psum_bar):
    bars = {}
    for e in ("act", "dve", "pool"):
        b = Buf("bar_" + e)
        bars[e] = b
        col = {"act": 0, "dve": 1, "pool": 2}[e]
        O.memset(e, sbuf_bar[:, col:col + 1], 0.0, [b]) if e != "act" else T.op(
            "act", lambda en: en.copy(out=sbuf_bar[:, 0:1], in_=sbuf_bar[:, 4:5]), (), [b])
    bpe = Buf("bar_pe")
    bars["pe"] = bpe
    T.op("pe", lambda en: en.matmul(psum_bar, lhsT=sbuf_bar[:, 8:9].bitcast(F32), rhs=sbuf_bar[:, 8:9].bitcast(F32), start=True, stop=True), (), [bpe], partial=True)
    allb = list(bars.values())
    for e in ("sync", "act", "dve", "pool", "pe"):
        o = T.op(e, None, reads=allb)
        for d, tot in T.dsem_total.items():
            o.waits.append(('d', d, tot))


def build(NH, NO, debug=False):
    NT = NH + NO
    nc = bass.Bass("TRN2", target_bir_lowering=False)

    def din(name, shape, dt=F32):
        return nc.dram_tensor(name, list(shape), dt, kind="ExternalInput").ap()

    xin = din("xin", [NT * 128, D])
    hmask_d = din("hmask", [128, max(NH, 1)])
    hvalid_d = din("hvalid", [128, 1])
    ln_in_g = din("ln_in_g", [D]); ln_in_b = din("ln_in_b", [D])
    w_in = din("w_in", [D, DIN]); b_gate = din("b_gate", [16])
    conv_w = din("conv_w", [128, 24]); conv_b = din("conv_b", [128, 8])
    mh_g = din("mh_norm_g", [1024]); w_out = din("w_out", [D, D])
    ln1_g = din("ln1_g", [D]); ln1_b = din("ln1_b", [D])
    wq = din("peer_wq", [D, D]); keys = din("peer_keys", [8, 2, 128, 128])
    pu = din("peer_u", [16384, D]); pv = din("peer_v", [16384, D])
    ln2_g = din("ln2_g", [D]); ln2_b = din("ln2_b", [D])
    identb_d = din("identb", [128, 128], BF16); identf_d = din("identf", [128, 128])
    tri_d = din("tri", [128, 128]); onesf_d = din("onesf", [128, 128])
    iota3_d = din("iota3", [128, 16 * 128], BF16); iota16_d = din("iota16", [128, 8 * 16 * 16])
    out_d = nc.dram_tensor("out", [NO * 128, D], F32, kind="ExternalOutput").ap()
    dbg_d = nc.dram_tensor("dbg", [NO * 128, D], F32, kind="ExternalOutput").ap() if debug else None

    def dscr(name, shape, dt):
        return nc.dram_tensor(name, list(shape), dt).ap()

    Wi_b = dscr("Wi_b", [D, DIN], BF16); Wo_b = dscr("Wo_b", [D, D], BF16); Wq_b = dscr("Wq_b", [D, D], BF16)
    vbh_d = dscr("vbh_d", [2, 16384, 1024], BF16); uT_d = dscr("uT_d", [NG, 128, 16 * 128], BF16)
    h0d = dscr("h0d", [NO * 128, D], F32); h1d = dscr("h1d", [NO * 128, D], F32)
    h1Td = dscr("h1Td", [NO, 128, 16 * 128], BF16)
    B_Wi = Buf("Wi_b"); B_Wo = Buf("Wo_b"); B_Wq = Buf("Wq_b"); B_vb = Buf("vb_d"); B_uT = Buf("uT_d")
    B_h0d = Buf("h0d"); B_h1d = Buf("h1d"); B_h1Td = Buf("h1Td")

    with ExitStack() as outer:
        with ExitStack() as st:
            T = Tracker(nc)
            O = Ops(T)
            bufs = {}

            def sb(name, shape, dt):
                t = st.enter_context(nc.sbuf_tensor("sb_" + name, list(shape), dt))
                bufs[name] = Buf(name)
                return t

            banks = [st.enter_context(nc.psum_tensor(f"bank{i}", [128, 512], F32)) for i in range(8)]
            BK = [Buf(f"bank{i}") for i in range(8)]

            identb = sb("identb", [128, 128], BF16); identf = sb("identf", [128, 128], F32)
            tri = sb("tri", [128, 128], F32); onesf = sb("onesf", [128, 128], F32)
            onesb = sb("onesb", [128, 8], BF16)
            bar = sb("bar", [128, 16], F32)
            hmask = sb("hmask", [128, max(NH, 1)], F32); hvalid = sb("hvalid", [128, 1], F32)
            g_in = sb("g_in", [128, D], F32); b_in = sb("b_in", [128, D], F32)
            g_1 = sb("g_1", [128, D], F32); b_1 = sb("b_1", [128, D], F32)
            mhg = sb("mhg", [128, 1024], F32); bgate = sb("bgate", [128, 16], F32)
            cw = sb("cw", [128, 3, 8], F32); cbias = sb("cbias", [128, 8], F32)
            wgate = sb("wgate", [128, 16, 16], BF16)
            wgate_f = sb("wgate_f", [128, 16, 16], F32)

            def load_const(t, src, name):
                O.dma(t, src, (), [bufs[name]], bufs[name])

            load_const(identb[:], identb_d, "identb"); load_const(identf[:], identf_d, "identf")
            load_const(tri[:], tri_d, "tri"); load_const(onesf[:], onesf_d, "onesf")
            load_const(hmask[:], hmask_d, "hmask"); load_const(hvalid[:], hvalid_d, "hvalid")
            load_const(g_in[:], ln_in_g.partition_broadcast(128), "g_in"); load_const(b_in[:], ln_in_b.partition_broadcast(128), "b_in")
            load_const(g_1[:], ln1_g.partition_broadcast(128), "g_1"); load_const(b_1[:], ln1_b.partition_broadcast(128), "b_1")
            load_const(mhg[:], mh_g.partition_broadcast(128), "mhg"); load_const(bgate[:], b_gate.partition_broadcast(128), "bgate")
            load_const(cw[:].rearrange("p j c -> p (j c)"), conv_w, "cw")
            load_const(cbias[:], conv_b, "cbias")
            load_const(wgate_f[:], w_in.rearrange("(k p) c -> p k c", p=128)[:, :, 7168:7184], "wgate_f")
            O.cp("dve", wgate[:], wgate_f[:], [bufs["wgate_f"]], [bufs["wgate"]])
            O.memset("pool", onesb[:], 1.0, [bufs["onesb"]])
            O.memset("pool", bar[:], 0.0, [bufs["bar"]])

            cin = [sb(f"cin{i}", [128, 2048], F32) for i in range(2)]
            cout = [sb(f"cout{i}", [128, 2048], BF16) for i in range(2)]
            cast_engs = ["dve", "pool", "act"]
            cnt = [0]

            def cast2d(src, dst, R, C, dstbuf, vsplit=False):
                for r in range(R // 128):
                    for c0 in range(0, C, 2048):
                        cwid = min(2048, C - c0)
                        s = cnt[0] % 2
                        eng = cast_engs[cnt[0] % 3]
                        cnt[0] += 1
                        bi = bufs[f"cin{s}"]; bo = bufs[f"cout{s}"]
                        O.dma(cin[s][:, 0:cwid], src[r * 128:(r + 1) * 128, c0:c0 + cwid], (), [bi], bi)
                        O.cp(eng, cout[s][:, 0:cwid], cin[s][:, 0:cwid], [bi], [bo])
                        if vsplit:
                            for hh in range(2):
                                O.dma(dst[hh][r * 128:(r + 1) * 128, :], cout[s][:, hh * 1024:(hh + 1) * 1024], [bo], [dstbuf], bo, partial=True)
                        else:
                            O.dma(dst[r * 128:(r + 1) * 128, c0:c0 + cwid], cout[s][:, 0:cwid], [bo], [dstbuf], bo, partial=True)

            cast2d(w_in, Wi_b, D, DIN, B_Wi)
            cast2d(w_out, Wo_b, D, D, B_Wo)
            cast2d(wq, Wq_b, D, D, B_Wq)
            cast2d(pv, vbh_d, 16384, D, B_vb, vsplit=True)
            for g in range(NG):
                s = cnt[0] % 2
                cnt[0] += 1
                bi = bufs[f"cin{s}"]; bo = bufs[f"cout{s}"]
                O.dma(cin[s][:], pu[g * 128:(g + 1) * 128, :], (), [bi], bi)
                for q4 in range(4):
                    bk = 4 + q4
                    for kk in range(4):
                        k = q4 * 4 + kk
                        O.tr(banks[bk][:, kk * 128:(kk + 1) * 128], cin[s][:, k * 128:(k + 1) * 128], identf[:],
                             [bi, bufs["identf"]], [BK[bk]])
                    eng = ["dve", "act"][q4 % 2]
                    O.cp(eng, cout[s][:, q4 * 512:(q4 + 1) * 512], banks[bk][:], [BK[bk]], [bo], partial=True)
                O.dma(uT_d[g], cout[s][:], [bo], [B_uT], bo, partial=True)

            xt = cin
            bufs["xt0"] = bufs["cin0"]; bufs["xt1"] = bufs["cin1"]
            h0 = sb("h0", [128, D], F32); h0b = cout[1]; bufs["h0b"] = bufs["cout1"]
            h0T = sb("h0T", [128, 16, 128], BF16)
            wslots = [sb(f"wg{i}", [128, 16, 512], BF16) for i in range(3)]
            Ktok = sb("Ktok", [128, 8, 128], BF16); Vt = sb("Vt", [128, 8, 128], BF16)
            sig = sb("sig", [128, 1024], F32)
            Cf = sb("Cf", [128, 8, 128], F32); zbuf = sb("zbuf", [128, 8, 130], F32); acc = sb("acc", [128, 128], F32)
            mixT = sb("mixT", [128, 16, 128], BF16)
            QT = sb("QT", [128, 8, 128], BF16); KT = sb("KT", [128, 8, 128], BF16)
            PT = sb("PT", [128, 8, 128], F32); sw = sb("sw", [128, 8, 128], BF16)
            EB = sb("EB", [128, 8, 128], F32); QsT = sb("QsT", [128, 8, 128], BF16)
            TriLF = sb("TriLF", [128, 8, 128], F32)
            wkV = sb("wkV", [128, 8, 128], BF16)
            yn = sb("yn", [128, 8, 128], F32); ym = sb("ym", [128, 1024], BF16)
            CT = sb("CT", [128, 8, 128], F32); CTb = sb("CTb", [128, 8, 128], BF16)
            nT = sb("nT", [128, 8], F32); nb = sb("nb", [128, 8], BF16)
            t1 = sb("t1", [128, D], F32); h1 = t1; bufs["h1"] = bufs["t1"]; h1b = cout[0]; bufs["h1b"] = bufs["cout0"]
            h1T = sb("h1T", [128, 16, 128], BF16)
            sm = sb("sm", [128, 256], F32)
            smb = sb("smb", [128, 16], BF16)
            stats = sb("stats", [128, 8, 6], F32); mv = sb("mv", [128, 8, 2], F32)
            Bf = bufs
            def smcol(name, c0, n):
                bufs[name] = Buf(name)
                return sm[:, c0:c0 + n]
            gx = smcol("gx", 0, 16); ef = smcol("ef", 16, 8); lf = smcol("lf", 24, 8)
            bc = smcol("bc", 32, 8); wkt = smcol("wkt", 40, 8); wk = smcol("wk", 48, 8)
            bias8 = smcol("bias8", 56, 8); dec = smcol("dec", 64, 8); rr = smcol("rr", 72, 8)
            t8 = smcol("t8", 80, 8); sc8 = smcol("sc8", 88, 8); lnmv = smcol("lnmv", 96, 2)
            lnr = smcol("lnr", 98, 1); lnst = smcol("lnst", 100, 24)
            bufs["wkb"] = Buf("wkb")
            wkb = smb[:, 0:8]

            O.memset("pool", CT[:], 0.0, [Bf["CT"]]); O.memset("pool", CTb[:], 0.0, [Bf["CTb"]])
            O.memset("pool", nT[:], 0.0, [Bf["nT"]]); O.memset("pool", nb[:], 0.0, [Bf["nb"]])
            O.memset("pool", zbuf[:], 0.0, [Bf["zbuf"]])

            Wi_v = Wi_b.rearrange("(k p) c -> p k c", p=128)
            Wo_v = Wo_b.rearrange("(k p) c -> p k c", p=128)
            loads = []
            plan = []

            def wload(view, c0, srcbuf):
                def f(slot):
                    return [(slot[:, :, :], view[:, :, c0:c0 + 512])]
                return f

            for i in range(NT):
                own = i >= NH
                tags = []
                if own or i == NH - 1:
                    for c0 in (1024, 1536):
                        tags.append(("fC", c0))
                    for c0 in (2048, 2560):
                        tags.append(("fh", c0))
                if own:
                    for c0 in (0, 512):
                        tags.append(("fB", c0))
                    for c0 in (3072, 3584):
                        tags.append(("fq", c0))
                for c0 in (4096, 4608):
                    tags.append(("k", c0))
                for c0 in (5120, 5632):
                    tags.append(("v", c0))
                if own:
                    for c0 in (6144, 6656):
                        tags.append(("o", c0))
                    for c0 in (0, 512, 1024, 1536):
                        tags.append(("wo", c0))
                plan.append(tags)
                for (tg, c0) in tags:
                    loads.append(wload(Wo_v if tg == "wo" else Wi_v, c0, None))
            wstream = Stream(O, wslots, [bufs[f"wg{i}"] for i in range(3)], loads)
            for i in range(3):
                pass
            orig_issue = wstream._issue

            def issue_with_deps():
                if wstream.issued >= len(wstream.loads):
                    return
                k = wstream.issued
                s = k % wstream.n
                for (o, i_) in wstream.loads[k](wstream.slots[s]):
                    O.dma(o, i_, [B_Wi, B_Wo], [wstream.bufs[s]], wstream.bufs[s], partial=True)
                wstream.issued += 1
            wstream._issue = issue_with_deps

            proj_banks = [6, 4, 5]
            pcount = [0]

            def next_pbank():
                b = proj_banks[pcount[0] % 3]
                pcount[0] += 1
                return b

            def layernorm_tile(src, src_bufs, dst, dst_buf, gt, bt, gbuf, bbuf):
                for q in range(4):
                    T.op("dve", (lambda q=q: (lambda e: e.bn_stats(out=lnst[:, q * 6:(q + 1) * 6], in_=src[:, q * 512:(q + 1) * 512])))(),
                         src_bufs, [Bf["lnst"]], partial=(q > 0))
                T.op("dve", lambda e: e.bn_aggr(out=lnmv, in_=lnst), [Bf["lnst"]], [Bf["lnmv"]])
                O.act(lnr, lnmv[:, 1:2], AF.Ln, [Bf["lnmv"]], [Bf["lnr"]], bias=EPS, scale=1.0)
                O.act(lnr, lnr, AF.Exp, [Bf["lnr"]], [Bf["lnr"]], scale=-0.5)
                O.ts("dve", dst, src, lnmv[:, 0:1], lnr[:, 0:1], ALU.subtract, ALU.mult, src_bufs + [Bf["lnmv"], Bf["lnr"]], [dst_buf])
                O.tt("pool", dst, dst, gt, ALU.mult, [dst_buf, gbuf], [dst_buf])
                O.tt("pool", dst, dst, bt, ALU.add, [dst_buf, bbuf], [dst_buf])

            def transpose_2048(srcb, srcb_buf, dstT, dstT_buf):
                for half in range(2):
                    bk = next_pbank()
                    pb = banks[bk][:].bitcast(BF16)
                    for kk in range(8):
                        k = half * 8 + kk
                        O.tr(pb[:, kk * 128:(kk + 1) * 128], srcb[:, k * 128:(k + 1) * 128], identb[:], [srcb_buf, Bf["identb"]], [BK[bk]])
                    eng = "act" if half == 0 else "dve"
                    O.cp(eng, dstT[:, half * 8:(half + 1) * 8, :].rearrange("p k t -> p (k t)"), pb, [BK[bk]], [dstT_buf], partial=True)

            def flat(ap3):
                return ap3.rearrange("p h l -> p (h l)")

            def mlstm_tile(own, i):
                tri_b = tri[:].unsqueeze(1).to_broadcast([128, 8, 128])
                if own:
                    O.tt("pool", TriLF[:], tri_b, lf.unsqueeze(2).to_broadcast([128, 8, 128]), ALU.mult, [Bf["tri"], Bf["lf"]], [Bf["TriLF"]])
                    TL2 = flat(TriLF[:])
                    for half in range(2):
                        O.mm(banks[half][:], onesf[:], TL2[:, half * 512:(half + 1) * 512], True, True, [Bf["onesf"], Bf["TriLF"]], [BK[half]])
                    O.tt("dve", bias8, gx[:, 0:8], bc, ALU.subtract, [Bf["gx"], Bf["bc"]], [Bf["bias8"]])
                    for h in range(8):
                        bk = h // 4; col = (h % 4) * 128
                        O.act(PT[:, h, :], banks[bk][:, col:col + 128], AF.Exp, [BK[bk], Bf["bias8"]], [Bf["PT"]],
                              bias=bias8[:, h:h + 1], scale=1.0, partial=True)
                    O.tt("pool", PT[:], PT[:], tri_b, ALU.mult, [Bf["PT"], Bf["tri"]], [Bf["PT"]])
                    for h in range(8):
                        bk = 2 + h // 4; col = (h % 4) * 128
                        O.mm(banks[bk][:, col:col + 128], KT[:, h, :], QT[:, h, :], True, True, [Bf["KT"], Bf["QT"]], [BK[bk]])
                    for half in range(2):
                        O.tt("dve", flat(sw[:, half * 4:(half + 1) * 4, :]), flat(PT[:, half * 4:(half + 1) * 4, :]), banks[2 + half][:], ALU.mult,
                             [Bf["PT"], BK[2 + half]], [Bf["sw"]], partial=True)
                    for half in range(2):
                        O.act(flat(EB[:, half * 4:(half + 1) * 4, :]), banks[half][:], AF.Exp, [BK[half]], [Bf["EB"]], partial=True)
                    O.tt("pool", QsT[:], QT[:], EB[:], ALU.mult, [Bf["QT"], Bf["EB"]], [Bf["QsT"]])
                    for h in range(8):
                        bk = 4 + h // 4; col = (h % 4) * 128
                        O.mm(banks[bk][:, col:col + 128], sw[:, h, :], Vt[:, h, :], True, False, [Bf["sw"], Bf["Vt"]], [BK[bk]])
                        O.mm(banks[bk][:, col:col + 128], QsT[:, h, :], CTb[:, h, :], False, True, [Bf["QsT"], Bf["CTb"]], [BK[bk]])
                    for h in range(8):
                        O.mm(banks[7][:, 32 + h:33 + h], sw[:, h, :], onesb[:, 0:1], True, False, [Bf["sw"], Bf["onesb"]], [BK[7]])
                        O.mm(banks[7][:, 32 + h:33 + h], QsT[:, h, :], nb[:, h:h + 1], False, True, [Bf["QsT"], Bf["nb"]], [BK[7]])
                    O.ts("dve", t8, banks[7][:, 32:40], -1.0, 1.0, ALU.mult, ALU.max, [BK[7]], [Bf["t8"]])
                    O.ts("dve", rr, banks[7][:, 32:40], 1.0, None, ALU.max, None, [BK[7]], [Bf["rr"]])
                    O.tt("dve", rr, rr, t8, ALU.max, [Bf["rr"], Bf["t8"]], [Bf["rr"]])
                    T.op("dve", lambda e: e.reciprocal(out=rr, in_=rr), [Bf["rr"]], [Bf["rr"]])
                    for h in range(8):
                        bk = 4 + h // 4; col = (h % 4) * 128
                        T.op("dve", (lambda h=h, bk=bk, col=col: (lambda e: e.bn_stats(out=stats[:, h, :], in_=banks[bk][:, col:col + 128])))(),
                             [BK[bk]], [Bf["stats"]], partial=True)
                    for h in range(8):
                        T.op("dve", (lambda h=h: (lambda e: e.bn_aggr(out=mv[:, h, :], in_=stats[:, h, :])))(), [Bf["stats"]], [Bf["mv"]], partial=True)
                    O.tt("dve", t8, rr, rr, ALU.mult, [Bf["rr"]], [Bf["t8"]])
                    O.tt("dve", t8, t8, mv[:, :, 1], ALU.mult, [Bf["t8"], Bf["mv"]], [Bf["t8"]])
                    O.act(t8, t8, AF.Ln, [Bf["t8"]], [Bf["t8"]], bias=EPS, scale=1.0)
                    O.act(t8, t8, AF.Exp, [Bf["t8"]], [Bf["t8"]], scale=-0.5)
                    O.tt("dve", sc8, t8, rr, ALU.mult, [Bf["t8"], Bf["rr"]], [Bf["sc8"]])
                    for h in range(8):
                        bk = 4 + h // 4; col = (h % 4) * 128
                        O.ts("dve", yn[:, h, :], banks[bk][:, col:col + 128], mv[:, h, 0:1], sc8[:, h:h + 1], ALU.subtract, ALU.mult,
                             [BK[bk], Bf["mv"], Bf["sc8"]], [Bf["yn"]], partial=True)
                    O.tt("pool", flat(yn[:]), flat(yn[:]), mhg[:], ALU.mult, [Bf["yn"], Bf["mhg"]], [Bf["yn"]])
                    O.tt("pool", ym[:], flat(yn[:]), sig[:], ALU.mult, [Bf["yn"], Bf["sig"]], [Bf["ym"]])
                    bk = next_pbank()
                    pb = banks[bk][:].bitcast(BF16)
                    for h in range(8):
                        O.tr(pb[:, h * 128:(h + 1) * 128], ym[:, h * 128:(h + 1) * 128], identb[:], [Bf["ym"], Bf["identb"]], [BK[bk]])
                    O.cp("act", flat(mixT[:, 8:16, :]), pb, [BK[bk]], [Bf["mixT"]], partial=True)
                O.tt("dve", wkt, banks[7][:, 24:32], bc, ALU.subtract, [BK[7], Bf["bc"]], [Bf["wkt"]])
                O.tt("dve", wkt, wkt, gx[:, 0:8], ALU.add, [Bf["wkt"], Bf["gx"]], [Bf["wkt"]])
                O.act(wk, wkt, AF.Exp, [Bf["wkt"]], [Bf["wk"]])
                O.cp("dve", wkb, wk, [Bf["wk"]], [Bf["wkb"]])
                O.act(dec, banks[7][:, 24:32], AF.Exp, [BK[7]], [Bf["dec"]])
                O.tt("pool", wkV[:], Vt[:], wk.unsqueeze(2).to_broadcast([128, 8, 128]), ALU.mult, [Bf["Vt"], Bf["wk"]], [Bf["wkV"]])
                for h in range(8):
                    bk = h // 4; col = (h % 4) * 128
                    O.mm(banks[bk][:, col:col + 128], Ktok[:, h, :], wkV[:, h, :], True, True, [Bf["Ktok"], Bf["wkV"]], [BK[bk]])
                for h in range(8):
                    O.mm(banks[7][:, 40 + h:41 + h], Ktok[:, h, :], wkb[:, h:h + 1], True, True, [Bf["Ktok"], Bf["wkb"]], [BK[7]])
                O.tt("pool", CT[:], CT[:], dec.unsqueeze(2).to_broadcast([128, 8, 128]), ALU.mult, [Bf["CT"], Bf["dec"]], [Bf["CT"]])
                for half in range(2):
                    O.tt("dve", flat(CT[:, half * 4:(half + 1) * 4, :]), flat(CT[:, half * 4:(half + 1) * 4, :]), banks[half][:], ALU.add,
                         [Bf["CT"], BK[half]], [Bf["CT"]], partial=True)
                O.cp("act", CTb[:], CT[:], [Bf["CT"]], [Bf["CTb"]])
                O.tt("dve", nT[:], nT[:], dec, ALU.mult, [Bf["nT"], Bf["dec"]], [Bf["nT"]])
                O.tt("dve", nT[:], nT[:], banks[7][:, 40:48], ALU.add, [Bf["nT"], BK[7]], [Bf["nT"]])
                O.cp("dve", nb[:], nT[:], [Bf["nT"]], [Bf["nb"]])

            own_idx = 0
            for i in range(NT):
                own = i >= NH
                last_hist = (i == NH - 1)
                xs = i % 2
                xtile = xt[xs]; xbuf = bufs[f"xt{xs}"]
                O.dma(xtile[:], xin[i * 128:(i + 1) * 128, :], (), [xbuf], xbuf)
                layernorm_tile(xtile[:], [xbuf], h0[:], Bf["h0"], g_in[:], b_in[:], Bf["g_in"], Bf["b_in"])
                if own:
                    O.dma(h0d[own_idx * 128:(own_idx + 1) * 128, :], h0[:], [Bf["h0"]], [B_h0d], Bf["h0"], partial=True)
                O.cp("act", h0b[:], h0[:], [Bf["h0"]], [Bf["h0b"]])
                transpose_2048(h0b, Bf["h0b"], h0T, Bf["h0T"])

                for k in range(16):
                    O.mm(banks[7][:, 0:16], h0T[:, k, :], wgate[:, k, :], k == 0, k == 15, [Bf["h0T"], Bf["wgate"]], [BK[7]])
                O.tt("dve", gx, banks[7][:, 0:16], bgate[:], ALU.add, [BK[7], Bf["bgate"]], [Bf["gx"]])
                if not own:
                    O.ts("dve", gx[:, 0:8], gx[:, 0:8], hmask[:, i:i + 1], None, ALU.add, None, [Bf["gx"], Bf["hmask"]], [Bf["gx"]])
                O.act(ef, gx[:, 8:16], AF.Exp, [Bf["gx"]], [Bf["ef"]], scale=-1.0)
                O.act(ef, ef, AF.Ln, [Bf["ef"]], [Bf["ef"]], bias=1.0, scale=1.0)
                O.ts("dve", lf, ef, -1.0, None, ALU.mult, None, [Bf["ef"]], [Bf["lf"]])
                O.mm(banks[7][:, 16:24], tri[:], lf, True, True, [Bf["tri"], Bf["lf"]], [BK[7]])
                O.mm(banks[7][:, 24:32], onesf[:], lf, True, True, [Bf["onesf"], Bf["lf"]], [BK[7]])
                O.cp("dve", bc, banks[7][:, 16:24], [BK[7]], [Bf["bc"]])

                for (tg, c0) in plan[i]:
                    slot, sbuf_ = wstream.next()
                    if tg in ("fC", "fh", "fB", "fq"):
                        bk = next_pbank()
                        for cc in range(4):
                            for k in range(16):
                                O.mm(banks[bk][:, cc * 128:(cc + 1) * 128], slot[:, k, cc * 128:(cc + 1) * 128], h0T[:, k, :],
                                     k == 0, k == 15, [sbuf_, Bf["h0T"]], [BK[bk]])
                        pv3 = banks[bk][:].rearrange("p (c t) -> p c t", c=4)
                        if tg == "fC":
                            ch0 = (c0 - 1024) // 128
                            O.cp("act", Cf[:, ch0:ch0 + 4, :], pv3, [BK[bk]], [Bf["Cf"]], partial=True)
                        elif tg == "fh":
                            ch0 = (c0 - 2048) // 128
                            O.tt("dve", zbuf[:, ch0:ch0 + 4, 2:130], pv3, Cf[:, ch0:ch0 + 4, :], ALU.mult, [BK[bk], Bf["Cf"]], [Bf["zbuf"]], partial=True)
                            if last_hist and c0 == 2560:
                                O.ts("pool", zbuf[:, :, 0:2], zbuf[:, :, 128:130], hvalid[:, 0:1], None, ALU.mult, None,
                                     [Bf["zbuf"], Bf["hvalid"]], [Bf["zbuf"]])
                        elif tg == "fB":
                            ch0 = c0 // 128
                            for cc in range(4):
                                c = ch0 + cc
                                O.ts("dve", acc[:], zbuf[:, c, 2:130], cw[:, 2, c:c + 1], cbias[:, c:c + 1], ALU.mult, ALU.add,
                                     [Bf["zbuf"], Bf["cw"], Bf["cbias"]], [Bf["acc"]])
                                O.stt("dve", acc[:], zbuf[:, c, 1:129], cw[:, 1, c:c + 1], acc[:], ALU.mult, ALU.add,
                                      [Bf["zbuf"], Bf["cw"], Bf["acc"]], [Bf["acc"]])
                                O.stt("dve", acc[:], zbuf[:, c, 0:128], cw[:, 0, c:c + 1], acc[:], ALU.mult, ALU.add,
                                      [Bf["zbuf"], Bf["cw"], Bf["acc"]], [Bf["acc"]])
                                O.tt("dve", mixT[:, c, :], banks[bk][:, cc * 128:(cc + 1) * 128], acc[:], ALU.mult, [BK[bk], Bf["acc"]], [Bf["mixT"]], partial=True)
                            if c0 == 512:
                                O.cp("pool", zbuf[:, :, 0:2], zbuf[:, :, 128:130], [Bf["zbuf"]], [Bf["zbuf"]])
                        elif tg == "fq":
                            ch0 = (c0 - 3072) // 128
                            O.cp("act", QT[:, ch0:ch0 + 4, :], pv3, [BK[bk]], [Bf["QT"]], partial=True)
                    elif tg == "k":
                        ch0 = (c0 - 4096) // 128
                        if own:
                            bk = next_pbank()
                            for cc in range(4):
                                for k in range(16):
                                    O.mm(banks[bk][:, cc * 128:(cc + 1) * 128], slot[:, k, cc * 128:(cc + 1) * 128], h0T[:, k, :],
                                         k == 0, k == 15, [sbuf_, Bf["h0T"]], [BK[bk]])
                            pv3 = banks[bk][:].rearrange("p (c t) -> p c t", c=4)
                            O.act(KT[:, ch0:ch0 + 4, :], pv3, AF.Copy, [BK[bk]], [Bf["KT"]], scale=KSCALE, partial=True)
                        bk = next_pbank()
                        for k in range(16):
                            O.mm(banks[bk][:], h0T[:, k, :], slot[:, k, :], k == 0, k == 15, [sbuf_, Bf["h0T"]], [BK[bk]])
                        O.act(Ktok[:, ch0:ch0 + 4, :].rearrange("p h d -> p (h d)"), banks[bk][:], AF.Copy, [BK[bk]], [Bf["Ktok"]], scale=KSCALE, partial=True)
                    elif tg == "v":
                        ch0 = (c0 - 5120) // 128
                        bk = next_pbank()
                        for k in range(16):
                            O.mm(banks[bk][:], h0T[:, k, :], slot[:, k, :], k == 0, k == 15, [sbuf_, Bf["h0T"]], [BK[bk]])
                        O.cp("dve", Vt[:, ch0:ch0 + 4, :].rearrange("p h d -> p (h d)"), banks[bk][:], [BK[bk]], [Bf["Vt"]], partial=True)
                    elif tg == "o":
                        cc0 = c0 - 6144
                        bk = next_pbank()
                        for k in range(16):
                            O.mm(banks[bk][:], h0T[:, k, :], slot[:, k, :], k == 0, k == 15, [sbuf_, Bf["h0T"]], [BK[bk]])
                        O.act(sig[:, cc0:cc0 + 512], banks[bk][:], AF.Exp, [BK[bk]], [Bf["sig"]], scale=-1.0, partial=True)
                        O.ts("pool", sig[:, cc0:cc0 + 512], sig[:, cc0:cc0 + 512], 1.0, None, ALU.add, None, [Bf["sig"]], [Bf["sig"]], partial=True)
                        if cc0 == 512:
                            T.op("dve", lambda e: e.reciprocal(out=sig[:], in_=sig[:]), [Bf["sig"]], [Bf["sig"]])
                            mlstm_tile(True, i)
                    elif tg == "wo":
                        bk = next_pbank()
                        for k in range(16):
                            O.mm(banks[bk][:], mixT[:, k, :], slot[:, k, :], k == 0, k == 15, [sbuf_, Bf["mixT"]], [BK[bk]])
                        O.stt("dve", t1[:, c0:c0 + 512], h0[:, c0:c0 + 512], ALPHA, banks[bk][:], ALU.mult, ALU.add,
                              [Bf["h0"], BK[bk]], [Bf["t1"]], partial=True)
                        if c0 == 1536:
                            layernorm_tile(t1[:], [Bf["t1"]], h1[:], Bf["h1"], g_1[:], b_1[:], Bf["g_1"], Bf["b_1"])
                            O.dma(h1d[own_idx * 128:(own_idx + 1) * 128, :], h1[:], [Bf["h1"]], [B_h1d], Bf["h1"], partial=True)
                            if debug:
                                O.dma(dbg_d[own_idx * 128:(own_idx + 1) * 128, :], h1[:], [Bf["h1"]], [], Bf["h1"])
                            O.cp("act", h1b[:], h1[:], [Bf["h1"]], [Bf["h1b"]])
                            transpose_2048(h1b, Bf["h1b"], h1T, Bf["h1T"])
                            O.dma(h1Td[own_idx], h1T[:].rearrange("p k t -> p (k t)"), [Bf["h1T"]], [B_h1Td], Bf["h1T"], partial=True)
                    if tg == "v" and c0 == 5632 and not own:
                        mlstm_tile(False, i)
                if own:
                    own_idx += 1

            final_barrier(T, O, None, bar, banks[7][0:1, 500:501])
            T.replay(outer, st)

        with ExitStack() as st:
            T = Tracker(nc)
            O = Ops(T)
            bufs = {}

            def sb(name, shape, dt):
                t = st.enter_context(nc.sbuf_tensor("sb_" + name, list(shape), dt))
                bufs[name] = Buf(name)
                return t

            NB = NO // 2
            banks = [st.enter_context(nc.psum_tensor(f"pbank{i}", [128, 512], F32)) for i in range(8)]
            BK = [Buf(f"pbank{i}") for i in range(8)]
            identb = sb("identb2", [128, 128], BF16); identf = sb("identf2", [128, 128], F32)
            bar = sb("bar2", [128, 16], F32)
            g_2 = sb("g_2", [128, D], F32); b_2 = sb("b_2", [128, D], F32)
            iota3 = sb("iota3", [128, 8, 128], BF16); iota16 = sb("iota16", [128, 16], F32)
            keysb = sb("keysb", [128, 16, 128], BF16)
            keysT = sb("keysT", [128, 16, 128], BF16)
            Bf = bufs

            def load_const(t, src, name):
                O.dma(t, src, (), [bufs[name]], bufs[name])
            load_const(identb[:], identb_d, "identb2"); load_const(identf[:], identf_d, "identf2")
            load_const(g_2[:], ln2_g.partition_broadcast(128), "g_2"); load_const(b_2[:], ln2_b.partition_broadcast(128), "b_2")
            load_const(iota3[:].rearrange("p t i -> p (t i)"), iota3_d[:, 0:1024], "iota3")
            load_const(iota16[:], iota16_d[:, 0:16], "iota16")

            h1T = [sb(f"h1T2_{i}", [128, 16, 256], BF16) for i in range(2)]
            wslots = [sb(f"wq{i}", [128, 16, 128], BF16) for i in range(2)]
            qT = sb("qT", [128, 16, 128], BF16)
            s2 = sb("s2", [128, 128], F32)
            sv = sb("sv", [128, 16, 16], F32); si = sb("si", [128, 16, 16], U32); sif = sb("sif", [128, 16, 16], F32)
            cand = sb("cand", [128, 8, 256], F32); c2 = sb("c2", [128, 256], F32)
            tv = sb("tv", [128, 8, 16], F32); ci = sb("ci", [128, 8, 16], U32)
            ca = sb("ca", [128, 8, 16], U32); cb_ = sb("cb_", [128, 8, 16], U32)
            caf = sb("caf", [128, 8, 16], F32); cbf = sb("cbf", [128, 8, 16], F32)
            oh = cand[:].rearrange("p h (a b) -> p h a b", a=16); bufs["oh"] = bufs["cand"]
            sel = sb("sel", [128, 3, 128], F32); selT = sb("selT", [128, 3, 256], F32)
            ee = sb("ee", [128, 8, 16], F32); zz = sb("zz", [128, 8], F32)
            Pb = [sb(f"Pb{i}", [128, 8, 128], BF16) for i in range(2)]
            Qb = [sb(f"Qb{i}", [128, 8, 128], BF16) for i in range(2)]
            Wsb = sb("Wsb", [128, 128, 256], BF16)
            WB = [Buf(f"WB{i}") for i in range(64)]
            uslots = [sb(f"us{i}", [128, 16, 128], BF16) for i in range(4)]
            vslots = [sb(f"vs{i}", [128, 2, 1024], BF16) for i in range(3)]
            ga = [sb(f"ga{i}", [128, 512], F32) for i in range(2)]
            t2 = [sb(f"t2_{i}", [128, D], F32) for i in range(2)]
            s_ = sb("s_", [128, 16, 128], F32)
            sm = sb("sm2", [128, 64], F32)

            keysf = t2[0][:].rearrange("p (a n) -> p a n", a=16); bufs["keysf"] = bufs["t2_0"]
            load_const(keysf, keys.rearrange("h p n c -> n (h p) c"), "keysf")
            O.memset("pool", bar[:], 0.0, [Bf["bar2"]])
            O.cp("dve", keysb[:], keysf, [Bf["keysf"]], [Bf["keysb"]])
            for half in range(2):
                bk = 6 + half
                pb = banks[bk][:].bitcast(BF16)
                for kk in range(8):
                    hp = half * 8 + kk
                    O.tr(pb[:, kk * 128:(kk + 1) * 128], keysb[:, hp, :], identb[:], [Bf["keysb"], Bf["identb2"]], [BK[bk]])
                O.cp("dve", keysT[:, half * 8:(half + 1) * 8, :].rearrange("p k t -> p (k t)"), pb, [BK[bk]], [Bf["keysT"]], partial=True)

            def smcol(name, c0, n):
                bufs[name] = Buf(name)
                return sm[:, c0:c0 + n]
            lnmv = smcol("lnmv", 0, 2); lnr = smcol("lnr", 2, 1); lnst = smcol("lnst", 4, 24)

            def layernorm_tile(src, src_bufs, dst, dst_buf, gt, bt, gbuf, bbuf):
                for q in range(4):
                    T.op("dve", (lambda q=q: (lambda e: e.bn_stats(out=lnst[:, q * 6:(q + 1) * 6], in_=src[:, q * 512:(q + 1) * 512])))(),
                         src_bufs, [Bf["lnst"]], partial=(q > 0))
                T.op("dve", lambda e: e.bn_aggr(out=lnmv, in_=lnst), [Bf["lnst"]], [Bf["lnmv"]])
                O.act(lnr, lnmv[:, 1:2], AF.Ln, [Bf["lnmv"]], [Bf["lnr"]], bias=EPS, scale=1.0)
                O.act(lnr, lnr, AF.Exp, [Bf["lnr"]], [Bf["lnr"]], scale=-0.5)
                O.ts("dve", dst, src, lnmv[:, 0:1], lnr[:, 0:1], ALU.subtract, ALU.mult, src_bufs + [Bf["lnmv"], Bf["lnr"]], [dst_buf])
                O.tt("pool", dst, dst, gt, ALU.mult, [dst_buf, gbuf], [dst_buf])
                O.tt("pool", dst, dst, bt, ALU.add, [dst_buf, bbuf], [dst_buf])

            Wq_v = Wq_b.rearrange("(k p) c -> p k c", p=128)
            qloads = []
            uloads = []
            vloads = []
            for blk in range(NB):
                for tile in range(2):
                    for c0 in range(0, 2048, 128):
                        qloads.append((lambda c0=c0: (lambda slot: [(slot[:, :, :], Wq_v[:, :, c0:c0 + 128])]))())
                for g in range(NG):
                    uloads.append((lambda g=g: (lambda slot: [(slot[:].rearrange("p k j -> p (k j)"), uT_d[g])]))())
                for half in range(2):
                    for gp in range(64):
                        vloads.append((lambda gp=gp, half=half: (lambda slot: [(slot[:, :, :], vbh_d[half][gp * 256:(gp + 1) * 256, :].rearrange("(g j) d -> j g d", j=128))]))())
            qstream = Stream(O, wslots, [bufs[f"wq{i}"] for i in range(2)], qloads)
            ustream = Stream(O, uslots, [bufs[f"us{i}"] for i in range(4)], uloads)
            vstream = Stream(O, vslots, [bufs[f"vs{i}"] for i in range(3)], vloads, eng="pool")

            iota16_b = iota16[:].unsqueeze(1).unsqueeze(1).to_broadcast([128, 8, 16, 16])

            def sel_gen(blk):
                hb = blk % 2
                hT = h1T[hb]; hTb = Bf[f"h1T2_{hb}"]
                for tile in range(2):
                    ti = blk * 2 + tile
                    O.dma(hT[:, :, tile * 128:(tile + 1) * 128], h1Td[ti].rearrange("p (k t) -> p k t", k=16), (), [hTb], hTb, partial=True)
                yield
                for tile in range(2):
                    tsl = slice(tile * 128, (tile + 1) * 128)
                    for grp in range(16):
                        slot, sbuf_ = qstream.next()
                        bk = 6 + (grp % 2)
                        for k in range(16):
                            O.mm(banks[bk][:, 0:128], slot[:, k, :], hT[:, k, tsl], k == 0, k == 15, [sbuf_, hTb], [BK[bk]])
                        O.cp("act", qT[:, grp, :], banks[bk][:, 0:128], [BK[bk]], [Bf["qT"]], partial=True)
                        if grp % 2 == 1:
                            yield
                    for q4 in range(4):
                        bk = 6 + (q4 % 2)
                        for a in range(4):
                            hp = q4 * 4 + a
                            O.mm(banks[bk][:, a * 128:(a + 1) * 128], qT[:, hp, :], keysT[:, hp, :], True, True, [Bf["qT"], Bf["keysT"]], [BK[bk]])
                        O.cp("act", s_[:, q4 * 4:(q4 + 1) * 4, :].rearrange("p a n -> p (a n)"), banks[bk][:], [BK[bk]], [Bf["s_"]], partial=True)
                    yield
                    for hp in range(16):
                        T.op("dve", (lambda hp=hp: (lambda e: e.max(out=sv[:, hp, 0:8], in_=s_[:, hp, :])))(), [Bf["s_"]], [Bf["sv"]], partial=True)
                        T.op("dve", (lambda hp=hp: (lambda e: e.match_replace(out=s2[:], in_to_replace=sv[:, hp, 0:8], in_values=s_[:, hp, :], imm_value=-1e30)))(),
                             [Bf["s_"], Bf["sv"]], [Bf["s2"]])
                        T.op("dve", (lambda hp=hp: (lambda e: e.max(out=sv[:, hp, 8:16], in_=s2[:])))(), [Bf["s2"]], [Bf["sv"]], partial=True)
                        T.op("dve", (lambda hp=hp: (lambda e: e.max_index(out=si[:, hp, 0:8], in_max=sv[:, hp, 0:8], in_values=s_[:, hp, :])))(),
                             [Bf["s_"], Bf["sv"]], [Bf["si"]], partial=True)
                        T.op("dve", (lambda hp=hp: (lambda e: e.max_index(out=si[:, hp, 8:16], in_max=sv[:, hp, 8:16], in_values=s_[:, hp, :])))(),
                             [Bf["s_"], Bf["sv"]], [Bf["si"]], partial=True)
                        yield
                    O.cp("dve", sif[:], si[:], [Bf["si"]], [Bf["sif"]])
                    sv4 = sv[:].rearrange("p (h two) a -> p h two a", two=2)
                    sif4 = sif[:].rearrange("p (h two) a -> p h two a", two=2)
                    O.tt("dve", cand[:].rearrange("p h (a b) -> p h a b", a=16),
                         sv4[:, :, 0, :].unsqueeze(3).to_broadcast([128, 8, 16, 16]),
                         sv4[:, :, 1, :].unsqueeze(2).to_broadcast([128, 8, 16, 16]), ALU.add, [Bf["sv"]], [Bf["cand"]])
                    yield
                    for h in range(8):
                        T.op("dve", (lambda h=h: (lambda e: e.max(out=tv[:, h, 0:8], in_=cand[:, h, :])))(), [Bf["cand"]], [Bf["tv"]], partial=True)
                        T.op("dve", (lambda h=h: (lambda e: e.match_replace(out=c2[:], in_to_replace=tv[:, h, 0:8], in_values=cand[:, h, :], imm_value=-1e30)))(),
                             [Bf["cand"], Bf["tv"]], [Bf["c2"]])
                        T.op("dve", (lambda h=h: (lambda e: e.max(out=tv[:, h, 8:16], in_=c2[:])))(), [Bf["c2"]], [Bf["tv"]], partial=True)
                        T.op("dve", (lambda h=h: (lambda e: e.max_index(out=ci[:, h, 0:8], in_max=tv[:, h, 0:8], in_values=cand[:, h, :])))(),
                             [Bf["cand"], Bf["tv"]], [Bf["ci"]], partial=True)
                        T.op("dve", (lambda h=h: (lambda e: e.max_index(out=ci[:, h, 8:16], in_max=tv[:, h, 8:16], in_values=cand[:, h, :])))(),
                             [Bf["cand"], Bf["tv"]], [Bf["ci"]], partial=True)
                        yield
                    T.op("dve", lambda e: e.tensor_single_scalar(out=ca[:], in_=ci[:], scalar=4, op=ALU.logical_shift_right), [Bf["ci"]], [Bf["ca"]])
                    T.op("dve", lambda e: e.tensor_single_scalar(out=cb_[:], in_=ci[:], scalar=15, op=ALU.bitwise_and), [Bf["ci"]], [Bf["cb_"]])
                    O.cp("dve", caf[:], ca[:], [Bf["ca"]], [Bf["caf"]])
                    O.cp("dve", cbf[:], cb_[:], [Bf["cb_"]], [Bf["cbf"]])
                    yield
                    for which, idxf in ((0, caf), (1, cbf)):
                        O.tt("dve", oh, iota16_b, idxf[:].unsqueeze(3).to_broadcast([128, 8, 16, 16]), ALU.is_equal,
                             [Bf["iota16"], Bf["caf" if which == 0 else "cbf"]], [Bf["oh"]])
                        yield
                        O.tt("pool", oh, oh, sif4[:, :, which, :].unsqueeze(2).to_broadcast([128, 8, 16, 16]), ALU.mult,
                             [Bf["oh"], Bf["sif"]], [Bf["oh"]])
                        T.op("dve", (lambda which=which: (lambda e: e.tensor_reduce(out=sel[:, which, :].rearrange("p (h k) -> p h k", h=8), in_=oh, axis=AX.X, op=ALU.add)))(),
                             [Bf["oh"]], [Bf["sel"]], partial=True)
                        yield
                    O.tt("dve", ee[:], tv[:], tv[:, :, 0:1].to_broadcast([128, 8, 16]), ALU.subtract, [Bf["tv"]], [Bf["ee"]])
                    O.act(ee[:], ee[:], AF.Exp, [Bf["ee"]], [Bf["ee"]])
                    T.op("dve", lambda e: e.tensor_reduce(out=zz[:], in_=ee[:], axis=AX.X, op=ALU.add), [Bf["ee"]], [Bf["zz"]])
                    T.op("dve", lambda e: e.reciprocal(out=zz[:], in_=zz[:]), [Bf["zz"]], [Bf["zz"]])
                    O.tt("dve", sel[:, 2, :].rearrange("p (h k) -> p h k", h=8), ee[:], zz[:].unsqueeze(2).to_broadcast([128, 8, 16]), ALU.mult,
                         [Bf["ee"], Bf["zz"]], [Bf["sel"]], partial=True)
                    yield
                    for w3 in range(3):
                        O.tr(banks[6][:, w3 * 128:(w3 + 1) * 128], sel[:, w3, :], identf[:], [Bf["sel"], Bf["identf2"]], [BK[6]])
                    O.cp("dve", selT[:, :, tsl], banks[6][:, 0:384].rearrange("p (w t) -> p w t", w=3), [BK[6]], [Bf["selT"]], partial=True)
                    yield

            def expand(blk):
                wb_rot = 0
                for tb in range(32):
                    pq = tb % 2
                    tk0 = tb * 8
                    i2b = selT[:, 1, tk0:tk0 + 8].unsqueeze(2).to_broadcast([128, 8, 128])
                    O.tt("dve", Qb[pq][:], iota3[:], i2b, ALU.is_equal, [Bf["iota3"], Bf["selT"]], [Bf[f"Qb{pq}"]])
                    for tl in range(8):
                        tk = tk0 + tl
                        O.ts("dve", Pb[pq][:, tl, :], iota3[:, 0, :], selT[:, 0, tk:tk + 1], selT[:, 2, tk:tk + 1], ALU.is_equal, ALU.mult,
                             [Bf["iota3"], Bf["selT"]], [Bf[f"Pb{pq}"]], partial=(tl > 0))
                    for t4 in range(2):
                        bk = 4 + (wb_rot % 4)
                        wb_rot += 1
                        for tt_ in range(4):
                            tl = t4 * 4 + tt_
                            O.mm(banks[bk][:, tt_ * 128:(tt_ + 1) * 128], Qb[pq][:, tl, :], Pb[pq][:, tl, :], True, True,
                                 [Bf[f"Qb{pq}"], Bf[f"Pb{pq}"]], [BK[bk]])
                        t0 = tk0 + t4 * 4
                        O.cp("act", Wsb[:, :, t0:t0 + 4].rearrange("j g t -> j t g"), banks[bk][:].rearrange("p (t g) -> p t g", t=4),
                             [BK[bk]], WB, partial=True)

            def step(gen):
                if gen is not None:
                    try:
                        next(gen)
                    except StopIteration:
                        return None
                return gen

            def drain(gen):
                while gen is not None:
                    gen = step(gen)

            drain(sel_gen(0))
            expand(0)
            for blk in range(NB):
                hb = blk % 2
                hT = h1T[hb]; hTb = Bf[f"h1T2_{hb}"]
                gen = sel_gen(blk + 1) if blk + 1 < NB else None
                for tile in range(2):
                    ti = blk * 2 + tile
                    O.dma(t2[tile][:], h1d[ti * 128:(ti + 1) * 128, :], (), [Bf[f"t2_{tile}"]], Bf[f"t2_{tile}"])

                def A_step(gp):
                    bkA = 4 + (gp % 2)
                    for gi in range(2):
                        us, ub = ustream.next()
                        for k in range(16):
                            O.mm(banks[bkA][:, gi * 256:(gi + 1) * 256], us[:, k, :], hT[:, k, :], k == 0, k == 15, [ub, hTb], [BK[bkA]])

                def V_step(gp, first, last):
                    vs, vbuf = vstream.next()
                    for gi in range(2):
                        g = gp * 2 + gi
                        for tile in range(2):
                            for dq in range(2):
                                O.mm(banks[tile * 2 + dq][:], Wsb[:, g, tile * 128:(tile + 1) * 128], vs[:, gi, dq * 512:(dq + 1) * 512],
                                     first and gi == 0, last and gi == 1, [WB[gp], vbuf], [BK[tile * 2 + dq]])

                A_step(0)
                for gp in range(64):
                    if gp + 1 < 64:
                        A_step(gp + 1)
                    bkA = 4 + (gp % 2)
                    sl = gp % 2
                    O.act(ga[sl][:], banks[bkA][:], AF.Gelu, [BK[bkA]], [Bf[f"ga{sl}"]])
                    wv = Wsb[:, gp * 2:(gp + 1) * 2, :].rearrange("p g t -> p (g t)")
                    O.tt("dve", wv, ga[sl][:], wv, ALU.mult, [Bf[f"ga{sl}"], WB[gp]], [WB[gp]])
                    V_step(gp, gp == 0, gp == 63)
                    gen = step(gen)
                for tile in range(2):
                    for dq in range(2):
                        cs = slice(dq * 512, (dq + 1) * 512)
                        O.stt("dve", t2[tile][:, cs], t2[tile][:, cs], ALPHA, banks[tile * 2 + dq][:], ALU.mult, ALU.add,
                              [Bf[f"t2_{tile}"], BK[tile * 2 + dq]], [Bf[f"t2_{tile}"]], partial=True)
                for gp in range(64):
                    V_step(gp, gp == 0, gp == 63)
                    gen = step(gen)
                drain(gen)
                for tile in range(2):
                    ti = blk * 2 + tile
                    for dq in range(2):
                        cs = slice(1024 + dq * 512, 1024 + (dq + 1) * 512)
                        O.stt("dve", t2[tile][:, cs], t2[tile][:, cs], ALPHA, banks[tile * 2 + dq][:], ALU.mult, ALU.add,
                              [Bf[f"t2_{tile}"], BK[tile * 2 + dq]], [Bf[f"t2_{tile}"]], partial=True)
                    layernorm_tile(t2[tile][:], [Bf[f"t2_{tile}"]], t2[tile][:], Bf[f"t2_{tile}"], g_2[:], b_2[:], Bf["g_2"], Bf["b_2"])
                    O.dma(out_d[ti * 128:(ti + 1) * 128, :], t2[tile][:], [Bf[f"t2_{tile}"]], [], Bf[f"t2_{tile}"])
                if blk + 1 < NB:
                    expand(blk + 1)
            final_barrier(T, O, None, bar, banks[7][0:1, 500:501])
            T.replay(outer, st)

    return nc


NH_FULL = 96
NO_FULL = 32


def _consts():
    bf = ml_dtypes.bfloat16
    c = {}
    c["identb"] = np.eye(128, dtype=np.float32).astype(bf)
    c["identf"] = np.eye(128, dtype=np.float32)
    c["tri"] = np.triu(np.ones((128, 128), dtype=np.float32))
    c["onesf"] = np.ones((128, 128), dtype=np.float32)
    c["iota3"] = np.broadcast_to(np.arange(128, dtype=np.float32)[None, None, :], (128, 16, 128)).reshape(128, 16 * 128).astype(bf)
    c["iota16"] = np.ascontiguousarray(np.broadcast_to(np.arange(16, dtype=np.float32)[None, None, None, :], (128, 8, 16, 16)).reshape(128, 2048))
    return c


def _weights(inp):
    f = lambda a: np.ascontiguousarray(np.asarray(a, dtype=np.float32))
    return {
        "ln_in_g": f(inp["ln_in_g"]), "ln_in_b": f(inp["ln_in_b"]), "w_in": f(inp["w_in"][0]), "b_gate": f(inp["b_gate"][0]),
        "conv_w": f(np.asarray(inp["conv_w"][0]).reshape(3, 8, 128).transpose(2, 0, 1).reshape(128, 24)), "conv_b": f(np.asarray(inp["conv_b"][0]).reshape(8, 128).T), "mh_norm_g": f(inp["mh_norm_g"][0]), "w_out": f(inp["w_out"][0]),
        "ln1_g": f(inp["ln1_g"][0]), "ln1_b": f(inp["ln1_b"][0]), "peer_wq": f(inp["peer_wq"][0]), "peer_keys": f(inp["peer_keys"][0]),
        "peer_u": f(inp["peer_u"][0]), "peer_v": f(inp["peer_v"][0]), "ln2_g": f(inp["ln2_g"][0]), "ln2_b": f(inp["ln2_b"][0]),
    }


def core_inputs(x_seq, start, n_own_tok, NH, common):
    hist_tok = NH * 128
    xin = np.zeros((hist_tok + n_own_tok, D), dtype=np.float32)
    real = min(start, hist_tok)
    if real > 0:
        xin[hist_tok - real:hist_tok] = x_seq[start - real:start]
    xin[hist_tok:] = x_seq[start:start + n_own_tok]
    hm = np.zeros((128, max(NH, 1)), dtype=np.float32)
    ndummy = (hist_tok - real) // 128
    hm[:, :ndummy] = NEG
    m = dict(common)
    m["xin"] = xin
    m["hmask"] = hm
    m["hvalid"] = np.full((128, 1), 1.0 if real > 0 else 0.0, dtype=np.float32)
    return m


_NC_CACHE = {}


def kernel(**inputs):
    x = np.asarray(inputs["x"], dtype=np.float32)
    Bn, S, _ = x.shape
    common = _weights(inputs)
    common.update(_consts())
    ncores = 8
    per = (Bn * S) // ncores
    segs = S // per
    NO = per // 128
    NH = (segs - 1) * NO
    key = (NH, NO)
    if key not in _NC_CACHE:
        _NC_CACHE[key] = build(NH, NO)
    nc = _NC_CACHE[key]
    in_maps = []
    for c in range(ncores):
        b, sg = divmod(c, segs)
        in_maps.append(core_inputs(x[b], sg * per, per, NH, common))
    res = run_bass_kernel_spmd(nc, in_maps, core_ids=list(range(ncores)))
    out = np.empty((Bn, S, D), dtype=np.float32)
    for c in range(ncores):
        b, sg = divmod(c, segs)
        out[b, sg * per:(sg + 1) * per] = res.results[c]["out"]
    return out
```

```python
import numpy as np
import concourse.bass as bass
import concourse.mybir as mybir
from concourse.bass_utils import run_bass_kernel_spmd

F32 = mybir.dt.float32
BF16 = mybir.dt.bfloat16
U32 = mybir.dt.uint32
ALU = mybir.AluOpType
AF = mybir.ActivationFunctionType
AX = mybir.AxisListType

SEM_WINDOW = 16384


class Buf:
    __slots__ = ("name", "writers", "readers", "dsem", "dcount")

    def __init__(self, name):
        self.name = name
        self.writers = {}
        self.readers = {}
        self.dsem = None
        self.dcount = 0


class Op:
    __slots__ = ("eng", "idx", "fn", "waits", "sig", "dma_ev")

    def __init__(self, eng, idx, fn):
        self.eng = eng
        self.idx = idx
        self.fn = fn
        self.waits = []
        self.sig = False
        self.dma_ev = None


class Tracker:
    ENGS = ("sync", "act", "dve", "pool", "pe")

    _uid = [0]

    def __init__(self, nc):
        self.nc = nc
        Tracker._uid[0] += 1
        self.uid = Tracker._uid[0]
        self.ops = {e: [] for e in self.ENGS}
        self.ndsem = 0
        self.dsem_total = {}

    def _dep(self, op, key, val):
        if key[0] == 'e':
            eng = key[1]
            if eng == op.eng and eng == "pe":
                return
            prod = self.ops[eng][val]
            prod.sig = True
            op.waits.append(('e', eng, val))
        else:
            op.waits.append(('d', key[1], self.dsem_total[key[1]]))

    def op(self, eng, fn, reads=(), writes=(), partial=False, dma_buf=None):
        lst = self.ops[eng]
        o = Op(eng, len(lst), fn)
        for b in reads:
            for k, v in b.writers.items():
                self._dep(o, k, v)
        for b in writes:
            for k, v in b.readers.items():
                self._dep(o, k, v)
            if not partial:
                for k, v in b.writers.items():
                    self._dep(o, k, v)
        if dma_buf is not None:
            if dma_buf.dsem is None:
                dma_buf.dsem = self.ndsem
                self.ndsem += 1
            dma_buf.dcount += 16
            o.dma_ev = (dma_buf.dsem, dma_buf.dcount)
            self.dsem_total[dma_buf.dsem] = dma_buf.dcount
            key, val = ('d', dma_buf.dsem), dma_buf.dcount
        else:
            key, val = ('e', eng), o.idx
        for b in reads:
            b.readers[key] = val
        for b in writes:
            if not partial:
                b.writers = {}
            b.readers = {}
            b.writers[key] = val
        lst.append(o)
        return o

    def replay(self, stack, bstack=None):
        nc = self.nc
        bstack = bstack or stack
        esems = {}
        for e in self.ENGS:
            nsig = sum(1 for o in self.ops[e] if o.sig and o.dma_ev is None)
            nwin = max(1, (nsig + SEM_WINDOW - 1) // SEM_WINDOW)
            esems[e] = [stack.enter_context(nc.semaphore(f"e{self.uid}_{e}_{i}")) for i in range(nwin)]
        dsems = {}
        for d in range(self.ndsem):
            dsems[d] = {}
        self._stack = stack
        sigcnt = {}
        for e in self.ENGS:
            c = 0
            arr = []
            for o in self.ops[e]:
                if o.sig and o.dma_ev is None:
                    c += 1
                arr.append(c)
            sigcnt[e] = arr

        def dsem_handle(d, w):
            if w not in dsems[d]:
                dsems[d][w] = stack.enter_context(nc.semaphore(f"d{self.uid}_{d}_{w}"))
            return dsems[d][w]

        DW = SEM_WINDOW * 2
        engh = {"sync": nc.sync, "act": nc.scalar, "dve": nc.vector, "pool": nc.gpsimd, "pe": nc.tensor}

        def run(e, eh):
            waited = {}
            for o in self.ops[e]:
                need = {}
                for w in o.waits:
                    if w[0] == 'e':
                        c = sigcnt[w[1]][w[2]]
                        k = ('e', w[1])
                    else:
                        c = w[2]
                        k = ('d', w[1])
                    if c > need.get(k, 0):
                        need[k] = c
                for k, c in need.items():
                    if waited.get(k, 0) >= c:
                        continue
                    waited[k] = c
                    if k[0] == 'e':
                        win = (c - 1) // SEM_WINDOW
                        eh.wait_ge(esems[k[1]][win], (c - 1) % SEM_WINDOW + 1)
                    else:
                        win = (c - 16) // DW
                        eh.wait_ge(dsem_handle(k[1], win), (c - 16) % DW + 16)
                if o.fn is None:
                    continue
                ins = o.fn(eh)
                if o.dma_ev is not None:
                    d, c = o.dma_ev
                    win = (c - 16) // DW
                    ins.then_inc(dsem_handle(d, win), 16)
                elif o.sig:
                    c = sigcnt[e][o.idx]
                    win = (c - 1) // SEM_WINDOW
                    ins.then_inc(esems[e][win], 1)

        block = bstack.enter_context(nc.Block())

        @block.sync
        def _(eh):
            run("sync", eh)

        @block.scalar
        def _(eh):
            run("act", eh)

        @block.vector
        def _(eh):
            run("dve", eh)

        @block.gpsimd
        def _(eh):
            run("pool", eh)

        @block.tensor
        def _(eh):
            run("pe", eh)

import math
from contextlib import ExitStack
import ml_dtypes

D = 2048
DIN = 7184
NG = 128
ALPHA = 2.0 ** 0.25
EPS = 1e-5
DH = 128
KSCALE = DH ** -0.5
NEG = -30000.0


class Ops:
    def __init__(self, T):
        self.T = T

    def mm(self, out, lhsT, rhs, start, stop, r, w):
        self.T.op("pe", lambda e: e.matmul(out, lhsT=lhsT, rhs=rhs, start=start, stop=stop), r, w, partial=True)

    def tr(self, out, in_, ident, r, w):
        self.T.op("pe", lambda e: e.transpose(out=out, in_=in_, identity=ident), r, w, partial=True)

    def act(self, out, in_, func, r, w, bias=0.0, scale=1.0, partial=False):
        self.T.op("act", lambda e: e.activation(out=out, in_=in_, func=func, bias=bias, scale=scale), r, w, partial=partial)

    def tt(self, eng, out, in0, in1, op, r, w, partial=False):
        self.T.op(eng, lambda e: e.tensor_tensor(out=out, in0=in0, in1=in1, op=op), r, w, partial=partial)

    def ts(self, eng, out, in0, s1, s2, op0, op1, r, w, partial=False):
        if s2 is None:
            self.T.op(eng, lambda e: e.tensor_scalar(out=out, in0=in0, scalar1=s1, scalar2=None, op0=op0), r, w, partial=partial)
        else:
            self.T.op(eng, lambda e: e.tensor_scalar(out=out, in0=in0, scalar1=s1, scalar2=s2, op0=op0, op1=op1), r, w, partial=partial)

    def stt(self, eng, out, in0, scalar, in1, op0, op1, r, w, partial=False):
        self.T.op(eng, lambda e: e.scalar_tensor_tensor(out=out, in0=in0, scalar=scalar, in1=in1, op0=op0, op1=op1), r, w, partial=partial)

    def cp(self, eng, out, in_, r, w, partial=False):
        if eng == "act":
            self.T.op("act", lambda e: e.copy(out=out, in_=in_), r, w, partial=partial)
        else:
            self.T.op(eng, lambda e: e.tensor_copy(out=out, in_=in_), r, w, partial=partial)

    def dma(self, out, in_, r, w, buf, partial=False, eng="sync"):
        self.T.op(eng, lambda e: e.dma_start(out=out, in_=in_), r, w, partial=partial, dma_buf=buf)

    def memset(self, eng, ap, val, w):
        self.T.op(eng, lambda e: e.memset(ap, val), (), w)


class Stream:
    def __init__(self, O, slots, bufs, loads, depth=None, eng="sync"):
        self.O = O
        self.eng = eng
        self.slots = slots
        self.bufs = bufs
        self.loads = loads
        self.n = len(slots)
        self.depth = depth or (self.n - 1)
        self.issued = 0
        self.consumed = 0

    def _issue(self):
        if self.issued >= len(self.loads):
            return
        k = self.issued
        s = k % self.n
        for (o, i) in self.loads[k](self.slots[s]):
            self.O.dma(o, i, (), [self.bufs[s]], self.bufs[s], partial=True, eng=self.eng)
        self.issued += 1

    def next(self):
        while self.issued < len(self.loads) and self.issued < self.consumed + self.depth:
            self._issue()
        k = self.consumed
        self.consumed += 1
        s = k % self.n
        return self.slots[s], self.bufs[s]


def final_barrier(T, O, scratch, sbuf_bar, psum_bar):
    bars = {}
    for e in ("act", "dve", "pool"):
        b = Buf("bar_" + e)
        bars[e] = b
        col = {"act": 0, "dve": 1, "pool": 2}[e]
        O.memset(e, sbuf_bar[:, col:col + 1], 0.0, [b]) if e != "act" else T.op(
            "act", lambda en: en.copy(out=sbuf_bar[:, 0:1], in_=sbuf_bar[:, 4:5]), (), [b])
    bpe = Buf("bar_pe")
    bars["pe"] = bpe
    T.op("pe", lambda en: en.matmul(psum_bar, lhsT=sbuf_bar[:, 8:9].bitcast(F32), rhs=sbuf_bar[:, 8:9].bitcast(F32), start=True, stop=True), (), [bpe], partial=True)
    allb = list(bars.values())
    for e in ("sync", "act", "dve", "pool", "pe"):
        o = T.op(e, None, reads=allb)
        for d, tot in T.dsem_total.items():
            o.waits.append(('d', d, tot))


def build(NH, NO, debug=False):
    NT = NH + NO
    nc = bass.Bass("TRN2", target_bir_lowering=False)

    def din(name, shape, dt=F32):
        return nc.dram_tensor(name, list(shape), dt, kind="ExternalInput").ap()

    xin = din("xin", [NT * 128, D])
    hmask_d = din("hmask", [128, max(NH, 1)])
    hvalid_d = din("hvalid", [128, 1])
    ln_in_g = din("ln_in_g", [D]); ln_in_b = din("ln_in_b", [D])
    w_in = din("w_in", [D, DIN]); b_gate = din("b_gate", [16])
    conv_w = din("conv_w", [128, 24]); conv_b = din("conv_b", [128, 8])
    mh_g = din("mh_norm_g", [1024]); w_out = din("w_out", [D, D])
    ln1_g = din("ln1_g", [D]); ln1_b = din("ln1_b", [D])
    wq = din("peer_wq", [D, D]); keys = din("peer_keys", [8, 2, 128, 128])
    pu = din("peer_u", [16384, D]); pv = din("peer_v", [16384, D])
    ln2_g = din("ln2_g", [D]); ln2_b = din("ln2_b", [D])
    identb_d = din("identb", [128, 128], BF16); identf_d = din("identf", [128, 128])
    tri_d = din("tri", [128, 128]); onesf_d = din("onesf", [128, 128])
    iota3_d = din("iota3", [128, 16 * 128], BF16); iota16_d = din("iota16", [128, 8 * 16 * 16])
    out_d = nc.dram_tensor("out", [NO * 128, D], F32, kind="ExternalOutput").ap()
    dbg_d = nc.dram_tensor("dbg", [NO * 128, D], F32, kind="ExternalOutput").ap() if debug else None

    def dscr(name, shape, dt):
        return nc.dram_tensor(name, list(shape), dt).ap()

    Wi_b = dscr("Wi_b", [D, DIN], BF16); Wo_b = dscr("Wo_b", [D, D], BF16); Wq_b = dscr("Wq_b", [D, D], BF16)
    vbh_d = dscr("vbh_d", [2, 16384, 1024], BF16); uT_d = dscr("uT_d", [NG, 128, 16 * 128], BF16)
    h0d = dscr("h0d", [NO * 128, D], F32); h1d = dscr("h1d", [NO * 128, D], F32)
    h1Td = dscr("h1Td", [NO, 128, 16 * 128], BF16)
    B_Wi = Buf("Wi_b"); B_Wo = Buf("Wo_b"); B_Wq = Buf("Wq_b"); B_vb = Buf("vb_d"); B_uT = Buf("uT_d")
    B_h0d = Buf("h0d"); B_h1d = Buf("h1d"); B_h1Td = Buf("h1Td")

    with ExitStack() as outer:
        with ExitStack() as st:
            T = Tracker(nc)
            O = Ops(T)
            bufs = {}

            def sb(name, shape, dt):
                t = st.enter_context(nc.sbuf_tensor("sb_" + name, list(shape), dt))
                bufs[name] = Buf(name)
                return t

            banks = [st.enter_context(nc.psum_tensor(f"bank{i}", [128, 512], F32)) for i in range(8)]
            BK = [Buf(f"bank{i}") for i in range(8)]

            identb = sb("identb", [128, 128], BF16); identf = sb("identf", [128, 128], F32)
            tri = sb("tri", [128, 128], F32); onesf = sb("onesf", [128, 128], F32)
            onesb = sb("onesb", [128, 8], BF16)
            bar = sb("bar", [128, 16], F32)
            hmask = sb("hmask", [128, max(NH, 1)], F32); hvalid = sb("hvalid", [128, 1], F32)
            g_in = sb("g_in", [128, D], F32); b_in = sb("b_in", [128, D], F32)
            g_1 = sb("g_1", [128, D], F32); b_1 = sb("b_1", [128, D], F32)
            mhg = sb("mhg", [128, 1024], F32); bgate = sb("bgate", [128, 16], F32)
            cw = sb("cw", [128, 3, 8], F32); cbias = sb("cbias", [128, 8], F32)
            wgate = sb("wgate", [128, 16, 16], BF16)
            wgate_f = sb("wgate_f", [128, 16, 16], F32)

            def load_const(t, src, name):
                O.dma(t, src, (), [bufs[name]], bufs[name])

            load_const(identb[:], identb_d, "identb"); load_const(identf[:], identf_d, "identf")
            load_const(tri[:], tri_d, "tri"); load_const(onesf[:], onesf_d, "onesf")
            load_const(hmask[:], hmask_d, "hmask"); load_const(hvalid[:], hvalid_d, "hvalid")
            load_const(g_in[:], ln_in_g.partition_broadcast(128), "g_in"); load_const(b_in[:], ln_in_b.partition_broadcast(128), "b_in")
            load_const(mhg[:], mh_g.partition_broadcast(128), "mhg"); load_const(bgate[:], b_gate.partition_broadcast(128), "bgate")
            load_const(cw[:].rearrange("p j c -> p (j c)"), conv_w, "cw")
            load_const(cbias[:], conv_b, "cbias")
            load_const(wgate_f[:], w_in.rearrange("(k p) c -> p k c", p=128)[:, :, 7168:7184], "wgate_f")
            O.cp("dve", wgate[:], wgate_f[:], [bufs["wgate_f"]], [bufs["wgate"]])
            O.memset("pool", onesb[:], 1.0, [bufs["onesb"]])
            O.memset("pool", bar[:], 0.0, [bufs["bar"]])

            cin = [sb(f"cin{i}", [128, 2048], F32) for i in range(2)]
            cout = [sb(f"cout{i}", [128, 2048], BF16) for i in range(2)]
            cast_engs = ["dve", "pool", "act"]
            cnt = [0]

            xt = cin
            bufs["xt0"] = bufs["cin0"]; bufs["xt1"] = bufs["cin1"]
            h0 = sb("h0", [128, D], F32); h0b = cout[1]; bufs["h0b"] = bufs["cout1"]
            h0T = sb("h0T", [128, 16, 128], BF16)
            wbig = sb("wbig", [128, 16, 2048], BF16)
            wslots = [wbig[:, :, i * 512:(i + 1) * 512] for i in range(3)]
            for i in range(3):
                bufs[f"wg{i}"] = Buf(f"wg{i}")
            Ktok = sb("Ktok", [128, 8, 128], BF16); Vt = sb("Vt", [128, 8, 128], BF16)
            sig = sb("sig", [128, 1024], F32)
            Cf = sb("Cf", [128, 8, 128], F32); zbuf = sb("zbuf", [128, 8, 130], F32); acc = sb("acc", [128, 128], F32)
            mixT = sb("mixT", [128, 16, 128], BF16)
            QT = sb("QT", [128, 8, 128], BF16); KT = sb("KT", [128, 8, 128], BF16)
            PT = sb("PT", [128, 8, 128], F32); sw = sb("sw", [128, 8, 128], BF16)
            EB = sb("EB", [128, 8, 128], F32); QsT = sb("QsT", [128, 8, 128], BF16)
            TriLF = sb("TriLF", [128, 8, 128], F32)
            wkV = sb("wkV", [128, 8, 128], BF16)
            yn = sb("yn", [128, 8, 128], F32); ym = sb("ym", [128, 1024], BF16)
            CT = sb("CT", [128, 8, 128], F32); CTb = sb("CTb", [128, 8, 128], BF16)
            nT = sb("nT", [128, 8], F32); nb = sb("nb", [128, 8], BF16)
            t1 = sb("t1", [128, D], F32); h1 = t1; bufs["h1"] = bufs["t1"]; h1b = cout[0]; bufs["h1b"] = bufs["cout0"]
            h1T = sb("h1T", [128, 16, 128], BF16)
            sm = sb("sm", [128, 256], F32)
            smb = sb("smb", [128, 16], BF16)
            stats = sb("stats", [128, 8, 6], F32); mv = sb("mv", [128, 8, 2], F32)
            Bf = bufs
            def smcol(name, c0, n):
                bufs[name] = Buf(name)
                return sm[:, c0:c0 + n]
            gx = smcol("gx", 0, 16); ef = smcol("ef", 16, 8); lf = smcol("lf", 24, 8)
            bc = smcol("bc", 32, 8); wkt = smcol("wkt", 40, 8); wk = smcol("wk", 48, 8)
            bias8 = smcol("bias8", 56, 8); dec = smcol("dec", 64, 8); rr = smcol("rr", 72, 8)
            t8 = smcol("t8", 80, 8); sc8 = smcol("sc8", 88, 8); lnmv = smcol("lnmv", 96, 2)
            lnr = smcol("lnr", 98, 1); lnst = smcol("lnst", 100, 24)
            bufs["wkb"] = Buf("wkb")
            wkb = smb[:, 0:8]


            assert NH >= 1
            ccin = [t1, g_1]; ccinb = [Bf["t1"], Bf["g_1"]]
            ccoutb = [Bf["cout0"], Bf["mixT"]]

            def ccout_ap(s):
                return cout[0][:] if s == 0 else mixT[:].rearrange("p k t -> p (k t)")

            for k in range(16):
                s = k % 2
                O.dma(ccin[s][:], w_in[k * 128:(k + 1) * 128, 4096:6144], (), [ccinb[s]], ccinb[s])
                O.cp(cast_engs[k % 3], wbig[:, k, :], ccin[s][:], [ccinb[s]], [Bf["wbig"]], partial=True)

            def conv_gen():
                n = 0
                for (src, dst, R, C, dbuf) in ((w_in, Wi_b, D, DIN, B_Wi), (w_out, Wo_b, D, D, B_Wo), (wq, Wq_b, D, D, B_Wq)):
                    for r in range(R // 128):
                        for c0 in range(0, C, 2048):
                            cwid = min(2048, C - c0)
                            s = n % 2; eng = cast_engs[n % 3]; n += 1
                            O.dma(ccin[s][:, 0:cwid], src[r * 128:(r + 1) * 128, c0:c0 + cwid], (), [ccinb[s]], ccinb[s])
                            O.cp(eng, ccout_ap(s)[:, 0:cwid], ccin[s][:, 0:cwid], [ccinb[s]], [ccoutb[s]])
                            O.dma(dst[r * 128:(r + 1) * 128, c0:c0 + cwid], ccout_ap(s)[:, 0:cwid], [ccoutb[s]], [dbuf], ccoutb[s], partial=True)
                            yield
                for r in range(128):
                    s = n % 2; eng = cast_engs[n % 3]; n += 1
                    O.dma(ccin[s][:], pv[r * 128:(r + 1) * 128, :], (), [ccinb[s]], ccinb[s])
                    O.cp(eng, ccout_ap(s), ccin[s][:], [ccinb[s]], [ccoutb[s]])
                    for hh in range(2):
                        O.dma(vbh_d[hh][r * 128:(r + 1) * 128, :], ccout_ap(s)[:, hh * 1024:(hh + 1) * 1024], [ccoutb[s]], [B_vb], ccoutb[s], partial=True)
                    yield
                for g in range(NG):
                    s = n % 2; n += 1
                    O.dma(ccin[s][:], pu[g * 128:(g + 1) * 128, :], (), [ccinb[s]], ccinb[s])
                    for q4 in range(4):
                        bk = 2 + (q4 % 2)
                        for kk in range(4):
                            k = q4 * 4 + kk
                            O.tr(banks[bk][:, kk * 128:(kk + 1) * 128], ccin[s][:, k * 128:(k + 1) * 128], identf[:],
                                 [ccinb[s], bufs["identf"]], [BK[bk]])
                        eng = ["dve", "act"][q4 % 2]
                        O.cp(eng, ccout_ap(s)[:, q4 * 512:(q4 + 1) * 512], banks[bk][:], [BK[bk]], [ccoutb[s]], partial=True)
                    O.dma(uT_d[g], ccout_ap(s), [ccoutb[s]], [B_uT], ccoutb[s], partial=True)
                    yield

            def step(gen):
                if gen is not None:
                    try:
                        next(gen)
                    except StopIteration:
                        return None
                return gen

            def drain(gen):
                while gen is not None:
                    gen = step(gen)
                return None

            O.memset("pool", CT[:], 0.0, [Bf["CT"]]); O.memset("pool", CTb[:], 0.0, [Bf["CTb"]])
            O.memset("pool", nT[:], 0.0, [Bf["nT"]]); O.memset("pool", nb[:], 0.0, [Bf["nb"]])
            O.memset("pool", zbuf[:], 0.0, [Bf["zbuf"]])

            Wi_v = Wi_b.rearrange("(k p) c -> p k c", p=128)
            Wo_v = Wo_b.rearrange("(k p) c -> p k c", p=128)
            loads = []
            plan = []

            def wload(view, c0, srcbuf):
                def f(slot):
                    return [(slot[:, :, :], view[:, :, c0:c0 + 512])]
                return f

            for i in range(NT):
                own = i >= NH
                tags = []
                if own:
                    for c0 in (1024, 1536):
                        tags.append(("fC", c0))
                    for c0 in (2048, 2560):
                        tags.append(("fh", c0))
                    for c0 in (0, 512):
                        tags.append(("fB", c0))
                    for c0 in (3072, 3584):
                        tags.append(("fq", c0))
                for c0 in (4096, 4608):
                    tags.append(("k", c0))
                for c0 in (5120, 5632):
                    tags.append(("v", c0))
                if i == NH - 1:
                    for c0 in (1024, 1536):
                        tags.append(("fC", c0))
                    for c0 in (2048, 2560):
                        tags.append(("fh", c0))
                if own:
                    for c0 in (6144, 6656):
                        tags.append(("o", c0))
                    for c0 in (0, 512, 1024, 1536):
                        tags.append(("wo", c0))
                plan.append(tags)
                for (tg, c0) in tags:
                    if (not own) and tg in ("k", "v"):
                        continue
                    loads.append(wload(Wo_v if tg == "wo" else Wi_v, c0, None))
            wstream = Stream(O, wslots, [bufs[f"wg{i}"] for i in range(3)], loads)
            for i in range(3):
                pass
            orig_issue = wstream._issue

            def issue_with_deps():
                if wstream.issued >= len(wstream.loads):
                    return
                k = wstream.issued
                s = k % wstream.n
                for (o, i_) in wstream.loads[k](wstream.slots[s]):
                    O.dma(o, i_, [B_Wi, B_Wo], [wstream.bufs[s]], wstream.bufs[s], partial=True)
                wstream.issued += 1
            wstream._issue = issue_with_deps

            proj_banks = [6, 4, 5]
            pcount = [0]

            def next_pbank():
                b = proj_banks[pcount[0] % 3]
                pcount[0] += 1
                return b

            def layernorm_tile(src, src_bufs, dst, dst_buf, gt, bt, gbuf, bbuf):
                for q in range(4):
                    T.op("dve", (lambda q=q: (lambda e: e.bn_stats(out=lnst[:, q * 6:(q + 1) * 6], in_=src[:, q * 512:(q + 1) * 512])))(),
                         src_bufs, [Bf["lnst"]], partial=(q > 0))
                T.op("dve", lambda e: e.bn_aggr(out=lnmv, in_=lnst), [Bf["lnst"]], [Bf["lnmv"]])
                O.act(lnr, lnmv[:, 1:2], AF.Ln, [Bf["lnmv"]], [Bf["lnr"]], bias=EPS, scale=1.0)
                O.act(lnr, lnr, AF.Exp, [Bf["lnr"]], [Bf["lnr"]], scale=-0.5)
                O.ts("dve", dst, src, lnmv[:, 0:1], lnr[:, 0:1], ALU.subtract, ALU.mult, src_bufs + [Bf["lnmv"], Bf["lnr"]], [dst_buf])
                O.tt("pool", dst, dst, gt, ALU.mult, [dst_buf, gbuf], [dst_buf])
                O.tt("pool", dst, dst, bt, ALU.add, [dst_buf, bbuf], [dst_buf])

            def transpose_2048(srcb, srcb_buf, dstT, dstT_buf):
                for half in range(2):
                    bk = next_pbank()
                    pb = banks[bk][:].bitcast(BF16)
                    for kk in range(8):
                        k = half * 8 + kk
                        O.tr(pb[:, kk * 128:(kk + 1) * 128], srcb[:, k * 128:(k + 1) * 128], identb[:], [srcb_buf, Bf["identb"]], [BK[bk]])
                    eng = "act" if half == 0 else "dve"
                    O.cp(eng, dstT[:, half * 8:(half + 1) * 8, :].rearrange("p k t -> p (k t)"), pb, [BK[bk]], [dstT_buf], partial=True)

            def flat(ap3):
                return ap3.rearrange("p h l -> p (h l)")

            def mlstm_tile(own, i):
                tri_b = tri[:].unsqueeze(1).to_broadcast([128, 8, 128])
                if own:
                    O.tt("pool", TriLF[:], tri_b, lf.unsqueeze(2).to_broadcast([128, 8, 128]), ALU.mult, [Bf["tri"], Bf["lf"]], [Bf["TriLF"]])
                    TL2 = flat(TriLF[:])
                    for half in range(2):
                        O.mm(banks[half][:], onesf[:], TL2[:, half * 512:(half + 1) * 512], True, True, [Bf["onesf"], Bf["TriLF"]], [BK[half]])
                    O.tt("dve", bias8, gx[:, 0:8], bc, ALU.subtract, [Bf["gx"], Bf["bc"]], [Bf["bias8"]])
                    for h in range(8):
                        bk = h // 4; col = (h % 4) * 128
                        O.act(PT[:, h, :], banks[bk][:, col:col + 128], AF.Exp, [BK[bk], Bf["bias8"]], [Bf["PT"]],
                              bias=bias8[:, h:h + 1], scale=1.0, partial=True)
                    O.tt("pool", PT[:], PT[:], tri_b, ALU.mult, [Bf["PT"], Bf["tri"]], [Bf["PT"]])
                    for h in range(8):
                        bk = 2 + h // 4; col = (h % 4) * 128
                        O.mm(banks[bk][:, col:col + 128], KT[:, h, :], QT[:, h, :], True, True, [Bf["KT"], Bf["QT"]], [BK[bk]])
                    for half in range(2):
                        O.tt("dve", flat(sw[:, half * 4:(half + 1) * 4, :]), flat(PT[:, half * 4:(half + 1) * 4, :]), banks[2 + half][:], ALU.mult,
                             [Bf["PT"], BK[2 + half]], [Bf["sw"]], partial=True)
                    for half in range(2):
                        O.act(flat(EB[:, half * 4:(half + 1) * 4, :]), banks[half][:], AF.Exp, [BK[half]], [Bf["EB"]], partial=True)
                    O.tt("pool", QsT[:], QT[:], EB[:], ALU.mult, [Bf["QT"], Bf["EB"]], [Bf["QsT"]])
                    for h in range(8):
                        bk = 4 + h // 4; col = (h % 4) * 128
                        O.mm(banks[bk][:, col:col + 128], sw[:, h, :], Vt[:, h, :], True, False, [Bf["sw"], Bf["Vt"]], [BK[bk]])
                        O.mm(banks[bk][:, col:col + 128], QsT[:, h, :], CTb[:, h, :], False, True, [Bf["QsT"], Bf["CTb"]], [BK[bk]])
                    for h in range(8):
                        O.mm(banks[7][:, 32 + h:33 + h], sw[:, h, :], onesb[:, 0:1], True, False, [Bf["sw"], Bf["onesb"]], [BK[7]])
                        O.mm(banks[7][:, 32 + h:33 + h], QsT[:, h, :], nb[:, h:h + 1], False, True, [Bf["QsT"], Bf["nb"]], [BK[7]])
                    O.ts("dve", t8, banks[7][:, 32:40], -1.0, 1.0, ALU.mult, ALU.max, [BK[7]], [Bf["t8"]])
                    O.ts("dve", rr, banks[7][:, 32:40], 1.0, None, ALU.max, None, [BK[7]], [Bf["rr"]])
                    O.tt("dve", rr, rr, t8, ALU.max, [Bf["rr"], Bf["t8"]], [Bf["rr"]])
                    T.op("dve", lambda e: e.reciprocal(out=rr, in_=rr), [Bf["rr"]], [Bf["rr"]])
                    for h in range(8):
                        bk = 4 + h // 4; col = (h % 4) * 128
                        T.op("dve", (lambda h=h, bk=bk, col=col: (lambda e: e.bn_stats(out=stats[:, h, :], in_=banks[bk][:, col:col + 128])))(),
                             [BK[bk]], [Bf["stats"]], partial=True)
                    for h in range(8):
                        T.op("dve", (lambda h=h: (lambda e: e.bn_aggr(out=mv[:, h, :], in_=stats[:, h, :])))(), [Bf["stats"]], [Bf["mv"]], partial=True)
                    O.tt("dve", t8, rr, rr, ALU.mult, [Bf["rr"]], [Bf["t8"]])
                    O.tt("dve", t8, t8, mv[:, :, 1], ALU.mult, [Bf["t8"], Bf["mv"]], [Bf["t8"]])
                    O.act(t8, t8, AF.Ln, [Bf["t8"]], [Bf["t8"]], bias=EPS, scale=1.0)
                    O.act(t8, t8, AF.Exp, [Bf["t8"]], [Bf["t8"]], scale=-0.5)
                    O.tt("dve", sc8, t8, rr, ALU.mult, [Bf["t8"], Bf["rr"]], [Bf["sc8"]])
                    for h in range(8):
                        bk = 4 + h // 4; col = (h % 4) * 128
                        O.ts("dve", yn[:, h, :], banks[bk][:, col:col + 128], mv[:, h, 0:1], sc8[:, h:h + 1], ALU.subtract, ALU.mult,
                             [BK[bk], Bf["mv"], Bf["sc8"]], [Bf["yn"]], partial=True)
                    O.tt("pool", flat(yn[:]), flat(yn[:]), mhg[:], ALU.mult, [Bf["yn"], Bf["mhg"]], [Bf["yn"]])
                    O.tt("pool", ym[:], flat(yn[:]), sig[:], ALU.mult, [Bf["yn"], Bf["sig"]], [Bf["ym"]])
                    bk = next_pbank()
                    pb = banks[bk][:].bitcast(BF16)
                    for h in range(8):
                        O.tr(pb[:, h * 128:(h + 1) * 128], ym[:, h * 128:(h + 1) * 128], identb[:], [Bf["ym"], Bf["identb"]], [BK[bk]])
                    O.cp("act", flat(mixT[:, 8:16, :]), pb, [BK[bk]], [Bf["mixT"]], partial=True)
                O.tt("dve", wkt, banks[7][:, 24:32], bc, ALU.subtract, [BK[7], Bf["bc"]], [Bf["wkt"]])
                O.tt("dve", wkt, wkt, gx[:, 0:8], ALU.add, [Bf["wkt"], Bf["gx"]], [Bf["wkt"]])
                O.act(wk, wkt, AF.Exp, [Bf["wkt"]], [Bf["wk"]])
                O.cp("dve", wkb, wk, [Bf["wk"]], [Bf["wkb"]])
                O.act(dec, banks[7][:, 24:32], AF.Exp, [BK[7]], [Bf["dec"]])
                O.tt("pool", wkV[:], Vt[:], wk.unsqueeze(2).to_broadcast([128, 8, 128]), ALU.mult, [Bf["Vt"], Bf["wk"]], [Bf["wkV"]])
                for h in range(8):
                    bk = h // 4; col = (h % 4) * 128
                    O.mm(banks[bk][:, col:col + 128], Ktok[:, h, :], wkV[:, h, :], True, True, [Bf["Ktok"], Bf["wkV"]], [BK[bk]])
                for h in range(8):
                    O.mm(banks[7][:, 40 + h:41 + h], Ktok[:, h, :], wkb[:, h:h + 1], True, True, [Bf["Ktok"], Bf["wkb"]], [BK[7]])
                O.tt("pool", CT[:], CT[:], dec.unsqueeze(2).to_broadcast([128, 8, 128]), ALU.mult, [Bf["CT"], Bf["dec"]], [Bf["CT"]])
                for half in range(2):
                    O.tt("dve", flat(CT[:, half * 4:(half + 1) * 4, :]), flat(CT[:, half * 4:(half + 1) * 4, :]), banks[half][:], ALU.add,
                         [Bf["CT"], BK[half]], [Bf["CT"]], partial=True)
                O.cp("act", CTb[:], CT[:], [Bf["CT"]], [Bf["CTb"]])
                O.tt("dve", nT[:], nT[:], dec, ALU.mult, [Bf["nT"], Bf["dec"]], [Bf["nT"]])
                O.tt("dve", nT[:], nT[:], banks[7][:, 40:48], ALU.add, [Bf["nT"], BK[7]], [Bf["nT"]])
                O.cp("dve", nb[:], nT[:], [Bf["nT"]], [Bf["nb"]])

            own_idx = 0
            cgen = [conv_gen()]
            handoff = [False]
            for i in range(NT):
                own = i >= NH
                last_hist = (i == NH - 1)
                if i == NH:
                    load_const(g_1[:], ln1_g.partition_broadcast(128), "g_1"); load_const(b_1[:], ln1_b.partition_broadcast(128), "b_1")
                xs = i % 2
                xtile = xt[xs]; xbuf = bufs[f"xt{xs}"]
                O.dma(xtile[:], xin[i * 128:(i + 1) * 128, :], (), [xbuf], xbuf)
                layernorm_tile(xtile[:], [xbuf], h0[:], Bf["h0"], g_in[:], b_in[:], Bf["g_in"], Bf["b_in"])
                if own:
                    O.dma(h0d[own_idx * 128:(own_idx + 1) * 128, :], h0[:], [Bf["h0"]], [B_h0d], Bf["h0"], partial=True)
                O.cp("act", h0b[:], h0[:], [Bf["h0"]], [Bf["h0b"]])
                transpose_2048(h0b, Bf["h0b"], h0T, Bf["h0T"])

                for k in range(16):
                    O.mm(banks[7][:, 0:16], h0T[:, k, :], wgate[:, k, :], k == 0, k == 15, [Bf["h0T"], Bf["wgate"]], [BK[7]])
                O.tt("dve", gx, banks[7][:, 0:16], bgate[:], ALU.add, [BK[7], Bf["bgate"]], [Bf["gx"]])
                if not own:
                    O.ts("dve", gx[:, 0:8], gx[:, 0:8], hmask[:, i:i + 1], None, ALU.add, None, [Bf["gx"], Bf["hmask"]], [Bf["gx"]])
                O.act(ef, gx[:, 8:16], AF.Exp, [Bf["gx"]], [Bf["ef"]], scale=-1.0)
                O.act(ef, ef, AF.Ln, [Bf["ef"]], [Bf["ef"]], bias=1.0, scale=1.0)
                O.ts("dve", lf, ef, -1.0, None, ALU.mult, None, [Bf["ef"]], [Bf["lf"]])
                O.mm(banks[7][:, 16:24], tri[:], lf, True, True, [Bf["tri"], Bf["lf"]], [BK[7]])
                O.mm(banks[7][:, 24:32], onesf[:], lf, True, True, [Bf["onesf"], Bf["lf"]], [BK[7]])
                O.cp("dve", bc, banks[7][:, 16:24], [BK[7]], [Bf["bc"]])

                if not own:
                    if i == NH - 1:
                        cgen[0] = drain(cgen[0])
                    else:
                        for _ in range(4):
                            cgen[0] = step(cgen[0])
                for (tg, c0) in plan[i]:
                    if (not own) and tg in ("k", "v"):
                        slot = wbig[:, :, c0 - 4096:c0 - 4096 + 512]; sbuf_ = Bf["wbig"]
                    else:
                        if not handoff[0]:
                            handoff[0] = True
                            for j in range(3):
                                bufs[f"wg{j}"].readers.update(Bf["wbig"].readers)
                                bufs[f"wg{j}"].writers.update(Bf["wbig"].writers)
                        slot, sbuf_ = wstream.next()
                    if tg in ("fC", "fh", "fB", "fq"):
                        bk = next_pbank()
                        for cc in range(4):
                            for k in range(16):
                                O.mm(banks[bk][:, cc * 128:(cc + 1) * 128], slot[:, k, cc * 128:(cc + 1) * 128], h0T[:, k, :],
                                     k == 0, k == 15, [sbuf_, Bf["h0T"]], [BK[bk]])
                        pv3 = banks[bk][:].rearrange("p (c t) -> p c t", c=4)
                        if tg == "fC":
                            ch0 = (c0 - 1024) // 128
                            O.cp("act", Cf[:, ch0:ch0 + 4, :], pv3, [BK[bk]], [Bf["Cf"]], partial=True)
                        elif tg == "fh":
                            ch0 = (c0 - 2048) // 128
                            O.tt("dve", zbuf[:, ch0:ch0 + 4, 2:130], pv3, Cf[:, ch0:ch0 + 4, :], ALU.mult, [BK[bk], Bf["Cf"]], [Bf["zbuf"]], partial=True)
                            if last_hist and c0 == 2560:
                                O.ts("pool", zbuf[:, :, 0:2], zbuf[:, :, 128:130], hvalid[:, 0:1], None, ALU.mult, None,
                                     [Bf["zbuf"], Bf["hvalid"]], [Bf["zbuf"]])
                        elif tg == "fB":
                            ch0 = c0 // 128
                            for cc in range(4):
                                c = ch0 + cc
                                O.ts("dve", acc[:], zbuf[:, c, 2:130], cw[:, 2, c:c + 1], cbias[:, c:c + 1], ALU.mult, ALU.add,
                                     [Bf["zbuf"], Bf["cw"], Bf["cbias"]], [Bf["acc"]])
                                O.stt("dve", acc[:], zbuf[:, c, 1:129], cw[:, 1, c:c + 1], acc[:], ALU.mult, ALU.add,
                                      [Bf["zbuf"], Bf["cw"], Bf["acc"]], [Bf["acc"]])
                                O.stt("dve", acc[:], zbuf[:, c, 0:128], cw[:, 0, c:c + 1], acc[:], ALU.mult, ALU.add,
                                      [Bf["zbuf"], Bf["cw"], Bf["acc"]], [Bf["acc"]])
                                O.tt("dve", mixT[:, c, :], banks[bk][:, cc * 128:(cc + 1) * 128], acc[:], ALU.mult, [BK[bk], Bf["acc"]], [Bf["mixT"]], partial=True)
                            if c0 == 512:
                                O.cp("pool", zbuf[:, :, 0:2], zbuf[:, :, 128:130], [Bf["zbuf"]], [Bf["zbuf"]])
                        elif tg == "fq":
                            ch0 = (c0 - 3072) // 128
                            O.cp("act", QT[:, ch0:ch0 + 4, :], pv3, [BK[bk]], [Bf["QT"]], partial=True)
                    elif tg == "k":
                        ch0 = (c0 - 4096) // 128
                        if own:
                            bk = next_pbank()
                            for cc in range(4):
                                for k in range(16):
                                    O.mm(banks[bk][:, cc * 128:(cc + 1) * 128], slot[:, k, cc * 128:(cc + 1) * 128], h0T[:, k, :],
                                         k == 0, k == 15, [sbuf_, Bf["h0T"]], [BK[bk]])
                            pv3 = banks[bk][:].rearrange("p (c t) -> p c t", c=4)
                            O.act(KT[:, ch0:ch0 + 4, :], pv3, AF.Copy, [BK[bk]], [Bf["KT"]], scale=KSCALE, partial=True)
                        bk = next_pbank()
                        for k in range(16):
                            O.mm(banks[bk][:], h0T[:, k, :], slot[:, k, :], k == 0, k == 15, [sbuf_, Bf["h0T"]], [BK[bk]])
                        O.act(Ktok[:, ch0:ch0 + 4, :].rearrange("p h d -> p (h d)"), banks[bk][:], AF.Copy, [BK[bk]], [Bf["Ktok"]], scale=KSCALE, partial=True)
                    elif tg == "v":
                        ch0 = (c0 - 5120) // 128
                        bk = next_pbank()
                        for k in range(16):
                            O.mm(banks[bk][:], h0T[:, k, :], slot[:, k, :], k == 0, k == 15, [sbuf_, Bf["h0T"]], [BK[bk]])
                        O.cp("dve", Vt[:, ch0:ch0 + 4, :].rearrange("p h d -> p (h d)"), banks[bk][:], [BK[bk]], [Bf["Vt"]], partial=True)
                    elif tg == "o":
                        cc0 = c0 - 6144
                        bk = next_pbank()
                        for k in range(16):
                            O.mm(banks[bk][:], h0T[:, k, :], slot[:, k, :], k == 0, k == 15, [sbuf_, Bf["h0T"]], [BK[bk]])
                        O.act(sig[:, cc0:cc0 + 512], banks[bk][:], AF.Exp, [BK[bk]], [Bf["sig"]], scale=-1.0, partial=True)
                        O.ts("pool", sig[:, cc0:cc0 + 512], sig[:, cc0:cc0 + 512], 1.0, None, ALU.add, None, [Bf["sig"]], [Bf["sig"]], partial=True)
                        if cc0 == 512:
                            T.op("dve", lambda e: e.reciprocal(out=sig[:], in_=sig[:]), [Bf["sig"]], [Bf["sig"]])
                            mlstm_tile(True, i)
                    elif tg == "wo":
                        bk = next_pbank()
                        for k in range(16):
                            O.mm(banks[bk][:], mixT[:, k, :], slot[:, k, :], k == 0, k == 15, [sbuf_, Bf["mixT"]], [BK[bk]])
                        O.stt("dve", t1[:, c0:c0 + 512], h0[:, c0:c0 + 512], ALPHA, banks[bk][:], ALU.mult, ALU.add,
                              [Bf["h0"], BK[bk]], [Bf["t1"]], partial=True)
                        if c0 == 1536:
                            layernorm_tile(t1[:], [Bf["t1"]], h1[:], Bf["h1"], g_1[:], b_1[:], Bf["g_1"], Bf["b_1"])
                            O.dma(h1d[own_idx * 128:(own_idx + 1) * 128, :], h1[:], [Bf["h1"]], [B_h1d], Bf["h1"], partial=True)
                            if debug:
                                O.dma(dbg_d[own_idx * 128:(own_idx + 1) * 128, :], h1[:], [Bf["h1"]], [], Bf["h1"])
                            O.cp("act", h1b[:], h1[:], [Bf["h1"]], [Bf["h1b"]])
                            transpose_2048(h1b, Bf["h1b"], h1T, Bf["h1T"])
                            O.dma(h1Td[own_idx], h1T[:].rearrange("p k t -> p (k t)"), [Bf["h1T"]], [B_h1Td], Bf["h1T"], partial=True)
                    if tg == "v" and c0 == 5632 and not own:
                        mlstm_tile(False, i)
                if own:
                    own_idx += 1

            final_barrier(T, O, None, bar, banks[7][0:1, 500:501])
            T.replay(outer, st)

        with ExitStack() as st:
            T = Tracker(nc)
            O = Ops(T)
            bufs = {}

            def sb(name, shape, dt):
                t = st.enter_context(nc.sbuf_tensor("sb_" + name, list(shape), dt))
                bufs[name] = Buf(name)
                return t

            NB = NO // 2
            banks = [st.enter_context(nc.psum_tensor(f"pbank{i}", [128, 512], F32)) for i in range(8)]
            BK = [Buf(f"pbank{i}") for i in range(8)]
            identb = sb("identb2", [128, 128], BF16); identf = sb("identf2", [128, 128], F32)
            bar = sb("bar2", [128, 16], F32)
            g_2 = sb("g_2", [128, D], F32); b_2 = sb("b_2", [128, D], F32)
            iota3 = sb("iota3", [128, 8, 128], BF16); iota16 = sb("iota16", [128, 16], F32)
            keysb = sb("keysb", [128, 16, 128], BF16)
            keysT = sb("keysT", [128, 16, 128], BF16)
            Bf = bufs

            def load_const(t, src, name):
                O.dma(t, src, (), [bufs[name]], bufs[name])
            load_const(identb[:], identb_d, "identb2"); load_const(identf[:], identf_d, "identf2")
            load_const(g_2[:], ln2_g.partition_broadcast(128), "g_2"); load_const(b_2[:], ln2_b.partition_broadcast(128), "b_2")
            load_const(iota3[:].rearrange("p t i -> p (t i)"), iota3_d[:, 0:1024], "iota3")
            load_const(iota16[:], iota16_d[:, 0:16], "iota16")

            h1T = [sb(f"h1T2_{i}", [128, 16, 256], BF16) for i in range(2)]
            wslots = [sb(f"wq{i}", [128, 16, 128], BF16) for i in range(2)]
            qT = sb("qT", [128, 16, 128], BF16)
            s2 = sb("s2", [128, 128], F32)
            sv = sb("sv", [128, 16, 16], F32); si = sb("si", [128, 16, 16], U32); sif = sb("sif", [128, 16, 16], F32)
            cand = sb("cand", [128, 8, 256], F32); c2 = sb("c2", [128, 256], F32)
            tv = sb("tv", [128, 8, 16], F32); ci = sb("ci", [128, 8, 16], U32)
            ca = sb("ca", [128, 8, 16], U32); cb_ = sb("cb_", [128, 8, 16], U32)
            caf = sb("caf", [128, 8, 16], F32); cbf = sb("cbf", [128, 8, 16], F32)
            oh = cand[:].rearrange("p h (a b) -> p h a b", a=16); bufs["oh"] = bufs["cand"]
            sel = sb("sel", [128, 3, 128], F32); selT = sb("selT", [128, 3, 256], F32)
            ee = sb("ee", [128, 8, 16], F32); zz = sb("zz", [128, 8], F32)
            Pb = [sb(f"Pb{i}", [128, 8, 128], BF16) for i in range(2)]
            Qb = [sb(f"Qb{i}", [128, 8, 128], BF16) for i in range(2)]
            Wsb = sb("Wsb", [128, 128, 256], BF16)
            WB = [Buf(f"WB{i}") for i in range(64)]
            uslots = [sb(f"us{i}", [128, 16, 128], BF16) for i in range(4)]
            vslots = [sb(f"vs{i}", [128, 2, 1024], BF16) for i in range(3)]
            ga = [sb(f"ga{i}", [128, 512], F32) for i in range(2)]
            t2 = [sb(f"t2_{i}", [128, D], F32) for i in range(2)]
            s_ = sb("s_", [128, 16, 128], F32)
            sm = sb("sm2", [128, 64], F32)

            keysf = t2[0][:].rearrange("p (a n) -> p a n", a=16); bufs["keysf"] = bufs["t2_0"]
            load_const(keysf, keys.rearrange("h p n c -> n (h p) c"), "keysf")
            O.memset("pool", bar[:], 0.0, [Bf["bar2"]])
            O.cp("dve", keysb[:], keysf, [Bf["keysf"]], [Bf["keysb"]])
            for half in range(2):
                bk = 6 + half
                pb = banks[bk][:].bitcast(BF16)
                for kk in range(8):
                    hp = half * 8 + kk
                    O.tr(pb[:, kk * 128:(kk + 1) * 128], keysb[:, hp, :], identb[:], [Bf["keysb"], Bf["identb2"]], [BK[bk]])
                O.cp("dve", keysT[:, half * 8:(half + 1) * 8, :].rearrange("p k t -> p (k t)"), pb, [BK[bk]], [Bf["keysT"]], partial=True)

            def smcol(name, c0, n):
                bufs[name] = Buf(name)
                return sm[:, c0:c0 + n]
            lnmv = smcol("lnmv", 0, 2); lnr = smcol("lnr", 2, 1); lnst = smcol("lnst", 4, 24)

            def layernorm_tile(src, src_bufs, dst, dst_buf, gt, bt, gbuf, bbuf):
                for q in range(4):
                    T.op("dve", (lambda q=q: (lambda e: e.bn_stats(out=lnst[:, q * 6:(q + 1) * 6], in_=src[:, q * 512:(q + 1) * 512])))(),
                         src_bufs, [Bf["lnst"]], partial=(q > 0))
                T.op("dve", lambda e: e.bn_aggr(out=lnmv, in_=lnst), [Bf["lnst"]], [Bf["lnmv"]])
                O.act(lnr, lnmv[:, 1:2], AF.Ln, [Bf["lnmv"]], [Bf["lnr"]], bias=EPS, scale=1.0)
                O.act(lnr, lnr, AF.Exp, [Bf["lnr"]], [Bf["lnr"]], scale=-0.5)
                O.ts("dve", dst, src, lnmv[:, 0:1], lnr[:, 0:1], ALU.subtract, ALU.mult, src_bufs + [Bf["lnmv"], Bf["lnr"]], [dst_buf])
                O.tt("pool", dst, dst, gt, ALU.mult, [dst_buf, gbuf], [dst_buf])
                O.tt("pool", dst, dst, bt, ALU.add, [dst_buf, bbuf], [dst_buf])

            Wq_v = Wq_b.rearrange("(k p) c -> p k c", p=128)
            qloads = []
            uloads = []
            vloads = []
            for blk in range(NB):
                for tile in range(2):
                    for c0 in range(0, 2048, 128):
                        qloads.append((lambda c0=c0: (lambda slot: [(slot[:, :, :], Wq_v[:, :, c0:c0 + 128])]))())
                for g in range(NG):
                    uloads.append((lambda g=g: (lambda slot: [(slot[:].rearrange("p k j -> p (k j)"), uT_d[g])]))())
                for half in range(2):
                    for gp in range(64):
                        vloads.append((lambda gp=gp, half=half: (lambda slot: [(slot[:, :, :], vbh_d[half][gp * 256:(gp + 1) * 256, :].rearrange("(g j) d -> j g d", j=128))]))())
            qstream = Stream(O, wslots, [bufs[f"wq{i}"] for i in range(2)], qloads)
            ustream = Stream(O, uslots, [bufs[f"us{i}"] for i in range(4)], uloads)
            vstream = Stream(O, vslots, [bufs[f"vs{i}"] for i in range(3)], vloads, eng="pool")

            iota16_b = iota16[:].unsqueeze(1).unsqueeze(1).to_broadcast([128, 8, 16, 16])

            def sel_gen(blk):
                hb = blk % 2
                hT = h1T[hb]; hTb = Bf[f"h1T2_{hb}"]
                for tile in range(2):
                    ti = blk * 2 + tile
                    O.dma(hT[:, :, tile * 128:(tile + 1) * 128], h1Td[ti].rearrange("p (k t) -> p k t", k=16), (), [hTb], hTb, partial=True)
                yield
                for tile in range(2):
                    tsl = slice(tile * 128, (tile + 1) * 128)
                    for grp in range(16):
                        slot, sbuf_ = qstream.next()
                        bk = 6 + (grp % 2)
                        for k in range(16):
                            O.mm(banks[bk][:, 0:128], slot[:, k, :], hT[:, k, tsl], k == 0, k == 15, [sbuf_, hTb], [BK[bk]])
                        O.cp("act", qT[:, grp, :], banks[bk][:, 0:128], [BK[bk]], [Bf["qT"]], partial=True)
                        if grp % 2 == 1:
                            yield
                    for q4 in range(4):
                        bk = 6 + (q4 % 2)
                        for a in range(4):
                            hp = q4 * 4 + a
                            O.mm(banks[bk][:, a * 128:(a + 1) * 128], qT[:, hp, :], keysT[:, hp, :], True, True, [Bf["qT"], Bf["keysT"]], [BK[bk]])
                        O.cp("act", s_[:, q4 * 4:(q4 + 1) * 4, :].rearrange("p a n -> p (a n)"), banks[bk][:], [BK[bk]], [Bf["s_"]], partial=True)
                    yield
                    for hp in range(16):
                        T.op("dve", (lambda hp=hp: (lambda e: e.max(out=sv[:, hp, 0:8], in_=s_[:, hp, :])))(), [Bf["s_"]], [Bf["sv"]], partial=True)
                        T.op("dve", (lambda hp=hp: (lambda e: e.match_replace(out=s2[:], in_to_replace=sv[:, hp, 0:8], in_values=s_[:, hp, :], imm_value=-1e30)))(),
                             [Bf["s_"], Bf["sv"]], [Bf["s2"]])
                        T.op("dve", (lambda hp=hp: (lambda e: e.max(out=sv[:, hp, 8:16], in_=s2[:])))(), [Bf["s2"]], [Bf["sv"]], partial=True)
                        T.op("dve", (lambda hp=hp: (lambda e: e.max_index(out=si[:, hp, 0:8], in_max=sv[:, hp, 0:8], in_values=s_[:, hp, :])))(),
                             [Bf["s_"], Bf["sv"]], [Bf["si"]], partial=True)
                        T.op("dve", (lambda hp=hp: (lambda e: e.max_index(out=si[:, hp, 8:16], in_max=sv[:, hp, 8:16], in_values=s_[:, hp, :])))(),
                             [Bf["s_"], Bf["sv"]], [Bf["si"]], partial=True)
                        yield
                    O.cp("dve", sif[:], si[:], [Bf["si"]], [Bf["sif"]])
                    sv4 = sv[:].rearrange("p (h two) a -> p h two a", two=2)
                    sif4 = sif[:].rearrange("p (h two) a -> p h two a", two=2)
                    O.tt("dve", cand[:].rearrange("p h (a b) -> p h a b", a=16),
                         sv4[:, :, 0, :].unsqueeze(3).to_broadcast([128, 8, 16, 16]),
                         sv4[:, :, 1, :].unsqueeze(2).to_broadcast([128, 8, 16, 16]), ALU.add, [Bf["sv"]], [Bf["cand"]])
                    yield
                    for h in range(8):
                        T.op("dve", (lambda h=h: (lambda e: e.max(out=tv[:, h, 0:8], in_=cand[:, h, :])))(), [Bf["cand"]], [Bf["tv"]], partial=True)
                        T.op("dve", (lambda h=h: (lambda e: e.match_replace(out=c2[:], in_to_replace=tv[:, h, 0:8], in_values=cand[:, h, :], imm_value=-1e30)))(),
                             [Bf["cand"], Bf["tv"]], [Bf["c2"]])
                        T.op("dve", (lambda h=h: (lambda e: e.max(out=tv[:, h, 8:16], in_=c2[:])))(), [Bf["c2"]], [Bf["tv"]], partial=True)
                        T.op("dve", (lambda h=h: (lambda e: e.max_index(out=ci[:, h, 0:8], in_max=tv[:, h, 0:8], in_values=cand[:, h, :])))(),
                             [Bf["cand"], Bf["tv"]], [Bf["ci"]], partial=True)
                        T.op("dve", (lambda h=h: (lambda e: e.max_index(out=ci[:, h, 8:16], in_max=tv[:, h, 8:16], in_values=cand[:, h, :])))(),
                             [Bf["cand"], Bf["tv"]], [Bf["ci"]], partial=True)
                        yield
                    T.op("dve", lambda e: e.tensor_single_scalar(out=ca[:], in_=ci[:], scalar=4, op=ALU.logical_shift_right), [Bf["ci"]], [Bf["ca"]])
                    T.op("dve", lambda e: e.tensor_single_scalar(out=cb_[:], in_=ci[:], scalar=15, op=ALU.bitwise_and), [Bf["ci"]], [Bf["cb_"]])
                    O.cp("dve", caf[:], ca[:], [Bf["ca"]], [Bf["caf"]])
                    O.cp("dve", cbf[:], cb_[:], [Bf["cb_"]], [Bf["cbf"]])
                    yield
                    for which, idxf in ((0, caf), (1, cbf)):
                        O.tt("dve", oh, iota16_b, idxf[:].unsqueeze(3).to_broadcast([128, 8, 16, 16]), ALU.is_equal,
                             [Bf["iota16"], Bf["caf" if which == 0 else "cbf"]], [Bf["oh"]])
                        yield
                        O.tt("pool", oh, oh, sif4[:, :, which, :].unsqueeze(2).to_broadcast([128, 8, 16, 16]), ALU.mult,
                             [Bf["oh"], Bf["sif"]], [Bf["oh"]])
                        T.op("dve", (lambda which=which: (lambda e: e.tensor_reduce(out=sel[:, which, :].rearrange("p (h k) -> p h k", h=8), in_=oh, axis=AX.X, op=ALU.add)))(),
                             [Bf["oh"]], [Bf["sel"]], partial=True)
                        yield
                    O.tt("dve", ee[:], tv[:], tv[:, :, 0:1].to_broadcast([128, 8, 16]), ALU.subtract, [Bf["tv"]], [Bf["ee"]])
                    O.act(ee[:], ee[:], AF.Exp, [Bf["ee"]], [Bf["ee"]])
                    T.op("dve", lambda e: e.tensor_reduce(out=zz[:], in_=ee[:], axis=AX.X, op=ALU.add), [Bf["ee"]], [Bf["zz"]])
                    T.op("dve", lambda e: e.reciprocal(out=zz[:], in_=zz[:]), [Bf["zz"]], [Bf["zz"]])
                    O.tt("dve", sel[:, 2, :].rearrange("p (h k) -> p h k", h=8), ee[:], zz[:].unsqueeze(2).to_broadcast([128, 8, 16]), ALU.mult,
                         [Bf["ee"], Bf["zz"]], [Bf["sel"]], partial=True)
                    yield
                    for w3 in range(3):
                        O.tr(banks[6][:, w3 * 128:(w3 + 1) * 128], sel[:, w3, :], identf[:], [Bf["sel"], Bf["identf2"]], [BK[6]])
                    O.cp("dve", selT[:, :, tsl], banks[6][:, 0:384].rearrange("p (w t) -> p w t", w=3), [BK[6]], [Bf["selT"]], partial=True)
                    yield

            def expand(blk):
                wb_rot = 0
                for tb in range(32):
                    pq = tb % 2
                    tk0 = tb * 8
                    i2b = selT[:, 1, tk0:tk0 + 8].unsqueeze(2).to_broadcast([128, 8, 128])
                    O.tt("dve", Qb[pq][:], iota3[:], i2b, ALU.is_equal, [Bf["iota3"], Bf["selT"]], [Bf[f"Qb{pq}"]])
                    for tl in range(8):
                        tk = tk0 + tl
                        O.ts("dve", Pb[pq][:, tl, :], iota3[:, 0, :], selT[:, 0, tk:tk + 1], selT[:, 2, tk:tk + 1], ALU.is_equal, ALU.mult,
                             [Bf["iota3"], Bf["selT"]], [Bf[f"Pb{pq}"]], partial=(tl > 0))
                    for t4 in range(2):
                        bk = 4 + (wb_rot % 4)
                        wb_rot += 1
                        for tt_ in range(4):
                            tl = t4 * 4 + tt_
                            O.mm(banks[bk][:, tt_ * 128:(tt_ + 1) * 128], Qb[pq][:, tl, :], Pb[pq][:, tl, :], True, True,
                                 [Bf[f"Qb{pq}"], Bf[f"Pb{pq}"]], [BK[bk]])
                        t0 = tk0 + t4 * 4
                        O.cp("act", Wsb[:, :, t0:t0 + 4].rearrange("j g t -> j t g"), banks[bk][:].rearrange("p (t g) -> p t g", t=4),
                             [BK[bk]], WB, partial=True)

            def step(gen):
                if gen is not None:
                    try:
                        next(gen)
                    except StopIteration:
                        return None
                return gen

            def drain(gen):
                while gen is not None:
                    gen = step(gen)

            drain(sel_gen(0))
            expand(0)
            for blk in range(NB):
                hb = blk % 2
                hT = h1T[hb]; hTb = Bf[f"h1T2_{hb}"]
                gen = sel_gen(blk + 1) if blk + 1 < NB else None
                for tile in range(2):
                    ti = blk * 2 + tile
                    O.dma(t2[tile][:], h1d[ti * 128:(ti + 1) * 128, :], (), [Bf[f"t2_{tile}"]], Bf[f"t2_{tile}"])

                def A_step(gp):
                    bkA = 4 + (gp % 2)
                    for gi in range(2):
                        us, ub = ustream.next()
                        for k in range(16):
                            O.mm(banks[bkA][:, gi * 256:(gi + 1) * 256], us[:, k, :], hT[:, k, :], k == 0, k == 15, [ub, hTb], [BK[bkA]])

                def V_step(gp, first, last):
                    vs, vbuf = vstream.next()
                    for gi in range(2):
                        g = gp * 2 + gi
                        for tile in range(2):
                            for dq in range(2):
                                O.mm(banks[tile * 2 + dq][:], Wsb[:, g, tile * 128:(tile + 1) * 128], vs[:, gi, dq * 512:(dq + 1) * 512],
                                     first and gi == 0, last and gi == 1, [WB[gp], vbuf], [BK[tile * 2 + dq]])

                A_step(0)
                for gp in range(64):
                    if gp + 1 < 64:
                        A_step(gp + 1)
                    bkA = 4 + (gp % 2)
                    sl = gp % 2
                    O.act(ga[sl][:], banks[bkA][:], AF.Gelu, [BK[bkA]], [Bf[f"ga{sl}"]])
                    wv = Wsb[:, gp * 2:(gp + 1) * 2, :].rearrange("p g t -> p (g t)")
                    O.tt("dve", wv, ga[sl][:], wv, ALU.mult, [Bf[f"ga{sl}"], WB[gp]], [WB[gp]])
                    V_step(gp, gp == 0, gp == 63)
                    gen = step(gen)
                for tile in range(2):
                    for dq in range(2):
                        cs = slice(dq * 512, (dq + 1) * 512)
                        O.stt("dve", t2[tile][:, cs], t2[tile][:, cs], ALPHA, banks[tile * 2 + dq][:], ALU.mult, ALU.add,
                              [Bf[f"t2_{tile}"], BK[tile * 2 + dq]], [Bf[f"t2_{tile}"]], partial=True)
                for gp in range(64):
                    V_step(gp, gp == 0, gp == 63)
                    gen = step(gen)
                drain(gen)
                for tile in range(2):
                    ti = blk * 2 + tile
                    for dq in range(2):
                        cs = slice(1024 + dq * 512, 1024 + (dq + 1) * 512)
                        O.stt("dve", t2[tile][:, cs], t2[tile][:, cs], ALPHA, banks[tile * 2 + dq][:], ALU.mult, ALU.add,
                              [Bf[f"t2_{tile}"], BK[tile * 2 + dq]], [Bf[f"t2_{tile}"]], partial=True)
                    layernorm_tile(t2[tile][:], [Bf[f"t2_{tile}"]], t2[tile][:], Bf[f"t2_{tile}"], g_2[:], b_2[:], Bf["g_2"], Bf["b_2"])
                    O.dma(out_d[ti * 128:(ti + 1) * 128, :], t2[tile][:], [Bf[f"t2_{tile}"]], [], Bf[f"t2_{tile}"])
                if blk + 1 < NB:
                    expand(blk + 1)
            final_barrier(T, O, None, bar, banks[7][0:1, 500:501])
            T.replay(outer, st)

    return nc


NH_FULL = 96
NO_FULL = 32


def _consts():
    bf = ml_dtypes.bfloat16
    c = {}
    c["identb"] = np.eye(128, dtype=np.float32).astype(bf)
    c["identf"] = np.eye(128, dtype=np.float32)
    c["tri"] = np.triu(np.ones((128, 128), dtype=np.float32))
    c["onesf"] = np.ones((128, 128), dtype=np.float32)
    c["iota3"] = np.broadcast_to(np.arange(128, dtype=np.float32)[None, None, :], (128, 16, 128)).reshape(128, 16 * 128).astype(bf)
    c["iota16"] = np.ascontiguousarray(np.broadcast_to(np.arange(16, dtype=np.float32)[None, None, None, :], (128, 8, 16, 16)).reshape(128, 2048))
    return c


def _weights(inp):
    f = lambda a: np.ascontiguousarray(np.asarray(a, dtype=np.float32))
    return {
        "ln_in_g": f(inp["ln_in_g"]), "ln_in_b": f(inp["ln_in_b"]), "w_in": f(inp["w_in"][0]), "b_gate": f(inp["b_gate"][0]),
        "conv_w": f(np.asarray(inp["conv_w"][0]).reshape(3, 8, 128).transpose(2, 0, 1).reshape(128, 24)), "conv_b": f(np.asarray(inp["conv_b"][0]).reshape(8, 128).T), "mh_norm_g": f(inp["mh_norm_g"][0]), "w_out": f(inp["w_out"][0]),
        "ln1_g": f(inp["ln1_g"][0]), "ln1_b": f(inp["ln1_b"][0]), "peer_wq": f(inp["peer_wq"][0]), "peer_keys": f(inp["peer_keys"][0]),
        "peer_u": f(inp["peer_u"][0]), "peer_v": f(inp["peer_v"][0]), "ln2_g": f(inp["ln2_g"][0]), "ln2_b": f(inp["ln2_b"][0]),
    }


def core_inputs(x_seq, start, n_own_tok, NH, common):
    hist_tok = NH * 128
    xin = np.zeros((hist_tok + n_own_tok, D), dtype=np.float32)
    real = min(start, hist_tok)
    if real > 0:
        xin[hist_tok - real:hist_tok] = x_seq[start - real:start]
    xin[hist_tok:] = x_seq[start:start + n_own_tok]
    hm = np.zeros((128, max(NH, 1)), dtype=np.float32)
    ndummy = (hist_tok - real) // 128
    hm[:, :ndummy] = NEG
    m = dict(common)
    m["xin"] = xin
    m["hmask"] = hm
    m["hvalid"] = np.full((128, 1), 1.0 if real > 0 else 0.0, dtype=np.float32)
    return m


_NC_CACHE = {}


def kernel(**inputs):
    x = np.asarray(inputs["x"], dtype=np.float32)
    Bn, S, _ = x.shape
    common = _weights(inputs)
    common.update(_consts())
    ncores = 8
    per = (Bn * S) // ncores
    segs = S // per
    NO = per // 128
    NH = (segs - 1) * NO
    key = (NH, NO)
    if key not in _NC_CACHE:
        _NC_CACHE[key] = build(NH, NO)
    nc = _NC_CACHE[key]
    in_maps = []
    for c in range(ncores):
        b, sg = divmod(c, segs)
        in_maps.append(core_inputs(x[b], sg * per, per, NH, common))
    res = run_bass_kernel_spmd(nc, in_maps, core_ids=list(range(ncores)))
    out = np.empty((Bn, S, D), dtype=np.float32)
    for c in range(ncores):
        b, sg = divmod(c, segs)
        out[b, sg * per:(sg + 1) * per] = res.results[c]["out"]
    return out
```

```python
import numpy as np
import concourse.bass as bass
import concourse.mybir as mybir
from concourse.bass_utils import run_bass_kernel_spmd

F32 = mybir.dt.float32
BF16 = mybir.dt.bfloat16
U32 = mybir.dt.uint32
ALU = mybir.AluOpType
AF = mybir.ActivationFunctionType
AX = mybir.AxisListType

SEM_WINDOW = 16384


class Buf:
    __slots__ = ("name", "writers", "readers", "dsem", "dcount")

    def __init__(self, name):
        self.name = name
        self.writers = {}
        self.readers = {}
        self.dsem = None
        self.dcount = 0


class Op:
    __slots__ = ("eng", "idx", "fn", "waits", "sig", "dma_ev")

    def __init__(self, eng, idx, fn):
        self.eng = eng
        self.idx = idx
        self.fn = fn
        self.waits = []
        self.sig = False
        self.dma_ev = None


class Tracker:
    ENGS = ("sync", "act", "dve", "pool", "pe")

    _uid = [0]

    def __init__(self, nc):
        self.nc = nc
        Tracker._uid[0] += 1
        self.uid = Tracker._uid[0]
        self.ops = {e: [] for e in self.ENGS}
        self.ndsem = 0
        self.dsem_total = {}

    def _dep(self, op, key, val):
        if key[0] == 'e':
            eng = key[1]
            if eng == op.eng and eng == "pe":
                return
            prod = self.ops[eng][val]
            prod.sig = True
            op.waits.append(('e', eng, val))
        else:
            op.waits.append(('d', key[1], self.dsem_total[key[1]]))

    def op(self, eng, fn, reads=(), writes=(), partial=False, dma_buf=None):
        lst = self.ops[eng]
        o = Op(eng, len(lst), fn)
        for b in reads:
            for k, v in b.writers.items():
                self._dep(o, k, v)
        for b in writes:
            for k, v in b.readers.items():
                self._dep(o, k, v)
            if not partial:
                for k, v in b.writers.items():
                    self._dep(o, k, v)
        if dma_buf is not None:
            if dma_buf.dsem is None:
                dma_buf.dsem = self.ndsem
                self.ndsem += 1
            dma_buf.dcount += 16
            o.dma_ev = (dma_buf.dsem, dma_buf.dcount)
            self.dsem_total[dma_buf.dsem] = dma_buf.dcount
            key, val = ('d', dma_buf.dsem), dma_buf.dcount
        else:
            key, val = ('e', eng), o.idx
        for b in reads:
            b.readers[key] = val
        for b in writes:
            if not partial:
                b.writers = {}
            b.readers = {}
            b.writers[key] = val
        lst.append(o)
        return o

    def replay(self, stack, bstack=None):
        nc = self.nc
        bstack = bstack or stack
        esems = {}
        for e in self.ENGS:
            nsig = sum(1 for o in self.ops[e] if o.sig and o.dma_ev is None)
            nwin = max(1, (nsig + SEM_WINDOW - 1) // SEM_WINDOW)
            esems[e] = [stack.enter_context(nc.semaphore(f"e{self.uid}_{e}_{i}")) for i in range(nwin)]
        dsems = {}
        for d in range(self.ndsem):
            dsems[d] = {}
        self._stack = stack
        sigcnt = {}
        for e in self.ENGS:
            c = 0
            arr = []
            for o in self.ops[e]:
                if o.sig and o.dma_ev is None:
                    c += 1
                arr.append(c)
            sigcnt[e] = arr

        def dsem_handle(d, w):
            if w not in dsems[d]:
                dsems[d][w] = stack.enter_context(nc.semaphore(f"d{self.uid}_{d}_{w}"))
            return dsems[d][w]

        DW = SEM_WINDOW * 2
        engh = {"sync": nc.sync, "act": nc.scalar, "dve": nc.vector, "pool": nc.gpsimd, "pe": nc.tensor}

        def run(e, eh):
            waited = {}
            for o in self.ops[e]:
                need = {}
                for w in o.waits:
                    if w[0] == 'e':
                        c = sigcnt[w[1]][w[2]]
                        k = ('e', w[1])
                    else:
                        c = w[2]
                        k = ('d', w[1])
                    if c > need.get(k, 0):
                        need[k] = c
                for k, c in need.items():
                    if waited.get(k, 0) >= c:
                        continue
                    waited[k] = c
                    if k[0] == 'e':
                        win = (c - 1) // SEM_WINDOW
                        eh.wait_ge(esems[k[1]][win], (c - 1) % SEM_WINDOW + 1)
                    else:
                        win = (c - 16) // DW
                        eh.wait_ge(dsem_handle(k[1], win), (c - 16) % DW + 16)
                if o.fn is None:
                    continue
                ins = o.fn(eh)
                if o.dma_ev is not None:
                    d, c = o.dma_ev
                    win = (c - 16) // DW
                    ins.then_inc(dsem_handle(d, win), 16)
                elif o.sig:
                    c = sigcnt[e][o.idx]
                    win = (c - 1) // SEM_WINDOW
                    ins.then_inc(esems[e][win], 1)

        block = bstack.enter_context(nc.Block())

        @block.sync
        def _(eh):
            run("sync", eh)

        @block.scalar
        def _(eh):
            run("act", eh)

        @block.vector
        def _(eh):
            run("dve", eh)

        @block.gpsimd
        def _(eh):
            run("pool", eh)

        @block.tensor
        def _(eh):
            run("pe", eh)

import math
from contextlib import ExitStack
import ml_dtypes

D = 2048
DIN = 7184
NG = 128
ALPHA = 2.0 ** 0.25
EPS = 1e-5
DH = 128
KSCALE = DH ** -0.5
NEG = -30000.0


class Ops:
    def __init__(self, T):
        self.T = T

    def mm(self, out, lhsT, rhs, start, stop, r, w):
        self.T.op("pe", lambda e: e.matmul(out, lhsT=lhsT, rhs=rhs, start=start, stop=stop), r, w, partial=True)

    def tr(self, out, in_, ident, r, w):
        self.T.op("pe", lambda e: e.transpose(out=out, in_=in_, identity=ident), r, w, partial=True)

    def act(self, out, in_, func, r, w, bias=0.0, scale=1.0, partial=False):
        self.T.op("act", lambda e: e.activation(out=out, in_=in_, func=func, bias=bias, scale=scale), r, w, partial=partial)

    def tt(self, eng, out, in0, in1, op, r, w, partial=False):
        self.T.op(eng, lambda e: e.tensor_tensor(out=out, in0=in0, in1=in1, op=op), r, w, partial=partial)

    def ts(self, eng, out, in0, s1, s2, op0, op1, r, w, partial=False):
        if s2 is None:
            self.T.op(eng, lambda e: e.tensor_scalar(out=out, in0=in0, scalar1=s1, scalar2=None, op0=op0), r, w, partial=partial)
        else:
            self.T.op(eng, lambda e: e.tensor_scalar(out=out, in0=in0, scalar1=s1, scalar2=s2, op0=op0, op1=op1), r, w, partial=partial)

    def stt(self, eng, out, in0, scalar, in1, op0, op1, r, w, partial=False):
        self.T.op(eng, lambda e: e.scalar_tensor_tensor(out=out, in0=in0, scalar=scalar, in1=in1, op0=op0, op1=op1), r, w, partial=partial)

    def cp(self, eng, out, in_, r, w, partial=False):
        if eng == "act":
            self.T.op("act", lambda e: e.copy(out=out, in_=in_), r, w, partial=partial)
        else:
            self.T.op(eng, lambda e: e.tensor_copy(out=out, in_=in_), r, w, partial=partial)

    def dma(self, out, in_, r, w, buf, partial=False, eng="sync"):
        self.T.op(eng, lambda e: e.dma_start(out=out, in_=in_), r, w, partial=partial, dma_buf=buf)

    def memset(self, eng, ap, val, w):
        self.T.op(eng, lambda e: e.memset(ap, val), (), w)


class Stream:
    def __init__(self, O, slots, bufs, loads, depth=None, eng="sync"):
        self.O = O
        self.eng = eng
        self.slots = slots
        self.bufs = bufs
        self.loads = loads
        self.n = len(slots)
        self.depth = depth or (self.n - 1)
        self.issued = 0
        self.consumed = 0

    def _issue(self):
        if self.issued >= len(self.loads):
            return
        k = self.issued
        s = k % self.n
        for (o, i) in self.loads[k](self.slots[s]):
            self.O.dma(o, i, (), [self.bufs[s]], self.bufs[s], partial=True, eng=self.eng)
        self.issued += 1

    def next(self):
        while self.issued < len(self.loads) and self.issued < self.consumed + self.depth:
            self._issue()
        k = self.consumed
        self.consumed += 1
        s = k % self.n
        return self.slots[s], self.bufs[s]


def final_barrier(T, O, scratch, sbuf_bar, psum_bar):
    bars = {}
    for e in ("act", "dve", "pool"):
        b = Buf("bar_" + e)
        bars[e] = b
        col = {"act": 0, "dve": 1, "pool": 2}[e]
        O.memset(e, sbuf_bar[:, col:col + 1], 0.0, [b]) if e != "act" else T.op(
            "act", lambda en: en.copy(out=sbuf_bar[:, 0:1], in_=sbuf_bar[:, 4:5]), (), [b])
    bpe = Buf("bar_pe")
    bars["pe"] = bpe
    T.op("pe", lambda en: en.matmul(psum_bar, lhsT=sbuf_bar[:, 8:9].bitcast(F32), rhs=sbuf_bar[:, 8:9].bitcast(F32), start=True, stop=True), (), [bpe], partial=True)
    allb = list(bars.values())
    for e in ("sync", "act", "dve", "pool", "pe"):
        o = T.op(e, None, reads=allb)
        for d, tot in T.dsem_total.items():
            o.waits.append(('d', d, tot))


def build(NH, NO, debug=False):
    NT = NH + NO
    nc = bass.Bass("TRN2", target_bir_lowering=False)

    def din(name, shape, dt=F32):
        return nc.dram_tensor(name, list(shape), dt, kind="ExternalInput").ap()

    xin = din("xin", [NT * 128, D])
    hmask_d = din("hmask", [128, max(NH, 1)])
    hvalid_d = din("hvalid", [128, 1])
    ln_in_g = din("ln_in_g", [D]); ln_in_b = din("ln_in_b", [D])
    w_in = din("w_in", [D, DIN]); b_gate = din("b_gate", [16])
    conv_w = din("conv_w", [128, 24]); conv_b = din("conv_b", [128, 8])
    mh_g = din("mh_norm_g", [1024]); w_out = din("w_out", [D, D])
    ln1_g = din("ln1_g", [D]); ln1_b = din("ln1_b", [D])
    wq = din("peer_wq", [D, D]); keys = din("peer_keys", [8, 2, 128, 128])
    pu = din("peer_u", [16384, D]); pv = din("peer_v", [16384, D])
    ln2_g = din("ln2_g", [D]); ln2_b = din("ln2_b", [D])
    identb_d = din("identb", [128, 128], BF16); identf_d = din("identf", [128, 128])
    tri_d = din("tri", [128, 128]); onesf_d = din("onesf", [128, 128])
    iota3_d = din("iota3", [128, 16 * 128], BF16); iota16_d = din("iota16", [128, 8 * 16 * 16])
    out_d = nc.dram_tensor("out", [NO * 128, D], F32, kind="ExternalOutput").ap()
    dbg_d = nc.dram_tensor("dbg", [NO * 128, D], F32, kind="ExternalOutput").ap() if debug else None

    def dscr(name, shape, dt):
        return nc.dram_tensor(name, list(shape), dt).ap()

    Wi_b = dscr("Wi_b", [D, DIN], BF16); Wo_b = dscr("Wo_b", [D, D], BF16); Wq_b = dscr("Wq_b", [D, D], BF16)
    vbh_d = dscr("vbh_d", [2, 16384, 1024], BF16); uT_d = dscr("uT_d", [NG, 128, 16 * 128], BF16)
    h0d = dscr("h0d", [NO * 128, D], F32); h1d = dscr("h1d", [NO * 128, D], F32)
    h1Td = dscr("h1Td", [NO, 128, 16 * 128], BF16)
    B_Wi = Buf("Wi_b"); B_Wo = Buf("Wo_b"); B_Wq = Buf("Wq_b"); B_vb = Buf("vb_d"); B_uT = Buf("uT_d")
    B_h0d = Buf("h0d"); B_h1d = Buf("h1d"); B_h1Td = Buf("h1Td")

    with ExitStack() as outer:
        with ExitStack() as st:
            T = Tracker(nc)
            O = Ops(T)
            bufs = {}

            def sb(name, shape, dt):
                t = st.enter_context(nc.sbuf_tensor("sb_" + name, list(shape), dt))
                bufs[name] = Buf(name)
                return t

            banks = [st.enter_context(nc.psum_tensor(f"bank{i}", [128, 512], F32)) for i in range(8)]
            BK = [Buf(f"bank{i}") for i in range(8)]

            identb = sb("identb", [128, 128], BF16); identf = sb("identf", [128, 128], F32)
            tri = sb("tri", [128, 128], F32); onesf = sb("onesf", [128, 128], F32)
            onesb = sb("onesb", [128, 8], BF16)
            bar = sb("bar", [128, 16], F32)
            hmask = sb("hmask", [128, max(NH, 1)], F32); hvalid = sb("hvalid", [128, 1], F32)
            g_in = sb("g_in", [128, D], F32); b_in = sb("b_in", [128, D], F32)
            g_1 = sb("g_1", [128, D], F32); b_1 = sb("b_1", [128, D], F32)
            mhg = sb("mhg", [128, 1024], F32); bgate = sb("bgate", [128, 16], F32)
            cw = sb("cw", [128, 3, 8], F32); cbias = sb("cbias", [128, 8], F32)
            wgate = sb("wgate", [128, 16, 16], BF16)
            wgate_f = sb("wgate_f", [128, 16, 16], F32)

            def load_const(t, src, name):
                O.dma(t, src, (), [bufs[name]], bufs[name])

            load_const(identb[:], identb_d, "identb"); load_const(identf[:], identf_d, "identf")
            load_const(tri[:], tri_d, "tri"); load_const(onesf[:], onesf_d, "onesf")
            load_const(hmask[:], hmask_d, "hmask"); load_const(hvalid[:], hvalid_d, "hvalid")
            load_const(g_in[:], ln_in_g.partition_broadcast(128), "g_in"); load_const(b_in[:], ln_in_b.partition_broadcast(128), "b_in")
            load_const(mhg[:], mh_g.partition_broadcast(128), "mhg"); load_const(bgate[:], b_gate.partition_broadcast(128), "bgate")
            load_const(cw[:].rearrange("p j c -> p (j c)"), conv_w, "cw")
            load_const(cbias[:], conv_b, "cbias")
            load_const(wgate_f[:], w_in.rearrange("(k p) c -> p k c", p=128)[:, :, 7168:7184], "wgate_f")
            O.cp("dve", wgate[:], wgate_f[:], [bufs["wgate_f"]], [bufs["wgate"]])
            O.memset("pool", onesb[:], 1.0, [bufs["onesb"]])
            O.memset("pool", bar[:], 0.0, [bufs["bar"]])

            cin = [sb(f"cin{i}", [128, 2048], F32) for i in range(2)]
            cout = [sb(f"cout{i}", [128, 2048], BF16) for i in range(2)]
            cast_engs = ["dve", "act", "dve"]
            cnt = [0]

            xt = cin
            bufs["xt0"] = bufs["cin0"]; bufs["xt1"] = bufs["cin1"]
            h0 = sb("h0", [128, D], F32); h0b = cout[1]; bufs["h0b"] = bufs["cout1"]
            h0T = sb("h0T", [128, 16, 128], BF16)
            wbig = sb("wbig", [128, 16, 2048], BF16)
            wslots = [wbig[:, :, i * 512:(i + 1) * 512] for i in range(3)]
            for i in range(3):
                bufs[f"wg{i}"] = Buf(f"wg{i}")
            Ktok = sb("Ktok", [128, 8, 128], BF16); Vt = sb("Vt", [128, 8, 128], BF16)
            sig = sb("sig", [128, 1024], F32)
            Cf = sb("Cf", [128, 8, 128], F32); zbuf = sb("zbuf", [128, 8, 130], F32); acc = sb("acc", [128, 128], F32)
            mixT = sb("mixT", [128, 16, 128], BF16)
            QT = sb("QT", [128, 8, 128], BF16); KT = sb("KT", [128, 8, 128], BF16)
            PT = sb("PT", [128, 8, 128], F32); sw = sb("sw", [128, 8, 128], BF16)
            EB = sb("EB", [128, 8, 128], F32); QsT = sb("QsT", [128, 8, 128], BF16)
            TriLF = sb("TriLF", [128, 8, 128], F32)
            wkV = sb("wkV", [128, 8, 128], BF16)
            yn = sb("yn", [128, 8, 128], F32); ym = sb("ym", [128, 1024], BF16)
            CT = sb("CT", [128, 8, 128], F32); CTb = sb("CTb", [128, 8, 128], BF16)
            nT = sb("nT", [128, 8], F32); nb = sb("nb", [128, 8], BF16)
            t1 = sb("t1", [128, D], F32); h1 = t1; bufs["h1"] = bufs["t1"]; h1b = cout[0]; bufs["h1b"] = bufs["cout0"]
            h1T = sb("h1T", [128, 16, 128], BF16)
            sm = sb("sm", [128, 256], F32)
            smb = sb("smb", [128, 16], BF16)
            stats = sb("stats", [128, 8, 6], F32); mv = sb("mv", [128, 8, 2], F32)
            Bf = bufs
            def smcol(name, c0, n):
                bufs[name] = Buf(name)
                return sm[:, c0:c0 + n]
            gx = smcol("gx", 0, 16); ef = smcol("ef", 16, 8); lf = smcol("lf", 24, 8)
            bc = smcol("bc", 32, 8); wkt = smcol("wkt", 40, 8); wk = smcol("wk", 48, 8)
            bias8 = smcol("bias8", 56, 8); dec = smcol("dec", 64, 8); rr = smcol("rr", 72, 8)
            t8 = smcol("t8", 80, 8); sc8 = smcol("sc8", 88, 8); lnmv = smcol("lnmv", 96, 2)
            lnr = smcol("lnr", 98, 1); lnst = smcol("lnst", 100, 24)
            bufs["wkb"] = Buf("wkb")
            wkb = smb[:, 0:8]


            assert NH >= 1
            ccin = [t1, g_1]; ccinb = [Bf["t1"], Bf["g_1"]]
            ccoutb = [Bf["cout0"], Bf["mixT"]]

            def ccout_ap(s):
                return cout[0][:] if s == 0 else mixT[:].rearrange("p k t -> p (k t)")

            for k in range(16):
                s = k % 2
                O.dma(ccin[s][:], w_in[k * 128:(k + 1) * 128, 4096:6144], (), [ccinb[s]], ccinb[s])
                O.cp(cast_engs[k % 3], wbig[:, k, :], ccin[s][:], [ccinb[s]], [Bf["wbig"]], partial=True)

            def conv_gen():
                n = 0
                for (src, dst, R, C, dbuf) in ((w_in, Wi_b, D, DIN, B_Wi), (w_out, Wo_b, D, D, B_Wo), (wq, Wq_b, D, D, B_Wq)):
                    for r in range(R // 128):
                        for c0 in range(0, C, 2048):
                            cwid = min(2048, C - c0)
                            s = n % 2; eng = cast_engs[n % 3]; n += 1
                            O.dma(ccin[s][:, 0:cwid], src[r * 128:(r + 1) * 128, c0:c0 + cwid], (), [ccinb[s]], ccinb[s])
                            O.cp(eng, ccout_ap(s)[:, 0:cwid], ccin[s][:, 0:cwid], [ccinb[s]], [ccoutb[s]])
                            O.dma(dst[r * 128:(r + 1) * 128, c0:c0 + cwid], ccout_ap(s)[:, 0:cwid], [ccoutb[s]], [dbuf], ccoutb[s], partial=True)
                            yield
                for r in range(128):
                    s = n % 2; eng = cast_engs[n % 3]; n += 1
                    O.dma(ccin[s][:], pv[r * 128:(r + 1) * 128, :], (), [ccinb[s]], ccinb[s])
                    O.cp(eng, ccout_ap(s), ccin[s][:], [ccinb[s]], [ccoutb[s]])
                    for hh in range(2):
                        O.dma(vbh_d[hh][r * 128:(r + 1) * 128, :], ccout_ap(s)[:, hh * 1024:(hh + 1) * 1024], [ccoutb[s]], [B_vb], ccoutb[s], partial=True)
                    yield
                for g in range(NG):
                    s = n % 2; n += 1
                    O.dma(ccin[s][:], pu[g * 128:(g + 1) * 128, :], (), [ccinb[s]], ccinb[s])
                    for q4 in range(4):
                        bk = 2 + (q4 % 2)
                        for kk in range(4):
                            k = q4 * 4 + kk
                            O.tr(banks[bk][:, kk * 128:(kk + 1) * 128], ccin[s][:, k * 128:(k + 1) * 128], identf[:],
                                 [ccinb[s], bufs["identf"]], [BK[bk]])
                        eng = ["dve", "act"][q4 % 2]
                        O.cp(eng, ccout_ap(s)[:, q4 * 512:(q4 + 1) * 512], banks[bk][:], [BK[bk]], [ccoutb[s]], partial=True)
                    O.dma(uT_d[g], ccout_ap(s), [ccoutb[s]], [B_uT], ccoutb[s], partial=True)
                    yield

            def step(gen):
                if gen is not None:
                    try:
                        next(gen)
                    except StopIteration:
                        return None
                return gen

            def drain(gen):
                while gen is not None:
                    gen = step(gen)
                return None

            O.memset("pool", CT[:], 0.0, [Bf["CT"]]); O.memset("pool", CTb[:], 0.0, [Bf["CTb"]])
            O.memset("pool", nT[:], 0.0, [Bf["nT"]]); O.memset("pool", nb[:], 0.0, [Bf["nb"]])
            O.memset("pool", zbuf[:], 0.0, [Bf["zbuf"]])

            Wi_v = Wi_b.rearrange("(k p) c -> p k c", p=128)
            Wo_v = Wo_b.rearrange("(k p) c -> p k c", p=128)
            loads = []
            plan = []

            def wload(view, c0, srcbuf):
                def f(slot):
                    return [(slot[:, :, :], view[:, :, c0:c0 + 512])]
                return f

            for i in range(NT):
                own = i >= NH
                tags = []
                if own:
                    for c0 in (1024, 1536):
                        tags.append(("fC", c0))
                    for c0 in (2048, 2560):
                        tags.append(("fh", c0))
                    for c0 in (0, 512):
                        tags.append(("fB", c0))
                    for c0 in (3072, 3584):
                        tags.append(("fq", c0))
                for c0 in (4096, 4608):
                    tags.append(("k", c0))
                for c0 in (5120, 5632):
                    tags.append(("v", c0))
                if i == NH - 1:
                    for c0 in (1024, 1536):
                        tags.append(("fC", c0))
                    for c0 in (2048, 2560):
                        tags.append(("fh", c0))
                if own:
                    for c0 in (6144, 6656):
                        tags.append(("o", c0))
                    for c0 in (0, 512, 1024, 1536):
                        tags.append(("wo", c0))
                plan.append(tags)
                for (tg, c0) in tags:
                    if (not own) and tg in ("k", "v"):
                        continue
                    loads.append(wload(Wo_v if tg == "wo" else Wi_v, c0, None))
            wstream = Stream(O, wslots, [bufs[f"wg{i}"] for i in range(3)], loads)
            for i in range(3):
                pass
            orig_issue = wstream._issue

            def issue_with_deps():
                if wstream.issued >= len(wstream.loads):
                    return
                k = wstream.issued
                s = k % wstream.n
                for (o, i_) in wstream.loads[k](wstream.slots[s]):
                    O.dma(o, i_, [B_Wi, B_Wo], [wstream.bufs[s]], wstream.bufs[s], partial=True)
                wstream.issued += 1
            wstream._issue = issue_with_deps

            proj_banks = [6, 4, 5]
            pcount = [0]

            def next_pbank():
                b = proj_banks[pcount[0] % 3]
                pcount[0] += 1
                return b

            def layernorm_tile(src, src_bufs, dst, dst_buf, gt, bt, gbuf, bbuf, tmp=None, tmp_buf=None):
                for q in range(4):
                    T.op("dve", (lambda q=q: (lambda e: e.bn_stats(out=lnst[:, q * 6:(q + 1) * 6], in_=src[:, q * 512:(q + 1) * 512])))(),
                         src_bufs, [Bf["lnst"]], partial=(q > 0))
                T.op("dve", lambda e: e.bn_aggr(out=lnmv, in_=lnst), [Bf["lnst"]], [Bf["lnmv"]])
                O.act(lnr, lnmv[:, 1:2], AF.Ln, [Bf["lnmv"]], [Bf["lnr"]], bias=EPS, scale=1.0)
                O.act(lnr, lnr, AF.Exp, [Bf["lnr"]], [Bf["lnr"]], scale=-0.5)
                if tmp is None:
                    tmp, tmp_buf = dst, dst_buf
                O.stt("dve", tmp, src, lnmv[:, 0:1], gt, ALU.subtract, ALU.mult, src_bufs + [Bf["lnmv"], gbuf], [tmp_buf])
                O.stt("dve", dst, tmp, lnr[:, 0:1], bt, ALU.mult, ALU.add, [tmp_buf, Bf["lnr"], bbuf], [dst_buf])

            def transpose_2048(srcb, srcb_buf, dstT, dstT_buf):
                for half in range(2):
                    bk = next_pbank()
                    pb = banks[bk][:].bitcast(BF16)
                    for kk in range(8):
                        k = half * 8 + kk
                        O.tr(pb[:, kk * 128:(kk + 1) * 128], srcb[:, k * 128:(k + 1) * 128], identb[:], [srcb_buf, Bf["identb"]], [BK[bk]])
                    eng = "act" if half == 0 else "dve"
                    O.cp(eng, dstT[:, half * 8:(half + 1) * 8, :].rearrange("p k t -> p (k t)"), pb, [BK[bk]], [dstT_buf], partial=True)

            def flat(ap3):
                return ap3.rearrange("p h l -> p (h l)")

            def mlstm_tile(own, i):
                tri_b = tri[:].unsqueeze(1).to_broadcast([128, 8, 128])
                if own:
                    O.tt("pool", TriLF[:], tri_b, lf.unsqueeze(2).to_broadcast([128, 8, 128]), ALU.mult, [Bf["tri"], Bf["lf"]], [Bf["TriLF"]])
                    TL2 = flat(TriLF[:])
                    for half in range(2):
                        O.mm(banks[half][:], onesf[:], TL2[:, half * 512:(half + 1) * 512], True, True, [Bf["onesf"], Bf["TriLF"]], [BK[half]])
                    O.tt("dve", bias8, gx[:, 0:8], bc, ALU.subtract, [Bf["gx"], Bf["bc"]], [Bf["bias8"]])
                    for h in range(8):
                        bk = h // 4; col = (h % 4) * 128
                        O.act(PT[:, h, :], banks[bk][:, col:col + 128], AF.Exp, [BK[bk], Bf["bias8"]], [Bf["PT"]],
                              bias=bias8[:, h:h + 1], scale=1.0, partial=True)
                    O.tt("pool", PT[:], PT[:], tri_b, ALU.mult, [Bf["PT"], Bf["tri"]], [Bf["PT"]])
                    for h in range(8):
                        bk = 2 + h // 4; col = (h % 4) * 128
                        O.mm(banks[bk][:, col:col + 128], KT[:, h, :], QT[:, h, :], True, True, [Bf["KT"], Bf["QT"]], [BK[bk]])
                    for half in range(2):
                        O.tt("dve", flat(sw[:, half * 4:(half + 1) * 4, :]), flat(PT[:, half * 4:(half + 1) * 4, :]), banks[2 + half][:], ALU.mult,
                             [Bf["PT"], BK[2 + half]], [Bf["sw"]], partial=True)
                    for half in range(2):
                        O.act(flat(EB[:, half * 4:(half + 1) * 4, :]), banks[half][:], AF.Exp, [BK[half]], [Bf["EB"]], partial=True)
                    O.tt("pool", QsT[:], QT[:], EB[:], ALU.mult, [Bf["QT"], Bf["EB"]], [Bf["QsT"]])
                    for h in range(8):
                        bk = 4 + h // 4; col = (h % 4) * 128
                        O.mm(banks[bk][:, col:col + 128], sw[:, h, :], Vt[:, h, :], True, False, [Bf["sw"], Bf["Vt"]], [BK[bk]])
                        O.mm(banks[bk][:, col:col + 128], QsT[:, h, :], CTb[:, h, :], False, True, [Bf["QsT"], Bf["CTb"]], [BK[bk]])
                    for h in range(8):
                        O.mm(banks[7][:, 32 + h:33 + h], sw[:, h, :], onesb[:, 0:1], True, False, [Bf["sw"], Bf["onesb"]], [BK[7]])
                        O.mm(banks[7][:, 32 + h:33 + h], QsT[:, h, :], nb[:, h:h + 1], False, True, [Bf["QsT"], Bf["nb"]], [BK[7]])
                    O.ts("dve", t8, banks[7][:, 32:40], -1.0, 1.0, ALU.mult, ALU.max, [BK[7]], [Bf["t8"]])
                    O.ts("dve", rr, banks[7][:, 32:40], 1.0, None, ALU.max, None, [BK[7]], [Bf["rr"]])
                    O.tt("dve", rr, rr, t8, ALU.max, [Bf["rr"], Bf["t8"]], [Bf["rr"]])
                    T.op("dve", lambda e: e.reciprocal(out=rr, in_=rr), [Bf["rr"]], [Bf["rr"]])
                    for h in range(8):
                        bk = 4 + h // 4; col = (h % 4) * 128
                        T.op("dve", (lambda h=h, bk=bk, col=col: (lambda e: e.bn_stats(out=stats[:, h, :], in_=banks[bk][:, col:col + 128])))(),
                             [BK[bk]], [Bf["stats"]], partial=True)
                    for h in range(8):
                        T.op("dve", (lambda h=h: (lambda e: e.bn_aggr(out=mv[:, h, :], in_=stats[:, h, :])))(), [Bf["stats"]], [Bf["mv"]], partial=True)
                    O.tt("dve", t8, rr, rr, ALU.mult, [Bf["rr"]], [Bf["t8"]])
                    O.tt("dve", t8, t8, mv[:, :, 1], ALU.mult, [Bf["t8"], Bf["mv"]], [Bf["t8"]])
                    O.act(t8, t8, AF.Ln, [Bf["t8"]], [Bf["t8"]], bias=EPS, scale=1.0)
                    O.act(t8, t8, AF.Exp, [Bf["t8"]], [Bf["t8"]], scale=-0.5)
                    O.tt("dve", sc8, t8, rr, ALU.mult, [Bf["t8"], Bf["rr"]], [Bf["sc8"]])
                    for h in range(8):
                        bk = 4 + h // 4; col = (h % 4) * 128
                        O.ts("dve", yn[:, h, :], banks[bk][:, col:col + 128], mv[:, h, 0:1], sc8[:, h:h + 1], ALU.subtract, ALU.mult,
                             [BK[bk], Bf["mv"], Bf["sc8"]], [Bf["yn"]], partial=True)
                    O.tt("pool", flat(yn[:]), flat(yn[:]), mhg[:], ALU.mult, [Bf["yn"], Bf["mhg"]], [Bf["yn"]])
                    O.tt("pool", ym[:], flat(yn[:]), sig[:], ALU.mult, [Bf["yn"], Bf["sig"]], [Bf["ym"]])
                    bk = next_pbank()
                    pb = banks[bk][:].bitcast(BF16)
                    for h in range(8):
                        O.tr(pb[:, h * 128:(h + 1) * 128], ym[:, h * 128:(h + 1) * 128], identb[:], [Bf["ym"], Bf["identb"]], [BK[bk]])
                    O.cp("act", flat(mixT[:, 8:16, :]), pb, [BK[bk]], [Bf["mixT"]], partial=True)
                O.tt("dve", wkt, banks[7][:, 24:32], bc, ALU.subtract, [BK[7], Bf["bc"]], [Bf["wkt"]])
                O.tt("dve", wkt, wkt, gx[:, 0:8], ALU.add, [Bf["wkt"], Bf["gx"]], [Bf["wkt"]])
                O.act(wk, wkt, AF.Exp, [Bf["wkt"]], [Bf["wk"]])
                O.cp("dve", wkb, wk, [Bf["wk"]], [Bf["wkb"]])
                O.act(dec, banks[7][:, 24:32], AF.Exp, [BK[7]], [Bf["dec"]])
                O.tt("pool", wkV[:], Vt[:], wk.unsqueeze(2).to_broadcast([128, 8, 128]), ALU.mult, [Bf["Vt"], Bf["wk"]], [Bf["wkV"]])
                for h in range(8):
                    bk = h // 4; col = (h % 4) * 128
                    O.mm(banks[bk][:, col:col + 128], Ktok[:, h, :], wkV[:, h, :], True, True, [Bf["Ktok"], Bf["wkV"]], [BK[bk]])
                for h in range(8):
                    O.mm(banks[7][:, 40 + h:41 + h], Ktok[:, h, :], wkb[:, h:h + 1], True, True, [Bf["Ktok"], Bf["wkb"]], [BK[7]])
                O.tt("pool", CT[:], CT[:], dec.unsqueeze(2).to_broadcast([128, 8, 128]), ALU.mult, [Bf["CT"], Bf["dec"]], [Bf["CT"]])
                for half in range(2):
                    O.tt("dve", flat(CT[:, half * 4:(half + 1) * 4, :]), flat(CT[:, half * 4:(half + 1) * 4, :]), banks[half][:], ALU.add,
                         [Bf["CT"], BK[half]], [Bf["CT"]], partial=True)
                O.cp("act", CTb[:], CT[:], [Bf["CT"]], [Bf["CTb"]])
                O.tt("dve", nT[:], nT[:], dec, ALU.mult, [Bf["nT"], Bf["dec"]], [Bf["nT"]])
                O.tt("dve", nT[:], nT[:], banks[7][:, 40:48], ALU.add, [Bf["nT"], BK[7]], [Bf["nT"]])
                O.cp("dve", nb[:], nT[:], [Bf["nT"]], [Bf["nb"]])

            own_idx = 0
            cgen = [conv_gen()]
            handoff = [False]
            H0B = [(h0b[:], Bf["h0b"]), (PT[:].rearrange("p h l -> p (h l)").bitcast(BF16), Bf["PT"])]
            H0T = [(h0T[:], Bf["h0T"]), (EB[:].rearrange("p h l -> p (h l)").bitcast(BF16).rearrange("p (k t) -> p k t", k=16), Bf["EB"])]

            def front(i):
                own = i >= NH
                st_ = 0 if own else (i % 2)
                hb_ap, hb_buf = H0B[st_]
                hT_ap, hT_buf = H0T[st_]
                xs = i % 2
                xtile = xt[xs]; xbuf = bufs[f"xt{xs}"]
                O.dma(xtile[:], xin[i * 128:(i + 1) * 128, :], (), [xbuf], xbuf)
                if own:
                    layernorm_tile(xtile[:], [xbuf], h0[:], Bf["h0"], g_in[:], b_in[:], Bf["g_in"], Bf["b_in"])
                    O.dma(h0d[(i - NH) * 128:(i - NH + 1) * 128, :], h0[:], [Bf["h0"]], [B_h0d], Bf["h0"], partial=True)
                    O.cp("act", hb_ap, h0[:], [Bf["h0"]], [hb_buf])
                else:
                    layernorm_tile(xtile[:], [xbuf], hb_ap, hb_buf, g_in[:], b_in[:], Bf["g_in"], Bf["b_in"], tmp=h0[:], tmp_buf=Bf["h0"])
                transpose_2048(hb_ap, hb_buf, hT_ap, hT_buf)

            front(0)
            for i in range(NT):
                own = i >= NH
                last_hist = (i == NH - 1)
                if i == NH:
                    load_const(g_1[:], ln1_g.partition_broadcast(128), "g_1"); load_const(b_1[:], ln1_b.partition_broadcast(128), "b_1")
                if own:
                    front(i)
                elif i + 1 < NH:
                    front(i + 1)
                h0T, h0Tb = H0T[0 if own else (i % 2)]

                for k in range(16):
                    O.mm(banks[7][:, 0:16], h0T[:, k, :], wgate[:, k, :], k == 0, k == 15, [h0Tb, Bf["wgate"]], [BK[7]])
                O.tt("dve", gx, banks[7][:, 0:16], bgate[:], ALU.add, [BK[7], Bf["bgate"]], [Bf["gx"]])
                if not own:
                    O.ts("dve", gx[:, 0:8], gx[:, 0:8], hmask[:, i:i + 1], None, ALU.add, None, [Bf["gx"], Bf["hmask"]], [Bf["gx"]])
                O.act(ef, gx[:, 8:16], AF.Exp, [Bf["gx"]], [Bf["ef"]], scale=-1.0)
                O.act(ef, ef, AF.Ln, [Bf["ef"]], [Bf["ef"]], bias=1.0, scale=1.0)
                O.ts("dve", lf, ef, -1.0, None, ALU.mult, None, [Bf["ef"]], [Bf["lf"]])
                O.mm(banks[7][:, 16:24], tri[:], lf, True, True, [Bf["tri"], Bf["lf"]], [BK[7]])
                O.mm(banks[7][:, 24:32], onesf[:], lf, True, True, [Bf["onesf"], Bf["lf"]], [BK[7]])
                O.cp("dve", bc, banks[7][:, 16:24], [BK[7]], [Bf["bc"]])

                if not own:
                    if i == NH - 1:
                        cgen[0] = drain(cgen[0])
                    else:
                        for _ in range(4):
                            cgen[0] = step(cgen[0])
                for (tg, c0) in plan[i]:
                    if (not own) and tg in ("k", "v"):
                        slot = wbig[:, :, c0 - 4096:c0 - 4096 + 512]; sbuf_ = Bf["wbig"]
                    else:
                        if not handoff[0]:
                            handoff[0] = True
                            for j in range(3):
                                bufs[f"wg{j}"].readers.update(Bf["wbig"].readers)
                                bufs[f"wg{j}"].writers.update(Bf["wbig"].writers)
                        slot, sbuf_ = wstream.next()
                    if tg in ("fC", "fh", "fB", "fq"):
                        bk = next_pbank()
                        for cc in range(4):
                            for k in range(16):
                                O.mm(banks[bk][:, cc * 128:(cc + 1) * 128], slot[:, k, cc * 128:(cc + 1) * 128], h0T[:, k, :],
                                     k == 0, k == 15, [sbuf_, h0Tb], [BK[bk]])
                        pv3 = banks[bk][:].rearrange("p (c t) -> p c t", c=4)
                        if tg == "fC":
                            ch0 = (c0 - 1024) // 128
                            O.cp("act", Cf[:, ch0:ch0 + 4, :], pv3, [BK[bk]], [Bf["Cf"]], partial=True)
                        elif tg == "fh":
                            ch0 = (c0 - 2048) // 128
                            O.tt("dve", zbuf[:, ch0:ch0 + 4, 2:130], pv3, Cf[:, ch0:ch0 + 4, :], ALU.mult, [BK[bk], Bf["Cf"]], [Bf["zbuf"]], partial=True)
                            if last_hist and c0 == 2560:
                                O.ts("pool", zbuf[:, :, 0:2], zbuf[:, :, 128:130], hvalid[:, 0:1], None, ALU.mult, None,
                                     [Bf["zbuf"], Bf["hvalid"]], [Bf["zbuf"]])
                        elif tg == "fB":
                            ch0 = c0 // 128
                            for cc in range(4):
                                c = ch0 + cc
                                O.ts("dve", acc[:], zbuf[:, c, 2:130], cw[:, 2, c:c + 1], cbias[:, c:c + 1], ALU.mult, ALU.add,
                                     [Bf["zbuf"], Bf["cw"], Bf["cbias"]], [Bf["acc"]])
                                O.stt("dve", acc[:], zbuf[:, c, 1:129], cw[:, 1, c:c + 1], acc[:], ALU.mult, ALU.add,
                                      [Bf["zbuf"], Bf["cw"], Bf["acc"]], [Bf["acc"]])
                                O.stt("dve", acc[:], zbuf[:, c, 0:128], cw[:, 0, c:c + 1], acc[:], ALU.mult, ALU.add,
                                      [Bf["zbuf"], Bf["cw"], Bf["acc"]], [Bf["acc"]])
                                O.tt("dve", mixT[:, c, :], banks[bk][:, cc * 128:(cc + 1) * 128], acc[:], ALU.mult, [BK[bk], Bf["acc"]], [Bf["mixT"]], partial=True)
                            if c0 == 512:
                                O.cp("pool", zbuf[:, :, 0:2], zbuf[:, :, 128:130], [Bf["zbuf"]], [Bf["zbuf"]])
                        elif tg == "fq":
                            ch0 = (c0 - 3072) // 128
                            O.cp("act", QT[:, ch0:ch0 + 4, :], pv3, [BK[bk]], [Bf["QT"]], partial=True)
                    elif tg == "k":
                        ch0 = (c0 - 4096) // 128
                        if own:
                            bk = next_pbank()
                            for cc in range(4):
                                for k in range(16):
                                    O.mm(banks[bk][:, cc * 128:(cc + 1) * 128], slot[:, k, cc * 128:(cc + 1) * 128], h0T[:, k, :],
                                         k == 0, k == 15, [sbuf_, h0Tb], [BK[bk]])
                            pv3 = banks[bk][:].rearrange("p (c t) -> p c t", c=4)
                            O.act(KT[:, ch0:ch0 + 4, :], pv3, AF.Copy, [BK[bk]], [Bf["KT"]], scale=KSCALE, partial=True)
                        bk = next_pbank()
                        for k in range(16):
                            O.mm(banks[bk][:], h0T[:, k, :], slot[:, k, :], k == 0, k == 15, [sbuf_, h0Tb], [BK[bk]])
                        O.act(Ktok[:, ch0:ch0 + 4, :].rearrange("p h d -> p (h d)"), banks[bk][:], AF.Copy, [BK[bk]], [Bf["Ktok"]], scale=KSCALE, partial=True)
                    elif tg == "v":
                        ch0 = (c0 - 5120) // 128
                        bk = next_pbank()
                        for k in range(16):
                            O.mm(banks[bk][:], h0T[:, k, :], slot[:, k, :], k == 0, k == 15, [sbuf_, h0Tb], [BK[bk]])
                        O.cp("dve", Vt[:, ch0:ch0 + 4, :].rearrange("p h d -> p (h d)"), banks[bk][:], [BK[bk]], [Bf["Vt"]], partial=True)
                    elif tg == "o":
                        cc0 = c0 - 6144
                        bk = next_pbank()
                        for k in range(16):
                            O.mm(banks[bk][:], h0T[:, k, :], slot[:, k, :], k == 0, k == 15, [sbuf_, h0Tb], [BK[bk]])
                        O.act(sig[:, cc0:cc0 + 512], banks[bk][:], AF.Exp, [BK[bk]], [Bf["sig"]], scale=-1.0, partial=True)
                        O.ts("pool", sig[:, cc0:cc0 + 512], sig[:, cc0:cc0 + 512], 1.0, None, ALU.add, None, [Bf["sig"]], [Bf["sig"]], partial=True)
                        if cc0 == 512:
                            T.op("dve", lambda e: e.reciprocal(out=sig[:], in_=sig[:]), [Bf["sig"]], [Bf["sig"]])
                            mlstm_tile(True, i)
                    elif tg == "wo":
                        bk = next_pbank()
                        for k in range(16):
                            O.mm(banks[bk][:], mixT[:, k, :], slot[:, k, :], k == 0, k == 15, [sbuf_, Bf["mixT"]], [BK[bk]])
                        O.stt("dve", t1[:, c0:c0 + 512], h0[:, c0:c0 + 512], ALPHA, banks[bk][:], ALU.mult, ALU.add,
                              [Bf["h0"], BK[bk]], [Bf["t1"]], partial=True)
                        if c0 == 1536:
                            layernorm_tile(t1[:], [Bf["t1"]], h1[:], Bf["h1"], g_1[:], b_1[:], Bf["g_1"], Bf["b_1"])
                            O.dma(h1d[own_idx * 128:(own_idx + 1) * 128, :], h1[:], [Bf["h1"]], [B_h1d], Bf["h1"], partial=True)
                            if debug:
                                O.dma(dbg_d[own_idx * 128:(own_idx + 1) * 128, :], h1[:], [Bf["h1"]], [], Bf["h1"])
                            O.cp("act", h1b[:], h1[:], [Bf["h1"]], [Bf["h1b"]])
                            transpose_2048(h1b, Bf["h1b"], h1T, Bf["h1T"])
                            O.dma(h1Td[own_idx], h1T[:].rearrange("p k t -> p (k t)"), [Bf["h1T"]], [B_h1Td], Bf["h1T"], partial=True)
                    if tg == "v" and c0 == 5632 and not own:
                        mlstm_tile(False, i)
                if own:
                    own_idx += 1

            final_barrier(T, O, None, bar, banks[7][0:1, 500:501])
            T.replay(outer, st)

        with ExitStack() as st:
            T = Tracker(nc)
            O = Ops(T)
            bufs = {}

            def sb(name, shape, dt):
                t = st.enter_context(nc.sbuf_tensor("sb_" + name, list(shape), dt))
                bufs[name] = Buf(name)
                return t

            NB = NO // 2
            banks = [st.enter_context(nc.psum_tensor(f"pbank{i}", [128, 512], F32)) for i in range(8)]
            BK = [Buf(f"pbank{i}") for i in range(8)]
            identb = sb("identb2", [128, 128], BF16); identf = sb("identf2", [128, 128], F32)
            bar = sb("bar2", [128, 16], F32)
            g_2 = sb("g_2", [128, D], F32); b_2 = sb("b_2", [128, D], F32)
            iota3 = sb("iota3", [128, 8, 128], BF16); iota16 = sb("iota16", [128, 16], F32)
            keysb = sb("keysb", [128, 16, 128], BF16)
            keysT = sb("keysT", [128, 16, 128], BF16)
            Bf = bufs

            def load_const(t, src, name):
                O.dma(t, src, (), [bufs[name]], bufs[name])
            load_const(identb[:], identb_d, "identb2"); load_const(identf[:], identf_d, "identf2")
            load_const(g_2[:], ln2_g.partition_broadcast(128), "g_2"); load_const(b_2[:], ln2_b.partition_broadcast(128), "b_2")
            load_const(iota3[:].rearrange("p t i -> p (t i)"), iota3_d[:, 0:1024], "iota3")
            load_const(iota16[:], iota16_d[:, 0:16], "iota16")

            h1T = [sb(f"h1T2_{i}", [128, 16, 256], BF16) for i in range(2)]
            wslots = [sb(f"wq{i}", [128, 16, 128], BF16) for i in range(2)]
            qT = sb("qT", [128, 16, 128], BF16)
            s2 = sb("s2", [128, 128], F32)
            sv = sb("sv", [128, 16, 16], F32); si = sb("si", [128, 16, 16], U32); sif = sb("sif", [128, 16, 16], F32)
            cand = sb("cand", [128, 8, 256], F32); c2 = sb("c2", [128, 256], F32)
            tv = sb("tv", [128, 8, 16], F32); ci = sb("ci", [128, 8, 16], U32)
            ca = sb("ca", [128, 8, 16], U32); cb_ = sb("cb_", [128, 8, 16], U32)
            caf = sb("caf", [128, 8, 16], F32); cbf = sb("cbf", [128, 8, 16], F32)
            oh = cand[:].rearrange("p h (a b) -> p h a b", a=16); bufs["oh"] = bufs["cand"]
            sel = sb("sel", [128, 3, 128], F32); selT = sb("selT", [128, 3, 256], F32)
            ee = sb("ee", [128, 8, 16], F32); zz = sb("zz", [128, 8], F32)
            Pb = [sb(f"Pb{i}", [128, 8, 128], BF16) for i in range(2)]
            Qb = [sb(f"Qb{i}", [128, 8, 128], BF16) for i in range(2)]
            Wsb = sb("Wsb", [128, 128, 256], BF16)
            WB = [Buf(f"WB{i}") for i in range(64)]
            uslots = [sb(f"us{i}", [128, 16, 128], BF16) for i in range(4)]
            vslots = [sb(f"vs{i}", [128, 2, 1024], BF16) for i in range(3)]
            ga = [sb(f"ga{i}", [128, 512], F32) for i in range(2)]
            t2 = [sb(f"t2_{i}", [128, D], F32) for i in range(2)]
            s_ = sb("s_", [128, 16, 128], F32)
            sm = sb("sm2", [128, 64], F32)

            keysf = t2[0][:].rearrange("p (a n) -> p a n", a=16); bufs["keysf"] = bufs["t2_0"]
            load_const(keysf, keys.rearrange("h p n c -> n (h p) c"), "keysf")
            O.memset("pool", bar[:], 0.0, [Bf["bar2"]])
            O.cp("dve", keysb[:], keysf, [Bf["keysf"]], [Bf["keysb"]])
            for half in range(2):
                bk = 6 + half
                pb = banks[bk][:].bitcast(BF16)
                for kk in range(8):
                    hp = half * 8 + kk
                    O.tr(pb[:, kk * 128:(kk + 1) * 128], keysb[:, hp, :], identb[:], [Bf["keysb"], Bf["identb2"]], [BK[bk]])
                O.cp("dve", keysT[:, half * 8:(half + 1) * 8, :].rearrange("p k t -> p (k t)"), pb, [BK[bk]], [Bf["keysT"]], partial=True)

            def smcol(name, c0, n):
                bufs[name] = Buf(name)
                return sm[:, c0:c0 + n]
            lnmv = smcol("lnmv", 0, 2); lnr = smcol("lnr", 2, 1); lnst = smcol("lnst", 4, 24)

            def layernorm_tile(src, src_bufs, dst, dst_buf, gt, bt, gbuf, bbuf, tmp=None, tmp_buf=None):
                for q in range(4):
                    T.op("dve", (lambda q=q: (lambda e: e.bn_stats(out=lnst[:, q * 6:(q + 1) * 6], in_=src[:, q * 512:(q + 1) * 512])))(),
                         src_bufs, [Bf["lnst"]], partial=(q > 0))
                T.op("dve", lambda e: e.bn_aggr(out=lnmv, in_=lnst), [Bf["lnst"]], [Bf["lnmv"]])
                O.act(lnr, lnmv[:, 1:2], AF.Ln, [Bf["lnmv"]], [Bf["lnr"]], bias=EPS, scale=1.0)
                O.act(lnr, lnr, AF.Exp, [Bf["lnr"]], [Bf["lnr"]], scale=-0.5)
                if tmp is None:
                    tmp, tmp_buf = dst, dst_buf
                O.stt("dve", tmp, src, lnmv[:, 0:1], gt, ALU.subtract, ALU.mult, src_bufs + [Bf["lnmv"], gbuf], [tmp_buf])
                O.stt("dve", dst, tmp, lnr[:, 0:1], bt, ALU.mult, ALU.add, [tmp_buf, Bf["lnr"], bbuf], [dst_buf])

            Wq_v = Wq_b.rearrange("(k p) c -> p k c", p=128)
            qloads = []
            uloads = []
            vloads = []
            for blk in range(NB):
                for tile in range(2):
                    for c0 in range(0, 2048, 128):
                        qloads.append((lambda c0=c0: (lambda slot: [(slot[:, :, :], Wq_v[:, :, c0:c0 + 128])]))())
                for g in range(NG):
                    uloads.append((lambda g=g: (lambda slot: [(slot[:].rearrange("p k j -> p (k j)"), uT_d[g])]))())
                for half in range(2):
                    for gp in range(64):
                        vloads.append((lambda gp=gp, half=half: (lambda slot: [(slot[:, :, :], vbh_d[half][gp * 256:(gp + 1) * 256, :].rearrange("(g j) d -> j g d", j=128))]))())
            qstream = Stream(O, wslots, [bufs[f"wq{i}"] for i in range(2)], qloads)
            ustream = Stream(O, uslots, [bufs[f"us{i}"] for i in range(4)], uloads)
            vstream = Stream(O, vslots, [bufs[f"vs{i}"] for i in range(3)], vloads, eng="pool")

            iota16_b = iota16[:].unsqueeze(1).unsqueeze(1).to_broadcast([128, 8, 16, 16])

            def sel_gen(blk):
                hb = blk % 2
                hT = h1T[hb]; hTb = Bf[f"h1T2_{hb}"]
                for tile in range(2):
                    ti = blk * 2 + tile
                    O.dma(hT[:, :, tile * 128:(tile + 1) * 128], h1Td[ti].rearrange("p (k t) -> p k t", k=16), (), [hTb], hTb, partial=True)
                yield
                for tile in range(2):
                    tsl = slice(tile * 128, (tile + 1) * 128)
                    for grp in range(16):
                        slot, sbuf_ = qstream.next()
                        bk = 6 + (grp % 2)
                        for k in range(16):
                            O.mm(banks[bk][:, 0:128], slot[:, k, :], hT[:, k, tsl], k == 0, k == 15, [sbuf_, hTb], [BK[bk]])
                        O.cp("act", qT[:, grp, :], banks[bk][:, 0:128], [BK[bk]], [Bf["qT"]], partial=True)
                        if grp % 2 == 1:
                            yield
                    for q4 in range(4):
                        bk = 6 + (q4 % 2)
                        for a in range(4):
                            hp = q4 * 4 + a
                            O.mm(banks[bk][:, a * 128:(a + 1) * 128], qT[:, hp, :], keysT[:, hp, :], True, True, [Bf["qT"], Bf["keysT"]], [BK[bk]])
                        O.cp("act", s_[:, q4 * 4:(q4 + 1) * 4, :].rearrange("p a n -> p (a n)"), banks[bk][:], [BK[bk]], [Bf["s_"]], partial=True)
                    yield
                    for hp in range(16):
                        T.op("dve", (lambda hp=hp: (lambda e: e.max(out=sv[:, hp, 0:8], in_=s_[:, hp, :])))(), [Bf["s_"]], [Bf["sv"]], partial=True)
                        T.op("dve", (lambda hp=hp: (lambda e: e.match_replace(out=s2[:], in_to_replace=sv[:, hp, 0:8], in_values=s_[:, hp, :], imm_value=-1e30)))(),
                             [Bf["s_"], Bf["sv"]], [Bf["s2"]])
                        T.op("dve", (lambda hp=hp: (lambda e: e.max(out=sv[:, hp, 8:16], in_=s2[:])))(), [Bf["s2"]], [Bf["sv"]], partial=True)
                        T.op("dve", (lambda hp=hp: (lambda e: e.max_index(out=si[:, hp, 0:8], in_max=sv[:, hp, 0:8], in_values=s_[:, hp, :])))(),
                             [Bf["s_"], Bf["sv"]], [Bf["si"]], partial=True)
                        T.op("dve", (lambda hp=hp: (lambda e: e.max_index(out=si[:, hp, 8:16], in_max=sv[:, hp, 8:16], in_values=s_[:, hp, :])))(),
                             [Bf["s_"], Bf["sv"]], [Bf["si"]], partial=True)
                        yield
                    O.cp("dve", sif[:], si[:], [Bf["si"]], [Bf["sif"]])
                    sv4 = sv[:].rearrange("p (h two) a -> p h two a", two=2)
                    sif4 = sif[:].rearrange("p (h two) a -> p h two a", two=2)
                    O.tt("dve", cand[:].rearrange("p h (a b) -> p h a b", a=16),
                         sv4[:, :, 0, :].unsqueeze(3).to_broadcast([128, 8, 16, 16]),
                         sv4[:, :, 1, :].unsqueeze(2).to_broadcast([128, 8, 16, 16]), ALU.add, [Bf["sv"]], [Bf["cand"]])
                    yield
                    for h in range(8):
                        T.op("dve", (lambda h=h: (lambda e: e.max(out=tv[:, h, 0:8], in_=cand[:, h, :])))(), [Bf["cand"]], [Bf["tv"]], partial=True)
                        T.op("dve", (lambda h=h: (lambda e: e.match_replace(out=c2[:], in_to_replace=tv[:, h, 0:8], in_values=cand[:, h, :], imm_value=-1e30)))(),
                             [Bf["cand"], Bf["tv"]], [Bf["c2"]])
                        T.op("dve", (lambda h=h: (lambda e: e.max(out=tv[:, h, 8:16], in_=c2[:])))(), [Bf["c2"]], [Bf["tv"]], partial=True)
                        T.op("dve", (lambda h=h: (lambda e: e.max_index(out=ci[:, h, 0:8], in_max=tv[:, h, 0:8], in_values=cand[:, h, :])))(),
                             [Bf["cand"], Bf["tv"]], [Bf["ci"]], partial=True)
                        T.op("dve", (lambda h=h: (lambda e: e.max_index(out=ci[:, h, 8:16], in_max=tv[:, h, 8:16], in_values=cand[:, h, :])))(),
                             [Bf["cand"], Bf["tv"]], [Bf["ci"]], partial=True)
                        yield
                    T.op("dve", lambda e: e.tensor_single_scalar(out=ca[:], in_=ci[:], scalar=4, op=ALU.logical_shift_right), [Bf["ci"]], [Bf["ca"]])
                    T.op("dve", lambda e: e.tensor_single_scalar(out=cb_[:], in_=ci[:], scalar=15, op=ALU.bitwise_and), [Bf["ci"]], [Bf["cb_"]])
                    O.cp("dve", caf[:], ca[:], [Bf["ca"]], [Bf["caf"]])
                    O.cp("dve", cbf[:], cb_[:], [Bf["cb_"]], [Bf["cbf"]])
                    yield
                    for which, idxf in ((0, caf), (1, cbf)):
                        O.tt("dve", oh, iota16_b, idxf[:].unsqueeze(3).to_broadcast([128, 8, 16, 16]), ALU.is_equal,
                             [Bf["iota16"], Bf["caf" if which == 0 else "cbf"]], [Bf["oh"]])
                        yield
                        O.tt("pool", oh, oh, sif4[:, :, which, :].unsqueeze(2).to_broadcast([128, 8, 16, 16]), ALU.mult,
                             [Bf["oh"], Bf["sif"]], [Bf["oh"]])
                        T.op("dve", (lambda which=which: (lambda e: e.tensor_reduce(out=sel[:, which, :].rearrange("p (h k) -> p h k", h=8), in_=oh, axis=AX.X, op=ALU.add)))(),
                             [Bf["oh"]], [Bf["sel"]], partial=True)
                        yield
                    O.tt("dve", ee[:], tv[:], tv[:, :, 0:1].to_broadcast([128, 8, 16]), ALU.subtract, [Bf["tv"]], [Bf["ee"]])
                    O.act(ee[:], ee[:], AF.Exp, [Bf["ee"]], [Bf["ee"]])
                    T.op("dve", lambda e: e.tensor_reduce(out=zz[:], in_=ee[:], axis=AX.X, op=ALU.add), [Bf["ee"]], [Bf["zz"]])
                    T.op("dve", lambda e: e.reciprocal(out=zz[:], in_=zz[:]), [Bf["zz"]], [Bf["zz"]])
                    O.tt("dve", sel[:, 2, :].rearrange("p (h k) -> p h k", h=8), ee[:], zz[:].unsqueeze(2).to_broadcast([128, 8, 16]), ALU.mult,
                         [Bf["ee"], Bf["zz"]], [Bf["sel"]], partial=True)
                    yield
                    for w3 in range(3):
                        O.tr(banks[6][:, w3 * 128:(w3 + 1) * 128], sel[:, w3, :], identf[:], [Bf["sel"], Bf["identf2"]], [BK[6]])
                    O.cp("dve", selT[:, :, tsl], banks[6][:, 0:384].rearrange("p (w t) -> p w t", w=3), [BK[6]], [Bf["selT"]], partial=True)
                    yield

            def expand(blk):
                wb_rot = 0
                for tb in range(32):
                    pq = tb % 2
                    tk0 = tb * 8
                    i2b = selT[:, 1, tk0:tk0 + 8].unsqueeze(2).to_broadcast([128, 8, 128])
                    O.tt("dve", Qb[pq][:], iota3[:], i2b, ALU.is_equal, [Bf["iota3"], Bf["selT"]], [Bf[f"Qb{pq}"]])
                    for tl in range(8):
                        tk = tk0 + tl
                        O.ts("dve", Pb[pq][:, tl, :], iota3[:, 0, :], selT[:, 0, tk:tk + 1], selT[:, 2, tk:tk + 1], ALU.is_equal, ALU.mult,
                             [Bf["iota3"], Bf["selT"]], [Bf[f"Pb{pq}"]], partial=(tl > 0))
                    for t4 in range(2):
                        bk = 4 + (wb_rot % 4)
                        wb_rot += 1
                        for tt_ in range(4):
                            tl = t4 * 4 + tt_
                            O.mm(banks[bk][:, tt_ * 128:(tt_ + 1) * 128], Qb[pq][:, tl, :], Pb[pq][:, tl, :], True, True,
                                 [Bf[f"Qb{pq}"], Bf[f"Pb{pq}"]], [BK[bk]])
                        t0 = tk0 + t4 * 4
                        O.cp("act", Wsb[:, :, t0:t0 + 4].rearrange("j g t -> j t g"), banks[bk][:].rearrange("p (t g) -> p t g", t=4),
                             [BK[bk]], WB, partial=True)

            def step(gen):
                if gen is not None:
                    try:
                        next(gen)
                    except StopIteration:
                        return None
                return gen

            def drain(gen):
                while gen is not None:
                    gen = step(gen)

            drain(sel_gen(0))
            expand(0)
            for blk in range(NB):
                hb = blk % 2
                hT = h1T[hb]; hTb = Bf[f"h1T2_{hb}"]
                gen = sel_gen(blk + 1) if blk + 1 < NB else None
                for tile in range(2):
                    ti = blk * 2 + tile
                    O.dma(t2[tile][:], h1d[ti * 128:(ti + 1) * 128, :], (), [Bf[f"t2_{tile}"]], Bf[f"t2_{tile}"])

                def A_step(gp):
                    bkA = 4 + (gp % 2)
                    for gi in range(2):
                        us, ub = ustream.next()
                        for k in range(16):
                            O.mm(banks[bkA][:, gi * 256:(gi + 1) * 256], us[:, k, :], hT[:, k, :], k == 0, k == 15, [ub, hTb], [BK[bkA]])

                def V_step(gp, first, last):
                    vs, vbuf = vstream.next()
                    for gi in range(2):
                        g = gp * 2 + gi
                        for tile in range(2):
                            for dq in range(2):
                                O.mm(banks[tile * 2 + dq][:], Wsb[:, g, tile * 128:(tile + 1) * 128], vs[:, gi, dq * 512:(dq + 1) * 512],
                                     first and gi == 0, last and gi == 1, [WB[gp], vbuf], [BK[tile * 2 + dq]])

                A_step(0)
                for gp in range(64):
                    if gp + 1 < 64:
                        A_step(gp + 1)
                    bkA = 4 + (gp % 2)
                    sl = gp % 2
                    O.act(ga[sl][:], banks[bkA][:], AF.Gelu, [BK[bkA]], [Bf[f"ga{sl}"]])
                    wv = Wsb[:, gp * 2:(gp + 1) * 2, :].rearrange("p g t -> p (g t)")
                    O.tt("dve", wv, ga[sl][:], wv, ALU.mult, [Bf[f"ga{sl}"], WB[gp]], [WB[gp]])
                    V_step(gp, gp == 0, gp == 63)
                    gen = step(gen)
                for tile in range(2):
                    for dq in range(2):
                        cs = slice(dq * 512, (dq + 1) * 512)
                        O.stt("dve", t2[tile][:, cs], t2[tile][:, cs], ALPHA, banks[tile * 2 + dq][:], ALU.mult, ALU.add,
                              [Bf[f"t2_{tile}"], BK[tile * 2 + dq]], [Bf[f"t2_{tile}"]], partial=True)
                for gp in range(64):
                    V_step(gp, gp == 0, gp == 63)
                    gen = step(gen)
                drain(gen)
                for tile in range(2):
                    ti = blk * 2 + tile
                    for dq in range(2):
                        cs = slice(1024 + dq * 512, 1024 + (dq + 1) * 512)
                        O.stt("dve", t2[tile][:, cs], t2[tile][:, cs], ALPHA, banks[tile * 2 + dq][:], ALU.mult, ALU.add,
                              [Bf[f"t2_{tile}"], BK[tile * 2 + dq]], [Bf[f"t2_{tile}"]], partial=True)
                    layernorm_tile(t2[tile][:], [Bf[f"t2_{tile}"]], t2[tile][:], Bf[f"t2_{tile}"], g_2[:], b_2[:], Bf["g_2"], Bf["b_2"])
                    O.dma(out_d[ti * 128:(ti + 1) * 128, :], t2[tile][:], [Bf[f"t2_{tile}"]], [], Bf[f"t2_{tile}"])
                if blk + 1 < NB:
                    expand(blk + 1)
            final_barrier(T, O, None, bar, banks[7][0:1, 500:501])
            T.replay(outer, st)

    return nc


NH_FULL = 96
NO_FULL = 32


def _consts():
    bf = ml_dtypes.bfloat16
    c = {}
    c["identb"] = np.eye(128, dtype=np.float32).astype(bf)
    c["identf"] = np.eye(128, dtype=np.float32)
    c["tri"] = np.triu(np.ones((128, 128), dtype=np.float32))
    c["onesf"] = np.ones((128, 128), dtype=np.float32)
    c["iota3"] = np.broadcast_to(np.arange(128, dtype=np.float32)[None, None, :], (128, 16, 128)).reshape(128, 16 * 128).astype(bf)
    c["iota16"] = np.ascontiguousarray(np.broadcast_to(np.arange(16, dtype=np.float32)[None, None, None, :], (128, 8, 16, 16)).reshape(128, 2048))
    return c


def _weights(inp):
    f = lambda a: np.ascontiguousarray(np.asarray(a, dtype=np.float32))
    return {
        "ln_in_g": f(inp["ln_in_g"]), "ln_in_b": f(inp["ln_in_b"]), "w_in": f(inp["w_in"][0]), "b_gate": f(inp["b_gate"][0]),
        "conv_w": f(np.asarray(inp["conv_w"][0]).reshape(3, 8, 128).transpose(2, 0, 1).reshape(128, 24)), "conv_b": f(np.asarray(inp["conv_b"][0]).reshape(8, 128).T), "mh_norm_g": f(inp["mh_norm_g"][0]), "w_out": f(inp["w_out"][0]),
        "ln1_g": f(inp["ln1_g"][0]), "ln1_b": f(inp["ln1_b"][0]), "peer_wq": f(inp["peer_wq"][0]), "peer_keys": f(inp["peer_keys"][0]),
        "peer_u": f(inp["peer_u"][0]), "peer_v": f(inp["peer_v"][0]), "ln2_g": f(inp["ln2_g"][0]), "ln2_b": f(inp["ln2_b"][0]),
    }


def core_inputs(x_seq, start, n_own_tok, NH, common):
    hist_tok = NH * 128
    xin = np.zeros((hist_tok + n_own_tok, D), dtype=np.float32)
    real = min(start, hist_tok)
    if real > 0:
        xin[hist_tok - real:hist_tok] = x_seq[start - real:start]
    xin[hist_tok:] = x_seq[start:start + n_own_tok]
    hm = np.zeros((128, max(NH, 1)), dtype=np.float32)
    ndummy = (hist_tok - real) // 128
    hm[:, :ndummy] = NEG
    m = dict(common)
    m["xin"] = xin
    m["hmask"] = hm
    m["hvalid"] = np.full((128, 1), 1.0 if real > 0 else 0.0, dtype=np.float32)
    return m


_NC_CACHE = {}


def kernel(**inputs):
    x = np.asarray(inputs["x"], dtype=np.float32)
    Bn, S, _ = x.shape
    common = _weights(inputs)
    common.update(_consts())
    ncores = 8
    per = (Bn * S) // ncores
    segs = S // per
    NO = per // 128
    NH = (segs - 1) * NO
    key = (NH, NO)
    if key not in _NC_CACHE:
        _NC_CACHE[key] = build(NH, NO)
    nc = _NC_CACHE[key]
    in_maps = []
    for c in range(ncores):
        b, sg = divmod(c, segs)
        in_maps.append(core_inputs(x[b], sg * per, per, NH, common))
    res = run_bass_kernel_spmd(nc, in_maps, core_ids=list(range(ncores)))
    out = np.empty((Bn, S, D), dtype=np.float32)
    for c in range(ncores):
        b, sg = divmod(c, segs)
        out[b, sg * per:(sg + 1) * per] = res.results[c]["out"]
    return out
```

```python
import numpy as np
import concourse.bass as bass
import concourse.mybir as mybir
from concourse.bass_utils import run_bass_kernel_spmd

F32 = mybir.dt.float32
BF16 = mybir.dt.bfloat16
U32 = mybir.dt.uint32
ALU = mybir.AluOpType
AF = mybir.ActivationFunctionType
AX = mybir.AxisListType

SEM_WINDOW = 16384


class Buf:
    __slots__ = ("name", "writers", "readers", "dsem", "dcount")

    def __init__(self, name):
        self.name = name
        self.writers = {}
        self.readers = {}
        self.dsem = None
        self.dcount = 0


class Op:
    __slots__ = ("eng", "idx", "fn", "waits", "sig", "dma_ev")

    def __init__(self, eng, idx, fn):
        self.eng = eng
        self.idx = idx
        self.fn = fn
        self.waits = []
        self.sig = False
        self.dma_ev = None


class Tracker:
    ENGS = ("sync", "act", "dve", "pool", "pe")

    _uid = [0]

    def __init__(self, nc):
        self.nc = nc
        Tracker._uid[0] += 1
        self.uid = Tracker._uid[0]
        self.ops = {e: [] for e in self.ENGS}
        self.ndsem = 0
        self.dsem_total = {}

    def _dep(self, op, key, val):
        if key[0] == 'e':
            eng = key[1]
            if eng == op.eng and eng == "pe":
                return
            prod = self.ops[eng][val]
            prod.sig = True
            op.waits.append(('e', eng, val))
        else:
            op.waits.append(('d', key[1], self.dsem_total[key[1]]))

    def op(self, eng, fn, reads=(), writes=(), partial=False, dma_buf=None):
        lst = self.ops[eng]
        o = Op(eng, len(lst), fn)
        for b in reads:
            for k, v in b.writers.items():
                self._dep(o, k, v)
        for b in writes:
            for k, v in b.readers.items():
                self._dep(o, k, v)
            if not partial:
                for k, v in b.writers.items():
                    self._dep(o, k, v)
        if dma_buf is not None:
            if dma_buf.dsem is None:
                dma_buf.dsem = self.ndsem
                self.ndsem += 1
            dma_buf.dcount += 16
            o.dma_ev = (dma_buf.dsem, dma_buf.dcount)
            self.dsem_total[dma_buf.dsem] = dma_buf.dcount
            key, val = ('d', dma_buf.dsem), dma_buf.dcount
        else:
            key, val = ('e', eng), o.idx
        for b in reads:
            b.readers[key] = val
        for b in writes:
            if not partial:
                b.writers = {}
            b.readers = {}
            b.writers[key] = val
        lst.append(o)
        return o

    def replay(self, stack, bstack=None):
        nc = self.nc
        bstack = bstack or stack
        esems = {}
        for e in self.ENGS:
            nsig = sum(1 for o in self.ops[e] if o.sig and o.dma_ev is None)
            nwin = max(1, (nsig + SEM_WINDOW - 1) // SEM_WINDOW)
            esems[e] = [stack.enter_context(nc.semaphore(f"e{self.uid}_{e}_{i}")) for i in range(nwin)]
        dsems = {}
        for d in range(self.ndsem):
            dsems[d] = {}
        self._stack = stack
        sigcnt = {}
        for e in self.ENGS:
            c = 0
            arr = []
            for o in self.ops[e]:
                if o.sig and o.dma_ev is None:
                    c += 1
                arr.append(c)
            sigcnt[e] = arr

        def dsem_handle(d, w):
            if w not in dsems[d]:
                dsems[d][w] = stack.enter_context(nc.semaphore(f"d{self.uid}_{d}_{w}"))
            return dsems[d][w]

        DW = SEM_WINDOW * 2
        engh = {"sync": nc.sync, "act": nc.scalar, "dve": nc.vector, "pool": nc.gpsimd, "pe": nc.tensor}

        def run(e, eh):
            waited = {}
            for o in self.ops[e]:
                need = {}
                for w in o.waits:
                    if w[0] == 'e':
                        c = sigcnt[w[1]][w[2]]
                        k = ('e', w[1])
                    else:
                        c = w[2]
                        k = ('d', w[1])
                    if c > need.get(k, 0):
                        need[k] = c
                for k, c in need.items():
                    if waited.get(k, 0) >= c:
                        continue
                    waited[k] = c
                    if k[0] == 'e':
                        win = (c - 1) // SEM_WINDOW
                        eh.wait_ge(esems[k[1]][win], (c - 1) % SEM_WINDOW + 1)
                    else:
                        win = (c - 16) // DW
                        eh.wait_ge(dsem_handle(k[1], win), (c - 16) % DW + 16)
                if o.fn is None:
                    continue
                ins = o.fn(eh)
                if o.dma_ev is not None:
                    d, c = o.dma_ev
                    win = (c - 16) // DW
                    ins.then_inc(dsem_handle(d, win), 16)
                elif o.sig:
                    c = sigcnt[e][o.idx]
                    win = (c - 1) // SEM_WINDOW
                    ins.then_inc(esems[e][win], 1)

        block = bstack.enter_context(nc.Block())

        @block.sync
        def _(eh):
            run("sync", eh)

        @block.scalar
        def _(eh):
            run("act", eh)

        @block.vector
        def _(eh):
            run("dve", eh)

        @block.gpsimd
        def _(eh):
            run("pool", eh)

        @block.tensor
        def _(eh):
            run("pe", eh)

import math
from contextlib import ExitStack
import ml_dtypes

D = 2048
DIN = 7184
NG = 128
ALPHA = 2.0 ** 0.25
EPS = 1e-5
DH = 128
KSCALE = DH ** -0.5
NEG = -30000.0


class Ops:
    def __init__(self, T):
        self.T = T

    def mm(self, out, lhsT, rhs, start, stop, r, w):
        self.T.op("pe", lambda e: e.matmul(out, lhsT=lhsT, rhs=rhs, start=start, stop=stop), r, w, partial=True)

    def tr(self, out, in_, ident, r, w):
        self.T.op("pe", lambda e: e.transpose(out=out, in_=in_, identity=ident), r, w, partial=True)

    def act(self, out, in_, func, r, w, bias=0.0, scale=1.0, partial=False):
        self.T.op("act", lambda e: e.activation(out=out, in_=in_, func=func, bias=bias, scale=scale), r, w, partial=partial)

    def tt(self, eng, out, in0, in1, op, r, w, partial=False):
        self.T.op(eng, lambda e: e.tensor_tensor(out=out, in0=in0, in1=in1, op=op), r, w, partial=partial)

    def ts(self, eng, out, in0, s1, s2, op0, op1, r, w, partial=False):
        if s2 is None:
            self.T.op(eng, lambda e: e.tensor_scalar(out=out, in0=in0, scalar1=s1, scalar2=None, op0=op0), r, w, partial=partial)
        else:
            self.T.op(eng, lambda e: e.tensor_scalar(out=out, in0=in0, scalar1=s1, scalar2=s2, op0=op0, op1=op1), r, w, partial=partial)

    def stt(self, eng, out, in0, scalar, in1, op0, op1, r, w, partial=False):
        self.T.op(eng, lambda e: e.scalar_tensor_tensor(out=out, in0=in0, scalar=scalar, in1=in1, op0=op0, op1=op1), r, w, partial=partial)

    def cp(self, eng, out, in_, r, w, partial=False):
        if eng == "act":
            self.T.op("act", lambda e: e.copy(out=out, in_=in_), r, w, partial=partial)
        else:
            self.T.op(eng, lambda e: e.tensor_copy(out=out, in_=in_), r, w, partial=partial)

    def dma(self, out, in_, r, w, buf, partial=False, eng="sync"):
        self.T.op(eng, lambda e: e.dma_start(out=out, in_=in_), r, w, partial=partial, dma_buf=buf)

    def memset(self, eng, ap, val, w):
        self.T.op(eng, lambda e: e.memset(ap, val), (), w)


class Stream:
    def __init__(self, O, slots, bufs, loads, depth=None, eng="sync"):
        self.O = O
        self.eng = eng
        self.slots = slots
        self.bufs = bufs
        self.loads = loads
        self.n = len(slots)
        self.depth = depth or (self.n - 1)
        self.issued = 0
        self.consumed = 0

    def _issue(self):
        if self.issued >= len(self.loads):
            return
        k = self.issued
        s = k % self.n
        for (o, i) in self.loads[k](self.slots[s]):
            self.O.dma(o, i, (), [self.bufs[s]], self.bufs[s], partial=True, eng=self.eng)
        self.issued += 1

    def next(self):
        while self.issued < len(self.loads) and self.issued < self.consumed + self.depth:
            self._issue()
        k = self.consumed
        self.consumed += 1
        s = k % self.n
        return self.slots[s], self.bufs[s]


def final_barrier(T, O, scratch, sbuf_bar, psum_bar):
    bars = {}
    for e in ("act", "dve", "pool"):
        b = Buf("bar_" + e)
        bars[e] = b
        col = {"act": 0, "dve": 1, "pool": 2}[e]
        O.memset(e, sbuf_bar[:, col:col + 1], 0.0, [b]) if e != "act" else T.op(
            "act", lambda en: en.copy(out=sbuf_bar[:, 0:1], in_=sbuf_bar[:, 4:5]), (), [b])
    bpe = Buf("bar_pe")
    bars["pe"] = bpe
    T.op("pe", lambda en: en.matmul(psum_bar, lhsT=sbuf_bar[:, 8:9].bitcast(F32), rhs=sbuf_bar[:, 8:9].bitcast(F32), start=True, stop=True), (), [bpe], partial=True)
    allb = list(bars.values())
    for e in ("sync", "act", "dve", "pool", "pe"):
        o = T.op(e, None, reads=allb)
        for d, tot in T.dsem_total.items():
            o.waits.append(('d', d, tot))


def build(NH, NO, debug=False):
    NT = NH + NO
    nc = bass.Bass("TRN2", target_bir_lowering=False)

    def din(name, shape, dt=F32):
        return nc.dram_tensor(name, list(shape), dt, kind="ExternalInput").ap()

    xin = din("xin", [NT * 128, D])
    hmask_d = din("hmask", [128, max(NH, 1)])
    hvalid_d = din("hvalid", [128, 1])
    ln_in_g = din("ln_in_g", [D]); ln_in_b = din("ln_in_b", [D])
    w_in = din("w_in", [D, DIN]); b_gate = din("b_gate", [16])
    conv_w = din("conv_w", [128, 24]); conv_b = din("conv_b", [128, 8])
    mh_g = din("mh_norm_g", [1024]); w_out = din("w_out", [D, D])
    ln1_g = din("ln1_g", [D]); ln1_b = din("ln1_b", [D])
    wq = din("peer_wq", [D, D]); keys = din("peer_keys", [8, 2, 128, 128])
    pu = din("peer_u", [16384, D]); pv = din("peer_v", [16384, D])
    ln2_g = din("ln2_g", [D]); ln2_b = din("ln2_b", [D])
    identb_d = din("identb", [128, 128], BF16); identf_d = din("identf", [128, 128])
    tri_d = din("tri", [128, 128]); onesf_d = din("onesf", [128, 128])
    iota3_d = din("iota3", [128, 16 * 128], BF16); iota16_d = din("iota16", [128, 8 * 16 * 16])
    out_d = nc.dram_tensor("out", [NO * 128, D], F32, kind="ExternalOutput").ap()
    dbg_d = nc.dram_tensor("dbg", [NO * 128, D], F32, kind="ExternalOutput").ap() if debug else None

    def dscr(name, shape, dt):
        return nc.dram_tensor(name, list(shape), dt).ap()

    Wi_b = dscr("Wi_b", [14, 128, 16, 512], BF16); Wo_b = dscr("Wo_b", [4, 128, 16, 512], BF16); Wq_b = dscr("Wq_b", [16, 128, 16, 128], BF16)
    vbh_d = dscr("vbh_d", [2, 16384, 1024], BF16); uT_d = dscr("uT_d", [NG, 128, 16 * 128], BF16)
    h0d = dscr("h0d", [NO * 128, D], F32); h1d = dscr("h1d", [NO * 128, D], F32)
    h1Td = dscr("h1Td", [NO, 128, 16 * 128], BF16)
    B_Wi = Buf("Wi_b"); B_Wo = Buf("Wo_b"); B_Wq = Buf("Wq_b"); B_vb = Buf("vb_d"); B_uT = Buf("uT_d")
    B_h0d = Buf("h0d"); B_h1d = Buf("h1d"); B_h1Td = Buf("h1Td")

    with ExitStack() as outer:
        with ExitStack() as st:
            T = Tracker(nc)
            O = Ops(T)
            bufs = {}

            def sb(name, shape, dt):
                t = st.enter_context(nc.sbuf_tensor("sb_" + name, list(shape), dt))
                bufs[name] = Buf(name)
                return t

            banks = [st.enter_context(nc.psum_tensor(f"bank{i}", [128, 512], F32)) for i in range(8)]
            BK = [Buf(f"bank{i}") for i in range(8)]

            identb = sb("identb", [128, 128], BF16); identf = sb("identf", [128, 128], F32)
            tri = sb("tri", [128, 128], F32); onesf = sb("onesf", [128, 128], F32)
            onesb = sb("onesb", [128, 8], BF16)
            bar = sb("bar", [128, 16], F32)
            hmask = sb("hmask", [128, max(NH, 1)], F32); hvalid = sb("hvalid", [128, 1], F32)
            g_in = sb("g_in", [128, D], F32); b_in = sb("b_in", [128, D], F32)
            g_1 = sb("g_1", [128, D], F32); b_1 = sb("b_1", [128, D], F32)
            mhg = sb("mhg", [128, 1024], F32); bgate = sb("bgate", [128, 16], F32)
            cw = sb("cw", [128, 3, 8], F32); cbias = sb("cbias", [128, 8], F32)
            wgate = sb("wgate", [128, 16, 16], BF16)
            wgate_f = sb("wgate_f", [128, 16, 16], F32)

            def load_const(t, src, name):
                O.dma(t, src, (), [bufs[name]], bufs[name])

            load_const(identb[:], identb_d, "identb"); load_const(identf[:], identf_d, "identf")
            load_const(tri[:], tri_d, "tri"); load_const(onesf[:], onesf_d, "onesf")
            load_const(hmask[:], hmask_d, "hmask"); load_const(hvalid[:], hvalid_d, "hvalid")
            load_const(g_in[:], ln_in_g.partition_broadcast(128), "g_in"); load_const(b_in[:], ln_in_b.partition_broadcast(128), "b_in")
            load_const(mhg[:], mh_g.partition_broadcast(128), "mhg"); load_const(bgate[:], b_gate.partition_broadcast(128), "bgate")
            load_const(cw[:].rearrange("p j c -> p (j c)"), conv_w, "cw")
            load_const(cbias[:], conv_b, "cbias")
            load_const(wgate_f[:], w_in.rearrange("(k p) c -> p k c", p=128)[:, :, 7168:7184], "wgate_f")
            O.cp("dve", wgate[:], wgate_f[:], [bufs["wgate_f"]], [bufs["wgate"]])
            O.memset("pool", onesb[:], 1.0, [bufs["onesb"]])
            O.memset("pool", bar[:], 0.0, [bufs["bar"]])

            cin = [sb(f"cin{i}", [128, 2048], F32) for i in range(2)]
            cout = [sb(f"cout{i}", [128, 2048], BF16) for i in range(2)]
            cast_engs = ["dve", "act", "dve"]
            cnt = [0]

            xt = cin
            bufs["xt0"] = bufs["cin0"]; bufs["xt1"] = bufs["cin1"]
            h0 = sb("h0", [128, D], F32); h0b = cout[1]; bufs["h0b"] = bufs["cout1"]
            h0T = sb("h0T", [128, 16, 128], BF16)
            wbig = sb("wbig", [128, 16, 2048], BF16)
            wslots = [wbig[:, :, i * 512:(i + 1) * 512] for i in range(3)]
            for i in range(3):
                bufs[f"wg{i}"] = Buf(f"wg{i}")
            Ktok = sb("Ktok", [128, 8, 128], BF16); Vt = sb("Vt", [128, 8, 128], BF16)
            sig = sb("sig", [128, 1024], F32)
            Cf = sb("Cf", [128, 8, 128], F32); zbuf = sb("zbuf", [128, 8, 130], F32); acc = sb("acc", [128, 128], F32)
            mixT = sb("mixT", [128, 16, 128], BF16)
            QT = sb("QT", [128, 8, 128], BF16); KT = sb("KT", [128, 8, 128], BF16)
            PT = sb("PT", [128, 8, 128], F32); sw = sb("sw", [128, 8, 128], BF16)
            EB = sb("EB", [128, 8, 128], F32); QsT = sb("QsT", [128, 8, 128], BF16)
            TriLF = sb("TriLF", [128, 8, 128], F32)
            wkV = sb("wkV", [128, 8, 128], BF16)
            yn = sb("yn", [128, 8, 128], F32); ym = sb("ym", [128, 1024], BF16)
            CT = sb("CT", [128, 8, 128], F32); CTb = sb("CTb", [128, 8, 128], BF16)
            nT = sb("nT", [128, 8], F32); nb = sb("nb", [128, 8], BF16)
            t1 = sb("t1", [128, D], F32); h1 = t1; bufs["h1"] = bufs["t1"]; h1b = cout[0]; bufs["h1b"] = bufs["cout0"]
            h1T = sb("h1T", [128, 16, 128], BF16)
            sm = sb("sm", [128, 256], F32)
            smb = sb("smb", [128, 16], BF16)
            stats = sb("stats", [128, 8, 6], F32); mv = sb("mv", [128, 8, 2], F32)
            Bf = bufs
            def smcol(name, c0, n):
                bufs[name] = Buf(name)
                return sm[:, c0:c0 + n]
            gx = smcol("gx", 0, 16); ef = smcol("ef", 16, 8); lf = smcol("lf", 24, 8)
            bc = smcol("bc", 32, 8); wkt = smcol("wkt", 40, 8); wk = smcol("wk", 48, 8)
            bias8 = smcol("bias8", 56, 8); dec = smcol("dec", 64, 8); rr = smcol("rr", 72, 8)
            t8 = smcol("t8", 80, 8); sc8 = smcol("sc8", 88, 8); lnmv = smcol("lnmv", 96, 2)
            lnr = smcol("lnr", 98, 1); lnst = smcol("lnst", 100, 24)
            bufs["wkb"] = Buf("wkb")
            wkb = smb[:, 0:8]


            assert NH >= 1
            ccin = [t1, g_1]; ccinb = [Bf["t1"], Bf["g_1"]]
            ccoutb = [Bf["cout0"], Bf["mixT"]]

            def ccout_ap(s):
                return cout[0][:] if s == 0 else mixT[:].rearrange("p k t -> p (k t)")

            for k in range(16):
                s = k % 2
                O.dma(ccin[s][:], w_in[k * 128:(k + 1) * 128, 4096:6144], (), [ccinb[s]], ccinb[s])
                O.cp(cast_engs[k % 3], wbig[:, k, :], ccin[s][:], [ccinb[s]], [Bf["wbig"]], partial=True)

            def conv_gen():
                n = 0
                for (src, dst, R, C, dbuf) in ((w_in, Wi_b, D, DIN, B_Wi), (w_out, Wo_b, D, D, B_Wo), (wq, Wq_b, D, D, B_Wq)):
                    for r in range(R // 128):
                        for c0 in range(0, C, 2048):
                            cwid = min(2048, C - c0)
                            s = n % 2; eng = cast_engs[n % 3]; n += 1
                            O.dma(ccin[s][:, 0:cwid], src[r * 128:(r + 1) * 128, c0:c0 + cwid], (), [ccinb[s]], ccinb[s])
                            O.cp(eng, ccout_ap(s)[:, 0:cwid], ccin[s][:, 0:cwid], [ccinb[s]], [ccoutb[s]])
                            gw = 128 if dst is Wq_b else 512
                            ng = cwid // gw
                            g0 = c0 // gw
                            O.dma(dst[g0:g0 + ng, :, r, :].rearrange("g p c -> p g c"),
                                  ccout_ap(s)[:, 0:ng * gw].rearrange("p (g c) -> p g c", c=gw), [ccoutb[s]], [dbuf], ccoutb[s], partial=True)
                            yield
                for r in range(128):
                    s = n % 2; eng = cast_engs[n % 3]; n += 1
                    O.dma(ccin[s][:], pv[r * 128:(r + 1) * 128, :], (), [ccinb[s]], ccinb[s])
                    O.cp(eng, ccout_ap(s), ccin[s][:], [ccinb[s]], [ccoutb[s]])
                    for hh in range(2):
                        O.dma(vbh_d[hh][r * 128:(r + 1) * 128, :], ccout_ap(s)[:, hh * 1024:(hh + 1) * 1024], [ccoutb[s]], [B_vb], ccoutb[s], partial=True)
                    yield
                for g in range(NG):
                    s = n % 2; n += 1
                    O.dma(ccin[s][:], pu[g * 128:(g + 1) * 128, :], (), [ccinb[s]], ccinb[s])
                    for q4 in range(4):
                        bk = 2 + (q4 % 2)
                        for kk in range(4):
                            k = q4 * 4 + kk
                            O.tr(banks[bk][:, kk * 128:(kk + 1) * 128], ccin[s][:, k * 128:(k + 1) * 128], identf[:],
                                 [ccinb[s], bufs["identf"]], [BK[bk]])
                        eng = ["dve", "act"][q4 % 2]
                        O.cp(eng, ccout_ap(s)[:, q4 * 512:(q4 + 1) * 512], banks[bk][:], [BK[bk]], [ccoutb[s]], partial=True)
                    O.dma(uT_d[g], ccout_ap(s), [ccoutb[s]], [B_uT], ccoutb[s], partial=True)
                    yield

            def step(gen):
                if gen is not None:
                    try:
                        next(gen)
                    except StopIteration:
                        return None
                return gen

            def drain(gen):
                while gen is not None:
                    gen = step(gen)
                return None

            O.memset("pool", CT[:], 0.0, [Bf["CT"]]); O.memset("pool", CTb[:], 0.0, [Bf["CTb"]])
            O.memset("pool", nT[:], 0.0, [Bf["nT"]]); O.memset("pool", nb[:], 0.0, [Bf["nb"]])
            O.memset("pool", zbuf[:], 0.0, [Bf["zbuf"]])

            Wi_v = Wi_b
            Wo_v = Wo_b
            loads = []
            plan = []

            def wload(view, c0, srcbuf):
                def f(slot):
                    return [(slot[:, :, :], view[c0 // 512])]
                return f

            for i in range(NT):
                own = i >= NH
                tags = []
                if own:
                    for c0 in (1024, 1536):
                        tags.append(("fC", c0))
                    for c0 in (2048, 2560):
                        tags.append(("fh", c0))
                    for c0 in (0, 512):
                        tags.append(("fB", c0))
                    for c0 in (3072, 3584):
                        tags.append(("fq", c0))
                for c0 in (4096, 4608):
                    tags.append(("k", c0))
                for c0 in (5120, 5632):
                    tags.append(("v", c0))
                if i == NH - 1:
                    for c0 in (1024, 1536):
                        tags.append(("fC", c0))
                    for c0 in (2048, 2560):
                        tags.append(("fh", c0))
                if own:
                    for c0 in (6144, 6656):
                        tags.append(("o", c0))
                    for c0 in (0, 512, 1024, 1536):
                        tags.append(("wo", c0))
                plan.append(tags)
                for (tg, c0) in tags:
                    if (not own) and tg in ("k", "v"):
                        continue
                    loads.append(wload(Wo_v if tg == "wo" else Wi_v, c0, None))
            wstream = Stream(O, wslots, [bufs[f"wg{i}"] for i in range(3)], loads)
            for i in range(3):
                pass
            orig_issue = wstream._issue

            def issue_with_deps():
                if wstream.issued >= len(wstream.loads):
                    return
                k = wstream.issued
                s = k % wstream.n
                for (o, i_) in wstream.loads[k](wstream.slots[s]):
                    O.dma(o, i_, [B_Wi, B_Wo], [wstream.bufs[s]], wstream.bufs[s], partial=True)
                wstream.issued += 1
            wstream._issue = issue_with_deps

            proj_banks = [6, 4, 5]
            pcount = [0]

            def next_pbank():
                b = proj_banks[pcount[0] % 3]
                pcount[0] += 1
                return b

            def layernorm_tile(src, src_bufs, dst, dst_buf, gt, bt, gbuf, bbuf, tmp=None, tmp_buf=None):
                for q in range(4):
                    T.op("dve", (lambda q=q: (lambda e: e.bn_stats(out=lnst[:, q * 6:(q + 1) * 6], in_=src[:, q * 512:(q + 1) * 512])))(),
                         src_bufs, [Bf["lnst"]], partial=(q > 0))
                T.op("dve", lambda e: e.bn_aggr(out=lnmv, in_=lnst), [Bf["lnst"]], [Bf["lnmv"]])
                O.act(lnr, lnmv[:, 1:2], AF.Ln, [Bf["lnmv"]], [Bf["lnr"]], bias=EPS, scale=1.0)
                O.act(lnr, lnr, AF.Exp, [Bf["lnr"]], [Bf["lnr"]], scale=-0.5)
                if tmp is None:
                    tmp, tmp_buf = dst, dst_buf
                O.stt("dve", tmp, src, lnmv[:, 0:1], gt, ALU.subtract, ALU.mult, src_bufs + [Bf["lnmv"], gbuf], [tmp_buf])
                O.stt("dve", dst, tmp, lnr[:, 0:1], bt, ALU.mult, ALU.add, [tmp_buf, Bf["lnr"], bbuf], [dst_buf])

            def transpose_2048(srcb, srcb_buf, dstT, dstT_buf):
                for half in range(2):
                    bk = next_pbank()
                    pb = banks[bk][:].bitcast(BF16)
                    for kk in range(8):
                        k = half * 8 + kk
                        O.tr(pb[:, kk * 128:(kk + 1) * 128], srcb[:, k * 128:(k + 1) * 128], identb[:], [srcb_buf, Bf["identb"]], [BK[bk]])
                    eng = "act" if half == 0 else "dve"
                    O.cp(eng, dstT[:, half * 8:(half + 1) * 8, :].rearrange("p k t -> p (k t)"), pb, [BK[bk]], [dstT_buf], partial=True)

            def flat(ap3):
                return ap3.rearrange("p h l -> p (h l)")

            def mlstm_tile(own, i):
                tri_b = tri[:].unsqueeze(1).to_broadcast([128, 8, 128])
                if own:
                    O.tt("pool", TriLF[:], tri_b, lf.unsqueeze(2).to_broadcast([128, 8, 128]), ALU.mult, [Bf["tri"], Bf["lf"]], [Bf["TriLF"]])
                    TL2 = flat(TriLF[:])
                    for half in range(2):
                        O.mm(banks[half][:], onesf[:], TL2[:, half * 512:(half + 1) * 512], True, True, [Bf["onesf"], Bf["TriLF"]], [BK[half]])
                    O.tt("dve", bias8, gx[:, 0:8], bc, ALU.subtract, [Bf["gx"], Bf["bc"]], [Bf["bias8"]])
                    for h in range(8):
                        bk = h // 4; col = (h % 4) * 128
                        O.act(PT[:, h, :], banks[bk][:, col:col + 128], AF.Exp, [BK[bk], Bf["bias8"]], [Bf["PT"]],
                              bias=bias8[:, h:h + 1], scale=1.0, partial=True)
                    O.tt("pool", PT[:], PT[:], tri_b, ALU.mult, [Bf["PT"], Bf["tri"]], [Bf["PT"]])
                    for h in range(8):
                        bk = 2 + h // 4; col = (h % 4) * 128
                        O.mm(banks[bk][:, col:col + 128], KT[:, h, :], QT[:, h, :], True, True, [Bf["KT"], Bf["QT"]], [BK[bk]])
                    for half in range(2):
                        O.tt("dve", flat(sw[:, half * 4:(half + 1) * 4, :]), flat(PT[:, half * 4:(half + 1) * 4, :]), banks[2 + half][:], ALU.mult,
                             [Bf["PT"], BK[2 + half]], [Bf["sw"]], partial=True)
                    for half in range(2):
                        O.act(flat(EB[:, half * 4:(half + 1) * 4, :]), banks[half][:], AF.Exp, [BK[half]], [Bf["EB"]], partial=True)
                    O.tt("pool", QsT[:], QT[:], EB[:], ALU.mult, [Bf["QT"], Bf["EB"]], [Bf["QsT"]])
                    for h in range(8):
                        bk = 4 + h // 4; col = (h % 4) * 128
                        O.mm(banks[bk][:, col:col + 128], sw[:, h, :], Vt[:, h, :], True, False, [Bf["sw"], Bf["Vt"]], [BK[bk]])
                        O.mm(banks[bk][:, col:col + 128], QsT[:, h, :], CTb[:, h, :], False, True, [Bf["QsT"], Bf["CTb"]], [BK[bk]])
                    for h in range(8):
                        O.mm(banks[7][:, 32 + h:33 + h], sw[:, h, :], onesb[:, 0:1], True, False, [Bf["sw"], Bf["onesb"]], [BK[7]])
                        O.mm(banks[7][:, 32 + h:33 + h], QsT[:, h, :], nb[:, h:h + 1], False, True, [Bf["QsT"], Bf["nb"]], [BK[7]])
                    O.ts("dve", t8, banks[7][:, 32:40], -1.0, 1.0, ALU.mult, ALU.max, [BK[7]], [Bf["t8"]])
                    O.ts("dve", rr, banks[7][:, 32:40], 1.0, None, ALU.max, None, [BK[7]], [Bf["rr"]])
                    O.tt("dve", rr, rr, t8, ALU.max, [Bf["rr"], Bf["t8"]], [Bf["rr"]])
                    T.op("dve", lambda e: e.reciprocal(out=rr, in_=rr), [Bf["rr"]], [Bf["rr"]])
                    for h in range(8):
                        bk = 4 + h // 4; col = (h % 4) * 128
                        T.op("dve", (lambda h=h, bk=bk, col=col: (lambda e: e.bn_stats(out=stats[:, h, :], in_=banks[bk][:, col:col + 128])))(),
                             [BK[bk]], [Bf["stats"]], partial=True)
                    for h in range(8):
                        T.op("dve", (lambda h=h: (lambda e: e.bn_aggr(out=mv[:, h, :], in_=stats[:, h, :])))(), [Bf["stats"]], [Bf["mv"]], partial=True)
                    O.tt("dve", t8, rr, rr, ALU.mult, [Bf["rr"]], [Bf["t8"]])
                    O.tt("dve", t8, t8, mv[:, :, 1], ALU.mult, [Bf["t8"], Bf["mv"]], [Bf["t8"]])
                    O.act(t8, t8, AF.Ln, [Bf["t8"]], [Bf["t8"]], bias=EPS, scale=1.0)
                    O.act(t8, t8, AF.Exp, [Bf["t8"]], [Bf["t8"]], scale=-0.5)
                    O.tt("dve", sc8, t8, rr, ALU.mult, [Bf["t8"], Bf["rr"]], [Bf["sc8"]])
                    for h in range(8):
                        bk = 4 + h // 4; col = (h % 4) * 128
                        O.ts("dve", yn[:, h, :], banks[bk][:, col:col + 128], mv[:, h, 0:1], sc8[:, h:h + 1], ALU.subtract, ALU.mult,
                             [BK[bk], Bf["mv"], Bf["sc8"]], [Bf["yn"]], partial=True)
                    O.tt("pool", flat(yn[:]), flat(yn[:]), mhg[:], ALU.mult, [Bf["yn"], Bf["mhg"]], [Bf["yn"]])
                    O.tt("pool", ym[:], flat(yn[:]), sig[:], ALU.mult, [Bf["yn"], Bf["sig"]], [Bf["ym"]])
                    bk = next_pbank()
                    pb = banks[bk][:].bitcast(BF16)
                    for h in range(8):
                        O.tr(pb[:, h * 128:(h + 1) * 128], ym[:, h * 128:(h + 1) * 128], identb[:], [Bf["ym"], Bf["identb"]], [BK[bk]])
                    O.cp("act", flat(mixT[:, 8:16, :]), pb, [BK[bk]], [Bf["mixT"]], partial=True)
                O.tt("dve", wkt, banks[7][:, 24:32], bc, ALU.subtract, [BK[7], Bf["bc"]], [Bf["wkt"]])
                O.tt("dve", wkt, wkt, gx[:, 0:8], ALU.add, [Bf["wkt"], Bf["gx"]], [Bf["wkt"]])
                O.act(wk, wkt, AF.Exp, [Bf["wkt"]], [Bf["wk"]])
                O.cp("dve", wkb, wk, [Bf["wk"]], [Bf["wkb"]])
                O.act(dec, banks[7][:, 24:32], AF.Exp, [BK[7]], [Bf["dec"]])
                O.tt("pool", wkV[:], Vt[:], wk.unsqueeze(2).to_broadcast([128, 8, 128]), ALU.mult, [Bf["Vt"], Bf["wk"]], [Bf["wkV"]])
                for h in range(8):
                    bk = h // 4; col = (h % 4) * 128
                    O.mm(banks[bk][:, col:col + 128], Ktok[:, h, :], wkV[:, h, :], True, True, [Bf["Ktok"], Bf["wkV"]], [BK[bk]])
                for h in range(8):
                    O.mm(banks[7][:, 40 + h:41 + h], Ktok[:, h, :], wkb[:, h:h + 1], True, True, [Bf["Ktok"], Bf["wkb"]], [BK[7]])
                O.tt("pool", CT[:], CT[:], dec.unsqueeze(2).to_broadcast([128, 8, 128]), ALU.mult, [Bf["CT"], Bf["dec"]], [Bf["CT"]])
                for half in range(2):
                    O.tt("dve", flat(CT[:, half * 4:(half + 1) * 4, :]), flat(CT[:, half * 4:(half + 1) * 4, :]), banks[half][:], ALU.add,
                         [Bf["CT"], BK[half]], [Bf["CT"]], partial=True)
                O.cp("act", CTb[:], CT[:], [Bf["CT"]], [Bf["CTb"]])
                O.tt("dve", nT[:], nT[:], dec, ALU.mult, [Bf["nT"], Bf["dec"]], [Bf["nT"]])
                O.tt("dve", nT[:], nT[:], banks[7][:, 40:48], ALU.add, [Bf["nT"], BK[7]], [Bf["nT"]])
                O.cp("dve", nb[:], nT[:], [Bf["nT"]], [Bf["nb"]])

            own_idx = 0
            cgen = [conv_gen()]
            handoff = [False]
            H0B = [(h0b[:], Bf["h0b"]), (PT[:].rearrange("p h l -> p (h l)").bitcast(BF16), Bf["PT"])]
            H0T = [(h0T[:], Bf["h0T"]), (EB[:].rearrange("p h l -> p (h l)").bitcast(BF16).rearrange("p (k t) -> p k t", k=16), Bf["EB"])]

            def front(i):
                own = i >= NH
                st_ = 0 if own else (i % 2)
                hb_ap, hb_buf = H0B[st_]
                hT_ap, hT_buf = H0T[st_]
                xs = i % 2
                xtile = xt[xs]; xbuf = bufs[f"xt{xs}"]
                O.dma(xtile[:], xin[i * 128:(i + 1) * 128, :], (), [xbuf], xbuf)
                if own:
                    layernorm_tile(xtile[:], [xbuf], h0[:], Bf["h0"], g_in[:], b_in[:], Bf["g_in"], Bf["b_in"])
                    O.dma(h0d[(i - NH) * 128:(i - NH + 1) * 128, :], h0[:], [Bf["h0"]], [B_h0d], Bf["h0"], partial=True)
                    O.cp("act", hb_ap, h0[:], [Bf["h0"]], [hb_buf])
                else:
                    layernorm_tile(xtile[:], [xbuf], hb_ap, hb_buf, g_in[:], b_in[:], Bf["g_in"], Bf["b_in"], tmp=h0[:], tmp_buf=Bf["h0"])
                transpose_2048(hb_ap, hb_buf, hT_ap, hT_buf)

            front(0)
            for i in range(NT):
                own = i >= NH
                last_hist = (i == NH - 1)
                if i == NH:
                    load_const(g_1[:], ln1_g.partition_broadcast(128), "g_1"); load_const(b_1[:], ln1_b.partition_broadcast(128), "b_1")
                if own:
                    front(i)
                elif i + 1 < NH:
                    front(i + 1)
                h0T, h0Tb = H0T[0 if own else (i % 2)]

                for k in range(16):
                    O.mm(banks[7][:, 0:16], h0T[:, k, :], wgate[:, k, :], k == 0, k == 15, [h0Tb, Bf["wgate"]], [BK[7]])
                O.tt("dve", gx, banks[7][:, 0:16], bgate[:], ALU.add, [BK[7], Bf["bgate"]], [Bf["gx"]])
                if not own:
                    O.ts("dve", gx[:, 0:8], gx[:, 0:8], hmask[:, i:i + 1], None, ALU.add, None, [Bf["gx"], Bf["hmask"]], [Bf["gx"]])
                O.act(ef, gx[:, 8:16], AF.Exp, [Bf["gx"]], [Bf["ef"]], scale=-1.0)
                O.act(ef, ef, AF.Ln, [Bf["ef"]], [Bf["ef"]], bias=1.0, scale=1.0)
                O.ts("dve", lf, ef, -1.0, None, ALU.mult, None, [Bf["ef"]], [Bf["lf"]])
                def gate_tail():
                    O.mm(banks[7][:, 16:24], tri[:], lf, True, True, [Bf["tri"], Bf["lf"]], [BK[7]])
                    O.mm(banks[7][:, 24:32], onesf[:], lf, True, True, [Bf["onesf"], Bf["lf"]], [BK[7]])
                    O.cp("dve", bc, banks[7][:, 16:24], [BK[7]], [Bf["bc"]])

                if not own:
                    if i == NH - 1:
                        cgen[0] = drain(cgen[0])
                    else:
                        for _ in range(4):
                            cgen[0] = step(cgen[0])
                for (tg, c0) in plan[i]:
                    if (not own) and tg in ("k", "v"):
                        slot = wbig[:, :, c0 - 4096:c0 - 4096 + 512]; sbuf_ = Bf["wbig"]
                    else:
                        if not handoff[0]:
                            handoff[0] = True
                            for j in range(3):
                                bufs[f"wg{j}"].readers.update(Bf["wbig"].readers)
                                bufs[f"wg{j}"].writers.update(Bf["wbig"].writers)
                        slot, sbuf_ = wstream.next()
                    if tg in ("fC", "fh", "fB", "fq"):
                        bk = next_pbank()
                        for cc in range(4):
                            for k in range(16):
                                O.mm(banks[bk][:, cc * 128:(cc + 1) * 128], slot[:, k, cc * 128:(cc + 1) * 128], h0T[:, k, :],
                                     k == 0, k == 15, [sbuf_, h0Tb], [BK[bk]])
                        pv3 = banks[bk][:].rearrange("p (c t) -> p c t", c=4)
                        if tg == "fC":
                            ch0 = (c0 - 1024) // 128
                            O.cp("act", Cf[:, ch0:ch0 + 4, :], pv3, [BK[bk]], [Bf["Cf"]], partial=True)
                        elif tg == "fh":
                            ch0 = (c0 - 2048) // 128
                            O.tt("dve", zbuf[:, ch0:ch0 + 4, 2:130], pv3, Cf[:, ch0:ch0 + 4, :], ALU.mult, [BK[bk], Bf["Cf"]], [Bf["zbuf"]], partial=True)
                            if last_hist and c0 == 2560:
                                O.ts("pool", zbuf[:, :, 0:2], zbuf[:, :, 128:130], hvalid[:, 0:1], None, ALU.mult, None,
                                     [Bf["zbuf"], Bf["hvalid"]], [Bf["zbuf"]])
                        elif tg == "fB":
                            ch0 = c0 // 128
                            for cc in range(4):
                                c = ch0 + cc
                                O.ts("dve", acc[:], zbuf[:, c, 2:130], cw[:, 2, c:c + 1], cbias[:, c:c + 1], ALU.mult, ALU.add,
                                     [Bf["zbuf"], Bf["cw"], Bf["cbias"]], [Bf["acc"]])
                                O.stt("dve", acc[:], zbuf[:, c, 1:129], cw[:, 1, c:c + 1], acc[:], ALU.mult, ALU.add,
                                      [Bf["zbuf"], Bf["cw"], Bf["acc"]], [Bf["acc"]])
                                O.stt("dve", acc[:], zbuf[:, c, 0:128], cw[:, 0, c:c + 1], acc[:], ALU.mult, ALU.add,
                                      [Bf["zbuf"], Bf["cw"], Bf["acc"]], [Bf["acc"]])
                                O.tt("dve", mixT[:, c, :], banks[bk][:, cc * 128:(cc + 1) * 128], acc[:], ALU.mult, [BK[bk], Bf["acc"]], [Bf["mixT"]], partial=True)
                            if c0 == 512:
                                O.cp("pool", zbuf[:, :, 0:2], zbuf[:, :, 128:130], [Bf["zbuf"]], [Bf["zbuf"]])
                        elif tg == "fq":
                            ch0 = (c0 - 3072) // 128
                            O.cp("act", QT[:, ch0:ch0 + 4, :], pv3, [BK[bk]], [Bf["QT"]], partial=True)
                    elif tg == "k":
                        ch0 = (c0 - 4096) // 128
                        if own:
                            bk = next_pbank()
                            for cc in range(4):
                                for k in range(16):
                                    O.mm(banks[bk][:, cc * 128:(cc + 1) * 128], slot[:, k, cc * 128:(cc + 1) * 128], h0T[:, k, :],
                                         k == 0, k == 15, [sbuf_, h0Tb], [BK[bk]])
                            pv3 = banks[bk][:].rearrange("p (c t) -> p c t", c=4)
                            O.act(KT[:, ch0:ch0 + 4, :], pv3, AF.Copy, [BK[bk]], [Bf["KT"]], scale=KSCALE, partial=True)
                        bk = next_pbank()
                        for k in range(16):
                            O.mm(banks[bk][:], h0T[:, k, :], slot[:, k, :], k == 0, k == 15, [sbuf_, h0Tb], [BK[bk]])
                        O.act(Ktok[:, ch0:ch0 + 4, :].rearrange("p h d -> p (h d)"), banks[bk][:], AF.Copy, [BK[bk]], [Bf["Ktok"]], scale=KSCALE, partial=True)
                    elif tg == "v":
                        ch0 = (c0 - 5120) // 128
                        bk = next_pbank()
                        for k in range(16):
                            O.mm(banks[bk][:], h0T[:, k, :], slot[:, k, :], k == 0, k == 15, [sbuf_, h0Tb], [BK[bk]])
                        O.cp("dve", Vt[:, ch0:ch0 + 4, :].rearrange("p h d -> p (h d)"), banks[bk][:], [BK[bk]], [Bf["Vt"]], partial=True)
                    elif tg == "o":
                        cc0 = c0 - 6144
                        bk = next_pbank()
                        for k in range(16):
                            O.mm(banks[bk][:], h0T[:, k, :], slot[:, k, :], k == 0, k == 15, [sbuf_, h0Tb], [BK[bk]])
                        O.act(sig[:, cc0:cc0 + 512], banks[bk][:], AF.Exp, [BK[bk]], [Bf["sig"]], scale=-1.0, partial=True)
                        O.ts("pool", sig[:, cc0:cc0 + 512], sig[:, cc0:cc0 + 512], 1.0, None, ALU.add, None, [Bf["sig"]], [Bf["sig"]], partial=True)
                        if cc0 == 512:
                            T.op("dve", lambda e: e.reciprocal(out=sig[:], in_=sig[:]), [Bf["sig"]], [Bf["sig"]])
                            gate_tail()
                            mlstm_tile(True, i)
                    elif tg == "wo":
                        bk = next_pbank()
                        for k in range(16):
                            O.mm(banks[bk][:], mixT[:, k, :], slot[:, k, :], k == 0, k == 15, [sbuf_, Bf["mixT"]], [BK[bk]])
                        O.stt("dve", t1[:, c0:c0 + 512], h0[:, c0:c0 + 512], ALPHA, banks[bk][:], ALU.mult, ALU.add,
                              [Bf["h0"], BK[bk]], [Bf["t1"]], partial=True)
                        if c0 == 1536:
                            layernorm_tile(t1[:], [Bf["t1"]], h1[:], Bf["h1"], g_1[:], b_1[:], Bf["g_1"], Bf["b_1"])
                            O.dma(h1d[own_idx * 128:(own_idx + 1) * 128, :], h1[:], [Bf["h1"]], [B_h1d], Bf["h1"], partial=True)
                            if debug:
                                O.dma(dbg_d[own_idx * 128:(own_idx + 1) * 128, :], h1[:], [Bf["h1"]], [], Bf["h1"])
                            O.cp("act", h1b[:], h1[:], [Bf["h1"]], [Bf["h1b"]])
                            transpose_2048(h1b, Bf["h1b"], h1T, Bf["h1T"])
                            O.dma(h1Td[own_idx], h1T[:].rearrange("p k t -> p (k t)"), [Bf["h1T"]], [B_h1Td], Bf["h1T"], partial=True)
                    if tg == "v" and c0 == 5632 and not own:
                        gate_tail()
                        mlstm_tile(False, i)
                if own:
                    own_idx += 1

            final_barrier(T, O, None, bar, banks[7][0:1, 500:501])
            T.replay(outer, st)

        with ExitStack() as st:
            T = Tracker(nc)
            O = Ops(T)
            bufs = {}

            def sb(name, shape, dt):
                t = st.enter_context(nc.sbuf_tensor("sb_" + name, list(shape), dt))
                bufs[name] = Buf(name)
                return t

            NB = NO // 2
            banks = [st.enter_context(nc.psum_tensor(f"pbank{i}", [128, 512], F32)) for i in range(8)]
            BK = [Buf(f"pbank{i}") for i in range(8)]
            identb = sb("identb2", [128, 128], BF16); identf = sb("identf2", [128, 128], F32)
            bar = sb("bar2", [128, 16], F32)
            g_2 = sb("g_2", [128, D], F32); b_2 = sb("b_2", [128, D], F32)
            iota3 = sb("iota3", [128, 8, 128], BF16); iota16 = sb("iota16", [128, 16], F32)
            keysb = sb("keysb", [128, 16, 128], BF16)
            keysT = sb("keysT", [128, 16, 128], BF16)
            Bf = bufs

            def load_const(t, src, name):
                O.dma(t, src, (), [bufs[name]], bufs[name])
            load_const(identb[:], identb_d, "identb2"); load_const(identf[:], identf_d, "identf2")
            load_const(g_2[:], ln2_g.partition_broadcast(128), "g_2"); load_const(b_2[:], ln2_b.partition_broadcast(128), "b_2")
            load_const(iota3[:].rearrange("p t i -> p (t i)"), iota3_d[:, 0:1024], "iota3")
            load_const(iota16[:], iota16_d[:, 0:16], "iota16")

            h1T = [sb(f"h1T2_{i}", [128, 16, 256], BF16) for i in range(2)]
            wslots = [sb(f"wq{i}", [128, 16, 128], BF16) for i in range(2)]
            qT = sb("qT", [128, 16, 128], BF16)
            s2 = sb("s2", [128, 128], F32)
            sv = sb("sv", [128, 16, 16], F32); si = sb("si", [128, 16, 16], U32); sif = sb("sif", [128, 16, 16], F32)
            cand = sb("cand", [128, 8, 256], F32); c2 = sb("c2", [128, 256], F32)
            tv = sb("tv", [128, 8, 16], F32); ci = sb("ci", [128, 8, 16], U32)
            ca = sb("ca", [128, 8, 16], U32); cb_ = sb("cb_", [128, 8, 16], U32)
            caf = sb("caf", [128, 8, 16], F32); cbf = sb("cbf", [128, 8, 16], F32)
            oh = cand[:].rearrange("p h (a b) -> p h a b", a=16); bufs["oh"] = bufs["cand"]
            sel = sb("sel", [128, 3, 128], F32); selT = sb("selT", [128, 3, 256], F32)
            ee = sb("ee", [128, 8, 16], F32); zz = sb("zz", [128, 8], F32)
            Pb = [sb(f"Pb{i}", [128, 8, 128], BF16) for i in range(2)]
            Qb = [sb(f"Qb{i}", [128, 8, 128], BF16) for i in range(2)]
            Wsb = sb("Wsb", [128, 128, 256], BF16)
            WB = [Buf(f"WB{i}") for i in range(64)]
            uslots = [sb(f"us{i}", [128, 16, 128], BF16) for i in range(4)]
            vslots = [sb(f"vs{i}", [128, 2, 1024], BF16) for i in range(3)]
            ga = [sb(f"ga{i}", [128, 512], F32) for i in range(2)]
            t2 = [sb(f"t2_{i}", [128, D], F32) for i in range(2)]
            s_ = sb("s_", [128, 16, 128], F32)
            sm = sb("sm2", [128, 64], F32)

            keysf = t2[0][:].rearrange("p (a n) -> p a n", a=16); bufs["keysf"] = bufs["t2_0"]
            load_const(keysf, keys.rearrange("h p n c -> n (h p) c"), "keysf")
            O.memset("pool", bar[:], 0.0, [Bf["bar2"]])
            O.cp("dve", keysb[:], keysf, [Bf["keysf"]], [Bf["keysb"]])
            for half in range(2):
                bk = 6 + half
                pb = banks[bk][:].bitcast(BF16)
                for kk in range(8):
                    hp = half * 8 + kk
                    O.tr(pb[:, kk * 128:(kk + 1) * 128], keysb[:, hp, :], identb[:], [Bf["keysb"], Bf["identb2"]], [BK[bk]])
                O.cp("dve", keysT[:, half * 8:(half + 1) * 8, :].rearrange("p k t -> p (k t)"), pb, [BK[bk]], [Bf["keysT"]], partial=True)

            def smcol(name, c0, n):
                bufs[name] = Buf(name)
                return sm[:, c0:c0 + n]
            lnmv = smcol("lnmv", 0, 2); lnr = smcol("lnr", 2, 1); lnst = smcol("lnst", 4, 24)

            def layernorm_tile(src, src_bufs, dst, dst_buf, gt, bt, gbuf, bbuf, tmp=None, tmp_buf=None):
                for q in range(4):
                    T.op("dve", (lambda q=q: (lambda e: e.bn_stats(out=lnst[:, q * 6:(q + 1) * 6], in_=src[:, q * 512:(q + 1) * 512])))(),
                         src_bufs, [Bf["lnst"]], partial=(q > 0))
                T.op("dve", lambda e: e.bn_aggr(out=lnmv, in_=lnst), [Bf["lnst"]], [Bf["lnmv"]])
                O.act(lnr, lnmv[:, 1:2], AF.Ln, [Bf["lnmv"]], [Bf["lnr"]], bias=EPS, scale=1.0)
                O.act(lnr, lnr, AF.Exp, [Bf["lnr"]], [Bf["lnr"]], scale=-0.5)
                if tmp is None:
                    tmp, tmp_buf = dst, dst_buf
                O.stt("dve", tmp, src, lnmv[:, 0:1], gt, ALU.subtract, ALU.mult, src_bufs + [Bf["lnmv"], gbuf], [tmp_buf])
                O.stt("dve", dst, tmp, lnr[:, 0:1], bt, ALU.mult, ALU.add, [tmp_buf, Bf["lnr"], bbuf], [dst_buf])

            Wq_v = Wq_b
            qloads = []
            uloads = []
            vloads = []
            for blk in range(NB):
                for tile in range(2):
                    for c0 in range(0, 2048, 128):
                        qloads.append((lambda c0=c0: (lambda slot: [(slot[:, :, :], Wq_v[c0 // 128])]))())
                for g in range(NG):
                    uloads.append((lambda g=g: (lambda slot: [(slot[:].rearrange("p k j -> p (k j)"), uT_d[g])]))())
                for half in range(2):
                    for gp in range(64):
                        vloads.append((lambda gp=gp, half=half: (lambda slot: [(slot[:, :, :], vbh_d[half][gp * 256:(gp + 1) * 256, :].rearrange("(g j) d -> j g d", j=128))]))())
            qstream = Stream(O, wslots, [bufs[f"wq{i}"] for i in range(2)], qloads)
            ustream = Stream(O, uslots, [bufs[f"us{i}"] for i in range(4)], uloads)
            vstream = Stream(O, vslots, [bufs[f"vs{i}"] for i in range(3)], vloads, eng="pool")

            iota16_b = iota16[:].unsqueeze(1).unsqueeze(1).to_broadcast([128, 8, 16, 16])

            def sel_gen(blk):
                hb = blk % 2
                hT = h1T[hb]; hTb = Bf[f"h1T2_{hb}"]
                for tile in range(2):
                    ti = blk * 2 + tile
                    O.dma(hT[:, :, tile * 128:(tile + 1) * 128], h1Td[ti].rearrange("p (k t) -> p k t", k=16), (), [hTb], hTb, partial=True)
                yield
                for tile in range(2):
                    tsl = slice(tile * 128, (tile + 1) * 128)
                    for grp in range(16):
                        slot, sbuf_ = qstream.next()
                        bk = 6 + (grp % 2)
                        for k in range(16):
                            O.mm(banks[bk][:, 0:128], slot[:, k, :], hT[:, k, tsl], k == 0, k == 15, [sbuf_, hTb], [BK[bk]])
                        O.cp("act", qT[:, grp, :], banks[bk][:, 0:128], [BK[bk]], [Bf["qT"]], partial=True)
                        if grp % 2 == 1:
                            yield
                    for q4 in range(4):
                        bk = 6 + (q4 % 2)
                        for a in range(4):
                            hp = q4 * 4 + a
                            O.mm(banks[bk][:, a * 128:(a + 1) * 128], qT[:, hp, :], keysT[:, hp, :], True, True, [Bf["qT"], Bf["keysT"]], [BK[bk]])
                        O.cp("act", s_[:, q4 * 4:(q4 + 1) * 4, :].rearrange("p a n -> p (a n)"), banks[bk][:], [BK[bk]], [Bf["s_"]], partial=True)
                    yield
                    for hp in range(16):
                        T.op("dve", (lambda hp=hp: (lambda e: e.max(out=sv[:, hp, 0:8], in_=s_[:, hp, :])))(), [Bf["s_"]], [Bf["sv"]], partial=True)
                        T.op("dve", (lambda hp=hp: (lambda e: e.match_replace(out=s2[:], in_to_replace=sv[:, hp, 0:8], in_values=s_[:, hp, :], imm_value=-1e30)))(),
                             [Bf["s_"], Bf["sv"]], [Bf["s2"]])
                        T.op("dve", (lambda hp=hp: (lambda e: e.max(out=sv[:, hp, 8:16], in_=s2[:])))(), [Bf["s2"]], [Bf["sv"]], partial=True)
                        T.op("dve", (lambda hp=hp: (lambda e: e.max_index(out=si[:, hp, 0:8], in_max=sv[:, hp, 0:8], in_values=s_[:, hp, :])))(),
                             [Bf["s_"], Bf["sv"]], [Bf["si"]], partial=True)
                        T.op("dve", (lambda hp=hp: (lambda e: e.max_index(out=si[:, hp, 8:16], in_max=sv[:, hp, 8:16], in_values=s_[:, hp, :])))(),
                             [Bf["s_"], Bf["sv"]], [Bf["si"]], partial=True)
                        yield
                    O.cp("dve", sif[:], si[:], [Bf["si"]], [Bf["sif"]])
                    sv4 = sv[:].rearrange("p (h two) a -> p h two a", two=2)
                    sif4 = sif[:].rearrange("p (h two) a -> p h two a", two=2)
                    O.tt("dve", cand[:].rearrange("p h (a b) -> p h a b", a=16),
                         sv4[:, :, 0, :].unsqueeze(3).to_broadcast([128, 8, 16, 16]),
                         sv4[:, :, 1, :].unsqueeze(2).to_broadcast([128, 8, 16, 16]), ALU.add, [Bf["sv"]], [Bf["cand"]])
                    yield
                    for h in range(8):
                        T.op("dve", (lambda h=h: (lambda e: e.max(out=tv[:, h, 0:8], in_=cand[:, h, :])))(), [Bf["cand"]], [Bf["tv"]], partial=True)
                        T.op("dve", (lambda h=h: (lambda e: e.match_replace(out=c2[:], in_to_replace=tv[:, h, 0:8], in_values=cand[:, h, :], imm_value=-1e30)))(),
                             [Bf["cand"], Bf["tv"]], [Bf["c2"]])
                        T.op("dve", (lambda h=h: (lambda e: e.max(out=tv[:, h, 8:16], in_=c2[:])))(), [Bf["c2"]], [Bf["tv"]], partial=True)
                        T.op("dve", (lambda h=h: (lambda e: e.max_index(out=ci[:, h, 0:8], in_max=tv[:, h, 0:8], in_values=cand[:, h, :])))(),
                             [Bf["cand"], Bf["tv"]], [Bf["ci"]], partial=True)
                        T.op("dve", (lambda h=h: (lambda e: e.max_index(out=ci[:, h, 8:16], in_max=tv[:, h, 8:16], in_values=cand[:, h, :])))(),
                             [Bf["cand"], Bf["tv"]], [Bf["ci"]], partial=True)
                        yield
                    T.op("dve", lambda e: e.tensor_single_scalar(out=ca[:], in_=ci[:], scalar=4, op=ALU.logical_shift_right), [Bf["ci"]], [Bf["ca"]])
                    T.op("dve", lambda e: e.tensor_single_scalar(out=cb_[:], in_=ci[:], scalar=15, op=ALU.bitwise_and), [Bf["ci"]], [Bf["cb_"]])
                    O.cp("dve", caf[:], ca[:], [Bf["ca"]], [Bf["caf"]])
                    O.cp("dve", cbf[:], cb_[:], [Bf["cb_"]], [Bf["cbf"]])
                    yield
                    for which, idxf in ((0, caf), (1, cbf)):
                        O.tt("dve", oh, iota16_b, idxf[:].unsqueeze(3).to_broadcast([128, 8, 16, 16]), ALU.is_equal,
                             [Bf["iota16"], Bf["caf" if which == 0 else "cbf"]], [Bf["oh"]])
                        yield
                        O.tt("pool", oh, oh, sif4[:, :, which, :].unsqueeze(2).to_broadcast([128, 8, 16, 16]), ALU.mult,
                             [Bf["oh"], Bf["sif"]], [Bf["oh"]])
                        T.op("dve", (lambda which=which: (lambda e: e.tensor_reduce(out=sel[:, which, :].rearrange("p (h k) -> p h k", h=8), in_=oh, axis=AX.X, op=ALU.add)))(),
                             [Bf["oh"]], [Bf["sel"]], partial=True)
                        yield
                    O.tt("dve", ee[:], tv[:], tv[:, :, 0:1].to_broadcast([128, 8, 16]), ALU.subtract, [Bf["tv"]], [Bf["ee"]])
                    O.act(ee[:], ee[:], AF.Exp, [Bf["ee"]], [Bf["ee"]])
                    T.op("dve", lambda e: e.tensor_reduce(out=zz[:], in_=ee[:], axis=AX.X, op=ALU.add), [Bf["ee"]], [Bf["zz"]])
                    T.op("dve", lambda e: e.reciprocal(out=zz[:], in_=zz[:]), [Bf["zz"]], [Bf["zz"]])
                    O.tt("dve", sel[:, 2, :].rearrange("p (h k) -> p h k", h=8), ee[:], zz[:].unsqueeze(2).to_broadcast([128, 8, 16]), ALU.mult,
                         [Bf["ee"], Bf["zz"]], [Bf["sel"]], partial=True)
                    yield
                    for w3 in range(3):
                        O.tr(banks[6][:, w3 * 128:(w3 + 1) * 128], sel[:, w3, :], identf[:], [Bf["sel"], Bf["identf2"]], [BK[6]])
                    O.cp("dve", selT[:, :, tsl], banks[6][:, 0:384].rearrange("p (w t) -> p w t", w=3), [BK[6]], [Bf["selT"]], partial=True)
                    yield

            def expand(blk):
                wb_rot = 0
                for tb in range(32):
                    pq = tb % 2
                    tk0 = tb * 8
                    i2b = selT[:, 1, tk0:tk0 + 8].unsqueeze(2).to_broadcast([128, 8, 128])
                    O.tt("dve", Qb[pq][:], iota3[:], i2b, ALU.is_equal, [Bf["iota3"], Bf["selT"]], [Bf[f"Qb{pq}"]])
                    for tl in range(8):
                        tk = tk0 + tl
                        O.ts("dve", Pb[pq][:, tl, :], iota3[:, 0, :], selT[:, 0, tk:tk + 1], selT[:, 2, tk:tk + 1], ALU.is_equal, ALU.mult,
                             [Bf["iota3"], Bf["selT"]], [Bf[f"Pb{pq}"]], partial=(tl > 0))
                    for t4 in range(2):
                        bk = 4 + (wb_rot % 4)
                        wb_rot += 1
                        for tt_ in range(4):
                            tl = t4 * 4 + tt_
                            O.mm(banks[bk][:, tt_ * 128:(tt_ + 1) * 128], Qb[pq][:, tl, :], Pb[pq][:, tl, :], True, True,
                                 [Bf[f"Qb{pq}"], Bf[f"Pb{pq}"]], [BK[bk]])
                        t0 = tk0 + t4 * 4
                        O.cp("act", Wsb[:, :, t0:t0 + 4].rearrange("j g t -> j t g"), banks[bk][:].rearrange("p (t g) -> p t g", t=4),
                             [BK[bk]], WB, partial=True)

            def step(gen):
                if gen is not None:
                    try:
                        next(gen)
                    except StopIteration:
                        return None
                return gen

            def drain(gen):
                while gen is not None:
                    gen = step(gen)

            drain(sel_gen(0))
            expand(0)
            for blk in range(NB):
                hb = blk % 2
                hT = h1T[hb]; hTb = Bf[f"h1T2_{hb}"]
                gen = sel_gen(blk + 1) if blk + 1 < NB else None
                for tile in range(2):
                    ti = blk * 2 + tile
                    O.dma(t2[tile][:], h1d[ti * 128:(ti + 1) * 128, :], (), [Bf[f"t2_{tile}"]], Bf[f"t2_{tile}"])

                def A_step(gp):
                    bkA = 4 + (gp % 2)
                    for gi in range(2):
                        us, ub = ustream.next()
                        for k in range(16):
                            O.mm(banks[bkA][:, gi * 256:(gi + 1) * 256], us[:, k, :], hT[:, k, :], k == 0, k == 15, [ub, hTb], [BK[bkA]])

                def V_step(gp, first, last):
                    vs, vbuf = vstream.next()
                    for gi in range(2):
                        g = gp * 2 + gi
                        for tile in range(2):
                            for dq in range(2):
                                O.mm(banks[tile * 2 + dq][:], Wsb[:, g, tile * 128:(tile + 1) * 128], vs[:, gi, dq * 512:(dq + 1) * 512],
                                     first and gi == 0, last and gi == 1, [WB[gp], vbuf], [BK[tile * 2 + dq]])

                A_step(0)
                for gp in range(64):
                    if gp + 1 < 64:
                        A_step(gp + 1)
                    bkA = 4 + (gp % 2)
                    sl = gp % 2
                    O.act(ga[sl][:], banks[bkA][:], AF.Gelu, [BK[bkA]], [Bf[f"ga{sl}"]])
                    wv = Wsb[:, gp * 2:(gp + 1) * 2, :].rearrange("p g t -> p (g t)")
                    O.tt("dve", wv, ga[sl][:], wv, ALU.mult, [Bf[f"ga{sl}"], WB[gp]], [WB[gp]])
                    V_step(gp, gp == 0, gp == 63)
                    gen = step(gen)
                for tile in range(2):
                    for dq in range(2):
                        cs = slice(dq * 512, (dq + 1) * 512)
                        O.stt("dve", t2[tile][:, cs], t2[tile][:, cs], ALPHA, banks[tile * 2 + dq][:], ALU.mult, ALU.add,
                              [Bf[f"t2_{tile}"], BK[tile * 2 + dq]], [Bf[f"t2_{tile}"]], partial=True)
                for gp in range(64):
                    V_step(gp, gp == 0, gp == 63)
                    gen = step(gen)
                drain(gen)
                for tile in range(2):
                    ti = blk * 2 + tile
                    for dq in range(2):
                        cs = slice(1024 + dq * 512, 1024 + (dq + 1) * 512)
                        O.stt("dve", t2[tile][:, cs], t2[tile][:, cs], ALPHA, banks[tile * 2 + dq][:], ALU.mult, ALU.add,
                              [Bf[f"t2_{tile}"], BK[tile * 2 + dq]], [Bf[f"t2_{tile}"]], partial=True)
                    layernorm_tile(t2[tile][:], [Bf[f"t2_{tile}"]], t2[tile][:], Bf[f"t2_{tile}"], g_2[:], b_2[:], Bf["g_2"], Bf["b_2"])
                    O.dma(out_d[ti * 128:(ti + 1) * 128, :], t2[tile][:], [Bf[f"t2_{tile}"]], [], Bf[f"t2_{tile}"])
                if blk + 1 < NB:
                    expand(blk + 1)
            final_barrier(T, O, None, bar, banks[7][0:1, 500:501])
            T.replay(outer, st)

    return nc


NH_FULL = 96
NO_FULL = 32


def _consts():
    bf = ml_dtypes.bfloat16
    c = {}
    c["identb"] = np.eye(128, dtype=np.float32).astype(bf)
    c["identf"] = np.eye(128, dtype=np.float32)
    c["tri"] = np.triu(np.ones((128, 128), dtype=np.float32))
    c["onesf"] = np.ones((128, 128), dtype=np.float32)
    c["iota3"] = np.broadcast_to(np.arange(128, dtype=np.float32)[None, None, :], (128, 16, 128)).reshape(128, 16 * 128).astype(bf)
    c["iota16"] = np.ascontiguousarray(np.broadcast_to(np.arange(16, dtype=np.float32)[None, None, None, :], (128, 8, 16, 16)).reshape(128, 2048))
    return c


def _weights(inp):
    f = lambda a: np.ascontiguousarray(np.asarray(a, dtype=np.float32))
    return {
        "ln_in_g": f(inp["ln_in_g"]), "ln_in_b": f(inp["ln_in_b"]), "w_in": f(inp["w_in"][0]), "b_gate": f(inp["b_gate"][0]),
        "conv_w": f(np.asarray(inp["conv_w"][0]).reshape(3, 8, 128).transpose(2, 0, 1).reshape(128, 24)), "conv_b": f(np.asarray(inp["conv_b"][0]).reshape(8, 128).T), "mh_norm_g": f(inp["mh_norm_g"][0]), "w_out": f(inp["w_out"][0]),
        "ln1_g": f(inp["ln1_g"][0]), "ln1_b": f(inp["ln1_b"][0]), "peer_wq": f(inp["peer_wq"][0]), "peer_keys": f(inp["peer_keys"][0]),
        "peer_u": f(inp["peer_u"][0]), "peer_v": f(inp["peer_v"][0]), "ln2_g": f(inp["ln2_g"][0]), "ln2_b": f(inp["ln2_b"][0]),
    }


def core_inputs(x_seq, start, n_own_tok, NH, common):
    hist_tok = NH * 128
    xin = np.zeros((hist_tok + n_own_tok, D), dtype=np.float32)
    real = min(start, hist_tok)
    if real > 0:
        xin[hist_tok - real:hist_tok] = x_seq[start - real:start]
    xin[hist_tok:] = x_seq[start:start + n_own_tok]
    hm = np.zeros((128, max(NH, 1)), dtype=np.float32)
    ndummy = (hist_tok - real) // 128
    hm[:, :ndummy] = NEG
    m = dict(common)
    m["xin"] = xin
    m["hmask"] = hm
    m["hvalid"] = np.full((128, 1), 1.0 if real > 0 else 0.0, dtype=np.float32)
    return m


_NC_CACHE = {}


def kernel(**inputs):
    x = np.asarray(inputs["x"], dtype=np.float32)
    Bn, S, _ = x.shape
    common = _weights(inputs)
    common.update(_consts())
    ncores = 8
    per = (Bn * S) // ncores
    segs = S // per
    NO = per // 128
    NH = (segs - 1) * NO
    key = (NH, NO)
    if key not in _NC_CACHE:
        _NC_CACHE[key] = build(NH, NO)
    nc = _NC_CACHE[key]
    in_maps = []
    for c in range(ncores):
        b, sg = divmod(c, segs)
        in_maps.append(core_inputs(x[b], sg * per, per, NH, common))
    res = run_bass_kernel_spmd(nc, in_maps, core_ids=list(range(ncores)))
    out = np.empty((Bn, S, D), dtype=np.float32)
    for c in range(ncores):
        b, sg = divmod(c, segs)
        out[b, sg * per:(sg + 1) * per] = res.results[c]["out"]
    return out
```
